# Optimizing a Trainium2 kernel written in Bass

```python
import jax, jax.numpy as jnp
from jax import lax
import numpy as np

D_MODEL = 1024
BATCH = 8
SEQ = 2048
DEPTH = 4
DEC_BATCH = 128
DEC_SEQ = 4
PAST_LEN = 16384
PAGE_SIZE = 128

A_HEADS = 4
A_DK = D_MODEL // A_HEADS
A_DV = D_MODEL // A_HEADS
A_WIDTH = A_HEADS * A_DV
A_CHUNK = 128
B_WIDTH = D_MODEL
B_GROUPS = 4
B_CHUNK = 128
C_WIDTH = D_MODEL
C_BLOCKS = 8
C_BLK = C_WIDTH // C_BLOCKS
CONV_W = 4
LRU_C = 8.0
N_BRANCH = 3
ALPHA = float((2 * DEPTH) ** 0.25)
BETA = float((8 * DEPTH) ** -0.25)
LN_EPS = 1e-5
IN_WIDTH = 5 * A_WIDTH + 2 * A_HEADS + 3 * B_WIDTH + 2 * C_WIDTH + N_BRANCH * D_MODEL
F_OFF = 5 * A_WIDTH + A_HEADS

kernel_name = 'hybrid_mlstm_chunkmlp_rglru_step'


def _split_points():
    sizes = (A_WIDTH,) * 5 + (A_HEADS,) * 2 + (B_WIDTH,) * 3 + (C_WIDTH,) * 2 + (D_MODEL,) * N_BRANCH
    return [int(s) for s in np.cumsum(sizes)[:-1]]


def layer_norm(x, g, b):
    xf = x.astype(jnp.float32)
    mu = xf.mean(-1, keepdims=True)
    var = jnp.mean(jnp.square(xf - mu), -1, keepdims=True)
    return ((xf - mu) * lax.rsqrt(var + LN_EPS) * g.astype(jnp.float32) + b.astype(jnp.float32)).astype(x.dtype)


def head_norm(h, g):
    mu = h.mean(-1, keepdims=True)
    var = jnp.mean(jnp.square(h - mu), -1, keepdims=True)
    y = (h - mu) * lax.rsqrt(var + LN_EPS) * g.reshape(A_HEADS, A_DV).astype(jnp.float32)
    return y.reshape(h.shape[0], h.shape[1], A_WIDTH)


def mlstm_chunk(carry, inp):
    c0, n0, m0 = carry
    q, k, v, it, lf = inp
    L = q.shape[2]
    b = jnp.cumsum(lf, axis=-1)
    g = it - b
    m = b + jnp.maximum(m0[..., None], lax.cummax(g, axis=2))
    causal = jnp.tril(jnp.ones((L, L), dtype=bool))
    logd = b[..., :, None] + g[..., None, :] - m[..., :, None]
    dmat = jnp.exp(jnp.where(causal, logd, -jnp.inf))
    s = jnp.einsum('bhtd,bhsd->bhts', q, k) * dmat
    inter = jnp.exp(b + m0[..., None] - m)
    num = inter[..., None] * jnp.einsum('bhtd,bhde->bhte', q, c0) + jnp.einsum('bhts,bhse->bhte', s, v)
    den = inter * jnp.einsum('bhtd,bhd->bht', q, n0) + s.sum(-1)
    h = num / jnp.maximum(jnp.abs(den), jnp.exp(-m))[..., None]
    m_new = m[..., -1]
    w = jnp.exp(b[..., -1:] + g - m_new[..., None])
    decay = jnp.exp(b[..., -1] + m0 - m_new)
    c_new = decay[..., None, None] * c0 + jnp.einsum('bhs,bhsd,bhse->bhde', w, k, v)
    n_new = decay[..., None] * n0 + jnp.einsum('bhs,bhsd->bhd', w, k)
    return (c_new, n_new, m_new), h


def mlstm_seq(q, k, v, it, lf, c0, n0, m0):
    bsz, T = q.shape[0], q.shape[1]
    L = min(A_CHUNK, T)
    nch = T // L

    def to_chunks(a):
        a = jnp.swapaxes(a, 1, 2)
        a = a.reshape(a.shape[:2] + (nch, L) + a.shape[3:])
        return jnp.moveaxis(a, 2, 0)

    (c, n, m), h = lax.scan(mlstm_chunk, (c0, n0, m0),
                            (to_chunks(q), to_chunks(k), to_chunks(v), to_chunks(it), to_chunks(lf)))
    h = jnp.moveaxis(h, 0, 2).reshape(bsz, A_HEADS, T, A_DV)
    return jnp.swapaxes(h, 1, 2), c, n, m


def chunk_spatial_gate(u, v, ws, bs):
    bsz, T, _ = v.shape
    Tp = -(-T // B_CHUNK) * B_CHUNK
    vp = jnp.pad(v, ((0, 0), (0, Tp - T), (0, 0)))
    vc = vp.reshape(bsz, Tp // B_CHUNK, B_CHUNK, B_GROUPS, B_WIDTH // B_GROUPS)
    wm = ws * jnp.tril(jnp.ones((B_CHUNK, B_CHUNK), ws.dtype))
    mixed = jnp.einsum('gts,bnsgc->bntgc', wm, vc) + bs.T[None, None, :, :, None]
    mixed = mixed.reshape(bsz, Tp, B_WIDTH)[:, :T]
    return u * mixed


def causal_conv(x, buf, w, b):
    T = x.shape[1]
    xp = jnp.concatenate([buf.astype(x.dtype), x], axis=1)
    y = b + w[0] * xp[:, 0:T]
    for j in range(1, CONV_W):
        y = y + w[j] * xp[:, j:j + T]
    return y, xp[:, -(CONV_W - 1):]


def rg_lru(x, h0, wa, ba, wx, bx, lam, reset_first):
    bsz, T, _ = x.shape
    xf = x.astype(jnp.float32)
    xb = xf.reshape(bsz, T, C_BLOCKS, C_BLK)
    r = jax.nn.sigmoid(jnp.einsum('btni,nij->btnj', xb, wa.astype(jnp.float32)).reshape(bsz, T, C_WIDTH) + ba)
    i = jax.nn.sigmoid(jnp.einsum('btni,nij->btnj', xb, wx.astype(jnp.float32)).reshape(bsz, T, C_WIDTH) + bx)
    log_a = LRU_C * r * jax.nn.log_sigmoid(lam.astype(jnp.float32))
    a = jnp.exp(log_a)
    mult = jnp.sqrt(-jnp.expm1(2.0 * log_a))
    if reset_first:
        mult = mult.at[:, 0].set(1.0)
    bterm = mult * i * xf
    if h0 is not None:
        bterm = bterm.at[:, 0].add(a[:, 0] * h0.astype(jnp.float32))

    def combine(lhs, rhs):
        a1, b1 = lhs
        a2, b2 = rhs
        return a1 * a2, a2 * b1 + b2

    _, h = lax.associative_scan(combine, (a, bterm), axis=1)
    return h, h[:, -1]


def mixer_layer(x, p, c0, n0, m0, conv_buf, h0, reset_first):
    bsz, T, _ = x.shape
    proj = jnp.einsum('btd,de->bte', x, p['w_in']) + p['b_in']
    (q, k, v_a, o_a, z_a, i_pre, f_pre, u_b, v_b, z_b, x_c, z_c,
     g_a, g_b, g_c) = jnp.split(proj, _split_points(), axis=-1)
    f32 = jnp.float32
    qh = q.astype(f32).reshape(bsz, T, A_HEADS, A_DK)
    kh = k.astype(f32).reshape(bsz, T, A_HEADS, A_DK) * (A_DK ** -0.5)
    vh = v_a.astype(f32).reshape(bsz, T, A_HEADS, A_DV)
    it = i_pre.astype(f32)
    lf = jax.nn.log_sigmoid(f_pre.astype(f32))
    h_a, c_new, n_new, m_new = mlstm_seq(qh, kh, vh, it, lf, c0, n0, m0)
    h_a = jax.nn.sigmoid(o_a.astype(f32)).reshape(bsz, T, A_HEADS, A_DV) * h_a
    y_a = (head_norm(h_a, p['mlstm_norm_g']) * jax.nn.silu(z_a.astype(f32))).astype(x.dtype)
    vn = layer_norm(v_b, p['gmlp_ln_g'], p['gmlp_ln_b'])
    y_b = chunk_spatial_gate(u_b, vn, p['gmlp_ws'], p['gmlp_bs']) * jax.nn.silu(z_b)
    xc, conv_new = causal_conv(x_c, conv_buf, p['lru_conv_w'], p['lru_conv_b'])
    h_c, h_last = rg_lru(xc, h0, p['lru_wa'], p['lru_ba'], p['lru_wx'], p['lru_bx'], p['lru_lambda'], reset_first)
    y_c = (h_c * jax.nn.silu(z_c.astype(f32))).astype(x.dtype)
    merged = (jax.nn.sigmoid(g_a) * (y_a @ p['w_proj_a'])
              + jax.nn.sigmoid(g_b) * (y_b @ p['w_proj_b'])
              + jax.nn.sigmoid(g_c) * (y_c @ p['w_proj_c']))
    out = merged @ p['w_out']
    x_new = layer_norm(ALPHA * x + out, p['ln_g'], p['ln_b'])
    return x_new, c_new, n_new, m_new, conv_new, h_last, vn


def setup_inputs(seed: int = 0) -> dict:
    key = jax.random.key(seed)
    ks = jax.random.split(key, 32)
    f32 = jnp.float32
    nrm = lambda k, shape, s: jax.random.normal(k, shape, f32) * s
    b_in = nrm(ks[8], (DEPTH, IN_WIDTH), 0.01)
    f_bias = jnp.linspace(3.0, 6.0, A_HEADS, dtype=f32)[None, :] + nrm(ks[9], (DEPTH, A_HEADS), 0.01)
    b_in = b_in.at[:, F_OFF:F_OFF + A_HEADS].set(f_bias)
    u = jax.random.uniform(ks[10], (DEPTH, C_WIDTH), f32, 0.9, 0.999)
    s = u ** (1.0 / LRU_C)
    lam = jnp.log(s) - jnp.log1p(-s)
    return {
        'x_prompt': nrm(ks[0], (BATCH, SEQ, D_MODEL), 1.0),
        'x_sample': nrm(ks[1], (DEC_BATCH, DEC_SEQ, D_MODEL), 1.0),
        'state_mlstm_c': nrm(ks[2], (DEPTH, DEC_BATCH, A_HEADS, A_DK, A_DV), 0.1),
        'state_mlstm_n': nrm(ks[3], (DEPTH, DEC_BATCH, A_HEADS, A_DK), 0.5),
        'state_mlstm_m': nrm(ks[4], (DEPTH, DEC_BATCH, A_HEADS), 1.0),
        'state_lru_conv': nrm(ks[5], (DEPTH, DEC_BATCH, CONV_W - 1, C_WIDTH), 1.0),
        'state_lru_h': nrm(ks[6], (DEPTH, DEC_BATCH, C_WIDTH), 0.5),
        'w_in': nrm(ks[7], (DEPTH, D_MODEL, IN_WIDTH), D_MODEL ** -0.5),
        'b_in': b_in,
        'mlstm_norm_g': 1.0 + nrm(ks[11], (DEPTH, A_WIDTH), 0.01),
        'gmlp_ln_g': 1.0 + nrm(ks[12], (DEPTH, B_WIDTH), 0.01),
        'gmlp_ln_b': nrm(ks[13], (DEPTH, B_WIDTH), 0.01),
        'gmlp_ws': nrm(ks[14], (DEPTH, B_GROUPS, B_CHUNK, B_CHUNK), 0.5 * B_CHUNK ** -0.5),
        'gmlp_bs': 1.0 + nrm(ks[15], (DEPTH, B_GROUPS, B_CHUNK), 0.01),
        'lru_conv_w': nrm(ks[16], (DEPTH, CONV_W, C_WIDTH), CONV_W ** -0.5),
        'lru_conv_b': nrm(ks[17], (DEPTH, C_WIDTH), 0.01),
        'lru_wa': nrm(ks[18], (DEPTH, C_BLOCKS, C_BLK, C_BLK), C_BLK ** -0.5),
        'lru_ba': nrm(ks[19], (DEPTH, C_WIDTH), 0.01),
        'lru_wx': nrm(ks[20], (DEPTH, C_BLOCKS, C_BLK, C_BLK), C_BLK ** -0.5),
        'lru_bx': nrm(ks[21], (DEPTH, C_WIDTH), 0.01),
        'lru_lambda': lam,
        'w_proj_a': nrm(ks[22], (DEPTH, A_WIDTH, D_MODEL), BETA * A_WIDTH ** -0.5),
        'w_proj_b': nrm(ks[23], (DEPTH, B_WIDTH, D_MODEL), BETA * B_WIDTH ** -0.5),
        'w_proj_c': nrm(ks[24], (DEPTH, C_WIDTH, D_MODEL), BETA * C_WIDTH ** -0.5),
        'w_out': nrm(ks[25], (DEPTH, D_MODEL, D_MODEL), BETA * D_MODEL ** -0.5),
        'ln_g': 1.0 + nrm(ks[26], (DEPTH, D_MODEL), 0.01),
        'ln_b': nrm(ks[27], (DEPTH, D_MODEL), 0.01),
    }


def reference(x_prompt, x_sample, state_mlstm_c, state_mlstm_n, state_mlstm_m, state_lru_conv, state_lru_h,
              w_in, b_in, mlstm_norm_g, gmlp_ln_g, gmlp_ln_b, gmlp_ws, gmlp_bs, lru_conv_w, lru_conv_b,
              lru_wa, lru_ba, lru_wx, lru_bx, lru_lambda, w_proj_a, w_proj_b, w_proj_c, w_out, ln_g, ln_b):
    f32 = jnp.float32
    bp = x_prompt.shape[0]
    xp, xs = x_prompt, x_sample
    cp_l, np_l, mp_l, convp_l, hp_l = [], [], [], [], []
    cs_l, ns_l, ms_l, convs_l, hs_l, vs_l = [], [], [], [], [], []
    for l in range(DEPTH):
        p = {'w_in': w_in[l], 'b_in': b_in[l], 'mlstm_norm_g': mlstm_norm_g[l],
             'gmlp_ln_g': gmlp_ln_g[l], 'gmlp_ln_b': gmlp_ln_b[l], 'gmlp_ws': gmlp_ws[l], 'gmlp_bs': gmlp_bs[l],
             'lru_conv_w': lru_conv_w[l], 'lru_conv_b': lru_conv_b[l], 'lru_wa': lru_wa[l], 'lru_ba': lru_ba[l],
             'lru_wx': lru_wx[l], 'lru_bx': lru_bx[l], 'lru_lambda': lru_lambda[l],
             'w_proj_a': w_proj_a[l], 'w_proj_b': w_proj_b[l], 'w_proj_c': w_proj_c[l],
             'w_out': w_out[l], 'ln_g': ln_g[l], 'ln_b': ln_b[l]}
        xp, c, n, m, conv, h, _ = mixer_layer(
            xp, p,
            jnp.zeros((bp, A_HEADS, A_DK, A_DV), f32), jnp.zeros((bp, A_HEADS, A_DK), f32),
            jnp.zeros((bp, A_HEADS), f32), jnp.zeros((bp, CONV_W - 1, C_WIDTH), xp.dtype), None, True)
        cp_l.append(c); np_l.append(n); mp_l.append(m); convp_l.append(conv); hp_l.append(h)
        xs, c, n, m, conv, h, vn = mixer_layer(
            xs, p,
            state_mlstm_c[l].astype(f32), state_mlstm_n[l].astype(f32), state_mlstm_m[l].astype(f32),
            state_lru_conv[l], state_lru_h[l], False)
        cs_l.append(c); ns_l.append(n); ms_l.append(m); convs_l.append(conv); hs_l.append(h); vs_l.append(vn)
    return (xp, xs,
            jnp.stack(cp_l), jnp.stack(np_l), jnp.stack(mp_l), jnp.stack(convp_l), jnp.stack(hp_l),
            jnp.stack(cs_l), jnp.stack(ns_l), jnp.stack(ms_l), jnp.stack(convs_l), jnp.stack(hs_l),
            jnp.stack(vs_l))
```

```python
import numpy as np
from contextlib import ExitStack
import concourse.bass as bass
import concourse.mybir as mybir
from concourse.bass_utils import run_bass_kernel_spmd

F32 = mybir.dt.float32
BF16 = mybir.dt.bfloat16
AF = mybir.ActivationFunctionType
ALU = mybir.AluOpType
AX = mybir.AxisListType
ENG = ("pe", "act", "dve", "pool", "sp")

NL = 4
D = 1024
KC = 8
NTP = 2048
NS = 64
NT = NTP + NS
NB = 16
NH = 4
DK = 256
IN_W = 13320
TGS = [(0, 512), (512, 512), (1024, 512), (1536, 512), (2048, 64)]
TTS = [(i * 128, 128) for i in range(16)] + [(2048, 64)]
OFF = dict(q=0, k=1024, v=2048, o=3072, za=4096, gate=5120, ub=5128, vb=6152, zb=7176, xc=8200, zc=9224,
           ga=10248, gb=11272, gc=12296)
BLK = ["q", "k", "v", "o", "za", "ub", "vb", "zb", "xc", "zc", "ga", "gb", "gc"]
ALPHA = float((2 * NL) ** 0.25)
LN_EPS = 1e-5
NSLOT = 10
AHEAD = 5
ARENA_F32 = 13312


class Tok:
    __slots__ = ("eng", "sem", "val")

    def __init__(self, eng, sem, val):
        self.eng, self.sem, self.val = eng, sem, val


class Res:
    __slots__ = ("name", "lw", "rd", "const", "dsem")

    def __init__(self, name, const=False):
        self.name, self.lw, self.rd, self.const, self.dsem = name, None, {}, const, None


class Prog:
    def __init__(self, nc, stack):
        self.nc, self.stack = nc, stack
        self.ops = {e: [] for e in ENG}
        self.sem = {e: stack.enter_context(nc.semaphore("sem_" + e)) for e in ENG}
        self.cnt = {e: 0 for e in ENG}
        self.cur = {e: Tok(e, self.sem[e], None) for e in ENG}
        self.waited = {}
        self.nds = 0
        self.free_ds = []
        self.store_toks = {}
        self.marks = []

    def mark(self, name):
        self.marks.append((name, sum(1 for o in self.ops["pe"] if o[1] is not None)))

    def _dsem(self, res):
        if res.dsem is None:
            if self.free_ds:
                res.dsem = self.free_ds.pop()
            else:
                self.nds += 1
                res.dsem = [self.stack.enter_context(self.nc.semaphore("ds_%d" % self.nds)), 0]
        return res.dsem

    def _wait(self, eng, t, waits):
        key = (eng, id(t.sem))
        if self.waited.get(key, 0) >= t.val:
            return
        self.waited[key] = t.val
        waits.append((t.sem, t.val))

    def _deps(self, eng, reads, writes, inorder):
        deps = []
        for r in reads:
            if r.lw is not None:
                deps.append((r.lw, "raw"))
        for w in writes:
            if w.lw is not None:
                deps.append((w.lw, "waw"))
            for t in w.rd.values():
                deps.append((t, "war"))
        waits = []
        for t, kind in deps:
            if t.eng == eng and inorder and (kind != "raw" or eng == "pe"):
                continue
            if t.val is None:
                raise RuntimeError("dependency on unsignaled op (%s)" % t.eng)
            self._wait(eng, t, waits)
        return waits

    def _commit(self, tok, reads, writes):
        for r in reads:
            if not r.const:
                r.rd[id(tok.sem)] = tok
        for w in writes:
            w.lw = tok
            w.rd = {}

    def op(self, eng, fn, reads=(), writes=(), sig=None):
        if sig is None:
            sig = eng != "pe"
        waits = self._deps(eng, reads, writes, True)
        tok = self.cur[eng]
        self.ops[eng].append((waits, fn, (self.sem[eng], 1) if sig else None))
        self._commit(tok, reads, writes)
        if sig:
            self.cnt[eng] += 1
            tok.val = self.cnt[eng]
            self.cur[eng] = Tok(eng, self.sem[eng], None)

    def dma(self, q, out, in_, reads=(), writes=(), sres=None, final=False):
        waits = self._deps(q, reads, writes, False)
        ds = self._dsem(sres)
        ds[1] += 1
        tok = Tok(None, ds[0], 16 * ds[1])
        self.ops[q].append((waits, lambda e: e.dma_start(out=out, in_=in_), (ds[0], 16)))
        self._commit(tok, reads, writes)
        if final:
            self.store_toks[id(tok.sem)] = tok

    def barrier(self, dma_res=()):
        toks = [Tok(e, self.sem[e], self.cnt[e]) for e in ENG if self.cnt[e] > 0]
        for r in dma_res:
            if r.dsem is not None and r.dsem[1] > 0:
                toks.append(Tok(None, r.dsem[0], 16 * r.dsem[1]))
        for e in ENG:
            waits = []
            for t in toks:
                if t.eng == e:
                    continue
                self._wait(e, t, waits)
            if waits:
                self.ops[e].append((waits, None, None))

    def finish(self):
        waits = [(t.sem, t.val) for t in self.store_toks.values()]
        self.ops["sp"].append((waits, None, None))

    def emit(self):
        def mk(name):
            def body(e):
                for waits, fn, inc in self.ops[name]:
                    for sem, val in waits:
                        e.wait_ge(sem, val)
                    if fn is None:
                        continue
                    ins = fn(e)
                    if inc is not None:
                        ins.then_inc(inc[0], inc[1])
            return body

        with self.nc.Block() as block:
            block.tensor(mk("pe"))
            block.scalar(mk("act"))
            block.vector(mk("dve"))
            block.gpsimd(mk("pool"))
            block.sync(mk("sp"))


def bc_last(ap, n):
    pat = [list(x) for x in ap.ap]
    return bass.AP(ap.tensor, ap.offset, pat + [[0, n]])


def bc_mid(ap, n):
    pat = [list(x) for x in ap.ap]
    return bass.AP(ap.tensor, ap.offset, [pat[0], [0, n]] + pat[1:])


def pbc(ap_row, n):
    pat = [list(x) for x in ap_row.ap]
    return bass.AP(ap_row.tensor, ap_row.offset, [[0, n]] + pat[-1:])


def build(depth=NL):
    nc = bass.Bass("TRN2", target_bir_lowering=False)

    def din(name, shape):
        return nc.dram_tensor(name, list(shape), F32, kind="ExternalInput").ap()

    def dout(name, shape):
        return nc.dram_tensor(name, list(shape), F32, kind="ExternalOutput").ap()

    x_all = din("x_all", [NT, D])
    w_in = din("w_in", [NL, D, IN_W])
    w_p = {"a": din("w_pa", [NL, D, D]), "b": din("w_pb", [NL, D, D]), "c": din("w_pc", [NL, D, D])}
    w_out = din("w_out", [NL, D, D])
    b_in = din("b_in", [NL, IN_W])
    bpm_d = din("bpm", [128, NL * 13 * 8])
    bgate_d = din("bgate", [4, NL * 2])
    gpm_d = din("gpm", [128, NL * 8])
    cw_d = din("cw", [128, NL * 8 * 4])
    cvec_d = din("cvec", [128, 4 * NL * 8])
    wa_d = din("lru_wa", [NL, 8, 128, 128])
    wx_d = din("lru_wx", [NL, 8, 128, 128])
    gln_d = din("gln", [NL, 2, D])
    fln_d = din("fln", [NL, 2, D])
    gws_d = din("gws", [NL, 4, 128, 128])
    gws_s_d = din("gws_s", [NL, 64, 4 * 64])
    gbs_d = din("gbs", [NL, 4 * 128])
    gbs_s_d = din("gbs_s", [NL, 4 * 64])
    st_c = din("st_c", [NL, NB, NH, DK, DK])
    st_n = din("st_n", [NL, NB, NH, DK])
    st_m = din("st_m", [4, NL * NB])
    st_conv = din("st_conv", [NL, 48, D])
    st_h = din("st_h", [NL, NB, D])
    c_ident = din("c_ident", [128, 128])
    c_mask = din("c_mask", [128, 128])
    c_mask_s = din("c_mask_s", [64, 64])
    c_a0s = din("c_a0s", [4, 64])
    c_hmask = din("c_hmask", [4, 128])
    c_colmask = din("c_colmask", [128, NB * 64])
    c_rowmask = din("c_rowmask", [64, NB])
    c_onesrow = din("c_onesrow", [128, 128])

    y_all = dout("y_all", [NT, D])
    o_cp = dout("o_cp", [NL, NH, DK, DK])
    o_np = dout("o_np", [NL, NH, DK])
    o_mp = dout("o_mp", [4, NL])
    o_convh = dout("o_convh", [NL, 68, D])
    o_cs = dout("o_cs", [NL, NB, NH, DK, DK])
    o_ns = dout("o_ns", [NL, NB, NH, DK])
    o_ms = dout("o_ms", [4, NL * NB])
    o_vs = dout("o_vs", [NL, NS, D])
    xres = nc.dram_tensor("xres", [NT, D], F32, kind="Internal").ap()
    r_xres = [Res("xres%d" % i) for i in range(17)]

    with ExitStack() as st:
        P = Prog(nc, st)

        def sb(name, shape, dt=F32):
            return st.enter_context(nc.sbuf_tensor("s_" + name, list(shape), dt))

        def mm(out, lhsT, rhs, start, stop, reads, writes, sig=None):
            P.op("pe", lambda e: e.matmul(out, lhsT=lhsT, rhs=rhs, start=start, stop=stop), reads, writes,
                 sig=(stop if sig is None else sig))

        def tr(out, in_, ident, reads, writes, sig):
            P.op("pe", lambda e: e.transpose(out, in_, ident), reads, writes, sig=sig)

        def act(out, in_, func, reads, writes, bias=None, scale=None):
            kw = {}
            if bias is not None:
                kw["bias"] = bias
            if scale is not None:
                kw["scale"] = scale
            P.op("act", lambda e: e.activation(out=out, in_=in_, func=func, **kw), reads, writes)

        def tt(eng, out, in0, in1, op, reads, writes):
            P.op(eng, lambda e: e.tensor_tensor(out=out, in0=in0, in1=in1, op=op), reads, writes)

        def ts(eng, out, in0, s1, op0, reads, writes, s2=None, op1=None):
            if op1 is None:
                P.op(eng, lambda e: e.tensor_scalar(out=out, in0=in0, scalar1=s1, scalar2=None, op0=op0), reads, writes)
            else:
                P.op(eng, lambda e: e.tensor_scalar(out=out, in0=in0, scalar1=s1, scalar2=s2, op0=op0, op1=op1), reads, writes)

        def stt(out, in0, scalar, in1, op0, op1, reads, writes):
            P.op("dve", lambda e: e.scalar_tensor_tensor(out=out, in0=in0, scalar=scalar, in1=in1, op0=op0, op1=op1), reads, writes)

        def cp(eng, out, in_, reads, writes):
            if eng == "act":
                act(out, in_, AF.Identity, reads, writes)
            else:
                P.op(eng, lambda e: e.tensor_copy(out=out, in_=in_), reads, writes)

        def memset(eng, ap, val, writes):
            P.op(eng, lambda e: e.memset(ap, val), (), writes)

        def bnstats(out, in_, reads, writes):
            P.op("dve", lambda e: e.bn_stats(out=out, in_=in_), reads, writes)

        def bnaggr(out, in_, reads, writes):
            P.op("dve", lambda e: e.bn_aggr(out=out, in_=in_), reads, writes)

        def recip(out, in_, reads, writes):
            P.op("dve", lambda e: e.reciprocal(out=out, in_=in_), reads, writes)

        def scan(out, d0, d1, init, op0, op1, reads, writes):
            P.op("dve", lambda e: e.tensor_tensor_scan(out=out, data0=d0, data1=d1, initial=init, op0=op0, op1=op1), reads, writes)

        def treduce(out, in_, op, reads, writes):
            P.op("dve", lambda e: e.tensor_reduce(out=out, in_=in_, axis=AX.X, op=op), reads, writes)

        xT = sb("xT", [128, KC, NT], BF16)
        r_xT = [Res("xT%d" % i) for i in range(5)]
        ybr = sb("ybr", [128, KC, NT], BF16)
        r_ybr = [[Res("ybr%d_%d" % (c, g)) for g in range(5)] for c in range(KC)]
        mrg = sb("mrg", [128, KC, NT], BF16)
        r_mrg = [Res("mrg%d" % g) for g in range(5)]
        vn_v = mrg[:].rearrange("p k t -> p (k t)")
        wsl = [sb("wsl%d" % i, [128, KC, 256], BF16) for i in range(NSLOT)]
        r_wsl = [Res("wsl%d" % i) for i in range(NSLOT)]
        arena = sb("arena", [128, ARENA_F32], F32)
        ident_f = sb("ident_f", [128, 128], F32); r_idf = Res("idf", True)
        ident_b = sb("ident_b", [128, 128], BF16); r_idb = Res("idb", True)
        mask_f = sb("mask_f", [128, 128], F32); r_mask = Res("mask", True)
        mask_s = sb("mask_s", [64, 64], F32); r_mask_s = Res("mask_s", True)
        a0s = sb("a0s", [4, 64], F32); r_a0s = Res("a0s", True)
        ones4 = sb("ones4", [4, 512], F32); r_ones4 = Res("ones4", True)
        ones4x = sb("ones4x", [4, 128], F32); r_ones4x = Res("ones4x", True)
        hmask = sb("hmask", [4, 128], F32); r_hmask = Res("hmask", True)
        colmask = sb("colmask", [128, NB, 64], BF16); r_colmask = Res("colmask", True)
        rowmask = sb("rowmask", [64, NB], F32); r_rowmask = Res("rowmask", True)
        onesrow = sb("onesrow", [128, 128], BF16); r_onesrow = Res("onesrow", True)
        bpm = sb("bpm", [128, NL, 13, 8], F32); r_bpm = Res("bpm", True)
        kbias = sb("kbias", [128, NL, 8], F32); r_kbias = Res("kbias", True)
        bgate = sb("bgate", [4, NL, 2], F32); r_bgate = Res("bgate", True)
        nbf = sb("nbf", [4, NL], F32); r_nbf = Res("nbf", True)
        gpm = sb("gpm", [128, NL, 8], F32); r_gpm = Res("gpm", True)
        gpmh = sb("gpmh", [128, NL, 8], F32); r_gpmh = Res("gpmh", True)
        cw = sb("cw", [128, NL, 8, 4], F32); r_cw = Res("cw", True)
        cvec = sb("cvec", [128, 4, NL, 8], F32); r_cvec = Res("cvec", True)
        cl = sb("cl", [128, 2, NL, 8], F32); r_cl = Res("cl", True)
        wg = sb("wg", [128, NL, KC, 8], BF16); r_wg = Res("wg", True)
        m0s = sb("m0s", [4, NL, NB], F32); r_m0s = Res("m0s", True)
        wthr = sb("wthr", [128, 17, 8], F32); r_wthr = Res("wthr")
        decbc = sb("decbc", [128, 128], F32); r_decbc = Res("decbc")
        mcall = sb("mcall", [4, 33], F32); r_mcall = Res("mcall")
        mcs = sb("mcs", [4, 3, NB], F32); r_mcs = Res("mcs")
        dec4 = sb("dec4", [4, 32], F32); r_dec4 = Res("dec4")
        decbd = sb("decbd", [4, 4, 32], F32); r_decbd = Res("decbd")
        mpo = sb("mpo", [4, 1], F32); r_mpo = Res("mpo")
        bbl = sb("bbl", [4, 2], F32); r_bbl = Res("bbl")
        eps_t = sb("eps_t", [128, 1], F32); r_eps = Res("eps", True)

        ps = [st.enter_context(nc.psum_tensor("ps%d" % i, [128, 1024], F32)) for i in range(4)]
        psb = [p.bitcast(BF16) for p in ps]
        r_ps = [Res("psb%d" % i) for i in range(8)]

        def bank(b):
            return ps[b // 2][:, (b % 2) * 512:(b % 2) * 512 + 512]

        def bank_bf(b):
            return psb[b // 2][:, (b % 2) * 1024:(b % 2) * 1024 + 1024]

        rot = {"mm": 0, "aux": 0}

        def next_mm():
            rot["mm"] = (rot["mm"] + 1) % 4
            return rot["mm"]

        def next_aux():
            rot["aux"] = (rot["aux"] + 1) % 2
            return 4 + rot["aux"]

        class Arena:
            def __init__(self):
                self.off = 0
                self.res = []

            def reset(self):
                P.barrier(self.res)
                for r in self.res:
                    if r.dsem is not None:
                        P.free_ds.append(r.dsem)
                        r.dsem = None
                self.off = 0
                self.res = []

            def f32(self, shape, name):
                n = int(np.prod(shape[1:]))
                ap = arena[0:shape[0], self.off:self.off + n]
                self.off += n
                assert self.off <= ARENA_F32, "arena overflow %d" % self.off
                r = Res(name)
                self.res.append(r)
                if len(shape) == 3:
                    ap = ap.rearrange("p (a b) -> p a b", a=shape[1])
                return ap, r

            def bf(self, shape, name):
                n = int(np.prod(shape[1:]))
                n32 = (n + 1) // 2
                ap = arena[0:shape[0], self.off:self.off + n32].bitcast(BF16)
                self.off += n32
                assert self.off <= ARENA_F32, "arena overflow %d" % self.off
                r = Res(name)
                self.res.append(r)
                ap = ap[:, 0:n]
                if len(shape) == 3:
                    ap = ap.rearrange("p (a b) -> p a b", a=shape[1])
                return ap, r

        A = Arena()

        wlist = []

        def wsrc(ap2d):
            return ap2d.rearrange("(kc p) n -> p kc n", p=128)

        for l in range(depth):
            for j in range(4):
                wlist.append(("vb%d" % j, wsrc(w_in[l, :, OFF["vb"] + j * 256:OFF["vb"] + (j + 1) * 256])))
            for j in range(4):
                wlist.append(("ub%d" % j, wsrc(w_in[l, :, OFF["ub"] + j * 256:OFF["ub"] + (j + 1) * 256])))
                wlist.append(("zb%d" % j, wsrc(w_in[l, :, OFF["zb"] + j * 256:OFF["zb"] + (j + 1) * 256])))
            for j in range(4):
                wlist.append(("pb%d" % j, wsrc(w_p["b"][l, :, j * 256:(j + 1) * 256])))
                wlist.append(("gb%d" % j, wsrc(w_in[l, :, OFF["gb"] + j * 256:OFF["gb"] + (j + 1) * 256])))
            for h in range(4):
                for nm in ("q", "k", "v", "o", "za"):
                    wlist.append(("%s%d" % (nm, h), wsrc(w_in[l, :, OFF[nm] + h * 256:OFF[nm] + (h + 1) * 256])))
            for j in range(4):
                wlist.append(("pa%d" % j, wsrc(w_p["a"][l, :, j * 256:(j + 1) * 256])))
                wlist.append(("ga%d" % j, wsrc(w_in[l, :, OFF["ga"] + j * 256:OFF["ga"] + (j + 1) * 256])))
            for j in range(4):
                wlist.append(("xc%d" % j, wsrc(w_in[l, :, OFF["xc"] + j * 256:OFF["xc"] + (j + 1) * 256])))
                wlist.append(("zc%d" % j, wsrc(w_in[l, :, OFF["zc"] + j * 256:OFF["zc"] + (j + 1) * 256])))
            for j in range(4):
                wlist.append(("pc%d" % j, wsrc(w_p["c"][l, :, j * 256:(j + 1) * 256])))
                wlist.append(("gc%d" % j, wsrc(w_in[l, :, OFF["gc"] + j * 256:OFF["gc"] + (j + 1) * 256])))
            for j in range(4):
                wlist.append(("wo%d" % j, wsrc(w_out[l, :, j * 256:(j + 1) * 256])))
        wst = {"issued": 0, "next": 0}

        def wneed(names):
            i0 = wst["next"]
            outl = []
            for k, nm in enumerate(names):
                assert wlist[i0 + k][0] == nm, (wlist[i0 + k][0], nm)
            last = min(i0 + len(names) - 1 + AHEAD, len(wlist) - 1)
            while wst["issued"] <= last:
                j = wst["issued"]
                s = j % NSLOT
                P.dma("pool", wsl[s][:], wlist[j][1], writes=[r_wsl[s]], sres=r_wsl[s])
                wst["issued"] += 1
            for k in range(len(names)):
                s = (i0 + k) % NSLOT
                outl.append((wsl[s], r_wsl[s]))
            wst["next"] = i0 + len(names)
            return outl

        P.dma("sp", ident_f[:], c_ident, writes=[r_idf], sres=r_idf)
        P.dma("pool", ident_b[:], c_ident, writes=[r_idb], sres=r_idb)
        P.dma("sp", mask_f[:], c_mask, writes=[r_mask], sres=r_mask)
        P.dma("sp", mask_s[:], c_mask_s, writes=[r_mask_s], sres=r_mask_s)
        P.dma("sp", a0s[:], c_a0s, writes=[r_a0s], sres=r_a0s)
        P.dma("sp", hmask[:], c_hmask, writes=[r_hmask], sres=r_hmask)
        P.dma("pool", colmask[:].rearrange("p a b -> p (a b)"), c_colmask, writes=[r_colmask], sres=r_colmask)
        P.dma("sp", rowmask[:], c_rowmask, writes=[r_rowmask], sres=r_rowmask)
        P.dma("pool", onesrow[:], c_onesrow, writes=[r_onesrow], sres=r_onesrow)
        P.dma("sp", bpm[:].rearrange("p a b c -> p (a b c)"), bpm_d, writes=[r_bpm], sres=r_bpm)
        P.dma("sp", bgate[:].rearrange("p a b -> p (a b)"), bgate_d, writes=[r_bgate], sres=r_bgate)
        P.dma("sp", gpm[:].rearrange("p a b -> p (a b)"), gpm_d, writes=[r_gpm], sres=r_gpm)
        P.dma("sp", cw[:].rearrange("p a b c -> p (a b c)"), cw_d, writes=[r_cw], sres=r_cw)
        P.dma("sp", cvec[:].rearrange("p a b c -> p (a b c)"), cvec_d, writes=[r_cvec], sres=r_cvec)
        P.dma("sp", m0s[:].rearrange("p a b -> p (a b)"), st_m, writes=[r_m0s], sres=r_m0s)
        for l in range(depth):
            P.dma("pool", wg[:, l, :, :], w_in[l, :, OFF["gate"]:OFF["gate"] + 8].rearrange("(kc p) n -> p kc n", p=128),
                  writes=[r_wg], sres=r_wg)
        memset("dve", ones4[:], 1.0, [r_ones4])
        memset("dve", ones4x[:], 1.0, [r_ones4x])
        memset("dve", eps_t[:], LN_EPS, [r_eps])
        memset("dve", mcall[:], 0.0, [r_mcall])
        ts("dve", kbias[:], bpm[:, :, 1, :], 1.0 / 16.0, ALU.mult, [r_bpm], [r_kbias])
        ts("dve", nbf[:], bgate[:, :, 1], -1.0, ALU.mult, [r_bgate], [r_nbf])
        ts("dve", gpmh[:], gpm[:], 0.5, ALU.mult, [r_gpm], [r_gpmh])
        act(cl[:, 0], cvec[:, 3], AF.Exp, [r_cvec], [r_cl], scale=-1.0)
        act(cl[:, 0], cl[:, 0], AF.Ln, [r_cl], [r_cl], bias=1.0)
        ts("dve", cl[:, 1], cl[:, 0], -16.0, ALU.mult, [r_cl], [r_cl])
        ts("dve", cl[:, 0], cl[:, 0], -8.0, ALU.mult, [r_cl], [r_cl])

        def tok_group_of(t0):
            return min(t0 // 512, 4)

        def tile_to_xT(src, r_src, ti, gscale=None):
            t0, tn = TTS[ti]
            g = tok_group_of(t0)
            for half in range(2):
                b = next_aux()
                for j in range(4):
                    kc = half * 4 + j
                    tr(bank(b)[:, j * 128:j * 128 + tn], src[0:tn, kc * 128:(kc + 1) * 128], ident_f[0:tn, 0:tn],
                       [r_src, r_idf], [r_ps[b]], sig=(j == 3))
                act(xT[:, half * 4:(half + 1) * 4, t0:t0 + tn],
                    bank(b).rearrange("p (a b) -> p a b", a=4)[:, :, 0:tn], AF.Identity, [r_ps[b]], [r_xT[g]])

        A.reset()
        xin = [A.f32([128, D], "xin%d" % i) for i in range(2)]
        for ti, (t0, tn) in enumerate(TTS):
            xa, rx = xin[ti % 2]
            P.dma("sp", xa[0:tn, :], x_all[t0:t0 + tn, :], writes=[rx], sres=rx)
            tile_to_xT(xa, rx, ti)

        for l in range(depth):
            last = (l == NL - 1)
            xsrc = x_all if l == 0 else xres

            A.reset()
            _gt = [A.f32([4, 512], "gt%d" % i) for i in range(3)]
            gt = [x[0] for x in _gt]
            r_gt = [x[1] for x in _gt]
            for g, (t0, tn) in enumerate(TGS):
                bi, bf_ = next_mm(), next_mm()
                for kc in range(KC):
                    mm(bank(bi)[0:4, 0:tn], wg[:, l, kc, 0:4], xT[:, kc, t0:t0 + tn], kc == 0, kc == KC - 1,
                       [r_wg, r_xT[g]], [r_ps[bi]])
                for kc in range(KC):
                    mm(bank(bf_)[0:4, 0:tn], wg[:, l, kc, 4:8], xT[:, kc, t0:t0 + tn], kc == 0, kc == KC - 1,
                       [r_wg, r_xT[g]], [r_ps[bf_]])
                it_, sp_, bb_ = gt[0][:, 0:tn], gt[1][:, 0:tn], gt[2][:, 0:tn]
                act(it_, bank(bi)[0:4, 0:tn], AF.Identity, [r_ps[bi], r_bgate], [r_gt[0]], bias=bgate[:, l, 0:1])
                act(sp_, bank(bf_)[0:4, 0:tn], AF.Exp, [r_ps[bf_], r_nbf], [r_gt[1]], bias=nbf[:, l:l + 1], scale=-1.0)
                act(sp_, sp_, AF.Ln, [r_gt[1]], [r_gt[1]], bias=1.0)
                if g < 4:
                    init = 0.0 if g == 0 else bbl[:, 0:1]
                    scan(bb_, ones4[:, 0:tn], sp_, init, ALU.mult, ALU.subtract, [r_ones4, r_gt[1], r_bbl], [r_gt[2]])
                    if g < 3:
                        cp("dve", bbl[:, 0:1], gt[2][:, tn - 1:tn], [r_gt[2]], [r_bbl])
                else:
                    scan(bb_, a0s[:], sp_, 0.0, ALU.mult, ALU.subtract, [r_a0s, r_gt[1]], [r_gt[2]])
                tt("dve", it_, it_, bb_, ALU.subtract, [r_gt[0], r_gt[2]], [r_gt[0]])
                if g < 4:
                    treduce(mcall[:, 17 + 4 * g:21 + 4 * g], gt[0][:, :].rearrange("p (c t) -> p c t", c=4), ALU.max, [r_gt[0]], [r_mcall])
                    scan(mcall[:, 1 + 4 * g:5 + 4 * g], ones4[:, 0:4], mcall[:, 17 + 4 * g:21 + 4 * g], mcall[:, 4 * g:4 * g + 1],
                         ALU.mult, ALU.max, [r_mcall, r_ones4], [r_mcall])
                    tt("dve", dec4[:, 4 * g:4 * g + 4], mcall[:, 4 * g:4 * g + 4], mcall[:, 4 * g + 1:4 * g + 5], ALU.subtract,
                       [r_mcall], [r_dec4])
                    mc_b = bc_last(mcall[:, 1 + 4 * g:5 + 4 * g], 128)
                    v3 = lambda a: a.rearrange("p (c t) -> p c t", c=4)
                    if g == 3:
                        tt("dve", mpo[:], gt[2][:, 511:512], mcall[:, 16:17], ALU.add, [r_gt[2], r_mcall], [r_mpo])
                        P.dma("sp", o_mp[:, l:l + 1], mpo[:], reads=[r_mpo], sres=r_mpo, final=True)
                else:
                    treduce(mcs[:, 0, :], gt[0][:, 0:64].rearrange("p (b t) -> p b t", t=4), ALU.max, [r_gt[0]], [r_mcs])
                    tt("dve", mcs[:, 1, :], mcs[:, 0, :], m0s[:, l, :], ALU.max, [r_mcs, r_m0s], [r_mcs])
                    tt("dve", dec4[:, 16:32], m0s[:, l, :], mcs[:, 1, :], ALU.subtract, [r_m0s, r_mcs], [r_dec4])
                    tt("dve", mcs[:, 2, :], gt[2][:, 0:64].rearrange("p (b t) -> p b t", t=4)[:, :, 3], mcs[:, 1, :], ALU.add,
                       [r_gt[2], r_mcs], [r_mcs])
                    P.dma("sp", o_ms[:, l * NB:(l + 1) * NB], mcs[:, 2, :], reads=[r_mcs], sres=r_mcs, final=True)
                    mc_b = bc_last(mcs[:, 1, :], 4)
                    v3 = lambda a: a.rearrange("p (c t) -> p c t", t=4)
                rd = [r_gt[0], r_gt[2], r_mcall, r_mcs]
                tt("dve", v3(sp_), v3(it_), mc_b, ALU.subtract, rd, [r_gt[1]])
                act(sp_, sp_, AF.Exp, [r_gt[1]], [r_gt[1]])
                tt("dve", v3(bb_), v3(bb_), mc_b, ALU.add, rd, [r_gt[2]])
                act(bb_, bb_, AF.Exp, [r_gt[2]], [r_gt[2]], scale=-1.0)
                ntile = 4 if g < 4 else 1
                b = next_aux()
                for c in range(ntile):
                    cn = 128 if g < 4 else 64
                    tr(bank(b)[0:cn, c * 8:c * 8 + 4], gt[1][:, c * 128:c * 128 + cn], ident_f[0:4, 0:4], [r_gt[1], r_idf], [r_ps[b]], False)
                    tr(bank(b)[0:cn, c * 8 + 4:c * 8 + 8], gt[2][:, c * 128:c * 128 + cn], ident_f[0:4, 0:4], [r_gt[2], r_idf], [r_ps[b]],
                       c == ntile - 1)
                cn = 128 if g < 4 else 64
                cp("dve", wthr[0:cn, 4 * g:4 * g + ntile, :], bank(b)[0:cn, 0:8 * ntile].rearrange("p (c k) -> p c k", k=8),
                   [r_ps[b]], [r_wthr])
            act(dec4[:], dec4[:], AF.Exp, [r_dec4], [r_dec4])
            tt("dve", decbd[:], bc_mid(dec4[:], 4), hmask[:].rearrange("p (a b) -> p a b", a=4), ALU.mult, [r_dec4, r_hmask], [r_decbd])
            b = next_aux()
            mm(bank(b)[:, 0:128], ones4x[:], decbd[:].rearrange("p a b -> p (a b)"), True, True, [r_ones4x, r_decbd], [r_ps[b]])
            cp("dve", decbc[:], bank(b)[:, 0:128], [r_ps[b]], [r_decbc])

            A.reset()
            lnbc, r_lnbc = A.f32([128, 2, D], "lnbc")
            vbrow, r_vbrow = A.bf([128, D], "vbrow")
            wsf, r_wsf = A.f32([128, 4, 128], "wsf")
            wmT, r_wmT = A.bf([128, 4, 128], "wmT")
            wss, r_wss = A.f32([64, 4, 64], "wss")
            mts, r_mts = A.bf([64, 4, 64], "mts")
            bsrow, r_bsrow = A.bf([128, 4 * 128], "bsrow")
            bsrow_s, r_bsrow_s = A.bf([128, 4 * 64], "bsrow_s")
            v32 = [A.f32([128, D], "v32_%d" % i) for i in range(2)]
            bst, r_bst = A.f32([128, 12], "bst")
            mv, r_mv = A.f32([128, 2], "mv")
            rstd, r_rstd = A.f32([128, 1], "rstd")
            szb = [A.f32([128, 512], "szb%d" % i) for i in range(2)]
            uzb = [A.f32([128, 512], "uzb%d" % i) for i in range(2)]
            vns, r_vns = A.bf([64, D], "vns")
            P.dma("sp", lnbc[:].rearrange("p a b -> p (a b)"), pbc(gln_d[l].rearrange("a b -> (a b)"), 128), writes=[r_lnbc], sres=r_lnbc)
            memset("pool", vbrow[:], 0.0, [r_vbrow])
            P.dma("pool", vbrow[0:1, :], b_in[l:l + 1, OFF["vb"]:OFF["vb"] + D], writes=[r_vbrow], sres=r_vbrow)
            memset("pool", bsrow[:], 0.0, [r_bsrow])
            P.dma("pool", bsrow[0:1, :], gbs_d[l:l + 1, :], writes=[r_bsrow], sres=r_bsrow)
            memset("pool", bsrow_s[:], 0.0, [r_bsrow_s])
            P.dma("pool", bsrow_s[0:1, :], gbs_s_d[l:l + 1, :], writes=[r_bsrow_s], sres=r_bsrow_s)
            P.dma("sp", wsf[:], gws_d[l].rearrange("g t s -> t g s"), writes=[r_wsf], sres=r_wsf)
            P.dma("sp", wss[:].rearrange("p a b -> p (a b)"), gws_s_d[l], writes=[r_wss], sres=r_wss)
            b = next_aux()
            for g4 in range(4):
                tr(bank(b)[:, g4 * 128:(g4 + 1) * 128], wsf[:, g4, :], ident_f[:], [r_wsf, r_idf], [r_ps[b]], g4 == 3)
            tt("dve", wmT[:], bank(b).rearrange("p (a b) -> p a b", a=4), bc_mid(mask_f[:], 4), ALU.mult, [r_ps[b], r_mask], [r_wmT])
            tt("dve", mts[:], wss[:], bc_mid(mask_s[:], 4), ALU.mult, [r_wss, r_mask_s], [r_mts])
            wv = wneed(["vb0", "vb1", "vb2", "vb3"])
            r_vn = r_mrg
            for ti, (t0, tn) in enumerate(TTS):
                g = tok_group_of(t0)
                va, rv = v32[ti % 2]
                pr = (ti % 2) * 2
                for j in range(4):
                    bq = pr + j // 2
                    o_ap = bank(bq)[0:tn, (j % 2) * 256:(j % 2) * 256 + 256]
                    for kc in range(KC):
                        mm(o_ap, xT[:, kc, t0:t0 + tn], wv[j][0][:, kc, :], kc == 0, False, [r_xT[g], wv[j][1]], [r_ps[bq]], sig=False)
                    mm(o_ap, onesrow[:, 0:tn], vbrow[:, j * 256:(j + 1) * 256], False, True, [r_onesrow, r_vbrow], [r_ps[bq]])
                for hf in range(2):
                    bnstats(bst[0:tn, hf * 6:hf * 6 + 6], bank(pr + hf)[0:tn, :], [r_ps[pr + hf]], [r_bst])
                bnaggr(mv[0:tn, :], bst[0:tn, :], [r_bst], [r_mv])
                act(rstd[0:tn, :], mv[0:tn, 1:2], AF.Sqrt, [r_mv, r_eps], [r_rstd], bias=eps_t[0:tn, :])
                recip(rstd[0:tn, :], rstd[0:tn, :], [r_rstd], [r_rstd])
                ts("dve", va[0:tn, :], ps[pr // 2][0:tn, :], mv[0:tn, 0:1], ALU.subtract, [r_ps[pr], r_ps[pr + 1], r_mv, r_rstd], [rv],
                   s2=rstd[0:tn, 0:1], op1=ALU.mult)
                tt("pool", va[0:tn, :], va[0:tn, :], lnbc[0:tn, 0, :], ALU.mult, [rv, r_lnbc], [rv])
                if ti < 16:
                    tt("pool", vn_v[0:tn, ti * 1024:(ti + 1) * 1024], va[0:tn, :], lnbc[0:tn, 1, :], ALU.add, [rv, r_lnbc], [r_vn[g]])
                else:
                    tt("pool", va[0:tn, :], va[0:tn, :], lnbc[0:tn, 1, :], ALU.add, [rv, r_lnbc], [rv])
                    P.dma("sp", o_vs[l], va[0:tn, :], reads=[rv], sres=rv, final=True)
                    cp("pool", vns[:, :], va[0:tn, :], [rv], [r_vns])
            for j in range(4):
                wu, wz = wneed(["ub%d" % j, "zb%d" % j])
                for cc in range(2):
                    c = 2 * j + cc
                    g4 = c // 2
                    for g, (t0, tn) in enumerate(TGS):
                        bu, bz, bm = next_mm(), next_mm(), next_aux()
                        for kc in range(KC):
                            mm(bank(bz)[:, 0:tn], wz[0][:, kc, cc * 128:(cc + 1) * 128], xT[:, kc, t0:t0 + tn], kc == 0, kc == KC - 1,
                               [wz[1], r_xT[g]], [r_ps[bz]])
                        for kc in range(KC):
                            mm(bank(bu)[:, 0:tn], wu[0][:, kc, cc * 128:(cc + 1) * 128], xT[:, kc, t0:t0 + tn], kc == 0, kc == KC - 1,
                               [wu[1], r_xT[g]], [r_ps[bu]])
                        if g < 4:
                            for ci in range(4):
                                ti = 4 * g + ci
                                o_ap = bank(bm)[:, ci * 128:(ci + 1) * 128]
                                mm(o_ap, vn_v[:, ti * 1024 + c * 128:ti * 1024 + (c + 1) * 128], wmT[:, g4, :], True, False,
                                   [r_vn[g], r_wmT], [r_ps[bm]], sig=False)
                                mm(o_ap, onesrow[:], bsrow[:, g4 * 128:(g4 + 1) * 128], False, True, [r_onesrow, r_bsrow], [r_ps[bm]],
                                   sig=(ci == 3))
                        else:
                            o_ap = bank(bm)[:, 0:64]
                            mm(o_ap, vns[:, c * 128:(c + 1) * 128], mts[:, g4, :], True, False,
                               [r_vns, r_mts], [r_ps[bm]], sig=False)
                            mm(o_ap, onesrow[:], bsrow_s[:, g4 * 64:(g4 + 1) * 64], False, True, [r_onesrow, r_bsrow_s], [r_ps[bm]])
                        sz, rsz = szb[g % 2]
                        uz, ruz = uzb[g % 2]
                        act(sz[:, 0:tn], bank(bz)[:, 0:tn], AF.Silu, [r_ps[bz], r_bpm], [rsz], bias=bpm[:, l, 7, c:c + 1])
                        stt(uz[:, 0:tn], bank(bu)[:, 0:tn], bpm[:, l, 5, c:c + 1], sz[:, 0:tn], ALU.add, ALU.mult,
                            [r_ps[bu], r_bpm, rsz], [ruz])
                        tt("dve", ybr[:, c, t0:t0 + tn], bank(bm)[:, 0:tn], uz[:, 0:tn], ALU.mult, [r_ps[bm], ruz], [r_ybr[c][g]])

            def proj_merge(br, first):
                A.reset()
                sgb = [A.f32([128, 512], "sg%d" % i) for i in range(2)]
                tmpb = [A.bf([128, 512], "pm%d" % i) for i in range(2)]
                gblk = {"a": 10, "b": 11, "c": 12}[br]
                for j in range(4):
                    wp, wgt = wneed(["p%s%d" % (br, j), "g%s%d" % (br, j)])
                    for cc in range(2):
                        dm = 2 * j + cc
                        for g, (t0, tn) in enumerate(TGS):
                            bp_, bg = next_mm(), next_mm()
                            for kc in range(KC):
                                mm(bank(bg)[:, 0:tn], wgt[0][:, kc, cc * 128:(cc + 1) * 128], xT[:, kc, t0:t0 + tn], kc == 0, kc == KC - 1,
                                   [wgt[1], r_xT[g]], [r_ps[bg]])
                            for fc in range(KC):
                                mm(bank(bp_)[:, 0:tn], wp[0][:, fc, cc * 128:(cc + 1) * 128], ybr[:, fc, t0:t0 + tn], fc == 0, fc == KC - 1,
                                   [wp[1], r_ybr[fc][g]], [r_ps[bp_]])
                            sg, rsg = sgb[g % 2]
                            act(sg[:, 0:tn], bank(bg)[:, 0:tn], AF.Sigmoid, [r_ps[bg], r_bpm], [rsg], bias=bpm[:, l, gblk, dm:dm + 1])
                            if first:
                                tt("dve", mrg[:, dm, t0:t0 + tn], bank(bp_)[:, 0:tn], sg[:, 0:tn], ALU.mult, [r_ps[bp_], rsg], [r_mrg[g]])
                            else:
                                tm, rtm = tmpb[g % 2]
                                tt("dve", tm[:, 0:tn], bank(bp_)[:, 0:tn], sg[:, 0:tn], ALU.mult, [r_ps[bp_], rsg], [rtm])
                                tt("dve", mrg[:, dm, t0:t0 + tn], mrg[:, dm, t0:t0 + tn], tm[:, 0:tn], ALU.add, [r_mrg[g], rtm], [r_mrg[g]])

            proj_merge("b", True)

            A.reset()
            cext = [A.f32([128, 2, 257], "cext%d" % i) for i in range(2)]
            cbf_t = [A.bf([128, 2, 258], "cbf%d" % i) for i in range(4)]
            qTg = [A.bf([128, 2, 512], "qTg%d" % i) for i in range(2)]
            kTg = [A.bf([128, 2, 512], "kTg%d" % i) for i in range(2)]
            vext = [A.bf([128, 258], "vext%d" % i) for i in range(6)]
            so_t = [A.f32([128, 256], "so%d" % i) for i in range(6)]
            sz_t = [A.f32([128, 256], "sza%d" % i) for i in range(6)]
            wk_t = [A.bf([128, 256], "wk%d" % i) for i in range(4)]
            sp_t = [A.bf([128, 128], "sp%d" % i) for i in range(4)]
            ya_t = [A.bf([128, 256], "ya%d" % i) for i in range(4)]
            qd_t = [A.bf([128, 2, 128], "qd%d" % i) for i in range(4)]
            dmx_t = [A.f32([128, 2], "dmx%d" % i) for i in range(4)]
            bst_t = [A.f32([128, 6], "bstA%d" % i) for i in range(4)]
            mv_t = [A.f32([128, 2], "mvA%d" % i) for i in range(4)]
            rstd_t = [A.f32([128, 1], "rstdA%d" % i) for i in range(4)]
            brow, r_brow = A.bf([128, 768], "browA")
            cs_t = [A.f32([128, 2, 257], "cs%d" % i) for i in range(4)]
            cdbs = [A.bf([128, 2, 258], "cdbs%d" % i) for i in range(2)]
            qtm = [A.bf([128, 2, 64], "qtm%d" % i) for i in range(2)]
            vms = [A.bf([64, 258], "vms%d" % i) for i in range(2)]
            nall, r_nall = A.f32([128, NB, 2], "nall")
            nout, r_nout = A.f32([128, NB, 2], "nout")
            memset("pool", brow[:], 0.0, [r_brow])
            for i in range(6):
                memset("pool", vext[i][0][:, 256:258], 1.0, [vext[i][1]])
            nrot = {"i": 0}

            def next_num():
                nrot["i"] = (nrot["i"] + 1) % 4
                return nrot["i"]

            def chunks_of(g):
                t0, tn = TGS[g]
                if g < 4:
                    return [(ci, 4 * g + ci, ci * 128, 128, t0 + ci * 128, (4 * g + ci) % 6) for ci in range(4)]
                return [(0, 16, 0, 64, t0, 16 % 6)]

            for h in range(NH):
                wq, wk_, wv_, wo_, wz_ = wneed(["q%d" % h, "k%d" % h, "v%d" % h, "o%d" % h, "za%d" % h])
                for i3, nm in enumerate(("v", "o", "za")):
                    P.dma("pool", brow[0:1, i3 * 256:(i3 + 1) * 256], b_in[l:l + 1, OFF[nm] + h * 256:OFF[nm] + (h + 1) * 256],
                          writes=[r_brow], sres=r_brow)
                ce, rce = cext[h % 2]
                memset("dve", ce[:], 0.0, [rce])

                def emit_qk(g, h=h, wq=wq, wk_=wk_):
                    t0, tn = TGS[g]
                    qt, rqt = qTg[g % 2]
                    kt, rkt = kTg[g % 2]
                    for dc in range(2):
                        bq, bk = next_mm(), next_mm()
                        for kc in range(KC):
                            mm(bank(bq)[:, 0:tn], wq[0][:, kc, dc * 128:(dc + 1) * 128], xT[:, kc, t0:t0 + tn], kc == 0, kc == KC - 1,
                               [wq[1], r_xT[g]], [r_ps[bq]])
                        for kc in range(KC):
                            mm(bank(bk)[:, 0:tn], wk_[0][:, kc, dc * 128:(dc + 1) * 128], xT[:, kc, t0:t0 + tn], kc == 0, kc == KC - 1,
                               [wk_[1], r_xT[g]], [r_ps[bk]])
                        act(qt[:, dc, 0:tn], bank(bq)[:, 0:tn], AF.Identity, [r_ps[bq], r_bpm], [rqt], bias=bpm[:, l, 0, 2 * h + dc:2 * h + dc + 1])
                        act(kt[:, dc, 0:tn], bank(bk)[:, 0:tn], AF.Identity, [r_ps[bk], r_kbias], [rkt],
                            bias=kbias[:, l, 2 * h + dc:2 * h + dc + 1], scale=1.0 / 16.0)

                def L1(g, sel, wv_=wv_, wo_=wo_, wz_=wz_):
                    for (ci, ti, c0, cn, tk0, bi) in chunks_of(g):
                        if ci not in sel:
                            continue
                        ve, rve = vext[bi]
                        so, rso = so_t[bi]
                        sza, rsza = sz_t[bi]
                        bv, bo = next_mm(), next_mm()
                        for (o_ap, wt, i3, rb) in ((bank(bv)[0:cn, 0:256], wv_, 0, r_ps[bv]), (bank(bv)[0:cn, 256:512], wo_, 1, r_ps[bv]),
                                                   (bank(bo)[0:cn, 0:256], wz_, 2, r_ps[bo])):
                            for kc in range(KC):
                                mm(o_ap, xT[:, kc, tk0:tk0 + cn], wt[0][:, kc, :], kc == 0, False, [r_xT[g], wt[1]], [rb], sig=False)
                            mm(o_ap, onesrow[:, 0:cn], brow[:, i3 * 256:(i3 + 1) * 256], False, True, [r_onesrow, r_brow], [rb])
                        cp("act", ve[0:cn, 0:256], bank(bv)[0:cn, 0:256], [r_ps[bv]], [rve])
                        act(so[0:cn, :], bank(bv)[0:cn, 256:512], AF.Tanh, [r_ps[bv]], [rso], scale=0.5)
                        act(sza[0:cn, :], bank(bo)[0:cn, 0:256], AF.Tanh, [r_ps[bo]], [rsza], scale=0.5)
                        ts("dve", so[0:cn, :], so[0:cn, :], 0.5, ALU.mult, [rso], [rso], s2=0.5, op1=ALU.add)
                        stt(sza[0:cn, :], sza[0:cn, :], 1.0, bank(bo)[0:cn, 0:256], ALU.add, ALU.mult, [rsza, r_ps[bo]], [rsza])

                def L2(g, h=h):
                    qt, rqt = qTg[g % 2]
                    kt, rkt = kTg[g % 2]
                    msk, rmsk = (mask_f, r_mask) if g < 4 else (mask_s, r_mask_s)
                    for (ci, ti, c0, cn, tk0, bi) in chunks_of(g):
                        wkt, rwk = wk_t[ci]
                        spt, rsp = sp_t[ci]
                        bt = next_aux()
                        for dc in range(2):
                            tr(bank_bf(bt)[0:cn, dc * 128:(dc + 1) * 128], kt[:, dc, c0:c0 + cn], ident_b[:], [rkt, r_idb], [r_ps[bt]], dc == 1)
                        ts("dve", wkt[0:cn, :], bank_bf(bt)[0:cn, 0:256], wthr[0:cn, ti, h:h + 1], ALU.mult, [r_ps[bt], r_wthr], [rwk])
                        bs_ = next_aux()
                        for dc in range(2):
                            mm(bank(bs_)[0:cn, 0:cn], kt[:, dc, c0:c0 + cn], qt[:, dc, c0:c0 + cn], dc == 0, dc == 1, [rkt, rqt], [r_ps[bs_]])
                        stt(spt[0:cn, 0:cn], bank(bs_)[0:cn, 0:cn], wthr[0:cn, ti, h:h + 1], msk[0:cn, 0:cn], ALU.mult, ALU.mult,
                            [r_ps[bs_], r_wthr, rmsk], [rsp])
                        if g < 4 and ti > 0:
                            qd, rqd = qd_t[ci]
                            ts("dve", qd[:], qt[:, :, c0:c0 + cn], decbc[:, h * 32 + ti:h * 32 + ti + 1], ALU.mult, [rqt, r_decbc], [rqd])

                def L3(g, h=h, ce=ce, rce=rce):
                    qt, rqt = qTg[g % 2]
                    pre_num = {}
                    post = []

                    def emit_num(ch):
                        (ci, ti, c0, cn, tk0, bi) = ch
                        ve, rve = vext[bi]
                        spt, rsp = sp_t[ci]
                        bn_ = next_num()
                        first = (ti == 0)
                        mm(bank(bn_)[0:cn, 0:257], spt[0:cn, 0:cn], ve[0:cn, 0:257], True, first, [rsp, rve], [r_ps[bn_]], sig=first)
                        if not first:
                            qd, rqd = qd_t[ci]
                            cbf, r_cbf = cbf_t[(ti - 1) % 4]
                            for dc in range(2):
                                mm(bank(bn_)[0:cn, 0:257], qd[:, dc, :], cbf[:, dc, 0:257], False, dc == 1, [rqd, r_cbf], [r_ps[bn_]])
                        return bn_

                    if g < 4:
                        pre_num[0] = emit_num(chunks_of(g)[0])
                        for (ci, ti, c0, cn, tk0, bi) in chunks_of(g):
                            ve, rve = vext[bi]
                            wkt, rwk = wk_t[ci]
                            up = 6 if ci % 2 == 0 else 4
                            for dc in range(2):
                                mm(bank(up + dc)[:, 0:257], wkt[0:cn, dc * 128:(dc + 1) * 128], ve[0:cn, 0:257], True, True, [rwk, rve], [r_ps[up + dc]])
                            stt(ce[:], ce[:], decbc[:, h * 32 + ti:h * 32 + ti + 1], ps[up // 2][:, :].rearrange("p (a b) -> p a b", a=2)[:, :, 0:257],
                                ALU.mult, ALU.add, [rce, r_decbc, r_ps[up], r_ps[up + 1]], [rce])
                            if ti < 15:
                                cbf, r_cbf = cbf_t[ti % 4]
                                cp("act", cbf[:, :, 0:257], ce[:], [rce], [r_cbf])
                            else:
                                P.dma("sp", o_cp[l, h].rearrange("(dc p) e -> p dc e", p=128), ce[:, :, 0:256], reads=[rce], sres=rce, final=True)
                                P.dma("sp", o_np[l, h].rearrange("(dc p) -> p dc", p=128), ce[:, :, 256], reads=[rce], sres=rce, final=True)
                    for (ci, ti, c0, cn, tk0, bi) in chunks_of(g):
                        ve, rve = vext[bi]
                        so, rso = so_t[bi]
                        wkt, rwk = wk_t[ci]
                        spt, rsp = sp_t[ci]
                        dmx, r_dmx = dmx_t[ci]
                        if g < 4:
                            bn_ = pre_num[ci] if ci in pre_num else emit_num((ci, ti, c0, cn, tk0, bi))
                        else:
                            bn_ = next_num()
                            def load_cs(b_):
                                cs, rcs = cs_t[b_ % 4]
                                P.dma("act", cs[:, :, 0:256], st_c[l, b_, h].rearrange("(dc p) e -> p dc e", p=128), writes=[rcs], sres=rcs)
                            for dc in range(2):
                                P.dma("sp", nall[:, :, dc], st_n[l, :, h, dc * 128:(dc + 1) * 128].rearrange("b p -> p b"), writes=[r_nall], sres=r_nall)
                            load_cs(0)
                            load_cs(1)
                            mm(bank(bn_)[0:cn, 0:257], spt[0:cn, 0:cn], ve[0:cn, 0:257], True, False, [rsp, rve], [r_ps[bn_]], sig=False)
                            for b_ in range(NB):
                                if b_ + 2 < NB:
                                    load_cs(b_ + 2)
                                cs, rcs = cs_t[b_ % 4]
                                cb_, rcb = cdbs[b_ % 2]
                                qm, rqm = qtm[b_ % 2]
                                vm, rvm = vms[b_ % 2]
                                up = 6 if b_ % 2 == 0 else 4
                                cp("dve", cs[:, :, 256], nall[:, b_, :], [r_nall], [rcs])
                                cp("act", cb_[:, :, 0:257], cs[:], [rcs], [rcb])
                                stt(qm[:], qt[:, :, 0:64], decbc[:, h * 32 + 16 + b_:h * 32 + 17 + b_], bc_mid(colmask[:, b_, :], 2),
                                    ALU.mult, ALU.mult, [rqt, r_decbc, r_colmask], [rqm])
                                for dc in range(2):
                                    mm(bank(bn_)[0:cn, 0:257], qm[:, dc, :], cb_[:, dc, 0:257], False, (b_ == NB - 1 and dc == 1),
                                       [rqm, rcb], [r_ps[bn_]])
                                act(vm[:, 0:257], ve[0:64, 0:257], AF.Copy, [rve, r_rowmask], [rvm], scale=rowmask[:, b_:b_ + 1])
                                for dc in range(2):
                                    mm(bank(up + dc)[:, 0:257], wkt[0:64, dc * 128:(dc + 1) * 128], vm[:, 0:257], True, True, [rwk, rvm], [r_ps[up + dc]])
                                stt(cs[:], cs[:], decbc[:, h * 32 + 16 + b_:h * 32 + 17 + b_],
                                    ps[up // 2][:, :].rearrange("p (a b) -> p a b", a=2)[:, :, 0:257], ALU.mult, ALU.add,
                                    [rcs, r_decbc, r_ps[up], r_ps[up + 1]], [rcs])
                                P.dma("sp", o_cs[l, b_, h].rearrange("(dc p) e -> p dc e", p=128), cs[:, :, 0:256], reads=[rcs], sres=rcs, final=True)
                                cp("dve", nout[:, b_, :], cs[:, :, 256], [rcs], [r_nout])
                            for dc in range(2):
                                P.dma("sp", o_ns[l, :, h, dc * 128:(dc + 1) * 128].rearrange("b p -> p b"), nout[:, :, dc], reads=[r_nout], sres=r_nout, final=True)
                        post.append((ci, ti, cn, bi, bn_))
                    for (ci, ti, cn, bi, bn_) in post:
                        dmx, r_dmx = dmx_t[ci]
                        act(dmx[0:cn, 0:1], bank(bn_)[0:cn, 256:257], AF.Abs, [r_ps[bn_]], [r_dmx])
                    for (ci, ti, cn, bi, bn_) in post:
                        dmx, r_dmx = dmx_t[ci]
                        ts("dve", dmx[0:cn, 0:1], dmx[0:cn, 0:1], wthr[0:cn, ti, 4 + h:5 + h], ALU.max, [r_dmx, r_wthr], [r_dmx])
                    for (ci, ti, cn, bi, bn_) in post:
                        dmx, r_dmx = dmx_t[ci]
                        recip(dmx[0:cn, 1:2], dmx[0:cn, 0:1], [r_dmx], [r_dmx])
                    for (ci, ti, cn, bi, bn_) in post:
                        dmx, r_dmx = dmx_t[ci]
                        so, rso = so_t[bi]
                        stt(so[0:cn, :], bank(bn_)[0:cn, 0:256], dmx[0:cn, 1:2], so[0:cn, :], ALU.mult, ALU.mult, [r_ps[bn_], r_dmx, rso], [rso])

                def L4(g):
                    CH = chunks_of(g)
                    for (ci, ti, c0, cn, tk0, bi) in CH:
                        hs, rhs = so_t[bi]
                        bst, r_bst = bst_t[ci]
                        mv, r_mv = mv_t[ci]
                        bnstats(bst[0:cn, :], hs[0:cn, :], [rhs], [r_bst])
                    for (ci, ti, c0, cn, tk0, bi) in CH:
                        bst, r_bst = bst_t[ci]
                        mv, r_mv = mv_t[ci]
                        bnaggr(mv[0:cn, :], bst[0:cn, :], [r_bst], [r_mv])
                    for (ci, ti, c0, cn, tk0, bi) in CH:
                        mv, r_mv = mv_t[ci]
                        rstd, r_rstd = rstd_t[ci]
                        act(rstd[0:cn, :], mv[0:cn, 1:2], AF.Sqrt, [r_mv, r_eps], [r_rstd], bias=eps_t[0:cn, :])
                    for (ci, ti, c0, cn, tk0, bi) in CH:
                        hs, rhs = so_t[bi]
                        sza, rsza = sz_t[bi]
                        ya, rya = ya_t[ci]
                        mv, r_mv = mv_t[ci]
                        rstd, r_rstd = rstd_t[ci]
                        recip(rstd[0:cn, :], rstd[0:cn, :], [r_rstd], [r_rstd])
                    for (ci, ti, c0, cn, tk0, bi) in CH:
                        hs, rhs = so_t[bi]
                        sza, rsza = sz_t[bi]
                        mv, r_mv = mv_t[ci]
                        stt(hs[0:cn, :], hs[0:cn, :], mv[0:cn, 0:1], sza[0:cn, :], ALU.subtract, ALU.mult, [rhs, r_mv, rsza], [rhs])
                    for (ci, ti, c0, cn, tk0, bi) in CH:
                        hs, rhs = so_t[bi]
                        ya, rya = ya_t[ci]
                        rstd, r_rstd = rstd_t[ci]
                        act(ya[0:cn, :], hs[0:cn, :], AF.Copy, [rhs, r_rstd], [rya], scale=rstd[0:cn, 0:1])

                def L5(g, h=h):
                    for (ci, ti, c0, cn, tk0, bi) in chunks_of(g):
                        ya, rya = ya_t[ci]
                        bt2 = next_aux()
                        for dc in range(2):
                            tr(bank_bf(bt2)[:, dc * 128:dc * 128 + cn], ya[0:cn, dc * 128:(dc + 1) * 128], ident_b[0:cn, 0:cn], [rya, r_idb],
                               [r_ps[bt2]], dc == 1)
                        for dc in range(2):
                            fc = 2 * h + dc
                            act(ybr[:, fc, tk0:tk0 + cn], bank_bf(bt2)[:, dc * 128:dc * 128 + cn], AF.Copy, [r_ps[bt2], r_gpmh], [r_ybr[fc][g]],
                                scale=gpmh[:, l, fc:fc + 1])

                emit_qk(0)
                L1(0, (0, 1, 2, 3))
                for g in range(len(TGS)):
                    if g == 0 or g == 4:
                        P.mark("A_h%d_%s" % (h, "prompt" if g < 4 else "sample"))
                    L2(g)
                    L3(g)
                    if g + 1 < len(TGS):
                        emit_qk(g + 1)
                        L1(g + 1, (0, 1))
                    L4(g)
                    if g + 1 < len(TGS):
                        L1(g + 1, (2, 3))
                    L5(g)
            P.mark("A_end")
            proj_merge("a", False)

            A.reset()
            waT, r_wa = A.bf([128, 8, 128], "wa")
            wxT, r_wx = A.bf([128, 8, 128], "wx")
            stc, r_stc = A.f32([48, D], "stc")
            sth, r_sth = A.f32([16, D], "sth")
            stg_t, r_stg_t = A.f32([68, D], "stg_t")
            stg, r_stg = A.f32([128, KC, 68], "stg")
            hprev2 = [A.f32([128, 17], "hprev%d" % i) for i in range(2)]
            hist2 = [A.f32([128, 48], "hist%d" % i) for i in range(2)]
            ctail2 = [A.f32([128, 3], "ctail%d" % i) for i in range(2)]
            cbuf = [[A.f32([128, 515], "cb%d_%d" % (i, k)) for k in range(5)] for i in range(2)]
            xcb = [A.bf([128, 512], "xcb%d" % i) for i in range(2)]
            szc = [A.f32([128, 512], "szc%d" % i) for i in range(2)]
            P.dma("pool", waT[:], wa_d[l].rearrange("n i j -> i n j"), writes=[r_wa], sres=r_wa)
            P.dma("pool", wxT[:], wx_d[l].rearrange("n i j -> i n j"), writes=[r_wx], sres=r_wx)
            P.dma("sp", stc[:], st_conv[l], writes=[r_stc], sres=r_stc)
            P.dma("sp", sth[:], st_h[l], writes=[r_sth], sres=r_sth)
            for j in range(4):
                wxc, wzc = wneed(["xc%d" % j, "zc%d" % j])
                for cc in range(2):
                    c = 2 * j + cc
                    hprev, r_hprev = hprev2[cc]
                    hist, r_hist = hist2[cc]
                    ctail, r_ctail = ctail2[cc]
                    b = next_aux()
                    tr(bank(b)[:, 0:48], stc[:, c * 128:(c + 1) * 128], ident_f[0:48, 0:48], [r_stc, r_idf], [r_ps[b]], False)
                    tr(bank(b)[:, 48:64], sth[:, c * 128:(c + 1) * 128], ident_f[0:16, 0:16], [r_sth, r_idf], [r_ps[b]], True)
                    cp("dve", hprev[:, 1:17], bank(b)[:, 48:64], [r_ps[b]], [r_hprev])
                    cp("dve", hist[:], bank(b)[:, 0:48], [r_ps[b]], [r_hist])
                    memset("dve", ctail[:], 0.0, [r_ctail])
                for g, (t0, tn) in enumerate(TGS):
                    views = {}
                    for cc in range(2):
                        c = 2 * j + cc
                        hprev, r_hprev = hprev2[cc]
                        hist, r_hist = hist2[cc]
                        ctail, r_ctail = ctail2[cc]
                        (xp, rxp), (xc_, rxc), (ra, rra), (ib, rib), (t1, rt1) = cbuf[cc]
                        sz, rsz = szc[cc]
                        bx_, bz = next_mm(), next_mm()
                        for kc in range(KC):
                            mm(bank(bx_)[:, 0:tn], wxc[0][:, kc, cc * 128:(cc + 1) * 128], xT[:, kc, t0:t0 + tn], kc == 0, kc == KC - 1,
                               [wxc[1], r_xT[g]], [r_ps[bx_]])
                        for kc in range(KC):
                            mm(bank(bz)[:, 0:tn], wzc[0][:, kc, cc * 128:(cc + 1) * 128], xT[:, kc, t0:t0 + tn], kc == 0, kc == KC - 1,
                               [wzc[1], r_xT[g]], [r_ps[bz]])
                        if g < 4:
                            cp("dve", xp[:, 0:3], ctail[:], [r_ctail], [rxp])
                            act(xp[:, 3:3 + tn], bank(bx_)[:, 0:tn], AF.Identity, [r_ps[bx_], r_bpm], [rxp], bias=bpm[:, l, 8, c:c + 1])
                            if g < 3:
                                cp("dve", ctail[:], xp[:, tn:tn + 3], [rxp], [r_ctail])
                            else:
                                cp("dve", stg[:, c, 17:20], xp[:, tn:tn + 3], [rxp], [r_stg])
                            xp_ = xp
                            xpv = (lambda xp_: (lambda jj: xp_[:, jj:jj + 512]))(xp)
                            xcv = xc_[:, 0:tn]
                        else:
                            xp3 = xp[:, 0:112].rearrange("p (b k) -> p b k", k=7)
                            cp("dve", xp3[:, :, 0:3], hist[:].rearrange("p (b k) -> p b k", k=3), [r_hist], [rxp])
                            act(xp3[:, :, 3:7], bank(bx_)[:, 0:64].rearrange("p (b k) -> p b k", k=4), AF.Identity, [r_ps[bx_], r_bpm], [rxp],
                                bias=bpm[:, l, 8, c:c + 1])
                            cp("dve", stg[:, c, 20:68].rearrange("p (b k) -> p b k", k=3), xp3[:, :, 4:7], [rxp], [r_stg])
                            xpv = (lambda xp3: (lambda jj: xp3[:, :, jj:jj + 4]))(xp3)
                            xcv = xc_[:, 0:64].rearrange("p (b k) -> p b k", k=4)
                        act(sz[:, 0:tn], bank(bz)[:, 0:tn], AF.Silu, [r_ps[bz], r_bpm], [rsz], bias=bpm[:, l, 9, c:c + 1])
                        views[cc] = (xpv, xcv)
                    for cc in range(2):
                        c = 2 * j + cc
                        (xp, rxp), (xc_, rxc), (ra, rra), (ib, rib), (t1, rt1) = cbuf[cc]
                        xb, rxb = xcb[cc]
                        xpv, xcv = views[cc]
                        ts("dve", xcv, xpv(0), cw[:, l, c, 0:1], ALU.mult, [rxp, r_cw, r_cvec], [rxc], s2=cvec[:, 0, l, c:c + 1], op1=ALU.add)
                        for jj in range(1, 4):
                            stt(xcv, xpv(jj), cw[:, l, c, jj:jj + 1], xcv, ALU.mult, ALU.add, [rxp, r_cw, rxc], [rxc])
                        cp("pool", xb[:, 0:tn], xc_[:, 0:tn], [rxc], [rxb])
                    for cc in range(2):
                        c = 2 * j + cc
                        (xp, rxp), (xc_, rxc), (ra, rra), (ib, rib), (t1, rt1) = cbuf[cc]
                        xb, rxb = xcb[cc]
                        br_, bi_ = next_aux(), next_aux()
                        mm(bank(br_)[:, 0:tn], waT[:, c, :], xb[:, 0:tn], True, True, [r_wa, rxb], [r_ps[br_]])
                        mm(bank(bi_)[:, 0:tn], wxT[:, c, :], xb[:, 0:tn], True, True, [r_wx, rxb], [r_ps[bi_]])
                        act(ra[:, 0:tn], bank(br_)[:, 0:tn], AF.Sigmoid, [r_ps[br_], r_cvec], [rra], bias=cvec[:, 1, l, c:c + 1])
                        act(ib[:, 0:tn], bank(bi_)[:, 0:tn], AF.Sigmoid, [r_ps[bi_], r_cvec], [rib], bias=cvec[:, 2, l, c:c + 1])
                    for cc in range(2):
                        c = 2 * j + cc
                        (xp, rxp), (xc_, rxc), (ra, rra), (ib, rib), (t1, rt1) = cbuf[cc]
                        act(t1[:, 0:tn], ra[:, 0:tn], AF.Exp, [rra, r_cl], [rt1], scale=cl[:, 1, l, c:c + 1])
                        act(ra[:, 0:tn], ra[:, 0:tn], AF.Exp, [rra, r_cl], [rra], scale=cl[:, 0, l, c:c + 1])
                    for cc in range(2):
                        (xp, rxp), (xc_, rxc), (ra, rra), (ib, rib), (t1, rt1) = cbuf[cc]
                        act(t1[:, 0:tn], t1[:, 0:tn], AF.Sqrt, [rt1], [rt1], bias=1.0, scale=-1.0)
                    for cc in range(2):
                        c = 2 * j + cc
                        hprev, r_hprev = hprev2[cc]
                        (xp, rxp), (xc_, rxc), (ra, rra), (ib, rib), (t1, rt1) = cbuf[cc]
                        sz, rsz = szc[cc]
                        if g == 0:
                            memset("dve", t1[:, 0:1], 1.0, [rt1])
                        tt("pool", ib[:, 0:tn], ib[:, 0:tn], t1[:, 0:tn], ALU.mult, [rib, rt1], [rib])
                        tt("dve", ib[:, 0:tn], ib[:, 0:tn], xc_[:, 0:tn], ALU.mult, [rib, rxc], [rib])
                        if g < 4:
                            init = 0.0 if g == 0 else hprev[:, 0:1]
                            scan(t1[:, 0:tn], ra[:, 0:tn], ib[:, 0:tn], init, ALU.mult, ALU.add, [rra, rib, r_hprev], [rt1])
                            if g < 3:
                                cp("dve", hprev[:, 0:1], t1[:, tn - 1:tn], [rt1], [r_hprev])
                            else:
                                cp("dve", stg[:, c, 0:1], t1[:, tn - 1:tn], [rt1], [r_stg])
                        else:
                            ib3 = ib[:, 0:64].rearrange("p (b k) -> p b k", k=4)
                            ra3 = ra[:, 0:64].rearrange("p (b k) -> p b k", k=4)
                            tt("dve", hprev[:, 1:17], hprev[:, 1:17], ra3[:, :, 0], ALU.mult, [r_hprev, rra], [r_hprev])
                            tt("dve", ib3[:, :, 0], ib3[:, :, 0], hprev[:, 1:17], ALU.add, [rib, r_hprev], [rib])
                            memset("dve", ra3[:, :, 0], 0.0, [rra])
                            scan(t1[:, 0:64], ra[:, 0:64], ib[:, 0:64], 0.0, ALU.mult, ALU.add, [rra, rib], [rt1])
                            cp("dve", stg[:, c, 1:17], t1[:, 0:64].rearrange("p (b k) -> p b k", k=4)[:, :, 3], [rt1], [r_stg])
                        tt("dve", ybr[:, c, t0:t0 + tn], t1[:, 0:tn], sz[:, 0:tn], ALU.mult, [rt1, rsz], [r_ybr[c][g]])
            for half in range(2):
                b = next_aux()
                for jj in range(4):
                    c = half * 4 + jj
                    tr(bank(b)[0:68, jj * 128:(jj + 1) * 128], stg[:, c, :], ident_f[:], [r_stg, r_idf], [r_ps[b]], jj == 3)
                cp("dve", stg_t[:, half * 512:(half + 1) * 512], bank(b)[0:68, :], [r_ps[b]], [r_stg_t])
            P.dma("sp", o_convh[l], stg_t[:], reads=[r_stg_t], sres=r_stg_t, final=True)
            proj_merge("c", False)

            A.reset()
            lnbc, r_lnbc = A.f32([128, 2, D], "lnbcO")
            xr = [A.f32([128, D], "xr%d" % i) for i in range(4)]
            zt = [A.f32([128, D], "zt%d" % i) for i in range(6)]
            bstO = [A.f32([128, 12], "bstO%d" % i) for i in range(4)]
            mvO = [A.f32([128, 2], "mvO%d" % i) for i in range(4)]
            rstdO = [A.f32([128, 1], "rstdO%d" % i) for i in range(4)]
            P.dma("sp", lnbc[:].rearrange("p a b -> p (a b)"), pbc(fln_d[l].rearrange("a b -> (a b)"), 128), writes=[r_lnbc], sres=r_lnbc)
            wo = wneed(["wo0", "wo1", "wo2", "wo3"])
            for ti, (t0, tn) in enumerate(TTS):
                if ti < 4:
                    xa, rx = xr[ti % 4]
                    rsrc = [] if l == 0 else [r_xres[ti]]
                    P.dma("sp", xa[0:tn, :], xsrc[t0:t0 + tn, :], reads=rsrc, writes=[rx], sres=rx)
            pairs = [list(range(p, min(p + 2, 17))) for p in range(0, 17, 2)]
            for pr_tiles in pairs:
                for ti in pr_tiles:
                    t0, tn = TTS[ti]
                    g = tok_group_of(t0)
                    xa, rx = xr[ti % 4]
                    za, rz = zt[ti % 6]
                    pr = (ti % 2) * 2
                    for j in range(4):
                        bq = pr + j // 2
                        o_ap = bank(bq)[0:tn, (j % 2) * 256:(j % 2) * 256 + 256]
                        for fc in range(KC):
                            mm(o_ap, mrg[:, fc, t0:t0 + tn], wo[j][0][:, fc, :], fc == 0, fc == KC - 1, [r_mrg[g], wo[j][1]], [r_ps[bq]])
                    stt(za[0:tn, :], xa[0:tn, :], ALPHA, ps[pr // 2][0:tn, :], ALU.mult, ALU.add, [rx, r_ps[pr], r_ps[pr + 1]], [rz])
                    if ti + 4 < 17:
                        t0n, tnn = TTS[ti + 4]
                        rsrc = [] if l == 0 else [r_xres[ti + 4]]
                        P.dma("sp", xa[0:tnn, :], xsrc[t0n:t0n + tnn, :], reads=rsrc, writes=[rx], sres=rx)
                for ti in pr_tiles:
                    t0, tn = TTS[ti]
                    za, rz = zt[ti % 6]
                    bst, r_bst = bstO[ti % 4]
                    mv, r_mv = mvO[ti % 4]
                    for hf in range(2):
                        bnstats(bst[0:tn, hf * 6:hf * 6 + 6], za[0:tn, hf * 512:(hf + 1) * 512], [rz], [r_bst])
                    bnaggr(mv[0:tn, :], bst[0:tn, :], [r_bst], [r_mv])
                for ti in pr_tiles:
                    t0, tn = TTS[ti]
                    mv, r_mv = mvO[ti % 4]
                    rstd, r_rstd = rstdO[ti % 4]
                    act(rstd[0:tn, :], mv[0:tn, 1:2], AF.Sqrt, [r_mv, r_eps], [r_rstd], bias=eps_t[0:tn, :])
                for ti in pr_tiles:
                    t0, tn = TTS[ti]
                    za, rz = zt[ti % 6]
                    mv, r_mv = mvO[ti % 4]
                    rstd, r_rstd = rstdO[ti % 4]
                    recip(rstd[0:tn, :], rstd[0:tn, :], [r_rstd], [r_rstd])
                    ts("dve", za[0:tn, :], za[0:tn, :], mv[0:tn, 0:1], ALU.subtract, [rz, r_mv, r_rstd], [rz], s2=rstd[0:tn, 0:1], op1=ALU.mult)
                    tt("pool", za[0:tn, :], za[0:tn, :], lnbc[0:tn, 0, :], ALU.mult, [rz, r_lnbc], [rz])
                    tt("pool", za[0:tn, :], za[0:tn, :], lnbc[0:tn, 1, :], ALU.add, [rz, r_lnbc], [rz])
                for ti in pr_tiles:
                    t0, tn = TTS[ti]
                    za, rz = zt[ti % 6]
                    if last or l == depth - 1:
                        P.dma("sp", y_all[t0:t0 + tn, :], za[0:tn, :], reads=[rz], sres=rz, final=True)
                    else:
                        P.dma("sp", xres[t0:t0 + tn, :], za[0:tn, :], reads=[rz], writes=[r_xres[ti]], sres=rz)
                        tile_to_xT(za, rz, ti)
        P.finish()
        global _LAST_PROG
        _LAST_PROG = P
        with nc.allow_non_contiguous_dma(reason="small strided state columns"):
            P.emit()
    return nc


def _consts():
    ident = np.eye(128, dtype=np.float32)
    s = np.arange(128)
    mask = (s[:, None] <= s[None, :]).astype(np.float32)
    t = np.arange(64)
    mask_s = ((t[:, None] // 4 == t[None, :] // 4) & (t[:, None] <= t[None, :])).astype(np.float32)
    a0s = np.ones((4, 64), np.float32)
    a0s[:, ::4] = 0.0
    hmask = np.zeros((4, 4, 32), np.float32)
    for h in range(4):
        hmask[h, h, :] = 1.0
    colmask = np.zeros((128, NB, 64), np.float32)
    rowmask = np.zeros((64, NB), np.float32)
    for b in range(NB):
        colmask[:, b, 4 * b:4 * b + 4] = 1.0
        rowmask[4 * b:4 * b + 4, b] = 1.0
    onesrow = np.zeros((128, 128), np.float32)
    onesrow[0, :] = 1.0
    return dict(c_ident=ident, c_mask=mask, c_mask_s=mask_s, c_a0s=a0s, c_hmask=hmask.reshape(4, 128),
                c_colmask=colmask.reshape(128, NB * 64), c_rowmask=rowmask, c_onesrow=onesrow)


_NC_CACHE = {}


def kernel(x_prompt, x_sample, state_mlstm_c, state_mlstm_n, state_mlstm_m, state_lru_conv, state_lru_h,
           w_in, b_in, mlstm_norm_g, gmlp_ln_g, gmlp_ln_b, gmlp_ws, gmlp_bs, lru_conv_w, lru_conv_b,
           lru_wa, lru_ba, lru_wx, lru_bx, lru_lambda, w_proj_a, w_proj_b, w_proj_c, w_out, ln_g, ln_b, _depth=NL):
    f = lambda a: np.ascontiguousarray(np.asarray(a, dtype=np.float32))
    x_prompt, x_sample = f(x_prompt), f(x_sample)
    w_in, b_in = f(w_in), f(b_in)
    if _depth not in _NC_CACHE:
        _NC_CACHE[_depth] = build(_depth)
    nc = _NC_CACHE[_depth]
    bblocks = np.stack([b_in[:, OFF[n]:OFF[n] + D] for n in BLK], axis=1)
    bpm = bblocks.reshape(NL, 13, 8, 128).transpose(3, 0, 1, 2).reshape(128, NL * 13 * 8)
    bgate = np.stack([b_in[:, 5120:5124], b_in[:, 5124:5128]], axis=2).transpose(1, 0, 2).reshape(4, NL * 2)
    pm = lambda a: f(a).reshape(NL, 8, 128).transpose(2, 0, 1)
    gpm = pm(mlstm_norm_g).reshape(128, NL * 8)
    cw = f(lru_conv_w).reshape(NL, 4, 8, 128).transpose(3, 0, 2, 1).reshape(128, NL * 8 * 4)
    cvec = np.stack([pm(lru_conv_b), pm(lru_ba), pm(lru_bx), pm(lru_lambda)], axis=1).reshape(128, 4 * NL * 8)
    gln = np.stack([f(gmlp_ln_g), f(gmlp_ln_b)], axis=1)
    fln = np.stack([f(ln_g), f(ln_b)], axis=1)
    gws = f(gmlp_ws)
    gws_s = np.stack([np.stack([np.tile(gws[l, g, :4, :4].T, (16, 16)) for g in range(4)], axis=1) for l in range(NL)], 0)
    gws_s = gws_s.reshape(NL, 64, 4 * 64)
    gbs = f(gmlp_bs).reshape(NL, 4 * 128)
    gbs_s = np.stack([np.concatenate([np.tile(f(gmlp_bs)[l, g, :4], 16) for g in range(4)]) for l in range(NL)], 0)
    shared = dict(w_in=w_in, w_pa=f(w_proj_a), w_pb=f(w_proj_b), w_pc=f(w_proj_c), w_out=f(w_out), b_in=b_in,
                  bpm=np.ascontiguousarray(bpm), bgate=np.ascontiguousarray(bgate), gpm=np.ascontiguousarray(gpm),
                  cw=np.ascontiguousarray(cw), cvec=np.ascontiguousarray(cvec), lru_wa=f(lru_wa), lru_wx=f(lru_wx),
                  gln=np.ascontiguousarray(gln), fln=np.ascontiguousarray(fln), gws=gws,
                  gws_s=np.ascontiguousarray(gws_s), gbs=np.ascontiguousarray(gbs), gbs_s=np.ascontiguousarray(gbs_s))
    shared.update(_consts())
    sc, sn, sm = f(state_mlstm_c), f(state_mlstm_n), f(state_mlstm_m)
    sconv, sh = f(state_lru_conv), f(state_lru_h)
    in_maps = []
    for c in range(8):
        b0 = c * NB
        m = dict(shared)
        m["x_all"] = np.ascontiguousarray(np.concatenate([x_prompt[c], x_sample[b0:b0 + NB].reshape(NS, D)], axis=0))
        m["st_c"] = np.ascontiguousarray(sc[:, b0:b0 + NB])
        m["st_n"] = np.ascontiguousarray(sn[:, b0:b0 + NB])
        m["st_m"] = np.ascontiguousarray(sm[:, b0:b0 + NB].transpose(2, 0, 1).reshape(4, NL * NB))
        m["st_conv"] = np.ascontiguousarray(sconv[:, b0:b0 + NB].reshape(NL, 48, D))
        m["st_h"] = np.ascontiguousarray(sh[:, b0:b0 + NB])
        in_maps.append(m)
    res = run_bass_kernel_spmd(nc, in_maps, core_ids=list(range(8)))
    R = res.results
    g = lambda k, c: np.asarray(R[c][k], dtype=np.float32)
    y_p = np.stack([g("y_all", c)[:NTP] for c in range(8)], 0)
    y_s = np.concatenate([g("y_all", c)[NTP:].reshape(NB, 4, D) for c in range(8)], 0)
    c_p = np.stack([g("o_cp", c) for c in range(8)], 1)
    n_p = np.stack([g("o_np", c) for c in range(8)], 1)
    m_p = np.stack([g("o_mp", c).T for c in range(8)], 1)
    ch = [g("o_convh", c) for c in range(8)]
    conv_p = np.stack([x[:, 17:20] for x in ch], 1)
    h_p = np.stack([x[:, 0] for x in ch], 1)
    c_s = np.concatenate([g("o_cs", c) for c in range(8)], 1)
    n_s = np.concatenate([g("o_ns", c) for c in range(8)], 1)
    m_s = np.concatenate([g("o_ms", c).reshape(4, NL, NB).transpose(1, 2, 0) for c in range(8)], 1)
    conv_s = np.concatenate([x[:, 20:68].reshape(NL, NB, 3, D) for x in ch], 1)
    h_s = np.concatenate([x[:, 1:17] for x in ch], 1)
    v_s = np.concatenate([g("o_vs", c).reshape(NL, NB, 4, D) for c in range(8)], 1)
    outs = (y_p, y_s, c_p, n_p, m_p, conv_p, h_p, c_s, n_s, m_s, conv_s, h_s, v_s)
    return tuple(np.ascontiguousarray(o, dtype=np.float32) for o in outs)
```

```python
import numpy as np
from contextlib import ExitStack
import concourse.bass as bass
import concourse.mybir as mybir
from concourse.bass_utils import run_bass_kernel_spmd

F32 = mybir.dt.float32
BF16 = mybir.dt.bfloat16
AF = mybir.ActivationFunctionType
ALU = mybir.AluOpType
AX = mybir.AxisListType
ENG = ("pe", "act", "dve", "pool", "sp")

NL = 4
D = 1024
KC = 8
NTP = 2048
NS = 64
NT = NTP + NS
NB = 16
NH = 4
DK = 256
IN_W = 13320
TGS = [(0, 512), (512, 512), (1024, 512), (1536, 512), (2048, 64)]
TTS = [(i * 128, 128) for i in range(16)] + [(2048, 64)]
OFF = dict(q=0, k=1024, v=2048, o=3072, za=4096, gate=5120, ub=5128, vb=6152, zb=7176, xc=8200, zc=9224,
           ga=10248, gb=11272, gc=12296)
BLK = ["q", "k", "v", "o", "za", "ub", "vb", "zb", "xc", "zc", "ga", "gb", "gc"]
ALPHA = float((2 * NL) ** 0.25)
LN_EPS = 1e-5
NSLOT = 10
AHEAD = 5
ARENA_F32 = 13312


class Tok:
    __slots__ = ("eng", "sem", "val")

    def __init__(self, eng, sem, val):
        self.eng, self.sem, self.val = eng, sem, val


class Res:
    __slots__ = ("name", "lw", "rd", "const", "dsem")

    def __init__(self, name, const=False):
        self.name, self.lw, self.rd, self.const, self.dsem = name, None, {}, const, None


class Prog:
    def __init__(self, nc, stack):
        self.nc, self.stack = nc, stack
        self.ops = {e: [] for e in ENG}
        self.sem = {e: stack.enter_context(nc.semaphore("sem_" + e)) for e in ENG}
        self.cnt = {e: 0 for e in ENG}
        self.cur = {e: Tok(e, self.sem[e], None) for e in ENG}
        self.waited = {}
        self.nds = 0
        self.free_ds = []
        self.store_toks = {}
        self.marks = []

    def mark(self, name):
        self.marks.append((name, sum(1 for o in self.ops["pe"] if o[1] is not None)))

    def _dsem(self, res):
        if res.dsem is None:
            if self.free_ds:
                res.dsem = self.free_ds.pop()
            else:
                self.nds += 1
                res.dsem = [self.stack.enter_context(self.nc.semaphore("ds_%d" % self.nds)), 0]
        return res.dsem

    def _wait(self, eng, t, waits):
        key = (eng, id(t.sem))
        if self.waited.get(key, 0) >= t.val:
            return
        self.waited[key] = t.val
        waits.append((t.sem, t.val))

    def _deps(self, eng, reads, writes, inorder):
        deps = []
        for r in reads:
            if r.lw is not None:
                deps.append((r.lw, "raw"))
        for w in writes:
            if w.lw is not None:
                deps.append((w.lw, "waw"))
            for t in w.rd.values():
                deps.append((t, "war"))
        waits = []
        for t, kind in deps:
            if t.eng == eng and inorder and (kind != "raw" or eng == "pe"):
                continue
            if t.val is None:
                raise RuntimeError("dependency on unsignaled op (%s)" % t.eng)
            self._wait(eng, t, waits)
        return waits

    def _commit(self, tok, reads, writes):
        for r in reads:
            if not r.const:
                r.rd[id(tok.sem)] = tok
        for w in writes:
            w.lw = tok
            w.rd = {}

    def op(self, eng, fn, reads=(), writes=(), sig=None):
        if sig is None:
            sig = eng != "pe"
        waits = self._deps(eng, reads, writes, True)
        tok = self.cur[eng]
        self.ops[eng].append((waits, fn, (self.sem[eng], 1) if sig else None))
        self._commit(tok, reads, writes)
        if sig:
            self.cnt[eng] += 1
            tok.val = self.cnt[eng]
            self.cur[eng] = Tok(eng, self.sem[eng], None)

    def dma(self, q, out, in_, reads=(), writes=(), sres=None, final=False):
        waits = self._deps(q, reads, writes, False)
        ds = self._dsem(sres)
        ds[1] += 1
        tok = Tok(None, ds[0], 16 * ds[1])
        self.ops[q].append((waits, lambda e: e.dma_start(out=out, in_=in_), (ds[0], 16)))
        self._commit(tok, reads, writes)
        if final:
            self.store_toks[id(tok.sem)] = tok

    def barrier(self, dma_res=()):
        toks = [Tok(e, self.sem[e], self.cnt[e]) for e in ENG if self.cnt[e] > 0]
        for r in dma_res:
            if r.dsem is not None and r.dsem[1] > 0:
                toks.append(Tok(None, r.dsem[0], 16 * r.dsem[1]))
        for e in ENG:
            waits = []
            for t in toks:
                if t.eng == e:
                    continue
                self._wait(e, t, waits)
            if waits:
                self.ops[e].append((waits, None, None))

    def finish(self):
        waits = [(t.sem, t.val) for t in self.store_toks.values()]
        self.ops["sp"].append((waits, None, None))

    def emit(self):
        def mk(name):
            def body(e):
                for waits, fn, inc in self.ops[name]:
                    for sem, val in waits:
                        e.wait_ge(sem, val)
                    if fn is None:
                        continue
                    ins = fn(e)
                    if inc is not None:
                        ins.then_inc(inc[0], inc[1])
            return body

        with self.nc.Block() as block:
            block.tensor(mk("pe"))
            block.scalar(mk("act"))
            block.vector(mk("dve"))
            block.gpsimd(mk("pool"))
            block.sync(mk("sp"))


def bc_last(ap, n):
    pat = [list(x) for x in ap.ap]
    return bass.AP(ap.tensor, ap.offset, pat + [[0, n]])


def bc_mid(ap, n):
    pat = [list(x) for x in ap.ap]
    return bass.AP(ap.tensor, ap.offset, [pat[0], [0, n]] + pat[1:])


def pbc(ap_row, n):
    pat = [list(x) for x in ap_row.ap]
    return bass.AP(ap_row.tensor, ap_row.offset, [[0, n]] + pat[-1:])


def build(depth=NL):
    nc = bass.Bass("TRN2", target_bir_lowering=False)

    def din(name, shape):
        return nc.dram_tensor(name, list(shape), F32, kind="ExternalInput").ap()

    def dout(name, shape):
        return nc.dram_tensor(name, list(shape), F32, kind="ExternalOutput").ap()

    x_all = din("x_all", [NT, D])
    w_in = din("w_in", [NL, D, IN_W])
    w_p = {"a": din("w_pa", [NL, D, D]), "b": din("w_pb", [NL, D, D]), "c": din("w_pc", [NL, D, D])}
    w_out = din("w_out", [NL, D, D])
    b_in = din("b_in", [NL, IN_W])
    bpm_d = din("bpm", [128, NL * 13 * 8])
    bgate_d = din("bgate", [4, NL * 2])
    gpm_d = din("gpm", [128, NL * 8])
    cw_d = din("cw", [128, NL * 8 * 4])
    cvec_d = din("cvec", [128, 4 * NL * 8])
    wa_d = din("lru_wa", [NL, 8, 128, 128])
    wx_d = din("lru_wx", [NL, 8, 128, 128])
    gln_d = din("gln", [NL, 2, D])
    fln_d = din("fln", [NL, 2, D])
    gws_d = din("gws", [NL, 4, 128, 128])
    gws_s_d = din("gws_s", [NL, 64, 4 * 64])
    gbs_d = din("gbs", [NL, 4 * 128])
    gbs_s_d = din("gbs_s", [NL, 4 * 64])
    st_c = din("st_c", [NL, NB, NH, DK, DK])
    st_n = din("st_n", [NL, NB, NH, DK])
    st_m = din("st_m", [4, NL * NB])
    st_conv = din("st_conv", [NL, 48, D])
    st_h = din("st_h", [NL, NB, D])
    c_ident = din("c_ident", [128, 128])
    c_mask = din("c_mask", [128, 128])
    c_mask_s = din("c_mask_s", [64, 64])
    c_a0s = din("c_a0s", [4, 64])
    c_hmask = din("c_hmask", [4, 128])
    c_colmask = din("c_colmask", [128, NB * 64])
    c_rowmask = din("c_rowmask", [64, NB])
    c_onesrow = din("c_onesrow", [128, 128])

    y_all = dout("y_all", [NT, D])
    o_cp = dout("o_cp", [NL, NH, DK, DK])
    o_np = dout("o_np", [NL, NH, DK])
    o_mp = dout("o_mp", [4, NL])
    o_convh = dout("o_convh", [NL, 68, D])
    o_cs = dout("o_cs", [NL, NB, NH, DK, DK])
    o_ns = dout("o_ns", [NL, NB, NH, DK])
    o_ms = dout("o_ms", [4, NL * NB])
    o_vs = dout("o_vs", [NL, NS, D])
    xres = nc.dram_tensor("xres", [NT, D], F32, kind="Internal").ap()
    r_xres = [Res("xres%d" % i) for i in range(17)]

    with ExitStack() as st:
        P = Prog(nc, st)

        def sb(name, shape, dt=F32):
            return st.enter_context(nc.sbuf_tensor("s_" + name, list(shape), dt))

        def mm(out, lhsT, rhs, start, stop, reads, writes, sig=None):
            P.op("pe", lambda e: e.matmul(out, lhsT=lhsT, rhs=rhs, start=start, stop=stop), reads, writes,
                 sig=(stop if sig is None else sig))

        def tr(out, in_, ident, reads, writes, sig):
            P.op("pe", lambda e: e.transpose(out, in_, ident), reads, writes, sig=sig)

        def act(out, in_, func, reads, writes, bias=None, scale=None):
            kw = {}
            if bias is not None:
                kw["bias"] = bias
            if scale is not None:
                kw["scale"] = scale
            P.op("act", lambda e: e.activation(out=out, in_=in_, func=func, **kw), reads, writes)

        def tt(eng, out, in0, in1, op, reads, writes):
            P.op(eng, lambda e: e.tensor_tensor(out=out, in0=in0, in1=in1, op=op), reads, writes)

        def ts(eng, out, in0, s1, op0, reads, writes, s2=None, op1=None):
            if op1 is None:
                P.op(eng, lambda e: e.tensor_scalar(out=out, in0=in0, scalar1=s1, scalar2=None, op0=op0), reads, writes)
            else:
                P.op(eng, lambda e: e.tensor_scalar(out=out, in0=in0, scalar1=s1, scalar2=s2, op0=op0, op1=op1), reads, writes)

        def stt(out, in0, scalar, in1, op0, op1, reads, writes):
            P.op("dve", lambda e: e.scalar_tensor_tensor(out=out, in0=in0, scalar=scalar, in1=in1, op0=op0, op1=op1), reads, writes)

        def cp(eng, out, in_, reads, writes):
            if eng == "act":
                act(out, in_, AF.Identity, reads, writes)
            else:
                P.op(eng, lambda e: e.tensor_copy(out=out, in_=in_), reads, writes)

        def memset(eng, ap, val, writes):
            P.op(eng, lambda e: e.memset(ap, val), (), writes)

        def bnstats(out, in_, reads, writes):
            P.op("dve", lambda e: e.bn_stats(out=out, in_=in_), reads, writes)

        def bnaggr(out, in_, reads, writes):
            P.op("dve", lambda e: e.bn_aggr(out=out, in_=in_), reads, writes)

        def recip(out, in_, reads, writes):
            P.op("dve", lambda e: e.reciprocal(out=out, in_=in_), reads, writes)

        def scan(out, d0, d1, init, op0, op1, reads, writes):
            P.op("dve", lambda e: e.tensor_tensor_scan(out=out, data0=d0, data1=d1, initial=init, op0=op0, op1=op1), reads, writes)

        def treduce(out, in_, op, reads, writes):
            P.op("dve", lambda e: e.tensor_reduce(out=out, in_=in_, axis=AX.X, op=op), reads, writes)

        xT = sb("xT", [128, KC, NT], BF16)
        r_xT = [Res("xT%d" % i) for i in range(5)]
        ybr = sb("ybr", [128, KC, NT], BF16)
        r_ybr = [[Res("ybr%d_%d" % (c, g)) for g in range(5)] for c in range(KC)]
        mrg = sb("mrg", [128, KC, NT], BF16)
        r_mrg = [Res("mrg%d" % g) for g in range(5)]
        vn_v = mrg[:].rearrange("p k t -> p (k t)")
        wsl = [sb("wsl%d" % i, [128, KC, 256], BF16) for i in range(NSLOT)]
        r_wsl = [Res("wsl%d" % i) for i in range(NSLOT)]
        arena = sb("arena", [128, ARENA_F32], F32)
        ident_f = sb("ident_f", [128, 128], F32); r_idf = Res("idf", True)
        ident_b = sb("ident_b", [128, 128], BF16); r_idb = Res("idb", True)
        mask_f = sb("mask_f", [128, 128], F32); r_mask = Res("mask", True)
        mask_s = sb("mask_s", [64, 64], F32); r_mask_s = Res("mask_s", True)
        a0s = sb("a0s", [4, 64], F32); r_a0s = Res("a0s", True)
        ones4 = sb("ones4", [4, 512], F32); r_ones4 = Res("ones4", True)
        ones4x = sb("ones4x", [4, 128], F32); r_ones4x = Res("ones4x", True)
        hmask = sb("hmask", [4, 128], F32); r_hmask = Res("hmask", True)
        colmask = sb("colmask", [128, NB, 64], BF16); r_colmask = Res("colmask", True)
        rowmask = sb("rowmask", [64, NB], F32); r_rowmask = Res("rowmask", True)
        onesrow = sb("onesrow", [128, 128], BF16); r_onesrow = Res("onesrow", True)
        bpm = sb("bpm", [128, NL, 13, 8], F32); r_bpm = Res("bpm", True)
        kbias = sb("kbias", [128, NL, 8], F32); r_kbias = Res("kbias", True)
        bgate = sb("bgate", [4, NL, 2], F32); r_bgate = Res("bgate", True)
        nbf = sb("nbf", [4, NL], F32); r_nbf = Res("nbf", True)
        gpm = sb("gpm", [128, NL, 8], F32); r_gpm = Res("gpm", True)
        gpmh = sb("gpmh", [128, NL, 8], F32); r_gpmh = Res("gpmh", True)
        cw = sb("cw", [128, NL, 8, 4], F32); r_cw = Res("cw", True)
        cvec = sb("cvec", [128, 4, NL, 8], F32); r_cvec = Res("cvec", True)
        cl = sb("cl", [128, 2, NL, 8], F32); r_cl = Res("cl", True)
        wg = sb("wg", [128, NL, KC, 8], BF16); r_wg = Res("wg", True)
        m0s = sb("m0s", [4, NL, NB], F32); r_m0s = Res("m0s", True)
        wthr = sb("wthr", [128, 17, 8], F32); r_wthr = Res("wthr")
        decbc = sb("decbc", [128, 128], F32); r_decbc = Res("decbc")
        mcall = sb("mcall", [4, 33], F32); r_mcall = Res("mcall")
        mcs = sb("mcs", [4, 3, NB], F32); r_mcs = Res("mcs")
        dec4 = sb("dec4", [4, 32], F32); r_dec4 = Res("dec4")
        decbd = sb("decbd", [4, 4, 32], F32); r_decbd = Res("decbd")
        mpo = sb("mpo", [4, 1], F32); r_mpo = Res("mpo")
        bbl = sb("bbl", [4, 2], F32); r_bbl = Res("bbl")
        eps_t = sb("eps_t", [128, 1], F32); r_eps = Res("eps", True)

        ps = [st.enter_context(nc.psum_tensor("ps%d" % i, [128, 1024], F32)) for i in range(4)]
        psb = [p.bitcast(BF16) for p in ps]
        r_ps = [Res("psb%d" % i) for i in range(8)]

        def bank(b):
            return ps[b // 2][:, (b % 2) * 512:(b % 2) * 512 + 512]

        def bank_bf(b):
            return psb[b // 2][:, (b % 2) * 1024:(b % 2) * 1024 + 1024]

        rot = {"mm": 0, "aux": 0}

        def next_mm():
            rot["mm"] = (rot["mm"] + 1) % 4
            return rot["mm"]

        def next_aux():
            rot["aux"] = (rot["aux"] + 1) % 2
            return 4 + rot["aux"]

        class Arena:
            def __init__(self):
                self.off = 0
                self.res = []

            def reset(self):
                P.barrier(self.res)
                for r in self.res:
                    if r.dsem is not None:
                        P.free_ds.append(r.dsem)
                        r.dsem = None
                self.off = 0
                self.res = []

            def f32(self, shape, name):
                n = int(np.prod(shape[1:]))
                ap = arena[0:shape[0], self.off:self.off + n]
                self.off += n
                assert self.off <= ARENA_F32, "arena overflow %d" % self.off
                r = Res(name)
                self.res.append(r)
                if len(shape) == 3:
                    ap = ap.rearrange("p (a b) -> p a b", a=shape[1])
                return ap, r

            def bf(self, shape, name):
                n = int(np.prod(shape[1:]))
                n32 = (n + 1) // 2
                ap = arena[0:shape[0], self.off:self.off + n32].bitcast(BF16)
                self.off += n32
                assert self.off <= ARENA_F32, "arena overflow %d" % self.off
                r = Res(name)
                self.res.append(r)
                ap = ap[:, 0:n]
                if len(shape) == 3:
                    ap = ap.rearrange("p (a b) -> p a b", a=shape[1])
                return ap, r

        A = Arena()

        wlist = []

        def wsrc(ap2d):
            return ap2d.rearrange("(kc p) n -> p kc n", p=128)

        for l in range(depth):
            for j in range(4):
                wlist.append(("vb%d" % j, wsrc(w_in[l, :, OFF["vb"] + j * 256:OFF["vb"] + (j + 1) * 256])))
            for j in range(4):
                wlist.append(("ub%d" % j, wsrc(w_in[l, :, OFF["ub"] + j * 256:OFF["ub"] + (j + 1) * 256])))
                wlist.append(("zb%d" % j, wsrc(w_in[l, :, OFF["zb"] + j * 256:OFF["zb"] + (j + 1) * 256])))
            for j in range(4):
                wlist.append(("pb%d" % j, wsrc(w_p["b"][l, :, j * 256:(j + 1) * 256])))
                wlist.append(("gb%d" % j, wsrc(w_in[l, :, OFF["gb"] + j * 256:OFF["gb"] + (j + 1) * 256])))
            for h in range(4):
                for nm in ("q", "k", "v", "o", "za"):
                    wlist.append(("%s%d" % (nm, h), wsrc(w_in[l, :, OFF[nm] + h * 256:OFF[nm] + (h + 1) * 256])))
            for j in range(4):
                wlist.append(("pa%d" % j, wsrc(w_p["a"][l, :, j * 256:(j + 1) * 256])))
                wlist.append(("ga%d" % j, wsrc(w_in[l, :, OFF["ga"] + j * 256:OFF["ga"] + (j + 1) * 256])))
            for j in range(4):
                wlist.append(("xc%d" % j, wsrc(w_in[l, :, OFF["xc"] + j * 256:OFF["xc"] + (j + 1) * 256])))
                wlist.append(("zc%d" % j, wsrc(w_in[l, :, OFF["zc"] + j * 256:OFF["zc"] + (j + 1) * 256])))
            for j in range(4):
                wlist.append(("pc%d" % j, wsrc(w_p["c"][l, :, j * 256:(j + 1) * 256])))
                wlist.append(("gc%d" % j, wsrc(w_in[l, :, OFF["gc"] + j * 256:OFF["gc"] + (j + 1) * 256])))
            for j in range(4):
                wlist.append(("wo%d" % j, wsrc(w_out[l, :, j * 256:(j + 1) * 256])))
        wst = {"issued": 0, "next": 0}

        def wneed(names):
            i0 = wst["next"]
            outl = []
            for k, nm in enumerate(names):
                assert wlist[i0 + k][0] == nm, (wlist[i0 + k][0], nm)
            last = min(i0 + len(names) - 1 + AHEAD, len(wlist) - 1)
            while wst["issued"] <= last:
                j = wst["issued"]
                s = j % NSLOT
                P.dma("pool", wsl[s][:], wlist[j][1], writes=[r_wsl[s]], sres=r_wsl[s])
                wst["issued"] += 1
            for k in range(len(names)):
                s = (i0 + k) % NSLOT
                outl.append((wsl[s], r_wsl[s]))
            wst["next"] = i0 + len(names)
            return outl

        P.dma("sp", ident_f[:], c_ident, writes=[r_idf], sres=r_idf)
        P.dma("pool", ident_b[:], c_ident, writes=[r_idb], sres=r_idb)
        P.dma("sp", mask_f[:], c_mask, writes=[r_mask], sres=r_mask)
        P.dma("sp", mask_s[:], c_mask_s, writes=[r_mask_s], sres=r_mask_s)
        P.dma("sp", a0s[:], c_a0s, writes=[r_a0s], sres=r_a0s)
        P.dma("sp", hmask[:], c_hmask, writes=[r_hmask], sres=r_hmask)
        P.dma("pool", colmask[:].rearrange("p a b -> p (a b)"), c_colmask, writes=[r_colmask], sres=r_colmask)
        P.dma("sp", rowmask[:], c_rowmask, writes=[r_rowmask], sres=r_rowmask)
        P.dma("pool", onesrow[:], c_onesrow, writes=[r_onesrow], sres=r_onesrow)
        P.dma("sp", bpm[:].rearrange("p a b c -> p (a b c)"), bpm_d, writes=[r_bpm], sres=r_bpm)
        P.dma("sp", bgate[:].rearrange("p a b -> p (a b)"), bgate_d, writes=[r_bgate], sres=r_bgate)
        P.dma("sp", gpm[:].rearrange("p a b -> p (a b)"), gpm_d, writes=[r_gpm], sres=r_gpm)
        P.dma("sp", cw[:].rearrange("p a b c -> p (a b c)"), cw_d, writes=[r_cw], sres=r_cw)
        P.dma("sp", cvec[:].rearrange("p a b c -> p (a b c)"), cvec_d, writes=[r_cvec], sres=r_cvec)
        P.dma("sp", m0s[:].rearrange("p a b -> p (a b)"), st_m, writes=[r_m0s], sres=r_m0s)
        for l in range(depth):
            P.dma("pool", wg[:, l, :, :], w_in[l, :, OFF["gate"]:OFF["gate"] + 8].rearrange("(kc p) n -> p kc n", p=128),
                  writes=[r_wg], sres=r_wg)
        memset("dve", ones4[:], 1.0, [r_ones4])
        memset("dve", ones4x[:], 1.0, [r_ones4x])
        memset("dve", eps_t[:], LN_EPS, [r_eps])
        memset("dve", mcall[:], 0.0, [r_mcall])
        ts("dve", kbias[:], bpm[:, :, 1, :], 1.0 / 16.0, ALU.mult, [r_bpm], [r_kbias])
        ts("dve", nbf[:], bgate[:, :, 1], -1.0, ALU.mult, [r_bgate], [r_nbf])
        ts("dve", gpmh[:], gpm[:], 0.5, ALU.mult, [r_gpm], [r_gpmh])
        act(cl[:, 0], cvec[:, 3], AF.Exp, [r_cvec], [r_cl], scale=-1.0)
        act(cl[:, 0], cl[:, 0], AF.Ln, [r_cl], [r_cl], bias=1.0)
        ts("dve", cl[:, 1], cl[:, 0], -16.0, ALU.mult, [r_cl], [r_cl])
        ts("dve", cl[:, 0], cl[:, 0], -8.0, ALU.mult, [r_cl], [r_cl])

        def tok_group_of(t0):
            return min(t0 // 512, 4)

        def tile_to_xT(src, r_src, ti, gscale=None):
            t0, tn = TTS[ti]
            g = tok_group_of(t0)
            for half in range(2):
                b = next_aux()
                for j in range(4):
                    kc = half * 4 + j
                    tr(bank(b)[:, j * 128:j * 128 + tn], src[0:tn, kc * 128:(kc + 1) * 128], ident_f[0:tn, 0:tn],
                       [r_src, r_idf], [r_ps[b]], sig=(j == 3))
                act(xT[:, half * 4:(half + 1) * 4, t0:t0 + tn],
                    bank(b).rearrange("p (a b) -> p a b", a=4)[:, :, 0:tn], AF.Identity, [r_ps[b]], [r_xT[g]])

        A.reset()
        xin = [A.f32([128, D], "xin%d" % i) for i in range(2)]
        for ti, (t0, tn) in enumerate(TTS):
            xa, rx = xin[ti % 2]
            P.dma("sp", xa[0:tn, :], x_all[t0:t0 + tn, :], writes=[rx], sres=rx)
            tile_to_xT(xa, rx, ti)

        for l in range(depth):
            last = (l == NL - 1)
            xsrc = x_all if l == 0 else xres

            A.reset()
            _gt = [A.f32([4, 512], "gt%d" % i) for i in range(3)]
            gt = [x[0] for x in _gt]
            r_gt = [x[1] for x in _gt]
            for g, (t0, tn) in enumerate(TGS):
                bi, bf_ = next_mm(), next_mm()
                for kc in range(KC):
                    mm(bank(bi)[0:4, 0:tn], wg[:, l, kc, 0:4], xT[:, kc, t0:t0 + tn], kc == 0, kc == KC - 1,
                       [r_wg, r_xT[g]], [r_ps[bi]])
                for kc in range(KC):
                    mm(bank(bf_)[0:4, 0:tn], wg[:, l, kc, 4:8], xT[:, kc, t0:t0 + tn], kc == 0, kc == KC - 1,
                       [r_wg, r_xT[g]], [r_ps[bf_]])
                it_, sp_, bb_ = gt[0][:, 0:tn], gt[1][:, 0:tn], gt[2][:, 0:tn]
                act(it_, bank(bi)[0:4, 0:tn], AF.Identity, [r_ps[bi], r_bgate], [r_gt[0]], bias=bgate[:, l, 0:1])
                act(sp_, bank(bf_)[0:4, 0:tn], AF.Exp, [r_ps[bf_], r_nbf], [r_gt[1]], bias=nbf[:, l:l + 1], scale=-1.0)
                act(sp_, sp_, AF.Ln, [r_gt[1]], [r_gt[1]], bias=1.0)
                if g < 4:
                    init = 0.0 if g == 0 else bbl[:, 0:1]
                    scan(bb_, ones4[:, 0:tn], sp_, init, ALU.mult, ALU.subtract, [r_ones4, r_gt[1], r_bbl], [r_gt[2]])
                    if g < 3:
                        cp("dve", bbl[:, 0:1], gt[2][:, tn - 1:tn], [r_gt[2]], [r_bbl])
                else:
                    scan(bb_, a0s[:], sp_, 0.0, ALU.mult, ALU.subtract, [r_a0s, r_gt[1]], [r_gt[2]])
                tt("dve", it_, it_, bb_, ALU.subtract, [r_gt[0], r_gt[2]], [r_gt[0]])
                if g < 4:
                    treduce(mcall[:, 17 + 4 * g:21 + 4 * g], gt[0][:, :].rearrange("p (c t) -> p c t", c=4), ALU.max, [r_gt[0]], [r_mcall])
                    scan(mcall[:, 1 + 4 * g:5 + 4 * g], ones4[:, 0:4], mcall[:, 17 + 4 * g:21 + 4 * g], mcall[:, 4 * g:4 * g + 1],
                         ALU.mult, ALU.max, [r_mcall, r_ones4], [r_mcall])
                    tt("dve", dec4[:, 4 * g:4 * g + 4], mcall[:, 4 * g:4 * g + 4], mcall[:, 4 * g + 1:4 * g + 5], ALU.subtract,
                       [r_mcall], [r_dec4])
                    mc_b = bc_last(mcall[:, 1 + 4 * g:5 + 4 * g], 128)
                    v3 = lambda a: a.rearrange("p (c t) -> p c t", c=4)
                    if g == 3:
                        tt("dve", mpo[:], gt[2][:, 511:512], mcall[:, 16:17], ALU.add, [r_gt[2], r_mcall], [r_mpo])
                        P.dma("sp", o_mp[:, l:l + 1], mpo[:], reads=[r_mpo], sres=r_mpo, final=True)
                else:
                    treduce(mcs[:, 0, :], gt[0][:, 0:64].rearrange("p (b t) -> p b t", t=4), ALU.max, [r_gt[0]], [r_mcs])
                    tt("dve", mcs[:, 1, :], mcs[:, 0, :], m0s[:, l, :], ALU.max, [r_mcs, r_m0s], [r_mcs])
                    tt("dve", dec4[:, 16:32], m0s[:, l, :], mcs[:, 1, :], ALU.subtract, [r_m0s, r_mcs], [r_dec4])
                    tt("dve", mcs[:, 2, :], gt[2][:, 0:64].rearrange("p (b t) -> p b t", t=4)[:, :, 3], mcs[:, 1, :], ALU.add,
                       [r_gt[2], r_mcs], [r_mcs])
                    P.dma("sp", o_ms[:, l * NB:(l + 1) * NB], mcs[:, 2, :], reads=[r_mcs], sres=r_mcs, final=True)
                    mc_b = bc_last(mcs[:, 1, :], 4)
                    v3 = lambda a: a.rearrange("p (c t) -> p c t", t=4)
                rd = [r_gt[0], r_gt[2], r_mcall, r_mcs]
                tt("dve", v3(sp_), v3(it_), mc_b, ALU.subtract, rd, [r_gt[1]])
                act(sp_, sp_, AF.Exp, [r_gt[1]], [r_gt[1]])
                tt("dve", v3(bb_), v3(bb_), mc_b, ALU.add, rd, [r_gt[2]])
                act(bb_, bb_, AF.Exp, [r_gt[2]], [r_gt[2]], scale=-1.0)
                ntile = 4 if g < 4 else 1
                b = next_aux()
                for c in range(ntile):
                    cn = 128 if g < 4 else 64
                    tr(bank(b)[0:cn, c * 8:c * 8 + 4], gt[1][:, c * 128:c * 128 + cn], ident_f[0:4, 0:4], [r_gt[1], r_idf], [r_ps[b]], False)
                    tr(bank(b)[0:cn, c * 8 + 4:c * 8 + 8], gt[2][:, c * 128:c * 128 + cn], ident_f[0:4, 0:4], [r_gt[2], r_idf], [r_ps[b]],
                       c == ntile - 1)
                cn = 128 if g < 4 else 64
                cp("dve", wthr[0:cn, 4 * g:4 * g + ntile, :], bank(b)[0:cn, 0:8 * ntile].rearrange("p (c k) -> p c k", k=8),
                   [r_ps[b]], [r_wthr])
            act(dec4[:], dec4[:], AF.Exp, [r_dec4], [r_dec4])
            tt("dve", decbd[:], bc_mid(dec4[:], 4), hmask[:].rearrange("p (a b) -> p a b", a=4), ALU.mult, [r_dec4, r_hmask], [r_decbd])
            b = next_aux()
            mm(bank(b)[:, 0:128], ones4x[:], decbd[:].rearrange("p a b -> p (a b)"), True, True, [r_ones4x, r_decbd], [r_ps[b]])
            cp("dve", decbc[:], bank(b)[:, 0:128], [r_ps[b]], [r_decbc])

            A.reset()
            lnbc, r_lnbc = A.f32([128, 2, D], "lnbc")
            vbrow, r_vbrow = A.bf([128, D], "vbrow")
            wsf, r_wsf = A.f32([128, 4, 128], "wsf")
            wmT, r_wmT = A.bf([128, 4, 128], "wmT")
            wss, r_wss = A.f32([64, 4, 64], "wss")
            mts, r_mts = A.bf([64, 4, 64], "mts")
            bsrow, r_bsrow = A.bf([128, 4 * 128], "bsrow")
            bsrow_s, r_bsrow_s = A.bf([128, 4 * 64], "bsrow_s")
            v32 = [A.f32([128, D], "v32_%d" % i) for i in range(2)]
            bst, r_bst = A.f32([128, 12], "bst")
            mv, r_mv = A.f32([128, 2], "mv")
            rstd, r_rstd = A.f32([128, 1], "rstd")
            szb = [A.f32([128, 512], "szb%d" % i) for i in range(2)]
            uzb = [A.f32([128, 512], "uzb%d" % i) for i in range(2)]
            vns, r_vns = A.bf([64, D], "vns")
            P.dma("sp", lnbc[:].rearrange("p a b -> p (a b)"), pbc(gln_d[l].rearrange("a b -> (a b)"), 128), writes=[r_lnbc], sres=r_lnbc)
            memset("pool", vbrow[:], 0.0, [r_vbrow])
            P.dma("pool", vbrow[0:1, :], b_in[l:l + 1, OFF["vb"]:OFF["vb"] + D], writes=[r_vbrow], sres=r_vbrow)
            memset("pool", bsrow[:], 0.0, [r_bsrow])
            P.dma("pool", bsrow[0:1, :], gbs_d[l:l + 1, :], writes=[r_bsrow], sres=r_bsrow)
            memset("pool", bsrow_s[:], 0.0, [r_bsrow_s])
            P.dma("pool", bsrow_s[0:1, :], gbs_s_d[l:l + 1, :], writes=[r_bsrow_s], sres=r_bsrow_s)
            P.dma("sp", wsf[:], gws_d[l].rearrange("g t s -> t g s"), writes=[r_wsf], sres=r_wsf)
            P.dma("sp", wss[:].rearrange("p a b -> p (a b)"), gws_s_d[l], writes=[r_wss], sres=r_wss)
            b = next_aux()
            for g4 in range(4):
                tr(bank(b)[:, g4 * 128:(g4 + 1) * 128], wsf[:, g4, :], ident_f[:], [r_wsf, r_idf], [r_ps[b]], g4 == 3)
            tt("dve", wmT[:], bank(b).rearrange("p (a b) -> p a b", a=4), bc_mid(mask_f[:], 4), ALU.mult, [r_ps[b], r_mask], [r_wmT])
            tt("dve", mts[:], wss[:], bc_mid(mask_s[:], 4), ALU.mult, [r_wss, r_mask_s], [r_mts])
            wv = wneed(["vb0", "vb1", "vb2", "vb3"])
            r_vn = r_mrg
            for ti, (t0, tn) in enumerate(TTS):
                g = tok_group_of(t0)
                va, rv = v32[ti % 2]
                pr = (ti % 2) * 2
                for j in range(4):
                    bq = pr + j // 2
                    o_ap = bank(bq)[0:tn, (j % 2) * 256:(j % 2) * 256 + 256]
                    for kc in range(KC):
                        mm(o_ap, xT[:, kc, t0:t0 + tn], wv[j][0][:, kc, :], kc == 0, False, [r_xT[g], wv[j][1]], [r_ps[bq]], sig=False)
                    mm(o_ap, onesrow[:, 0:tn], vbrow[:, j * 256:(j + 1) * 256], False, True, [r_onesrow, r_vbrow], [r_ps[bq]])
                for hf in range(2):
                    bnstats(bst[0:tn, hf * 6:hf * 6 + 6], bank(pr + hf)[0:tn, :], [r_ps[pr + hf]], [r_bst])
                bnaggr(mv[0:tn, :], bst[0:tn, :], [r_bst], [r_mv])
                act(rstd[0:tn, :], mv[0:tn, 1:2], AF.Sqrt, [r_mv, r_eps], [r_rstd], bias=eps_t[0:tn, :])
                recip(rstd[0:tn, :], rstd[0:tn, :], [r_rstd], [r_rstd])
                ts("dve", va[0:tn, :], ps[pr // 2][0:tn, :], mv[0:tn, 0:1], ALU.subtract, [r_ps[pr], r_ps[pr + 1], r_mv, r_rstd], [rv],
                   s2=rstd[0:tn, 0:1], op1=ALU.mult)
                tt("pool", va[0:tn, :], va[0:tn, :], lnbc[0:tn, 0, :], ALU.mult, [rv, r_lnbc], [rv])
                if ti < 16:
                    tt("pool", vn_v[0:tn, ti * 1024:(ti + 1) * 1024], va[0:tn, :], lnbc[0:tn, 1, :], ALU.add, [rv, r_lnbc], [r_vn[g]])
                else:
                    tt("pool", va[0:tn, :], va[0:tn, :], lnbc[0:tn, 1, :], ALU.add, [rv, r_lnbc], [rv])
                    P.dma("sp", o_vs[l], va[0:tn, :], reads=[rv], sres=rv, final=True)
                    cp("pool", vns[:, :], va[0:tn, :], [rv], [r_vns])
            for j in range(4):
                wu, wz = wneed(["ub%d" % j, "zb%d" % j])
                for cc in range(2):
                    c = 2 * j + cc
                    g4 = c // 2
                    for g, (t0, tn) in enumerate(TGS):
                        bu, bz, bm = next_mm(), next_mm(), next_aux()
                        for kc in range(KC):
                            mm(bank(bz)[:, 0:tn], wz[0][:, kc, cc * 128:(cc + 1) * 128], xT[:, kc, t0:t0 + tn], kc == 0, kc == KC - 1,
                               [wz[1], r_xT[g]], [r_ps[bz]])
                        for kc in range(KC):
                            mm(bank(bu)[:, 0:tn], wu[0][:, kc, cc * 128:(cc + 1) * 128], xT[:, kc, t0:t0 + tn], kc == 0, kc == KC - 1,
                               [wu[1], r_xT[g]], [r_ps[bu]])
                        if g < 4:
                            for ci in range(4):
                                ti = 4 * g + ci
                                o_ap = bank(bm)[:, ci * 128:(ci + 1) * 128]
                                mm(o_ap, vn_v[:, ti * 1024 + c * 128:ti * 1024 + (c + 1) * 128], wmT[:, g4, :], True, False,
                                   [r_vn[g], r_wmT], [r_ps[bm]], sig=False)
                                mm(o_ap, onesrow[:], bsrow[:, g4 * 128:(g4 + 1) * 128], False, True, [r_onesrow, r_bsrow], [r_ps[bm]],
                                   sig=(ci == 3))
                        else:
                            o_ap = bank(bm)[:, 0:64]
                            mm(o_ap, vns[:, c * 128:(c + 1) * 128], mts[:, g4, :], True, False,
                               [r_vns, r_mts], [r_ps[bm]], sig=False)
                            mm(o_ap, onesrow[:], bsrow_s[:, g4 * 64:(g4 + 1) * 64], False, True, [r_onesrow, r_bsrow_s], [r_ps[bm]])
                        sz, rsz = szb[g % 2]
                        uz, ruz = uzb[g % 2]
                        act(sz[:, 0:tn], bank(bz)[:, 0:tn], AF.Silu, [r_ps[bz], r_bpm], [rsz], bias=bpm[:, l, 7, c:c + 1])
                        stt(uz[:, 0:tn], bank(bu)[:, 0:tn], bpm[:, l, 5, c:c + 1], sz[:, 0:tn], ALU.add, ALU.mult,
                            [r_ps[bu], r_bpm, rsz], [ruz])
                        tt("dve", ybr[:, c, t0:t0 + tn], bank(bm)[:, 0:tn], uz[:, 0:tn], ALU.mult, [r_ps[bm], ruz], [r_ybr[c][g]])

            def proj_merge(br, first):
                A.reset()
                sgb = [A.f32([128, 512], "sg%d" % i) for i in range(2)]
                tmpb = [A.bf([128, 512], "pm%d" % i) for i in range(2)]
                gblk = {"a": 10, "b": 11, "c": 12}[br]
                for j in range(4):
                    wp, wgt = wneed(["p%s%d" % (br, j), "g%s%d" % (br, j)])
                    for cc in range(2):
                        dm = 2 * j + cc
                        for g, (t0, tn) in enumerate(TGS):
                            bp_, bg = next_mm(), next_mm()
                            for kc in range(KC):
                                mm(bank(bg)[:, 0:tn], wgt[0][:, kc, cc * 128:(cc + 1) * 128], xT[:, kc, t0:t0 + tn], kc == 0, kc == KC - 1,
                                   [wgt[1], r_xT[g]], [r_ps[bg]])
                            for fc in range(KC):
                                mm(bank(bp_)[:, 0:tn], wp[0][:, fc, cc * 128:(cc + 1) * 128], ybr[:, fc, t0:t0 + tn], fc == 0, fc == KC - 1,
                                   [wp[1], r_ybr[fc][g]], [r_ps[bp_]])
                            sg, rsg = sgb[g % 2]
                            act(sg[:, 0:tn], bank(bg)[:, 0:tn], AF.Sigmoid, [r_ps[bg], r_bpm], [rsg], bias=bpm[:, l, gblk, dm:dm + 1])
                            if first:
                                tt("dve", mrg[:, dm, t0:t0 + tn], bank(bp_)[:, 0:tn], sg[:, 0:tn], ALU.mult, [r_ps[bp_], rsg], [r_mrg[g]])
                            else:
                                tm, rtm = tmpb[g % 2]
                                tt("dve", tm[:, 0:tn], bank(bp_)[:, 0:tn], sg[:, 0:tn], ALU.mult, [r_ps[bp_], rsg], [rtm])
                                tt("dve", mrg[:, dm, t0:t0 + tn], mrg[:, dm, t0:t0 + tn], tm[:, 0:tn], ALU.add, [r_mrg[g], rtm], [r_mrg[g]])

            proj_merge("b", True)

            A.reset()
            cext = [A.f32([128, 2, 257], "cext%d" % i) for i in range(2)]
            cbf_t = [A.bf([128, 2, 258], "cbf%d" % i) for i in range(4)]
            qTg = [A.bf([128, 2, 512], "qTg%d" % i) for i in range(2)]
            kTg = [A.bf([128, 2, 512], "kTg%d" % i) for i in range(2)]
            vext = [A.bf([128, 258], "vext%d" % i) for i in range(6)]
            so_t = [A.f32([128, 256], "so%d" % i) for i in range(6)]
            sz_t = [A.f32([128, 256], "sza%d" % i) for i in range(6)]
            wk_t = [A.bf([128, 256], "wk%d" % i) for i in range(4)]
            sp_t = [A.bf([128, 128], "sp%d" % i) for i in range(4)]
            ya_t = [A.bf([128, 256], "ya%d" % i) for i in range(4)]
            qd_t = [A.bf([128, 2, 128], "qd%d" % i) for i in range(4)]
            dmx_t = [A.f32([128, 2], "dmx%d" % i) for i in range(4)]
            bst_t = [A.f32([128, 6], "bstA%d" % i) for i in range(4)]
            mv_t = [A.f32([128, 2], "mvA%d" % i) for i in range(4)]
            rstd_t = [A.f32([128, 1], "rstdA%d" % i) for i in range(4)]
            brow, r_brow = A.bf([128, 768], "browA")
            cs_t = [A.f32([128, 2, 257], "cs%d" % i) for i in range(4)]
            cdbs = [A.bf([128, 2, 258], "cdbs%d" % i) for i in range(2)]
            qtm = [A.bf([128, 2, 64], "qtm%d" % i) for i in range(2)]
            vms = [A.bf([64, 258], "vms%d" % i) for i in range(2)]
            nall, r_nall = A.f32([128, NB, 2], "nall")
            nout, r_nout = A.f32([128, NB, 2], "nout")
            memset("pool", brow[:], 0.0, [r_brow])
            for i in range(6):
                memset("pool", vext[i][0][:, 256:258], 1.0, [vext[i][1]])
            nrot = {"i": 0}

            def next_num():
                nrot["i"] = (nrot["i"] + 1) % 4
                return nrot["i"]

            def chunks_of(g):
                t0, tn = TGS[g]
                if g < 4:
                    return [(ci, 4 * g + ci, ci * 128, 128, t0 + ci * 128, (4 * g + ci) % 6) for ci in range(4)]
                return [(0, 16, 0, 64, t0, 16 % 6)]

            for h in range(NH):
                wq, wk_, wv_, wo_, wz_ = wneed(["q%d" % h, "k%d" % h, "v%d" % h, "o%d" % h, "za%d" % h])
                for i3, nm in enumerate(("v", "o", "za")):
                    P.dma("pool", brow[0:1, i3 * 256:(i3 + 1) * 256], b_in[l:l + 1, OFF[nm] + h * 256:OFF[nm] + (h + 1) * 256],
                          writes=[r_brow], sres=r_brow)
                ce, rce = cext[h % 2]
                memset("dve", ce[:], 0.0, [rce])

                def emit_qk(g, h=h, wq=wq, wk_=wk_):
                    t0, tn = TGS[g]
                    qt, rqt = qTg[g % 2]
                    kt, rkt = kTg[g % 2]
                    for dc in range(2):
                        bq, bk = next_mm(), next_mm()
                        for kc in range(KC):
                            mm(bank(bq)[:, 0:tn], wq[0][:, kc, dc * 128:(dc + 1) * 128], xT[:, kc, t0:t0 + tn], kc == 0, kc == KC - 1,
                               [wq[1], r_xT[g]], [r_ps[bq]])
                        for kc in range(KC):
                            mm(bank(bk)[:, 0:tn], wk_[0][:, kc, dc * 128:(dc + 1) * 128], xT[:, kc, t0:t0 + tn], kc == 0, kc == KC - 1,
                               [wk_[1], r_xT[g]], [r_ps[bk]])
                        act(qt[:, dc, 0:tn], bank(bq)[:, 0:tn], AF.Identity, [r_ps[bq], r_bpm], [rqt], bias=bpm[:, l, 0, 2 * h + dc:2 * h + dc + 1])
                        act(kt[:, dc, 0:tn], bank(bk)[:, 0:tn], AF.Identity, [r_ps[bk], r_kbias], [rkt],
                            bias=kbias[:, l, 2 * h + dc:2 * h + dc + 1], scale=1.0 / 16.0)

                def L1(g, sel, wv_=wv_, wo_=wo_, wz_=wz_):
                    for (ci, ti, c0, cn, tk0, bi) in chunks_of(g):
                        if ci not in sel:
                            continue
                        ve, rve = vext[bi]
                        so, rso = so_t[bi]
                        sza, rsza = sz_t[bi]
                        bv, bo = next_mm(), next_mm()
                        for (o_ap, wt, i3, rb) in ((bank(bv)[0:cn, 0:256], wv_, 0, r_ps[bv]), (bank(bv)[0:cn, 256:512], wo_, 1, r_ps[bv]),
                                                   (bank(bo)[0:cn, 0:256], wz_, 2, r_ps[bo])):
                            for kc in range(KC):
                                mm(o_ap, xT[:, kc, tk0:tk0 + cn], wt[0][:, kc, :], kc == 0, False, [r_xT[g], wt[1]], [rb], sig=False)
                            mm(o_ap, onesrow[:, 0:cn], brow[:, i3 * 256:(i3 + 1) * 256], False, True, [r_onesrow, r_brow], [rb])
                        cp("act", ve[0:cn, 0:256], bank(bv)[0:cn, 0:256], [r_ps[bv]], [rve])
                        act(so[0:cn, :], bank(bv)[0:cn, 256:512], AF.Tanh, [r_ps[bv]], [rso], scale=0.5)
                        act(sza[0:cn, :], bank(bo)[0:cn, 0:256], AF.Tanh, [r_ps[bo]], [rsza], scale=0.5)
                        ts("dve", so[0:cn, :], so[0:cn, :], 0.5, ALU.mult, [rso], [rso], s2=0.5, op1=ALU.add)
                        stt(sza[0:cn, :], sza[0:cn, :], 1.0, bank(bo)[0:cn, 0:256], ALU.add, ALU.mult, [rsza, r_ps[bo]], [rsza])

                def L2(g, h=h):
                    qt, rqt = qTg[g % 2]
                    kt, rkt = kTg[g % 2]
                    msk, rmsk = (mask_f, r_mask) if g < 4 else (mask_s, r_mask_s)
                    for (ci, ti, c0, cn, tk0, bi) in chunks_of(g):
                        wkt, rwk = wk_t[ci]
                        spt, rsp = sp_t[ci]
                        bt = next_aux()
                        for dc in range(2):
                            tr(bank_bf(bt)[0:cn, dc * 128:(dc + 1) * 128], kt[:, dc, c0:c0 + cn], ident_b[:], [rkt, r_idb], [r_ps[bt]], dc == 1)
                        ts("dve", wkt[0:cn, :], bank_bf(bt)[0:cn, 0:256], wthr[0:cn, ti, h:h + 1], ALU.mult, [r_ps[bt], r_wthr], [rwk])
                        bs_ = next_aux()
                        for dc in range(2):
                            mm(bank(bs_)[0:cn, 0:cn], kt[:, dc, c0:c0 + cn], qt[:, dc, c0:c0 + cn], dc == 0, dc == 1, [rkt, rqt], [r_ps[bs_]])
                        stt(spt[0:cn, 0:cn], bank(bs_)[0:cn, 0:cn], wthr[0:cn, ti, h:h + 1], msk[0:cn, 0:cn], ALU.mult, ALU.mult,
                            [r_ps[bs_], r_wthr, rmsk], [rsp])
                        if g < 4 and ti > 0:
                            qd, rqd = qd_t[ci]
                            ts("dve", qd[:], qt[:, :, c0:c0 + cn], decbc[:, h * 32 + ti:h * 32 + ti + 1], ALU.mult, [rqt, r_decbc], [rqd])

                def L3(g, h=h, ce=ce, rce=rce):
                    qt, rqt = qTg[g % 2]
                    pre_num = {}
                    post = []

                    def emit_num(ch):
                        (ci, ti, c0, cn, tk0, bi) = ch
                        ve, rve = vext[bi]
                        spt, rsp = sp_t[ci]
                        bn_ = next_num()
                        first = (ti == 0)
                        mm(bank(bn_)[0:cn, 0:257], spt[0:cn, 0:cn], ve[0:cn, 0:257], True, first, [rsp, rve], [r_ps[bn_]], sig=first)
                        if not first:
                            qd, rqd = qd_t[ci]
                            cbf, r_cbf = cbf_t[(ti - 1) % 4]
                            for dc in range(2):
                                mm(bank(bn_)[0:cn, 0:257], qd[:, dc, :], cbf[:, dc, 0:257], False, dc == 1, [rqd, r_cbf], [r_ps[bn_]])
                        return bn_

                    if g < 4:
                        pre_num[0] = emit_num(chunks_of(g)[0])
                        for (ci, ti, c0, cn, tk0, bi) in chunks_of(g):
                            ve, rve = vext[bi]
                            wkt, rwk = wk_t[ci]
                            up = 6 if ci % 2 == 0 else 4
                            for dc in range(2):
                                mm(bank(up + dc)[:, 0:257], wkt[0:cn, dc * 128:(dc + 1) * 128], ve[0:cn, 0:257], True, True, [rwk, rve], [r_ps[up + dc]])
                            stt(ce[:], ce[:], decbc[:, h * 32 + ti:h * 32 + ti + 1], ps[up // 2][:, :].rearrange("p (a b) -> p a b", a=2)[:, :, 0:257],
                                ALU.mult, ALU.add, [rce, r_decbc, r_ps[up], r_ps[up + 1]], [rce])
                            if ti < 15:
                                cbf, r_cbf = cbf_t[ti % 4]
                                cp("act", cbf[:, :, 0:257], ce[:], [rce], [r_cbf])
                            else:
                                P.dma("sp", o_cp[l, h].rearrange("(dc p) e -> p dc e", p=128), ce[:, :, 0:256], reads=[rce], sres=rce, final=True)
                                P.dma("sp", o_np[l, h].rearrange("(dc p) -> p dc", p=128), ce[:, :, 256], reads=[rce], sres=rce, final=True)
                    for (ci, ti, c0, cn, tk0, bi) in chunks_of(g):
                        ve, rve = vext[bi]
                        so, rso = so_t[bi]
                        wkt, rwk = wk_t[ci]
                        spt, rsp = sp_t[ci]
                        dmx, r_dmx = dmx_t[ci]
                        if g < 4:
                            bn_ = pre_num[ci] if ci in pre_num else emit_num((ci, ti, c0, cn, tk0, bi))
                        else:
                            bn_ = next_num()
                            def load_cs(b_):
                                cs, rcs = cs_t[b_ % 4]
                                P.dma("act", cs[:, :, 0:256], st_c[l, b_, h].rearrange("(dc p) e -> p dc e", p=128), writes=[rcs], sres=rcs)
                            for dc in range(2):
                                P.dma("sp", nall[:, :, dc], st_n[l, :, h, dc * 128:(dc + 1) * 128].rearrange("b p -> p b"), writes=[r_nall], sres=r_nall)
                            load_cs(0)
                            load_cs(1)
                            mm(bank(bn_)[0:cn, 0:257], spt[0:cn, 0:cn], ve[0:cn, 0:257], True, False, [rsp, rve], [r_ps[bn_]], sig=False)
                            for b_ in range(NB):
                                if b_ + 2 < NB:
                                    load_cs(b_ + 2)
                                cs, rcs = cs_t[b_ % 4]
                                cb_, rcb = cdbs[b_ % 2]
                                qm, rqm = qtm[b_ % 2]
                                vm, rvm = vms[b_ % 2]
                                up = 6 if b_ % 2 == 0 else 4
                                cp("dve", cs[:, :, 256], nall[:, b_, :], [r_nall], [rcs])
                                cp("act", cb_[:, :, 0:257], cs[:], [rcs], [rcb])
                                stt(qm[:], qt[:, :, 0:64], decbc[:, h * 32 + 16 + b_:h * 32 + 17 + b_], bc_mid(colmask[:, b_, :], 2),
                                    ALU.mult, ALU.mult, [rqt, r_decbc, r_colmask], [rqm])
                                for dc in range(2):
                                    mm(bank(bn_)[0:cn, 0:257], qm[:, dc, :], cb_[:, dc, 0:257], False, (b_ == NB - 1 and dc == 1),
                                       [rqm, rcb], [r_ps[bn_]])
                                act(vm[:, 0:257], ve[0:64, 0:257], AF.Copy, [rve, r_rowmask], [rvm], scale=rowmask[:, b_:b_ + 1])
                                for dc in range(2):
                                    mm(bank(up + dc)[:, 0:257], wkt[0:64, dc * 128:(dc + 1) * 128], vm[:, 0:257], True, True, [rwk, rvm], [r_ps[up + dc]])
                                stt(cs[:], cs[:], decbc[:, h * 32 + 16 + b_:h * 32 + 17 + b_],
                                    ps[up // 2][:, :].rearrange("p (a b) -> p a b", a=2)[:, :, 0:257], ALU.mult, ALU.add,
                                    [rcs, r_decbc, r_ps[up], r_ps[up + 1]], [rcs])
                                P.dma("sp", o_cs[l, b_, h].rearrange("(dc p) e -> p dc e", p=128), cs[:, :, 0:256], reads=[rcs], sres=rcs, final=True)
                                cp("dve", nout[:, b_, :], cs[:, :, 256], [rcs], [r_nout])
                            for dc in range(2):
                                P.dma("sp", o_ns[l, :, h, dc * 128:(dc + 1) * 128].rearrange("b p -> p b"), nout[:, :, dc], reads=[r_nout], sres=r_nout, final=True)
                        post.append((ci, ti, cn, bi, bn_))
                    for (ci, ti, cn, bi, bn_) in post:
                        dmx, r_dmx = dmx_t[ci]
                        act(dmx[0:cn, 0:1], bank(bn_)[0:cn, 256:257], AF.Abs, [r_ps[bn_]], [r_dmx])
                    for (ci, ti, cn, bi, bn_) in post:
                        dmx, r_dmx = dmx_t[ci]
                        ts("dve", dmx[0:cn, 0:1], dmx[0:cn, 0:1], wthr[0:cn, ti, 4 + h:5 + h], ALU.max, [r_dmx, r_wthr], [r_dmx])
                    for (ci, ti, cn, bi, bn_) in post:
                        dmx, r_dmx = dmx_t[ci]
                        recip(dmx[0:cn, 1:2], dmx[0:cn, 0:1], [r_dmx], [r_dmx])
                    for (ci, ti, cn, bi, bn_) in post:
                        dmx, r_dmx = dmx_t[ci]
                        so, rso = so_t[bi]
                        stt(so[0:cn, :], bank(bn_)[0:cn, 0:256], dmx[0:cn, 1:2], so[0:cn, :], ALU.mult, ALU.mult, [r_ps[bn_], r_dmx, rso], [rso])

                def L4(g):
                    CH = chunks_of(g)
                    for (ci, ti, c0, cn, tk0, bi) in CH:
                        hs, rhs = so_t[bi]
                        bst, r_bst = bst_t[ci]
                        mv, r_mv = mv_t[ci]
                        bnstats(bst[0:cn, :], hs[0:cn, :], [rhs], [r_bst])
                    for (ci, ti, c0, cn, tk0, bi) in CH:
                        bst, r_bst = bst_t[ci]
                        mv, r_mv = mv_t[ci]
                        bnaggr(mv[0:cn, :], bst[0:cn, :], [r_bst], [r_mv])
                    for (ci, ti, c0, cn, tk0, bi) in CH:
                        mv, r_mv = mv_t[ci]
                        rstd, r_rstd = rstd_t[ci]
                        act(rstd[0:cn, :], mv[0:cn, 1:2], AF.Sqrt, [r_mv, r_eps], [r_rstd], bias=eps_t[0:cn, :])
                    for (ci, ti, c0, cn, tk0, bi) in CH:
                        hs, rhs = so_t[bi]
                        sza, rsza = sz_t[bi]
                        ya, rya = ya_t[ci]
                        mv, r_mv = mv_t[ci]
                        rstd, r_rstd = rstd_t[ci]
                        recip(rstd[0:cn, :], rstd[0:cn, :], [r_rstd], [r_rstd])
                    for (ci, ti, c0, cn, tk0, bi) in CH:
                        hs, rhs = so_t[bi]
                        sza, rsza = sz_t[bi]
                        mv, r_mv = mv_t[ci]
                        stt(hs[0:cn, :], hs[0:cn, :], mv[0:cn, 0:1], sza[0:cn, :], ALU.subtract, ALU.mult, [rhs, r_mv, rsza], [rhs])
                    for (ci, ti, c0, cn, tk0, bi) in CH:
                        hs, rhs = so_t[bi]
                        ya, rya = ya_t[ci]
                        rstd, r_rstd = rstd_t[ci]
                        act(ya[0:cn, :], hs[0:cn, :], AF.Copy, [rhs, r_rstd], [rya], scale=rstd[0:cn, 0:1])

                def L5(g, h=h):
                    for (ci, ti, c0, cn, tk0, bi) in chunks_of(g):
                        ya, rya = ya_t[ci]
                        bt2 = next_aux()
                        for dc in range(2):
                            tr(bank_bf(bt2)[:, dc * 128:dc * 128 + cn], ya[0:cn, dc * 128:(dc + 1) * 128], ident_b[0:cn, 0:cn], [rya, r_idb],
                               [r_ps[bt2]], dc == 1)
                        for dc in range(2):
                            fc = 2 * h + dc
                            act(ybr[:, fc, tk0:tk0 + cn], bank_bf(bt2)[:, dc * 128:dc * 128 + cn], AF.Copy, [r_ps[bt2], r_gpmh], [r_ybr[fc][g]],
                                scale=gpmh[:, l, fc:fc + 1])

                emit_qk(0)
                L1(0, (0, 1, 2, 3))
                for g in range(len(TGS)):
                    if g == 0 or g == 4:
                        P.mark("A_h%d_%s" % (h, "prompt" if g < 4 else "sample"))
                    L2(g)
                    L3(g)
                    if g + 1 < len(TGS):
                        emit_qk(g + 1)
                        L1(g + 1, (0, 1))
                    L4(g)
                    if g + 1 < len(TGS):
                        L1(g + 1, (2, 3))
                    L5(g)
            P.mark("A_end")
            proj_merge("a", False)

            A.reset()
            waT, r_wa = A.bf([128, 8, 128], "wa")
            wxT, r_wx = A.bf([128, 8, 128], "wx")
            stc, r_stc = A.f32([48, D], "stc")
            sth, r_sth = A.f32([16, D], "sth")
            stg_t, r_stg_t = A.f32([68, D], "stg_t")
            stg, r_stg = A.f32([128, KC, 68], "stg")
            hprev2 = [A.f32([128, 17], "hprev%d" % i) for i in range(2)]
            hist2 = [A.f32([128, 48], "hist%d" % i) for i in range(2)]
            ctail2 = [A.f32([128, 3], "ctail%d" % i) for i in range(2)]
            cbuf = [[A.f32([128, 515], "cb%d_%d" % (i, k)) for k in range(5)] for i in range(2)]
            xcb = [A.bf([128, 512], "xcb%d" % i) for i in range(2)]
            szc = [A.f32([128, 512], "szc%d" % i) for i in range(2)]
            P.dma("pool", waT[:], wa_d[l].rearrange("n i j -> i n j"), writes=[r_wa], sres=r_wa)
            P.dma("pool", wxT[:], wx_d[l].rearrange("n i j -> i n j"), writes=[r_wx], sres=r_wx)
            P.dma("sp", stc[:], st_conv[l], writes=[r_stc], sres=r_stc)
            P.dma("sp", sth[:], st_h[l], writes=[r_sth], sres=r_sth)
            for j in range(4):
                wxc, wzc = wneed(["xc%d" % j, "zc%d" % j])
                for cc in range(2):
                    c = 2 * j + cc
                    hprev, r_hprev = hprev2[cc]
                    hist, r_hist = hist2[cc]
                    ctail, r_ctail = ctail2[cc]
                    b = next_aux()
                    tr(bank(b)[:, 0:48], stc[:, c * 128:(c + 1) * 128], ident_f[0:48, 0:48], [r_stc, r_idf], [r_ps[b]], False)
                    tr(bank(b)[:, 48:64], sth[:, c * 128:(c + 1) * 128], ident_f[0:16, 0:16], [r_sth, r_idf], [r_ps[b]], True)
                    cp("dve", hprev[:, 1:17], bank(b)[:, 48:64], [r_ps[b]], [r_hprev])
                    cp("dve", hist[:], bank(b)[:, 0:48], [r_ps[b]], [r_hist])
                    memset("dve", ctail[:], 0.0, [r_ctail])
                for g, (t0, tn) in enumerate(TGS):
                    views = {}
                    for cc in range(2):
                        c = 2 * j + cc
                        hprev, r_hprev = hprev2[cc]
                        hist, r_hist = hist2[cc]
                        ctail, r_ctail = ctail2[cc]
                        (xp, rxp), (xc_, rxc), (ra, rra), (ib, rib), (t1, rt1) = cbuf[cc]
                        sz, rsz = szc[cc]
                        bx_, bz = next_mm(), next_mm()
                        for kc in range(KC):
                            mm(bank(bx_)[:, 0:tn], wxc[0][:, kc, cc * 128:(cc + 1) * 128], xT[:, kc, t0:t0 + tn], kc == 0, kc == KC - 1,
                               [wxc[1], r_xT[g]], [r_ps[bx_]])
                        for kc in range(KC):
                            mm(bank(bz)[:, 0:tn], wzc[0][:, kc, cc * 128:(cc + 1) * 128], xT[:, kc, t0:t0 + tn], kc == 0, kc == KC - 1,
                               [wzc[1], r_xT[g]], [r_ps[bz]])
                        if g < 4:
                            cp("dve", xp[:, 0:3], ctail[:], [r_ctail], [rxp])
                            act(xp[:, 3:3 + tn], bank(bx_)[:, 0:tn], AF.Identity, [r_ps[bx_], r_bpm], [rxp], bias=bpm[:, l, 8, c:c + 1])
                            if g < 3:
                                cp("dve", ctail[:], xp[:, tn:tn + 3], [rxp], [r_ctail])
                            else:
                                cp("dve", stg[:, c, 17:20], xp[:, tn:tn + 3], [rxp], [r_stg])
                            xp_ = xp
                            xpv = (lambda xp_: (lambda jj: xp_[:, jj:jj + 512]))(xp)
                            xcv = xc_[:, 0:tn]
                        else:
                            xp3 = xp[:, 0:112].rearrange("p (b k) -> p b k", k=7)
                            cp("dve", xp3[:, :, 0:3], hist[:].rearrange("p (b k) -> p b k", k=3), [r_hist], [rxp])
                            act(xp3[:, :, 3:7], bank(bx_)[:, 0:64].rearrange("p (b k) -> p b k", k=4), AF.Identity, [r_ps[bx_], r_bpm], [rxp],
                                bias=bpm[:, l, 8, c:c + 1])
                            cp("dve", stg[:, c, 20:68].rearrange("p (b k) -> p b k", k=3), xp3[:, :, 4:7], [rxp], [r_stg])
                            xpv = (lambda xp3: (lambda jj: xp3[:, :, jj:jj + 4]))(xp3)
                            xcv = xc_[:, 0:64].rearrange("p (b k) -> p b k", k=4)
                        act(sz[:, 0:tn], bank(bz)[:, 0:tn], AF.Silu, [r_ps[bz], r_bpm], [rsz], bias=bpm[:, l, 9, c:c + 1])
                        views[cc] = (xpv, xcv)
                    for cc in range(2):
                        c = 2 * j + cc
                        (xp, rxp), (xc_, rxc), (ra, rra), (ib, rib), (t1, rt1) = cbuf[cc]
                        xb, rxb = xcb[cc]
                        xpv, xcv = views[cc]
                        ts("dve", xcv, xpv(0), cw[:, l, c, 0:1], ALU.mult, [rxp, r_cw, r_cvec], [rxc], s2=cvec[:, 0, l, c:c + 1], op1=ALU.add)
                        for jj in range(1, 4):
                            stt(xcv, xpv(jj), cw[:, l, c, jj:jj + 1], xcv, ALU.mult, ALU.add, [rxp, r_cw, rxc], [rxc])
                        cp("dve", xb[:, 0:tn], xc_[:, 0:tn], [rxc], [rxb])
                    for cc in range(2):
                        c = 2 * j + cc
                        (xp, rxp), (xc_, rxc), (ra, rra), (ib, rib), (t1, rt1) = cbuf[cc]
                        xb, rxb = xcb[cc]
                        br_, bi_ = next_aux(), next_aux()
                        mm(bank(br_)[:, 0:tn], waT[:, c, :], xb[:, 0:tn], True, True, [r_wa, rxb], [r_ps[br_]])
                        mm(bank(bi_)[:, 0:tn], wxT[:, c, :], xb[:, 0:tn], True, True, [r_wx, rxb], [r_ps[bi_]])
                        act(ra[:, 0:tn], bank(br_)[:, 0:tn], AF.Sigmoid, [r_ps[br_], r_cvec], [rra], bias=cvec[:, 1, l, c:c + 1])
                        act(ib[:, 0:tn], bank(bi_)[:, 0:tn], AF.Sigmoid, [r_ps[bi_], r_cvec], [rib], bias=cvec[:, 2, l, c:c + 1])
                    for cc in range(2):
                        c = 2 * j + cc
                        (xp, rxp), (xc_, rxc), (ra, rra), (ib, rib), (t1, rt1) = cbuf[cc]
                        act(t1[:, 0:tn], ra[:, 0:tn], AF.Exp, [rra, r_cl], [rt1], scale=cl[:, 1, l, c:c + 1])
                        act(ra[:, 0:tn], ra[:, 0:tn], AF.Exp, [rra, r_cl], [rra], scale=cl[:, 0, l, c:c + 1])
                    for cc in range(2):
                        (xp, rxp), (xc_, rxc), (ra, rra), (ib, rib), (t1, rt1) = cbuf[cc]
                        act(t1[:, 0:tn], t1[:, 0:tn], AF.Sqrt, [rt1], [rt1], bias=1.0, scale=-1.0)
                    for cc in range(2):
                        c = 2 * j + cc
                        hprev, r_hprev = hprev2[cc]
                        (xp, rxp), (xc_, rxc), (ra, rra), (ib, rib), (t1, rt1) = cbuf[cc]
                        sz, rsz = szc[cc]
                        if g == 0:
                            memset("dve", t1[:, 0:1], 1.0, [rt1])
                        tt("dve", ib[:, 0:tn], ib[:, 0:tn], t1[:, 0:tn], ALU.mult, [rib, rt1], [rib])
                        tt("dve", ib[:, 0:tn], ib[:, 0:tn], xc_[:, 0:tn], ALU.mult, [rib, rxc], [rib])
                        if g < 4:
                            init = 0.0 if g == 0 else hprev[:, 0:1]
                            scan(t1[:, 0:tn], ra[:, 0:tn], ib[:, 0:tn], init, ALU.mult, ALU.add, [rra, rib, r_hprev], [rt1])
                            if g < 3:
                                cp("dve", hprev[:, 0:1], t1[:, tn - 1:tn], [rt1], [r_hprev])
                            else:
                                cp("dve", stg[:, c, 0:1], t1[:, tn - 1:tn], [rt1], [r_stg])
                        else:
                            ib3 = ib[:, 0:64].rearrange("p (b k) -> p b k", k=4)
                            ra3 = ra[:, 0:64].rearrange("p (b k) -> p b k", k=4)
                            tt("dve", hprev[:, 1:17], hprev[:, 1:17], ra3[:, :, 0], ALU.mult, [r_hprev, rra], [r_hprev])
                            tt("dve", ib3[:, :, 0], ib3[:, :, 0], hprev[:, 1:17], ALU.add, [rib, r_hprev], [rib])
                            memset("dve", ra3[:, :, 0], 0.0, [rra])
                            scan(t1[:, 0:64], ra[:, 0:64], ib[:, 0:64], 0.0, ALU.mult, ALU.add, [rra, rib], [rt1])
                            cp("dve", stg[:, c, 1:17], t1[:, 0:64].rearrange("p (b k) -> p b k", k=4)[:, :, 3], [rt1], [r_stg])
                        tt("dve", ybr[:, c, t0:t0 + tn], t1[:, 0:tn], sz[:, 0:tn], ALU.mult, [rt1, rsz], [r_ybr[c][g]])
            for half in range(2):
                b = next_aux()
                for jj in range(4):
                    c = half * 4 + jj
                    tr(bank(b)[0:68, jj * 128:(jj + 1) * 128], stg[:, c, :], ident_f[:], [r_stg, r_idf], [r_ps[b]], jj == 3)
                cp("dve", stg_t[:, half * 512:(half + 1) * 512], bank(b)[0:68, :], [r_ps[b]], [r_stg_t])
            P.dma("sp", o_convh[l], stg_t[:], reads=[r_stg_t], sres=r_stg_t, final=True)
            proj_merge("c", False)

            A.reset()
            lnbc, r_lnbc = A.f32([128, 2, D], "lnbcO")
            xr = [A.f32([128, D], "xr%d" % i) for i in range(4)]
            zt = [A.f32([128, D], "zt%d" % i) for i in range(6)]
            bstO = [A.f32([128, 12], "bstO%d" % i) for i in range(4)]
            mvO = [A.f32([128, 2], "mvO%d" % i) for i in range(4)]
            rstdO = [A.f32([128, 1], "rstdO%d" % i) for i in range(4)]
            P.dma("sp", lnbc[:].rearrange("p a b -> p (a b)"), pbc(fln_d[l].rearrange("a b -> (a b)"), 128), writes=[r_lnbc], sres=r_lnbc)
            wo = wneed(["wo0", "wo1", "wo2", "wo3"])
            for ti, (t0, tn) in enumerate(TTS):
                if ti < 4:
                    xa, rx = xr[ti % 4]
                    rsrc = [] if l == 0 else [r_xres[ti]]
                    P.dma("sp", xa[0:tn, :], xsrc[t0:t0 + tn, :], reads=rsrc, writes=[rx], sres=rx)
            pairs = [list(range(p, min(p + 2, 17))) for p in range(0, 17, 2)]

            def O_mm(pr_tiles):
                for ti in pr_tiles:
                    t0, tn = TTS[ti]
                    g = tok_group_of(t0)
                    xa, rx = xr[ti % 4]
                    za, rz = zt[ti % 6]
                    pr = (ti % 2) * 2
                    for j in range(4):
                        bq = pr + j // 2
                        o_ap = bank(bq)[0:tn, (j % 2) * 256:(j % 2) * 256 + 256]
                        for fc in range(KC):
                            mm(o_ap, mrg[:, fc, t0:t0 + tn], wo[j][0][:, fc, :], fc == 0, fc == KC - 1, [r_mrg[g], wo[j][1]], [r_ps[bq]])
                    stt(za[0:tn, :], xa[0:tn, :], ALPHA, ps[pr // 2][0:tn, :], ALU.mult, ALU.add, [rx, r_ps[pr], r_ps[pr + 1]], [rz])
                    if ti + 4 < 17:
                        t0n, tnn = TTS[ti + 4]
                        rsrc = [] if l == 0 else [r_xres[ti + 4]]
                        P.dma("sp", xa[0:tnn, :], xsrc[t0n:t0n + tnn, :], reads=rsrc, writes=[rx], sres=rx)

            def O_ln(pr_tiles):
                for ti in pr_tiles:
                    t0, tn = TTS[ti]
                    za, rz = zt[ti % 6]
                    bst, r_bst = bstO[ti % 4]
                    for hf in range(2):
                        bnstats(bst[0:tn, hf * 6:hf * 6 + 6], za[0:tn, hf * 512:(hf + 1) * 512], [rz], [r_bst])
                for ti in pr_tiles:
                    t0, tn = TTS[ti]
                    bst, r_bst = bstO[ti % 4]
                    mv, r_mv = mvO[ti % 4]
                    bnaggr(mv[0:tn, :], bst[0:tn, :], [r_bst], [r_mv])
                for ti in pr_tiles:
                    t0, tn = TTS[ti]
                    mv, r_mv = mvO[ti % 4]
                    rstd, r_rstd = rstdO[ti % 4]
                    act(rstd[0:tn, :], mv[0:tn, 1:2], AF.Sqrt, [r_mv, r_eps], [r_rstd], bias=eps_t[0:tn, :])
                for ti in pr_tiles:
                    t0, tn = TTS[ti]
                    rstd, r_rstd = rstdO[ti % 4]
                    recip(rstd[0:tn, :], rstd[0:tn, :], [r_rstd], [r_rstd])
                for ti in pr_tiles:
                    t0, tn = TTS[ti]
                    za, rz = zt[ti % 6]
                    mv, r_mv = mvO[ti % 4]
                    rstd, r_rstd = rstdO[ti % 4]
                    ts("dve", za[0:tn, :], za[0:tn, :], mv[0:tn, 0:1], ALU.subtract, [rz, r_mv, r_rstd], [rz], s2=rstd[0:tn, 0:1], op1=ALU.mult)
                for ti in pr_tiles:
                    t0, tn = TTS[ti]
                    za, rz = zt[ti % 6]
                    tt("dve", za[0:tn, :], za[0:tn, :], lnbc[0:tn, 0, :], ALU.mult, [rz, r_lnbc], [rz])
                for ti in pr_tiles:
                    t0, tn = TTS[ti]
                    za, rz = zt[ti % 6]
                    tt("dve", za[0:tn, :], za[0:tn, :], lnbc[0:tn, 1, :], ALU.add, [rz, r_lnbc], [rz])

            def O_out(pr_tiles):
                for ti in pr_tiles:
                    t0, tn = TTS[ti]
                    za, rz = zt[ti % 6]
                    if last or l == depth - 1:
                        P.dma("sp", y_all[t0:t0 + tn, :], za[0:tn, :], reads=[rz], sres=rz, final=True)
                    else:
                        P.dma("sp", xres[t0:t0 + tn, :], za[0:tn, :], reads=[rz], writes=[r_xres[ti]], sres=rz)
                        tile_to_xT(za, rz, ti)

            O_mm(pairs[0])
            for pi in range(len(pairs)):
                O_ln(pairs[pi])
                if pi + 1 < len(pairs):
                    O_mm(pairs[pi + 1])
                O_out(pairs[pi])
        P.finish()
        global _LAST_PROG
        _LAST_PROG = P
        with nc.allow_non_contiguous_dma(reason="small strided state columns"):
            P.emit()
    return nc


def _consts():
    ident = np.eye(128, dtype=np.float32)
    s = np.arange(128)
    mask = (s[:, None] <= s[None, :]).astype(np.float32)
    t = np.arange(64)
    mask_s = ((t[:, None] // 4 == t[None, :] // 4) & (t[:, None] <= t[None, :])).astype(np.float32)
    a0s = np.ones((4, 64), np.float32)
    a0s[:, ::4] = 0.0
    hmask = np.zeros((4, 4, 32), np.float32)
    for h in range(4):
        hmask[h, h, :] = 1.0
    colmask = np.zeros((128, NB, 64), np.float32)
    rowmask = np.zeros((64, NB), np.float32)
    for b in range(NB):
        colmask[:, b, 4 * b:4 * b + 4] = 1.0
        rowmask[4 * b:4 * b + 4, b] = 1.0
    onesrow = np.zeros((128, 128), np.float32)
    onesrow[0, :] = 1.0
    return dict(c_ident=ident, c_mask=mask, c_mask_s=mask_s, c_a0s=a0s, c_hmask=hmask.reshape(4, 128),
                c_colmask=colmask.reshape(128, NB * 64), c_rowmask=rowmask, c_onesrow=onesrow)


_NC_CACHE = {}


def kernel(x_prompt, x_sample, state_mlstm_c, state_mlstm_n, state_mlstm_m, state_lru_conv, state_lru_h,
           w_in, b_in, mlstm_norm_g, gmlp_ln_g, gmlp_ln_b, gmlp_ws, gmlp_bs, lru_conv_w, lru_conv_b,
           lru_wa, lru_ba, lru_wx, lru_bx, lru_lambda, w_proj_a, w_proj_b, w_proj_c, w_out, ln_g, ln_b, _depth=NL):
    f = lambda a: np.ascontiguousarray(np.asarray(a, dtype=np.float32))
    x_prompt, x_sample = f(x_prompt), f(x_sample)
    w_in, b_in = f(w_in), f(b_in)
    if _depth not in _NC_CACHE:
        _NC_CACHE[_depth] = build(_depth)
    nc = _NC_CACHE[_depth]
    bblocks = np.stack([b_in[:, OFF[n]:OFF[n] + D] for n in BLK], axis=1)
    bpm = bblocks.reshape(NL, 13, 8, 128).transpose(3, 0, 1, 2).reshape(128, NL * 13 * 8)
    bgate = np.stack([b_in[:, 5120:5124], b_in[:, 5124:5128]], axis=2).transpose(1, 0, 2).reshape(4, NL * 2)
    pm = lambda a: f(a).reshape(NL, 8, 128).transpose(2, 0, 1)
    gpm = pm(mlstm_norm_g).reshape(128, NL * 8)
    cw = f(lru_conv_w).reshape(NL, 4, 8, 128).transpose(3, 0, 2, 1).reshape(128, NL * 8 * 4)
    cvec = np.stack([pm(lru_conv_b), pm(lru_ba), pm(lru_bx), pm(lru_lambda)], axis=1).reshape(128, 4 * NL * 8)
    gln = np.stack([f(gmlp_ln_g), f(gmlp_ln_b)], axis=1)
    fln = np.stack([f(ln_g), f(ln_b)], axis=1)
    gws = f(gmlp_ws)
    gws_s = np.stack([np.stack([np.tile(gws[l, g, :4, :4].T, (16, 16)) for g in range(4)], axis=1) for l in range(NL)], 0)
    gws_s = gws_s.reshape(NL, 64, 4 * 64)
    gbs = f(gmlp_bs).reshape(NL, 4 * 128)
    gbs_s = np.stack([np.concatenate([np.tile(f(gmlp_bs)[l, g, :4], 16) for g in range(4)]) for l in range(NL)], 0)
    shared = dict(w_in=w_in, w_pa=f(w_proj_a), w_pb=f(w_proj_b), w_pc=f(w_proj_c), w_out=f(w_out), b_in=b_in,
                  bpm=np.ascontiguousarray(bpm), bgate=np.ascontiguousarray(bgate), gpm=np.ascontiguousarray(gpm),
                  cw=np.ascontiguousarray(cw), cvec=np.ascontiguousarray(cvec), lru_wa=f(lru_wa), lru_wx=f(lru_wx),
                  gln=np.ascontiguousarray(gln), fln=np.ascontiguousarray(fln), gws=gws,
                  gws_s=np.ascontiguousarray(gws_s), gbs=np.ascontiguousarray(gbs), gbs_s=np.ascontiguousarray(gbs_s))
    shared.update(_consts())
    sc, sn, sm = f(state_mlstm_c), f(state_mlstm_n), f(state_mlstm_m)
    sconv, sh = f(state_lru_conv), f(state_lru_h)
    in_maps = []
    for c in range(8):
        b0 = c * NB
        m = dict(shared)
        m["x_all"] = np.ascontiguousarray(np.concatenate([x_prompt[c], x_sample[b0:b0 + NB].reshape(NS, D)], axis=0))
        m["st_c"] = np.ascontiguousarray(sc[:, b0:b0 + NB])
        m["st_n"] = np.ascontiguousarray(sn[:, b0:b0 + NB])
        m["st_m"] = np.ascontiguousarray(sm[:, b0:b0 + NB].transpose(2, 0, 1).reshape(4, NL * NB))
        m["st_conv"] = np.ascontiguousarray(sconv[:, b0:b0 + NB].reshape(NL, 48, D))
        m["st_h"] = np.ascontiguousarray(sh[:, b0:b0 + NB])
        in_maps.append(m)
    res = run_bass_kernel_spmd(nc, in_maps, core_ids=list(range(8)))
    R = res.results
    g = lambda k, c: np.asarray(R[c][k], dtype=np.float32)
    y_p = np.stack([g("y_all", c)[:NTP] for c in range(8)], 0)
    y_s = np.concatenate([g("y_all", c)[NTP:].reshape(NB, 4, D) for c in range(8)], 0)
    c_p = np.stack([g("o_cp", c) for c in range(8)], 1)
    n_p = np.stack([g("o_np", c) for c in range(8)], 1)
    m_p = np.stack([g("o_mp", c).T for c in range(8)], 1)
    ch = [g("o_convh", c) for c in range(8)]
    conv_p = np.stack([x[:, 17:20] for x in ch], 1)
    h_p = np.stack([x[:, 0] for x in ch], 1)
    c_s = np.concatenate([g("o_cs", c) for c in range(8)], 1)
    n_s = np.concatenate([g("o_ns", c) for c in range(8)], 1)
    m_s = np.concatenate([g("o_ms", c).reshape(4, NL, NB).transpose(1, 2, 0) for c in range(8)], 1)
    conv_s = np.concatenate([x[:, 20:68].reshape(NL, NB, 3, D) for x in ch], 1)
    h_s = np.concatenate([x[:, 1:17] for x in ch], 1)
    v_s = np.concatenate([g("o_vs", c).reshape(NL, NB, 4, D) for c in range(8)], 1)
    outs = (y_p, y_s, c_p, n_p, m_p, conv_p, h_p, c_s, n_s, m_s, conv_s, h_s, v_s)
    return tuple(np.ascontiguousarray(o, dtype=np.float32) for o in outs)
```

```python
import numpy as np
from contextlib import ExitStack
import concourse.bass as bass
import concourse.mybir as mybir
from concourse.bass_utils import run_bass_kernel_spmd

F32 = mybir.dt.float32
BF16 = mybir.dt.bfloat16
AF = mybir.ActivationFunctionType
ALU = mybir.AluOpType
AX = mybir.AxisListType
ENG = ("pe", "act", "dve", "pool", "sp")

NL = 4
D = 1024
KC = 8
NTP = 2048
NS = 64
NT = NTP + NS
NB = 16
NH = 4
DK = 256
IN_W = 13320
TGS = [(0, 512), (512, 512), (1024, 512), (1536, 512), (2048, 64)]
TTS = [(i * 128, 128) for i in range(16)] + [(2048, 64)]
OFF = dict(q=0, k=1024, v=2048, o=3072, za=4096, gate=5120, ub=5128, vb=6152, zb=7176, xc=8200, zc=9224,
           ga=10248, gb=11272, gc=12296)
BLK = ["q", "k", "v", "o", "za", "ub", "vb", "zb", "xc", "zc", "ga", "gb", "gc"]
ALPHA = float((2 * NL) ** 0.25)
LN_EPS = 1e-5
NSLOT = 10
AHEAD = 5
ARENA_F32 = 13312


class Tok:
    __slots__ = ("eng", "sem", "val")

    def __init__(self, eng, sem, val):
        self.eng, self.sem, self.val = eng, sem, val


class Res:
    __slots__ = ("name", "lw", "rd", "const", "dsem")

    def __init__(self, name, const=False):
        self.name, self.lw, self.rd, self.const, self.dsem = name, None, {}, const, None


class Prog:
    def __init__(self, nc, stack):
        self.nc, self.stack = nc, stack
        self.ops = {e: [] for e in ENG}
        self.sem = {e: stack.enter_context(nc.semaphore("sem_" + e)) for e in ENG}
        self.cnt = {e: 0 for e in ENG}
        self.cur = {e: Tok(e, self.sem[e], None) for e in ENG}
        self.waited = {}
        self.nds = 0
        self.free_ds = []
        self.store_toks = {}
        self.marks = []

    def mark(self, name):
        self.marks.append((name, sum(1 for o in self.ops["pe"] if o[1] is not None)))

    def _dsem(self, res):
        if res.dsem is None:
            if self.free_ds:
                res.dsem = self.free_ds.pop()
            else:
                self.nds += 1
                res.dsem = [self.stack.enter_context(self.nc.semaphore("ds_%d" % self.nds)), 0]
        return res.dsem

    def _wait(self, eng, t, waits):
        key = (eng, id(t.sem))
        if self.waited.get(key, 0) >= t.val:
            return
        self.waited[key] = t.val
        waits.append((t.sem, t.val))

    def _deps(self, eng, reads, writes, inorder):
        deps = []
        for r in reads:
            if r.lw is not None:
                deps.append((r.lw, "raw"))
        for w in writes:
            if w.lw is not None:
                deps.append((w.lw, "waw"))
            for t in w.rd.values():
                deps.append((t, "war"))
        waits = []
        for t, kind in deps:
            if t.eng == eng and inorder and (kind != "raw" or eng == "pe"):
                continue
            if t.val is None:
                raise RuntimeError("dependency on unsignaled op (%s)" % t.eng)
            self._wait(eng, t, waits)
        return waits

    def _commit(self, tok, reads, writes):
        for r in reads:
            if not r.const:
                r.rd[id(tok.sem)] = tok
        for w in writes:
            w.lw = tok
            w.rd = {}

    def op(self, eng, fn, reads=(), writes=(), sig=None):
        if sig is None:
            sig = eng != "pe"
        waits = self._deps(eng, reads, writes, True)
        tok = self.cur[eng]
        self.ops[eng].append((waits, fn, (self.sem[eng], 1) if sig else None))
        self._commit(tok, reads, writes)
        if sig:
            self.cnt[eng] += 1
            tok.val = self.cnt[eng]
            self.cur[eng] = Tok(eng, self.sem[eng], None)

    def dma(self, q, out, in_, reads=(), writes=(), sres=None, final=False):
        waits = self._deps(q, reads, writes, False)
        ds = self._dsem(sres)
        ds[1] += 1
        tok = Tok(None, ds[0], 16 * ds[1])
        self.ops[q].append((waits, lambda e: e.dma_start(out=out, in_=in_), (ds[0], 16)))
        self._commit(tok, reads, writes)
        if final:
            self.store_toks[id(tok.sem)] = tok

    def barrier(self, dma_res=()):
        toks = [Tok(e, self.sem[e], self.cnt[e]) for e in ENG if self.cnt[e] > 0]
        for r in dma_res:
            if r.dsem is not None and r.dsem[1] > 0:
                toks.append(Tok(None, r.dsem[0], 16 * r.dsem[1]))
        for e in ENG:
            waits = []
            for t in toks:
                if t.eng == e:
                    continue
                self._wait(e, t, waits)
            if waits:
                self.ops[e].append((waits, None, None))

    def finish(self):
        waits = [(t.sem, t.val) for t in self.store_toks.values()]
        self.ops["sp"].append((waits, None, None))

    def emit(self):
        def mk(name):
            def body(e):
                for waits, fn, inc in self.ops[name]:
                    for sem, val in waits:
                        e.wait_ge(sem, val)
                    if fn is None:
                        continue
                    ins = fn(e)
                    if inc is not None:
                        ins.then_inc(inc[0], inc[1])
            return body

        with self.nc.Block() as block:
            block.tensor(mk("pe"))
            block.scalar(mk("act"))
            block.vector(mk("dve"))
            block.gpsimd(mk("pool"))
            block.sync(mk("sp"))


def bc_last(ap, n):
    pat = [list(x) for x in ap.ap]
    return bass.AP(ap.tensor, ap.offset, pat + [[0, n]])


def bc_mid(ap, n):
    pat = [list(x) for x in ap.ap]
    return bass.AP(ap.tensor, ap.offset, [pat[0], [0, n]] + pat[1:])


def pbc(ap_row, n):
    pat = [list(x) for x in ap_row.ap]
    return bass.AP(ap_row.tensor, ap_row.offset, [[0, n]] + pat[-1:])


def build(depth=NL):
    nc = bass.Bass("TRN2", target_bir_lowering=False)

    def din(name, shape):
        return nc.dram_tensor(name, list(shape), F32, kind="ExternalInput").ap()

    def dout(name, shape):
        return nc.dram_tensor(name, list(shape), F32, kind="ExternalOutput").ap()

    x_all = din("x_all", [NT, D])
    w_in = din("w_in", [NL, D, IN_W])
    w_p = {"a": din("w_pa", [NL, D, D]), "b": din("w_pb", [NL, D, D]), "c": din("w_pc", [NL, D, D])}
    w_out = din("w_out", [NL, D, D])
    b_in = din("b_in", [NL, IN_W])
    bpm_d = din("bpm", [128, NL * 13 * 8])
    bgate_d = din("bgate", [4, NL * 2])
    gpm_d = din("gpm", [128, NL * 8])
    cw_d = din("cw", [128, NL * 8 * 4])
    cvec_d = din("cvec", [128, 4 * NL * 8])
    wa_d = din("lru_wa", [NL, 8, 128, 128])
    wx_d = din("lru_wx", [NL, 8, 128, 128])
    gln_d = din("gln", [NL, 2, D])
    fln_d = din("fln", [NL, 2, D])
    gws_d = din("gws", [NL, 4, 128, 128])
    gws_s_d = din("gws_s", [NL, 64, 4 * 64])
    gbs_d = din("gbs", [NL, 4 * 128])
    gbs_s_d = din("gbs_s", [NL, 4 * 64])
    st_c = din("st_c", [NL, NB, NH, DK, DK])
    st_n = din("st_n", [NL, NB, NH, DK])
    st_m = din("st_m", [4, NL * NB])
    st_conv = din("st_conv", [NL, 48, D])
    st_h = din("st_h", [NL, NB, D])
    c_ident = din("c_ident", [128, 128])
    c_mask = din("c_mask", [128, 128])
    c_mask_s = din("c_mask_s", [64, 64])
    c_a0s = din("c_a0s", [4, 64])
    c_hmask = din("c_hmask", [4, 128])
    c_colmask = din("c_colmask", [128, NB * 64])
    c_rowmask = din("c_rowmask", [64, NB])
    c_onesrow = din("c_onesrow", [128, 128])

    y_all = dout("y_all", [NT, D])
    o_cp = dout("o_cp", [NL, NH, DK, DK])
    o_np = dout("o_np", [NL, NH, DK])
    o_mp = dout("o_mp", [4, NL])
    o_convh = dout("o_convh", [NL, 68, D])
    o_cs = dout("o_cs", [NL, NB, NH, DK, DK])
    o_ns = dout("o_ns", [NL, NB, NH, DK])
    o_ms = dout("o_ms", [4, NL * NB])
    o_vs = dout("o_vs", [NL, NS, D])
    xres = nc.dram_tensor("xres", [NT, D], F32, kind="Internal").ap()
    r_xres = [Res("xres%d" % i) for i in range(17)]

    with ExitStack() as st:
        P = Prog(nc, st)

        def sb(name, shape, dt=F32):
            return st.enter_context(nc.sbuf_tensor("s_" + name, list(shape), dt))

        def mm(out, lhsT, rhs, start, stop, reads, writes, sig=None):
            P.op("pe", lambda e: e.matmul(out, lhsT=lhsT, rhs=rhs, start=start, stop=stop), reads, writes,
                 sig=(stop if sig is None else sig))

        def tr(out, in_, ident, reads, writes, sig):
            P.op("pe", lambda e: e.transpose(out, in_, ident), reads, writes, sig=sig)

        def act(out, in_, func, reads, writes, bias=None, scale=None):
            kw = {}
            if bias is not None:
                kw["bias"] = bias
            if scale is not None:
                kw["scale"] = scale
            P.op("act", lambda e: e.activation(out=out, in_=in_, func=func, **kw), reads, writes)

        def tt(eng, out, in0, in1, op, reads, writes):
            P.op(eng, lambda e: e.tensor_tensor(out=out, in0=in0, in1=in1, op=op), reads, writes)

        def ts(eng, out, in0, s1, op0, reads, writes, s2=None, op1=None):
            if op1 is None:
                P.op(eng, lambda e: e.tensor_scalar(out=out, in0=in0, scalar1=s1, scalar2=None, op0=op0), reads, writes)
            else:
                P.op(eng, lambda e: e.tensor_scalar(out=out, in0=in0, scalar1=s1, scalar2=s2, op0=op0, op1=op1), reads, writes)

        def stt(out, in0, scalar, in1, op0, op1, reads, writes):
            P.op("dve", lambda e: e.scalar_tensor_tensor(out=out, in0=in0, scalar=scalar, in1=in1, op0=op0, op1=op1), reads, writes)

        def cp(eng, out, in_, reads, writes):
            if eng == "act":
                act(out, in_, AF.Identity, reads, writes)
            else:
                P.op(eng, lambda e: e.tensor_copy(out=out, in_=in_), reads, writes)

        def memset(eng, ap, val, writes):
            P.op(eng, lambda e: e.memset(ap, val), (), writes)

        def bnstats(out, in_, reads, writes):
            P.op("dve", lambda e: e.bn_stats(out=out, in_=in_), reads, writes)

        def bnaggr(out, in_, reads, writes):
            P.op("dve", lambda e: e.bn_aggr(out=out, in_=in_), reads, writes)

        def recip(out, in_, reads, writes):
            P.op("dve", lambda e: e.reciprocal(out=out, in_=in_), reads, writes)

        def scan(out, d0, d1, init, op0, op1, reads, writes):
            P.op("dve", lambda e: e.tensor_tensor_scan(out=out, data0=d0, data1=d1, initial=init, op0=op0, op1=op1), reads, writes)

        def treduce(out, in_, op, reads, writes):
            P.op("dve", lambda e: e.tensor_reduce(out=out, in_=in_, axis=AX.X, op=op), reads, writes)

        xT = sb("xT", [128, KC, NT], BF16)
        r_xT = [Res("xT%d" % i) for i in range(5)]
        ybr = sb("ybr", [128, KC, NT], BF16)
        r_ybr = [[Res("ybr%d_%d" % (c, g)) for g in range(5)] for c in range(KC)]
        mrg = sb("mrg", [128, KC, NT], BF16)
        r_mrg = [Res("mrg%d" % g) for g in range(5)]
        vn_v = mrg[:].rearrange("p k t -> p (k t)")
        wsl = [sb("wsl%d" % i, [128, KC, 256], BF16) for i in range(NSLOT)]
        r_wsl = [Res("wsl%d" % i) for i in range(NSLOT)]
        arena = sb("arena", [128, ARENA_F32], F32)
        ident_f = sb("ident_f", [128, 128], F32); r_idf = Res("idf", True)
        ident_b = sb("ident_b", [128, 128], BF16); r_idb = Res("idb", True)
        mask_f = sb("mask_f", [128, 128], F32); r_mask = Res("mask", True)
        mask_s = sb("mask_s", [64, 64], F32); r_mask_s = Res("mask_s", True)
        a0s = sb("a0s", [4, 64], F32); r_a0s = Res("a0s", True)
        ones4 = sb("ones4", [4, 512], F32); r_ones4 = Res("ones4", True)
        ones4x = sb("ones4x", [4, 128], F32); r_ones4x = Res("ones4x", True)
        hmask = sb("hmask", [4, 128], F32); r_hmask = Res("hmask", True)
        colmask = sb("colmask", [128, NB, 64], BF16); r_colmask = Res("colmask", True)
        rowmask = sb("rowmask", [64, NB], F32); r_rowmask = Res("rowmask", True)
        onesrow = sb("onesrow", [128, 128], BF16); r_onesrow = Res("onesrow", True)
        bpm = sb("bpm", [128, NL, 13, 8], F32); r_bpm = Res("bpm", True)
        kbias = sb("kbias", [128, NL, 8], F32); r_kbias = Res("kbias", True)
        bgate = sb("bgate", [4, NL, 2], F32); r_bgate = Res("bgate", True)
        nbf = sb("nbf", [4, NL], F32); r_nbf = Res("nbf", True)
        gpm = sb("gpm", [128, NL, 8], F32); r_gpm = Res("gpm", True)
        gpmh = sb("gpmh", [128, NL, 8], F32); r_gpmh = Res("gpmh", True)
        cw = sb("cw", [128, NL, 8, 4], F32); r_cw = Res("cw", True)
        cvec = sb("cvec", [128, 4, NL, 8], F32); r_cvec = Res("cvec", True)
        cl = sb("cl", [128, 2, NL, 8], F32); r_cl = Res("cl", True)
        wg = sb("wg", [128, NL, KC, 8], BF16); r_wg = Res("wg", True)
        m0s = sb("m0s", [4, NL, NB], F32); r_m0s = Res("m0s", True)
        wthr = sb("wthr", [128, 17, 8], F32); r_wthr = Res("wthr")
        decbc = sb("decbc", [128, 128], F32); r_decbc = Res("decbc")
        mcall = sb("mcall", [4, 33], F32); r_mcall = Res("mcall")
        mcs = sb("mcs", [4, 3, NB], F32); r_mcs = Res("mcs")
        dec4 = sb("dec4", [4, 32], F32); r_dec4 = Res("dec4")
        decbd = sb("decbd", [4, 4, 32], F32); r_decbd = Res("decbd")
        mpo = sb("mpo", [4, 1], F32); r_mpo = Res("mpo")
        bbl = sb("bbl", [4, 2], F32); r_bbl = Res("bbl")
        eps_t = sb("eps_t", [128, 1], F32); r_eps = Res("eps", True)

        ps = [st.enter_context(nc.psum_tensor("ps%d" % i, [128, 1024], F32)) for i in range(4)]
        psb = [p.bitcast(BF16) for p in ps]
        r_ps = [Res("psb%d" % i) for i in range(8)]

        def bank(b):
            return ps[b // 2][:, (b % 2) * 512:(b % 2) * 512 + 512]

        def bank_bf(b):
            return psb[b // 2][:, (b % 2) * 1024:(b % 2) * 1024 + 1024]

        rot = {"mm": 0, "aux": 0}

        def next_mm():
            rot["mm"] = (rot["mm"] + 1) % 4
            return rot["mm"]

        def next_aux():
            rot["aux"] = (rot["aux"] + 1) % 2
            return 4 + rot["aux"]

        class Arena:
            def __init__(self):
                self.off = 0
                self.res = []

            def reset(self):
                P.barrier(self.res)
                for r in self.res:
                    if r.dsem is not None:
                        P.free_ds.append(r.dsem)
                        r.dsem = None
                self.off = 0
                self.res = []

            def f32(self, shape, name):
                n = int(np.prod(shape[1:]))
                ap = arena[0:shape[0], self.off:self.off + n]
                self.off += n
                assert self.off <= ARENA_F32, "arena overflow %d" % self.off
                r = Res(name)
                self.res.append(r)
                if len(shape) == 3:
                    ap = ap.rearrange("p (a b) -> p a b", a=shape[1])
                return ap, r

            def bf(self, shape, name):
                n = int(np.prod(shape[1:]))
                n32 = (n + 1) // 2
                ap = arena[0:shape[0], self.off:self.off + n32].bitcast(BF16)
                self.off += n32
                assert self.off <= ARENA_F32, "arena overflow %d" % self.off
                r = Res(name)
                self.res.append(r)
                ap = ap[:, 0:n]
                if len(shape) == 3:
                    ap = ap.rearrange("p (a b) -> p a b", a=shape[1])
                return ap, r

        A = Arena()

        wlist = []

        def wsrc(ap2d):
            return ap2d.rearrange("(kc p) n -> p kc n", p=128)

        for l in range(depth):
            for j in range(4):
                wlist.append(("vb%d" % j, wsrc(w_in[l, :, OFF["vb"] + j * 256:OFF["vb"] + (j + 1) * 256])))
            for j in range(4):
                wlist.append(("ub%d" % j, wsrc(w_in[l, :, OFF["ub"] + j * 256:OFF["ub"] + (j + 1) * 256])))
                wlist.append(("zb%d" % j, wsrc(w_in[l, :, OFF["zb"] + j * 256:OFF["zb"] + (j + 1) * 256])))
            for j in range(4):
                wlist.append(("pb%d" % j, wsrc(w_p["b"][l, :, j * 256:(j + 1) * 256])))
                wlist.append(("gb%d" % j, wsrc(w_in[l, :, OFF["gb"] + j * 256:OFF["gb"] + (j + 1) * 256])))
            for h in range(4):
                for nm in ("q", "k", "v", "o", "za"):
                    wlist.append(("%s%d" % (nm, h), wsrc(w_in[l, :, OFF[nm] + h * 256:OFF[nm] + (h + 1) * 256])))
            for j in range(4):
                wlist.append(("pa%d" % j, wsrc(w_p["a"][l, :, j * 256:(j + 1) * 256])))
                wlist.append(("ga%d" % j, wsrc(w_in[l, :, OFF["ga"] + j * 256:OFF["ga"] + (j + 1) * 256])))
            for j in range(4):
                wlist.append(("xc%d" % j, wsrc(w_in[l, :, OFF["xc"] + j * 256:OFF["xc"] + (j + 1) * 256])))
                wlist.append(("zc%d" % j, wsrc(w_in[l, :, OFF["zc"] + j * 256:OFF["zc"] + (j + 1) * 256])))
            for j in range(4):
                wlist.append(("pc%d" % j, wsrc(w_p["c"][l, :, j * 256:(j + 1) * 256])))
                wlist.append(("gc%d" % j, wsrc(w_in[l, :, OFF["gc"] + j * 256:OFF["gc"] + (j + 1) * 256])))
            for j in range(4):
                wlist.append(("wo%d" % j, wsrc(w_out[l, :, j * 256:(j + 1) * 256])))
        wst = {"issued": 0, "next": 0}

        def wneed(names):
            i0 = wst["next"]
            outl = []
            for k, nm in enumerate(names):
                assert wlist[i0 + k][0] == nm, (wlist[i0 + k][0], nm)
            last = min(i0 + len(names) - 1 + AHEAD, len(wlist) - 1)
            while wst["issued"] <= last:
                j = wst["issued"]
                s = j % NSLOT
                P.dma("pool", wsl[s][:], wlist[j][1], writes=[r_wsl[s]], sres=r_wsl[s])
                wst["issued"] += 1
            for k in range(len(names)):
                s = (i0 + k) % NSLOT
                outl.append((wsl[s], r_wsl[s]))
            wst["next"] = i0 + len(names)
            return outl

        P.dma("sp", ident_f[:], c_ident, writes=[r_idf], sres=r_idf)
        P.dma("pool", ident_b[:], c_ident, writes=[r_idb], sres=r_idb)
        P.dma("sp", mask_f[:], c_mask, writes=[r_mask], sres=r_mask)
        P.dma("sp", mask_s[:], c_mask_s, writes=[r_mask_s], sres=r_mask_s)
        P.dma("sp", a0s[:], c_a0s, writes=[r_a0s], sres=r_a0s)
        P.dma("sp", hmask[:], c_hmask, writes=[r_hmask], sres=r_hmask)
        P.dma("pool", colmask[:].rearrange("p a b -> p (a b)"), c_colmask, writes=[r_colmask], sres=r_colmask)
        P.dma("sp", rowmask[:], c_rowmask, writes=[r_rowmask], sres=r_rowmask)
        P.dma("pool", onesrow[:], c_onesrow, writes=[r_onesrow], sres=r_onesrow)
        P.dma("sp", bpm[:].rearrange("p a b c -> p (a b c)"), bpm_d, writes=[r_bpm], sres=r_bpm)
        P.dma("sp", bgate[:].rearrange("p a b -> p (a b)"), bgate_d, writes=[r_bgate], sres=r_bgate)
        P.dma("sp", gpm[:].rearrange("p a b -> p (a b)"), gpm_d, writes=[r_gpm], sres=r_gpm)
        P.dma("sp", cw[:].rearrange("p a b c -> p (a b c)"), cw_d, writes=[r_cw], sres=r_cw)
        P.dma("sp", cvec[:].rearrange("p a b c -> p (a b c)"), cvec_d, writes=[r_cvec], sres=r_cvec)
        P.dma("sp", m0s[:].rearrange("p a b -> p (a b)"), st_m, writes=[r_m0s], sres=r_m0s)
        for l in range(depth):
            P.dma("pool", wg[:, l, :, :], w_in[l, :, OFF["gate"]:OFF["gate"] + 8].rearrange("(kc p) n -> p kc n", p=128),
                  writes=[r_wg], sres=r_wg)
        memset("dve", ones4[:], 1.0, [r_ones4])
        memset("dve", ones4x[:], 1.0, [r_ones4x])
        memset("dve", eps_t[:], LN_EPS, [r_eps])
        memset("dve", mcall[:], 0.0, [r_mcall])
        ts("dve", kbias[:], bpm[:, :, 1, :], 1.0 / 16.0, ALU.mult, [r_bpm], [r_kbias])
        ts("dve", nbf[:], bgate[:, :, 1], -1.0, ALU.mult, [r_bgate], [r_nbf])
        ts("dve", gpmh[:], gpm[:], 0.5, ALU.mult, [r_gpm], [r_gpmh])
        act(cl[:, 0], cvec[:, 3], AF.Exp, [r_cvec], [r_cl], scale=-1.0)
        act(cl[:, 0], cl[:, 0], AF.Ln, [r_cl], [r_cl], bias=1.0)
        ts("dve", cl[:, 1], cl[:, 0], -16.0, ALU.mult, [r_cl], [r_cl])
        ts("dve", cl[:, 0], cl[:, 0], -8.0, ALU.mult, [r_cl], [r_cl])

        def tok_group_of(t0):
            return min(t0 // 512, 4)

        def tile_to_xT(src, r_src, ti, gscale=None):
            t0, tn = TTS[ti]
            g = tok_group_of(t0)
            for half in range(2):
                b = next_aux()
                for j in range(4):
                    kc = half * 4 + j
                    tr(bank(b)[:, j * 128:j * 128 + tn], src[0:tn, kc * 128:(kc + 1) * 128], ident_f[0:tn, 0:tn],
                       [r_src, r_idf], [r_ps[b]], sig=(j == 3))
                act(xT[:, half * 4:(half + 1) * 4, t0:t0 + tn],
                    bank(b).rearrange("p (a b) -> p a b", a=4)[:, :, 0:tn], AF.Identity, [r_ps[b]], [r_xT[g]])

        A.reset()
        xin = [A.f32([128, D], "xin%d" % i) for i in range(2)]
        for ti, (t0, tn) in enumerate(TTS):
            xa, rx = xin[ti % 2]
            P.dma("sp", xa[0:tn, :], x_all[t0:t0 + tn, :], writes=[rx], sres=rx)
            tile_to_xT(xa, rx, ti)

        for l in range(depth):
            last = (l == NL - 1)
            xsrc = x_all if l == 0 else xres

            A.reset()
            _gt = [A.f32([4, 512], "gt%d" % i) for i in range(3)]
            gt = [x[0] for x in _gt]
            r_gt = [x[1] for x in _gt]
            for g, (t0, tn) in enumerate(TGS):
                bi, bf_ = next_mm(), next_mm()
                for kc in range(KC):
                    mm(bank(bi)[0:4, 0:tn], wg[:, l, kc, 0:4], xT[:, kc, t0:t0 + tn], kc == 0, kc == KC - 1,
                       [r_wg, r_xT[g]], [r_ps[bi]])
                for kc in range(KC):
                    mm(bank(bf_)[0:4, 0:tn], wg[:, l, kc, 4:8], xT[:, kc, t0:t0 + tn], kc == 0, kc == KC - 1,
                       [r_wg, r_xT[g]], [r_ps[bf_]])
                it_, sp_, bb_ = gt[0][:, 0:tn], gt[1][:, 0:tn], gt[2][:, 0:tn]
                act(it_, bank(bi)[0:4, 0:tn], AF.Identity, [r_ps[bi], r_bgate], [r_gt[0]], bias=bgate[:, l, 0:1])
                act(sp_, bank(bf_)[0:4, 0:tn], AF.Exp, [r_ps[bf_], r_nbf], [r_gt[1]], bias=nbf[:, l:l + 1], scale=-1.0)
                act(sp_, sp_, AF.Ln, [r_gt[1]], [r_gt[1]], bias=1.0)
                if g < 4:
                    init = 0.0 if g == 0 else bbl[:, 0:1]
                    scan(bb_, ones4[:, 0:tn], sp_, init, ALU.mult, ALU.subtract, [r_ones4, r_gt[1], r_bbl], [r_gt[2]])
                    if g < 3:
                        cp("dve", bbl[:, 0:1], gt[2][:, tn - 1:tn], [r_gt[2]], [r_bbl])
                else:
                    scan(bb_, a0s[:], sp_, 0.0, ALU.mult, ALU.subtract, [r_a0s, r_gt[1]], [r_gt[2]])
                tt("dve", it_, it_, bb_, ALU.subtract, [r_gt[0], r_gt[2]], [r_gt[0]])
                if g < 4:
                    treduce(mcall[:, 17 + 4 * g:21 + 4 * g], gt[0][:, :].rearrange("p (c t) -> p c t", c=4), ALU.max, [r_gt[0]], [r_mcall])
                    scan(mcall[:, 1 + 4 * g:5 + 4 * g], ones4[:, 0:4], mcall[:, 17 + 4 * g:21 + 4 * g], mcall[:, 4 * g:4 * g + 1],
                         ALU.mult, ALU.max, [r_mcall, r_ones4], [r_mcall])
                    tt("dve", dec4[:, 4 * g:4 * g + 4], mcall[:, 4 * g:4 * g + 4], mcall[:, 4 * g + 1:4 * g + 5], ALU.subtract,
                       [r_mcall], [r_dec4])
                    mc_b = bc_last(mcall[:, 1 + 4 * g:5 + 4 * g], 128)
                    v3 = lambda a: a.rearrange("p (c t) -> p c t", c=4)
                    if g == 3:
                        tt("dve", mpo[:], gt[2][:, 511:512], mcall[:, 16:17], ALU.add, [r_gt[2], r_mcall], [r_mpo])
                        P.dma("sp", o_mp[:, l:l + 1], mpo[:], reads=[r_mpo], sres=r_mpo, final=True)
                else:
                    treduce(mcs[:, 0, :], gt[0][:, 0:64].rearrange("p (b t) -> p b t", t=4), ALU.max, [r_gt[0]], [r_mcs])
                    tt("dve", mcs[:, 1, :], mcs[:, 0, :], m0s[:, l, :], ALU.max, [r_mcs, r_m0s], [r_mcs])
                    tt("dve", dec4[:, 16:32], m0s[:, l, :], mcs[:, 1, :], ALU.subtract, [r_m0s, r_mcs], [r_dec4])
                    tt("dve", mcs[:, 2, :], gt[2][:, 0:64].rearrange("p (b t) -> p b t", t=4)[:, :, 3], mcs[:, 1, :], ALU.add,
                       [r_gt[2], r_mcs], [r_mcs])
                    P.dma("sp", o_ms[:, l * NB:(l + 1) * NB], mcs[:, 2, :], reads=[r_mcs], sres=r_mcs, final=True)
                    mc_b = bc_last(mcs[:, 1, :], 4)
                    v3 = lambda a: a.rearrange("p (c t) -> p c t", t=4)
                rd = [r_gt[0], r_gt[2], r_mcall, r_mcs]
                tt("dve", v3(sp_), v3(it_), mc_b, ALU.subtract, rd, [r_gt[1]])
                act(sp_, sp_, AF.Exp, [r_gt[1]], [r_gt[1]])
                tt("dve", v3(bb_), v3(bb_), mc_b, ALU.add, rd, [r_gt[2]])
                act(bb_, bb_, AF.Exp, [r_gt[2]], [r_gt[2]], scale=-1.0)
                ntile = 4 if g < 4 else 1
                b = next_aux()
                for c in range(ntile):
                    cn = 128 if g < 4 else 64
                    tr(bank(b)[0:cn, c * 8:c * 8 + 4], gt[1][:, c * 128:c * 128 + cn], ident_f[0:4, 0:4], [r_gt[1], r_idf], [r_ps[b]], False)
                    tr(bank(b)[0:cn, c * 8 + 4:c * 8 + 8], gt[2][:, c * 128:c * 128 + cn], ident_f[0:4, 0:4], [r_gt[2], r_idf], [r_ps[b]],
                       c == ntile - 1)
                cn = 128 if g < 4 else 64
                cp("dve", wthr[0:cn, 4 * g:4 * g + ntile, :], bank(b)[0:cn, 0:8 * ntile].rearrange("p (c k) -> p c k", k=8),
                   [r_ps[b]], [r_wthr])
            act(dec4[:], dec4[:], AF.Exp, [r_dec4], [r_dec4])
            tt("dve", decbd[:], bc_mid(dec4[:], 4), hmask[:].rearrange("p (a b) -> p a b", a=4), ALU.mult, [r_dec4, r_hmask], [r_decbd])
            b = next_aux()
            mm(bank(b)[:, 0:128], ones4x[:], decbd[:].rearrange("p a b -> p (a b)"), True, True, [r_ones4x, r_decbd], [r_ps[b]])
            cp("dve", decbc[:], bank(b)[:, 0:128], [r_ps[b]], [r_decbc])

            A.reset()
            lnbc, r_lnbc = A.f32([128, 2, D], "lnbc")
            vbrow, r_vbrow = A.bf([128, D], "vbrow")
            wsf, r_wsf = A.f32([128, 4, 128], "wsf")
            wmT, r_wmT = A.bf([128, 4, 128], "wmT")
            wss, r_wss = A.f32([64, 4, 64], "wss")
            mts, r_mts = A.bf([64, 4, 64], "mts")
            bsrow, r_bsrow = A.bf([128, 4 * 128], "bsrow")
            bsrow_s, r_bsrow_s = A.bf([128, 4 * 64], "bsrow_s")
            v32 = [A.f32([128, D], "v32_%d" % i) for i in range(2)]
            bst, r_bst = A.f32([128, 12], "bst")
            mv, r_mv = A.f32([128, 2], "mv")
            rstd, r_rstd = A.f32([128, 1], "rstd")
            szb = [A.f32([128, 512], "szb%d" % i) for i in range(2)]
            uzb = [A.f32([128, 512], "uzb%d" % i) for i in range(2)]
            vns, r_vns = A.bf([64, D], "vns")
            P.dma("sp", lnbc[:].rearrange("p a b -> p (a b)"), pbc(gln_d[l].rearrange("a b -> (a b)"), 128), writes=[r_lnbc], sres=r_lnbc)
            memset("pool", vbrow[:], 0.0, [r_vbrow])
            P.dma("pool", vbrow[0:1, :], b_in[l:l + 1, OFF["vb"]:OFF["vb"] + D], writes=[r_vbrow], sres=r_vbrow)
            memset("pool", bsrow[:], 0.0, [r_bsrow])
            P.dma("pool", bsrow[0:1, :], gbs_d[l:l + 1, :], writes=[r_bsrow], sres=r_bsrow)
            memset("pool", bsrow_s[:], 0.0, [r_bsrow_s])
            P.dma("pool", bsrow_s[0:1, :], gbs_s_d[l:l + 1, :], writes=[r_bsrow_s], sres=r_bsrow_s)
            P.dma("sp", wsf[:], gws_d[l].rearrange("g t s -> t g s"), writes=[r_wsf], sres=r_wsf)
            P.dma("sp", wss[:].rearrange("p a b -> p (a b)"), gws_s_d[l], writes=[r_wss], sres=r_wss)
            b = next_aux()
            for g4 in range(4):
                tr(bank(b)[:, g4 * 128:(g4 + 1) * 128], wsf[:, g4, :], ident_f[:], [r_wsf, r_idf], [r_ps[b]], g4 == 3)
            tt("dve", wmT[:], bank(b).rearrange("p (a b) -> p a b", a=4), bc_mid(mask_f[:], 4), ALU.mult, [r_ps[b], r_mask], [r_wmT])
            tt("dve", mts[:], wss[:], bc_mid(mask_s[:], 4), ALU.mult, [r_wss, r_mask_s], [r_mts])
            wv = wneed(["vb0", "vb1", "vb2", "vb3"])
            r_vn = r_mrg
            for ti, (t0, tn) in enumerate(TTS):
                g = tok_group_of(t0)
                va, rv = v32[ti % 2]
                pr = (ti % 2) * 2
                for j in range(4):
                    bq = pr + j // 2
                    o_ap = bank(bq)[0:tn, (j % 2) * 256:(j % 2) * 256 + 256]
                    for kc in range(KC):
                        mm(o_ap, xT[:, kc, t0:t0 + tn], wv[j][0][:, kc, :], kc == 0, False, [r_xT[g], wv[j][1]], [r_ps[bq]], sig=False)
                    mm(o_ap, onesrow[:, 0:tn], vbrow[:, j * 256:(j + 1) * 256], False, True, [r_onesrow, r_vbrow], [r_ps[bq]])
                for hf in range(2):
                    bnstats(bst[0:tn, hf * 6:hf * 6 + 6], bank(pr + hf)[0:tn, :], [r_ps[pr + hf]], [r_bst])
                bnaggr(mv[0:tn, :], bst[0:tn, :], [r_bst], [r_mv])
                act(rstd[0:tn, :], mv[0:tn, 1:2], AF.Sqrt, [r_mv, r_eps], [r_rstd], bias=eps_t[0:tn, :])
                recip(rstd[0:tn, :], rstd[0:tn, :], [r_rstd], [r_rstd])
                ts("dve", va[0:tn, :], ps[pr // 2][0:tn, :], mv[0:tn, 0:1], ALU.subtract, [r_ps[pr], r_ps[pr + 1], r_mv, r_rstd], [rv],
                   s2=rstd[0:tn, 0:1], op1=ALU.mult)
                tt("pool", va[0:tn, :], va[0:tn, :], lnbc[0:tn, 0, :], ALU.mult, [rv, r_lnbc], [rv])
                if ti < 16:
                    tt("pool", vn_v[0:tn, ti * 1024:(ti + 1) * 1024], va[0:tn, :], lnbc[0:tn, 1, :], ALU.add, [rv, r_lnbc], [r_vn[g]])
                else:
                    tt("pool", va[0:tn, :], va[0:tn, :], lnbc[0:tn, 1, :], ALU.add, [rv, r_lnbc], [rv])
                    P.dma("sp", o_vs[l], va[0:tn, :], reads=[rv], sres=rv, final=True)
                    cp("pool", vns[:, :], va[0:tn, :], [rv], [r_vns])
            for j in range(4):
                wu, wz = wneed(["ub%d" % j, "zb%d" % j])
                for cc in range(2):
                    c = 2 * j + cc
                    g4 = c // 2
                    for g, (t0, tn) in enumerate(TGS):
                        bu, bz, bm = next_mm(), next_mm(), next_aux()
                        for kc in range(KC):
                            mm(bank(bz)[:, 0:tn], wz[0][:, kc, cc * 128:(cc + 1) * 128], xT[:, kc, t0:t0 + tn], kc == 0, kc == KC - 1,
                               [wz[1], r_xT[g]], [r_ps[bz]])
                        for kc in range(KC):
                            mm(bank(bu)[:, 0:tn], wu[0][:, kc, cc * 128:(cc + 1) * 128], xT[:, kc, t0:t0 + tn], kc == 0, kc == KC - 1,
                               [wu[1], r_xT[g]], [r_ps[bu]])
                        if g < 4:
                            for ci in range(4):
                                ti = 4 * g + ci
                                o_ap = bank(bm)[:, ci * 128:(ci + 1) * 128]
                                mm(o_ap, vn_v[:, ti * 1024 + c * 128:ti * 1024 + (c + 1) * 128], wmT[:, g4, :], True, False,
                                   [r_vn[g], r_wmT], [r_ps[bm]], sig=False)
                                mm(o_ap, onesrow[:], bsrow[:, g4 * 128:(g4 + 1) * 128], False, True, [r_onesrow, r_bsrow], [r_ps[bm]],
                                   sig=(ci == 3))
                        else:
                            o_ap = bank(bm)[:, 0:64]
                            mm(o_ap, vns[:, c * 128:(c + 1) * 128], mts[:, g4, :], True, False,
                               [r_vns, r_mts], [r_ps[bm]], sig=False)
                            mm(o_ap, onesrow[:], bsrow_s[:, g4 * 64:(g4 + 1) * 64], False, True, [r_onesrow, r_bsrow_s], [r_ps[bm]])
                        sz, rsz = szb[g % 2]
                        uz, ruz = uzb[g % 2]
                        act(sz[:, 0:tn], bank(bz)[:, 0:tn], AF.Silu, [r_ps[bz], r_bpm], [rsz], bias=bpm[:, l, 7, c:c + 1])
                        stt(uz[:, 0:tn], bank(bu)[:, 0:tn], bpm[:, l, 5, c:c + 1], sz[:, 0:tn], ALU.add, ALU.mult,
                            [r_ps[bu], r_bpm, rsz], [ruz])
                        tt("dve", ybr[:, c, t0:t0 + tn], bank(bm)[:, 0:tn], uz[:, 0:tn], ALU.mult, [r_ps[bm], ruz], [r_ybr[c][g]])

            def proj_merge(br, first):
                A.reset()
                sgb = [A.f32([128, 512], "sg%d" % i) for i in range(2)]
                tmpb = [A.bf([128, 512], "pm%d" % i) for i in range(2)]
                gblk = {"a": 10, "b": 11, "c": 12}[br]
                for j in range(4):
                    wp, wgt = wneed(["p%s%d" % (br, j), "g%s%d" % (br, j)])
                    for cc in range(2):
                        dm = 2 * j + cc
                        for g, (t0, tn) in enumerate(TGS):
                            bp_, bg = next_mm(), next_mm()
                            for kc in range(KC):
                                mm(bank(bg)[:, 0:tn], wgt[0][:, kc, cc * 128:(cc + 1) * 128], xT[:, kc, t0:t0 + tn], kc == 0, kc == KC - 1,
                                   [wgt[1], r_xT[g]], [r_ps[bg]])
                            for fc in range(KC):
                                mm(bank(bp_)[:, 0:tn], wp[0][:, fc, cc * 128:(cc + 1) * 128], ybr[:, fc, t0:t0 + tn], fc == 0, fc == KC - 1,
                                   [wp[1], r_ybr[fc][g]], [r_ps[bp_]])
                            sg, rsg = sgb[g % 2]
                            act(sg[:, 0:tn], bank(bg)[:, 0:tn], AF.Sigmoid, [r_ps[bg], r_bpm], [rsg], bias=bpm[:, l, gblk, dm:dm + 1])
                            if first:
                                tt("dve", mrg[:, dm, t0:t0 + tn], bank(bp_)[:, 0:tn], sg[:, 0:tn], ALU.mult, [r_ps[bp_], rsg], [r_mrg[g]])
                            else:
                                tm, rtm = tmpb[g % 2]
                                tt("dve", tm[:, 0:tn], bank(bp_)[:, 0:tn], sg[:, 0:tn], ALU.mult, [r_ps[bp_], rsg], [rtm])
                                tt("dve", mrg[:, dm, t0:t0 + tn], mrg[:, dm, t0:t0 + tn], tm[:, 0:tn], ALU.add, [r_mrg[g], rtm], [r_mrg[g]])

            proj_merge("b", True)

            A.reset()
            cext = [A.f32([128, 2, 257], "cext%d" % i) for i in range(2)]
            cbf_t = [A.bf([128, 2, 258], "cbf%d" % i) for i in range(4)]
            qTg = [A.bf([128, 2, 512], "qTg%d" % i) for i in range(2)]
            kTg = [A.bf([128, 2, 512], "kTg%d" % i) for i in range(2)]
            vext = [A.bf([128, 258], "vext%d" % i) for i in range(6)]
            so_t = [A.f32([128, 256], "so%d" % i) for i in range(6)]
            sz_t = [A.f32([128, 256], "sza%d" % i) for i in range(6)]
            wk_t = [A.bf([128, 256], "wk%d" % i) for i in range(4)]
            sp_t = [A.bf([128, 128], "sp%d" % i) for i in range(4)]
            ya_t = [A.bf([128, 256], "ya%d" % i) for i in range(4)]
            qd_t = [A.bf([128, 2, 128], "qd%d" % i) for i in range(4)]
            dmx_t = [A.f32([128, 2], "dmx%d" % i) for i in range(4)]
            bst_t = [A.f32([128, 6], "bstA%d" % i) for i in range(4)]
            mv_t = [A.f32([128, 2], "mvA%d" % i) for i in range(4)]
            rstd_t = [A.f32([128, 1], "rstdA%d" % i) for i in range(4)]
            brow, r_brow = A.bf([128, 768], "browA")
            cs_t = [A.f32([128, 2, 257], "cs%d" % i) for i in range(4)]
            cdbs = [A.bf([128, 2, 258], "cdbs%d" % i) for i in range(2)]
            qtm = [A.bf([128, 2, 64], "qtm%d" % i) for i in range(2)]
            vms = [A.bf([64, 258], "vms%d" % i) for i in range(2)]
            nall, r_nall = A.f32([128, NB, 2], "nall")
            nout, r_nout = A.f32([128, NB, 2], "nout")
            memset("pool", brow[:], 0.0, [r_brow])
            for i in range(6):
                memset("pool", vext[i][0][:, 256:258], 1.0, [vext[i][1]])
            nrot = {"i": 0}

            def next_num():
                nrot["i"] = (nrot["i"] + 1) % 4
                return nrot["i"]

            def chunks_of(g):
                t0, tn = TGS[g]
                if g < 4:
                    return [(ci, 4 * g + ci, ci * 128, 128, t0 + ci * 128, (4 * g + ci) % 6) for ci in range(4)]
                return [(0, 16, 0, 64, t0, 16 % 6)]

            for h in range(NH):
                wq, wk_, wv_, wo_, wz_ = wneed(["q%d" % h, "k%d" % h, "v%d" % h, "o%d" % h, "za%d" % h])
                for i3, nm in enumerate(("v", "o", "za")):
                    P.dma("pool", brow[0:1, i3 * 256:(i3 + 1) * 256], b_in[l:l + 1, OFF[nm] + h * 256:OFF[nm] + (h + 1) * 256],
                          writes=[r_brow], sres=r_brow)
                ce, rce = cext[h % 2]
                memset("dve", ce[:], 0.0, [rce])

                def emit_qk(g, h=h, wq=wq, wk_=wk_):
                    t0, tn = TGS[g]
                    qt, rqt = qTg[g % 2]
                    kt, rkt = kTg[g % 2]
                    for dc in range(2):
                        bq, bk = next_mm(), next_mm()
                        for kc in range(KC):
                            mm(bank(bq)[:, 0:tn], wq[0][:, kc, dc * 128:(dc + 1) * 128], xT[:, kc, t0:t0 + tn], kc == 0, kc == KC - 1,
                               [wq[1], r_xT[g]], [r_ps[bq]])
                        for kc in range(KC):
                            mm(bank(bk)[:, 0:tn], wk_[0][:, kc, dc * 128:(dc + 1) * 128], xT[:, kc, t0:t0 + tn], kc == 0, kc == KC - 1,
                               [wk_[1], r_xT[g]], [r_ps[bk]])
                        act(qt[:, dc, 0:tn], bank(bq)[:, 0:tn], AF.Identity, [r_ps[bq], r_bpm], [rqt], bias=bpm[:, l, 0, 2 * h + dc:2 * h + dc + 1])
                        act(kt[:, dc, 0:tn], bank(bk)[:, 0:tn], AF.Identity, [r_ps[bk], r_kbias], [rkt],
                            bias=kbias[:, l, 2 * h + dc:2 * h + dc + 1], scale=1.0 / 16.0)

                def L1(g, sel, wv_=wv_, wo_=wo_, wz_=wz_):
                    for (ci, ti, c0, cn, tk0, bi) in chunks_of(g):
                        if ci not in sel:
                            continue
                        ve, rve = vext[bi]
                        so, rso = so_t[bi]
                        sza, rsza = sz_t[bi]
                        bv, bo = next_mm(), next_mm()
                        for (o_ap, wt, i3, rb) in ((bank(bv)[0:cn, 0:256], wv_, 0, r_ps[bv]), (bank(bv)[0:cn, 256:512], wo_, 1, r_ps[bv]),
                                                   (bank(bo)[0:cn, 0:256], wz_, 2, r_ps[bo])):
                            for kc in range(KC):
                                mm(o_ap, xT[:, kc, tk0:tk0 + cn], wt[0][:, kc, :], kc == 0, False, [r_xT[g], wt[1]], [rb], sig=False)
                            mm(o_ap, onesrow[:, 0:cn], brow[:, i3 * 256:(i3 + 1) * 256], False, True, [r_onesrow, r_brow], [rb])
                        cp("act", ve[0:cn, 0:256], bank(bv)[0:cn, 0:256], [r_ps[bv]], [rve])
                        act(so[0:cn, :], bank(bv)[0:cn, 256:512], AF.Tanh, [r_ps[bv]], [rso], scale=0.5)
                        act(sza[0:cn, :], bank(bo)[0:cn, 0:256], AF.Tanh, [r_ps[bo]], [rsza], scale=0.5)
                        ts("dve", so[0:cn, :], so[0:cn, :], 0.5, ALU.mult, [rso], [rso], s2=0.5, op1=ALU.add)
                        stt(sza[0:cn, :], sza[0:cn, :], 1.0, bank(bo)[0:cn, 0:256], ALU.add, ALU.mult, [rsza, r_ps[bo]], [rsza])

                def L2(g, h=h):
                    qt, rqt = qTg[g % 2]
                    kt, rkt = kTg[g % 2]
                    msk, rmsk = (mask_f, r_mask) if g < 4 else (mask_s, r_mask_s)
                    for (ci, ti, c0, cn, tk0, bi) in chunks_of(g):
                        wkt, rwk = wk_t[ci]
                        spt, rsp = sp_t[ci]
                        bt = next_aux()
                        for dc in range(2):
                            tr(bank_bf(bt)[0:cn, dc * 128:(dc + 1) * 128], kt[:, dc, c0:c0 + cn], ident_b[:], [rkt, r_idb], [r_ps[bt]], dc == 1)
                        ts("dve", wkt[0:cn, :], bank_bf(bt)[0:cn, 0:256], wthr[0:cn, ti, h:h + 1], ALU.mult, [r_ps[bt], r_wthr], [rwk])
                        bs_ = next_aux()
                        for dc in range(2):
                            mm(bank(bs_)[0:cn, 0:cn], kt[:, dc, c0:c0 + cn], qt[:, dc, c0:c0 + cn], dc == 0, dc == 1, [rkt, rqt], [r_ps[bs_]])
                        stt(spt[0:cn, 0:cn], bank(bs_)[0:cn, 0:cn], wthr[0:cn, ti, h:h + 1], msk[0:cn, 0:cn], ALU.mult, ALU.mult,
                            [r_ps[bs_], r_wthr, rmsk], [rsp])
                        if g < 4 and ti > 0:
                            qd, rqd = qd_t[ci]
                            ts("dve", qd[:], qt[:, :, c0:c0 + cn], decbc[:, h * 32 + ti:h * 32 + ti + 1], ALU.mult, [rqt, r_decbc], [rqd])

                def L3(g, h=h, ce=ce, rce=rce):
                    qt, rqt = qTg[g % 2]
                    pre_num = {}
                    post = []

                    def emit_num(ch):
                        (ci, ti, c0, cn, tk0, bi) = ch
                        ve, rve = vext[bi]
                        spt, rsp = sp_t[ci]
                        bn_ = next_num()
                        first = (ti == 0)
                        mm(bank(bn_)[0:cn, 0:257], spt[0:cn, 0:cn], ve[0:cn, 0:257], True, first, [rsp, rve], [r_ps[bn_]], sig=first)
                        if not first:
                            qd, rqd = qd_t[ci]
                            cbf, r_cbf = cbf_t[(ti - 1) % 4]
                            for dc in range(2):
                                mm(bank(bn_)[0:cn, 0:257], qd[:, dc, :], cbf[:, dc, 0:257], False, dc == 1, [rqd, r_cbf], [r_ps[bn_]])
                        return bn_

                    if g < 4:
                        pre_num[0] = emit_num(chunks_of(g)[0])
                        for (ci, ti, c0, cn, tk0, bi) in chunks_of(g):
                            ve, rve = vext[bi]
                            wkt, rwk = wk_t[ci]
                            up = 6 if ci % 2 == 0 else 4
                            for dc in range(2):
                                mm(bank(up + dc)[:, 0:257], wkt[0:cn, dc * 128:(dc + 1) * 128], ve[0:cn, 0:257], True, True, [rwk, rve], [r_ps[up + dc]])
                            stt(ce[:], ce[:], decbc[:, h * 32 + ti:h * 32 + ti + 1], ps[up // 2][:, :].rearrange("p (a b) -> p a b", a=2)[:, :, 0:257],
                                ALU.mult, ALU.add, [rce, r_decbc, r_ps[up], r_ps[up + 1]], [rce])
                            if ti < 15:
                                cbf, r_cbf = cbf_t[ti % 4]
                                cp("act", cbf[:, :, 0:257], ce[:], [rce], [r_cbf])
                            else:
                                P.dma("sp", o_cp[l, h].rearrange("(dc p) e -> p dc e", p=128), ce[:, :, 0:256], reads=[rce], sres=rce, final=True)
                                P.dma("sp", o_np[l, h].rearrange("(dc p) -> p dc", p=128), ce[:, :, 256], reads=[rce], sres=rce, final=True)
                    for (ci, ti, c0, cn, tk0, bi) in chunks_of(g):
                        ve, rve = vext[bi]
                        so, rso = so_t[bi]
                        wkt, rwk = wk_t[ci]
                        spt, rsp = sp_t[ci]
                        dmx, r_dmx = dmx_t[ci]
                        if g < 4:
                            bn_ = pre_num[ci] if ci in pre_num else emit_num((ci, ti, c0, cn, tk0, bi))
                        else:
                            bn_ = next_num()
                            def load_cs(b_):
                                cs, rcs = cs_t[b_ % 4]
                                P.dma("act", cs[:, :, 0:256], st_c[l, b_, h].rearrange("(dc p) e -> p dc e", p=128), writes=[rcs], sres=rcs)
                            for dc in range(2):
                                P.dma("sp", nall[:, :, dc], st_n[l, :, h, dc * 128:(dc + 1) * 128].rearrange("b p -> p b"), writes=[r_nall], sres=r_nall)
                            load_cs(0)
                            load_cs(1)
                            mm(bank(bn_)[0:cn, 0:257], spt[0:cn, 0:cn], ve[0:cn, 0:257], True, False, [rsp, rve], [r_ps[bn_]], sig=False)
                            for b_ in range(NB):
                                if b_ + 2 < NB:
                                    load_cs(b_ + 2)
                                cs, rcs = cs_t[b_ % 4]
                                cb_, rcb = cdbs[b_ % 2]
                                qm, rqm = qtm[b_ % 2]
                                vm, rvm = vms[b_ % 2]
                                up = 6 if b_ % 2 == 0 else 4
                                cp("dve", cs[:, :, 256], nall[:, b_, :], [r_nall], [rcs])
                                cp("act", cb_[:, :, 0:257], cs[:], [rcs], [rcb])
                                stt(qm[:], qt[:, :, 0:64], decbc[:, h * 32 + 16 + b_:h * 32 + 17 + b_], bc_mid(colmask[:, b_, :], 2),
                                    ALU.mult, ALU.mult, [rqt, r_decbc, r_colmask], [rqm])
                                for dc in range(2):
                                    mm(bank(bn_)[0:cn, 0:257], qm[:, dc, :], cb_[:, dc, 0:257], False, (b_ == NB - 1 and dc == 1),
                                       [rqm, rcb], [r_ps[bn_]])
                                act(vm[:, 0:257], ve[0:64, 0:257], AF.Copy, [rve, r_rowmask], [rvm], scale=rowmask[:, b_:b_ + 1])
                                for dc in range(2):
                                    mm(bank(up + dc)[:, 0:257], wkt[0:64, dc * 128:(dc + 1) * 128], vm[:, 0:257], True, True, [rwk, rvm], [r_ps[up + dc]])
                                stt(cs[:], cs[:], decbc[:, h * 32 + 16 + b_:h * 32 + 17 + b_],
                                    ps[up // 2][:, :].rearrange("p (a b) -> p a b", a=2)[:, :, 0:257], ALU.mult, ALU.add,
                                    [rcs, r_decbc, r_ps[up], r_ps[up + 1]], [rcs])
                                P.dma("sp", o_cs[l, b_, h].rearrange("(dc p) e -> p dc e", p=128), cs[:, :, 0:256], reads=[rcs], sres=rcs, final=True)
                                cp("dve", nout[:, b_, :], cs[:, :, 256], [rcs], [r_nout])
                            for dc in range(2):
                                P.dma("sp", o_ns[l, :, h, dc * 128:(dc + 1) * 128].rearrange("b p -> p b"), nout[:, :, dc], reads=[r_nout], sres=r_nout, final=True)
                        post.append((ci, ti, cn, bi, bn_))
                    for (ci, ti, cn, bi, bn_) in post:
                        dmx, r_dmx = dmx_t[ci]
                        act(dmx[0:cn, 0:1], bank(bn_)[0:cn, 256:257], AF.Abs, [r_ps[bn_]], [r_dmx])
                    for (ci, ti, cn, bi, bn_) in post:
                        dmx, r_dmx = dmx_t[ci]
                        ts("dve", dmx[0:cn, 0:1], dmx[0:cn, 0:1], wthr[0:cn, ti, 4 + h:5 + h], ALU.max, [r_dmx, r_wthr], [r_dmx])
                    for (ci, ti, cn, bi, bn_) in post:
                        dmx, r_dmx = dmx_t[ci]
                        recip(dmx[0:cn, 1:2], dmx[0:cn, 0:1], [r_dmx], [r_dmx])
                    for (ci, ti, cn, bi, bn_) in post:
                        dmx, r_dmx = dmx_t[ci]
                        so, rso = so_t[bi]
                        stt(so[0:cn, :], bank(bn_)[0:cn, 0:256], dmx[0:cn, 1:2], so[0:cn, :], ALU.mult, ALU.mult, [r_ps[bn_], r_dmx, rso], [rso])

                def L4(g):
                    CH = chunks_of(g)
                    for (ci, ti, c0, cn, tk0, bi) in CH:
                        hs, rhs = so_t[bi]
                        bst, r_bst = bst_t[ci]
                        mv, r_mv = mv_t[ci]
                        bnstats(bst[0:cn, :], hs[0:cn, :], [rhs], [r_bst])
                    for (ci, ti, c0, cn, tk0, bi) in CH:
                        bst, r_bst = bst_t[ci]
                        mv, r_mv = mv_t[ci]
                        bnaggr(mv[0:cn, :], bst[0:cn, :], [r_bst], [r_mv])
                    for (ci, ti, c0, cn, tk0, bi) in CH:
                        mv, r_mv = mv_t[ci]
                        rstd, r_rstd = rstd_t[ci]
                        act(rstd[0:cn, :], mv[0:cn, 1:2], AF.Sqrt, [r_mv, r_eps], [r_rstd], bias=eps_t[0:cn, :])
                    for (ci, ti, c0, cn, tk0, bi) in CH:
                        hs, rhs = so_t[bi]
                        sza, rsza = sz_t[bi]
                        ya, rya = ya_t[ci]
                        mv, r_mv = mv_t[ci]
                        rstd, r_rstd = rstd_t[ci]
                        recip(rstd[0:cn, :], rstd[0:cn, :], [r_rstd], [r_rstd])
                    for (ci, ti, c0, cn, tk0, bi) in CH:
                        hs, rhs = so_t[bi]
                        sza, rsza = sz_t[bi]
                        mv, r_mv = mv_t[ci]
                        stt(hs[0:cn, :], hs[0:cn, :], mv[0:cn, 0:1], sza[0:cn, :], ALU.subtract, ALU.mult, [rhs, r_mv, rsza], [rhs])
                    for (ci, ti, c0, cn, tk0, bi) in CH:
                        hs, rhs = so_t[bi]
                        ya, rya = ya_t[ci]
                        rstd, r_rstd = rstd_t[ci]
                        act(ya[0:cn, :], hs[0:cn, :], AF.Copy, [rhs, r_rstd], [rya], scale=rstd[0:cn, 0:1])

                def L5(g, h=h):
                    for (ci, ti, c0, cn, tk0, bi) in chunks_of(g):
                        ya, rya = ya_t[ci]
                        bt2 = next_aux()
                        for dc in range(2):
                            tr(bank_bf(bt2)[:, dc * 128:dc * 128 + cn], ya[0:cn, dc * 128:(dc + 1) * 128], ident_b[0:cn, 0:cn], [rya, r_idb],
                               [r_ps[bt2]], dc == 1)
                        for dc in range(2):
                            fc = 2 * h + dc
                            act(ybr[:, fc, tk0:tk0 + cn], bank_bf(bt2)[:, dc * 128:dc * 128 + cn], AF.Copy, [r_ps[bt2], r_gpmh], [r_ybr[fc][g]],
                                scale=gpmh[:, l, fc:fc + 1])

                emit_qk(0)
                L1(0, (0, 1, 2, 3))
                for g in range(len(TGS)):
                    if g == 0 or g == 4:
                        P.mark("A_h%d_%s" % (h, "prompt" if g < 4 else "sample"))
                    L2(g)
                    if g + 1 < len(TGS):
                        emit_qk(g + 1)
                    L3(g)
                    if g + 1 < len(TGS):
                        L1(g + 1, (0, 1))
                    L4(g)
                    if g + 1 < len(TGS):
                        L1(g + 1, (2, 3))
                    L5(g)
            P.mark("A_end")
            proj_merge("a", False)

            A.reset()
            waT, r_wa = A.bf([128, 8, 128], "wa")
            wxT, r_wx = A.bf([128, 8, 128], "wx")
            stc, r_stc = A.f32([48, D], "stc")
            sth, r_sth = A.f32([16, D], "sth")
            stg_t, r_stg_t = A.f32([68, D], "stg_t")
            stg, r_stg = A.f32([128, KC, 68], "stg")
            hprev2 = [A.f32([128, 17], "hprev%d" % i) for i in range(2)]
            hist2 = [A.f32([128, 48], "hist%d" % i) for i in range(2)]
            ctail2 = [A.f32([128, 3], "ctail%d" % i) for i in range(2)]
            cbuf = [[A.f32([128, 515], "cb%d_%d" % (i, k)) for k in range(5)] for i in range(2)]
            xcb = [A.bf([128, 512], "xcb%d" % i) for i in range(2)]
            szc = [A.f32([128, 512], "szc%d" % i) for i in range(4)]
            P.dma("pool", waT[:], wa_d[l].rearrange("n i j -> i n j"), writes=[r_wa], sres=r_wa)
            P.dma("pool", wxT[:], wx_d[l].rearrange("n i j -> i n j"), writes=[r_wx], sres=r_wx)
            P.dma("sp", stc[:], st_conv[l], writes=[r_stc], sres=r_stc)
            P.dma("sp", sth[:], st_h[l], writes=[r_sth], sres=r_sth)
            for j in range(4):
                wxc, wzc = wneed(["xc%d" % j, "zc%d" % j])
                for cc in range(2):
                    c = 2 * j + cc
                    hprev, r_hprev = hprev2[cc]
                    hist, r_hist = hist2[cc]
                    ctail, r_ctail = ctail2[cc]
                    b = next_aux()
                    tr(bank(b)[:, 0:48], stc[:, c * 128:(c + 1) * 128], ident_f[0:48, 0:48], [r_stc, r_idf], [r_ps[b]], False)
                    tr(bank(b)[:, 48:64], sth[:, c * 128:(c + 1) * 128], ident_f[0:16, 0:16], [r_sth, r_idf], [r_ps[b]], True)
                    cp("dve", hprev[:, 1:17], bank(b)[:, 48:64], [r_ps[b]], [r_hprev])
                    cp("dve", hist[:], bank(b)[:, 0:48], [r_ps[b]], [r_hist])
                    memset("dve", ctail[:], 0.0, [r_ctail])
                views_all = {}
                def C1(g, j=j, wxc=wxc, wzc=wzc):
                    t0, tn = TGS[g]
                    for cc in range(2):
                        c = 2 * j + cc
                        hprev, r_hprev = hprev2[cc]
                        hist, r_hist = hist2[cc]
                        ctail, r_ctail = ctail2[cc]
                        (xp, rxp), (xc_, rxc), (ra, rra), (ib, rib), (t1, rt1) = cbuf[cc]
                        sz, rsz = szc[cc * 2 + g % 2]
                        bx_, bz = next_mm(), next_mm()
                        for kc in range(KC):
                            mm(bank(bx_)[:, 0:tn], wxc[0][:, kc, cc * 128:(cc + 1) * 128], xT[:, kc, t0:t0 + tn], kc == 0, kc == KC - 1,
                               [wxc[1], r_xT[g]], [r_ps[bx_]])
                        for kc in range(KC):
                            mm(bank(bz)[:, 0:tn], wzc[0][:, kc, cc * 128:(cc + 1) * 128], xT[:, kc, t0:t0 + tn], kc == 0, kc == KC - 1,
                               [wzc[1], r_xT[g]], [r_ps[bz]])
                        if g < 4:
                            cp("dve", xp[:, 0:3], ctail[:], [r_ctail], [rxp])
                            act(xp[:, 3:3 + tn], bank(bx_)[:, 0:tn], AF.Identity, [r_ps[bx_], r_bpm], [rxp], bias=bpm[:, l, 8, c:c + 1])
                            if g < 3:
                                cp("dve", ctail[:], xp[:, tn:tn + 3], [rxp], [r_ctail])
                            else:
                                cp("dve", stg[:, c, 17:20], xp[:, tn:tn + 3], [rxp], [r_stg])
                            xp_ = xp
                            xpv = (lambda xp_: (lambda jj: xp_[:, jj:jj + 512]))(xp)
                            xcv = xc_[:, 0:tn]
                        else:
                            xp3 = xp[:, 0:112].rearrange("p (b k) -> p b k", k=7)
                            cp("dve", xp3[:, :, 0:3], hist[:].rearrange("p (b k) -> p b k", k=3), [r_hist], [rxp])
                            act(xp3[:, :, 3:7], bank(bx_)[:, 0:64].rearrange("p (b k) -> p b k", k=4), AF.Identity, [r_ps[bx_], r_bpm], [rxp],
                                bias=bpm[:, l, 8, c:c + 1])
                            cp("dve", stg[:, c, 20:68].rearrange("p (b k) -> p b k", k=3), xp3[:, :, 4:7], [rxp], [r_stg])
                            xpv = (lambda xp3: (lambda jj: xp3[:, :, jj:jj + 4]))(xp3)
                            xcv = xc_[:, 0:64].rearrange("p (b k) -> p b k", k=4)
                        act(sz[:, 0:tn], bank(bz)[:, 0:tn], AF.Silu, [r_ps[bz], r_bpm], [rsz], bias=bpm[:, l, 9, c:c + 1])
                        views_all[(g, cc)] = (xpv, xcv)
                def C2(g, j=j, wxc=wxc, wzc=wzc):
                    t0, tn = TGS[g]
                    for cc in range(2):
                        c = 2 * j + cc
                        (xp, rxp), (xc_, rxc), (ra, rra), (ib, rib), (t1, rt1) = cbuf[cc]
                        xb, rxb = xcb[cc]
                        xpv, xcv = views_all[(g, cc)]
                        ts("dve", xcv, xpv(0), cw[:, l, c, 0:1], ALU.mult, [rxp, r_cw, r_cvec], [rxc], s2=cvec[:, 0, l, c:c + 1], op1=ALU.add)
                        for jj in range(1, 4):
                            stt(xcv, xpv(jj), cw[:, l, c, jj:jj + 1], xcv, ALU.mult, ALU.add, [rxp, r_cw, rxc], [rxc])
                        cp("dve", xb[:, 0:tn], xc_[:, 0:tn], [rxc], [rxb])
                def C3(g, j=j, wxc=wxc, wzc=wzc):
                    t0, tn = TGS[g]
                    for cc in range(2):
                        c = 2 * j + cc
                        (xp, rxp), (xc_, rxc), (ra, rra), (ib, rib), (t1, rt1) = cbuf[cc]
                        xb, rxb = xcb[cc]
                        br_, bi_ = next_aux(), next_aux()
                        mm(bank(br_)[:, 0:tn], waT[:, c, :], xb[:, 0:tn], True, True, [r_wa, rxb], [r_ps[br_]])
                        mm(bank(bi_)[:, 0:tn], wxT[:, c, :], xb[:, 0:tn], True, True, [r_wx, rxb], [r_ps[bi_]])
                        act(ra[:, 0:tn], bank(br_)[:, 0:tn], AF.Sigmoid, [r_ps[br_], r_cvec], [rra], bias=cvec[:, 1, l, c:c + 1])
                        act(ib[:, 0:tn], bank(bi_)[:, 0:tn], AF.Sigmoid, [r_ps[bi_], r_cvec], [rib], bias=cvec[:, 2, l, c:c + 1])
                    for cc in range(2):
                        c = 2 * j + cc
                        (xp, rxp), (xc_, rxc), (ra, rra), (ib, rib), (t1, rt1) = cbuf[cc]
                        act(t1[:, 0:tn], ra[:, 0:tn], AF.Exp, [rra, r_cl], [rt1], scale=cl[:, 1, l, c:c + 1])
                        act(ra[:, 0:tn], ra[:, 0:tn], AF.Exp, [rra, r_cl], [rra], scale=cl[:, 0, l, c:c + 1])
                    for cc in range(2):
                        (xp, rxp), (xc_, rxc), (ra, rra), (ib, rib), (t1, rt1) = cbuf[cc]
                        act(t1[:, 0:tn], t1[:, 0:tn], AF.Sqrt, [rt1], [rt1], bias=1.0, scale=-1.0)
                def C4(g, j=j, wxc=wxc, wzc=wzc):
                    t0, tn = TGS[g]
                    for cc in range(2):
                        c = 2 * j + cc
                        hprev, r_hprev = hprev2[cc]
                        (xp, rxp), (xc_, rxc), (ra, rra), (ib, rib), (t1, rt1) = cbuf[cc]
                        sz, rsz = szc[cc * 2 + g % 2]
                        if g == 0:
                            memset("dve", t1[:, 0:1], 1.0, [rt1])
                        tt("dve", ib[:, 0:tn], ib[:, 0:tn], t1[:, 0:tn], ALU.mult, [rib, rt1], [rib])
                        tt("dve", ib[:, 0:tn], ib[:, 0:tn], xc_[:, 0:tn], ALU.mult, [rib, rxc], [rib])
                        if g < 4:
                            init = 0.0 if g == 0 else hprev[:, 0:1]
                            scan(t1[:, 0:tn], ra[:, 0:tn], ib[:, 0:tn], init, ALU.mult, ALU.add, [rra, rib, r_hprev], [rt1])
                            if g < 3:
                                cp("dve", hprev[:, 0:1], t1[:, tn - 1:tn], [rt1], [r_hprev])
                            else:
                                cp("dve", stg[:, c, 0:1], t1[:, tn - 1:tn], [rt1], [r_stg])
                        else:
                            ib3 = ib[:, 0:64].rearrange("p (b k) -> p b k", k=4)
                            ra3 = ra[:, 0:64].rearrange("p (b k) -> p b k", k=4)
                            tt("dve", hprev[:, 1:17], hprev[:, 1:17], ra3[:, :, 0], ALU.mult, [r_hprev, rra], [r_hprev])
                            tt("dve", ib3[:, :, 0], ib3[:, :, 0], hprev[:, 1:17], ALU.add, [rib, r_hprev], [rib])
                            memset("dve", ra3[:, :, 0], 0.0, [rra])
                            scan(t1[:, 0:64], ra[:, 0:64], ib[:, 0:64], 0.0, ALU.mult, ALU.add, [rra, rib], [rt1])
                            cp("dve", stg[:, c, 1:17], t1[:, 0:64].rearrange("p (b k) -> p b k", k=4)[:, :, 3], [rt1], [r_stg])
                        tt("dve", ybr[:, c, t0:t0 + tn], t1[:, 0:tn], sz[:, 0:tn], ALU.mult, [rt1, rsz], [r_ybr[c][g]])
                C1(0)
                for g in range(len(TGS)):
                    C2(g)
                    if g + 1 < len(TGS):
                        C1(g + 1)
                    C3(g)
                    C4(g)
            for half in range(2):
                b = next_aux()
                for jj in range(4):
                    c = half * 4 + jj
                    tr(bank(b)[0:68, jj * 128:(jj + 1) * 128], stg[:, c, :], ident_f[:], [r_stg, r_idf], [r_ps[b]], jj == 3)
                cp("dve", stg_t[:, half * 512:(half + 1) * 512], bank(b)[0:68, :], [r_ps[b]], [r_stg_t])
            P.dma("sp", o_convh[l], stg_t[:], reads=[r_stg_t], sres=r_stg_t, final=True)
            proj_merge("c", False)

            A.reset()
            lnbc, r_lnbc = A.f32([128, 2, D], "lnbcO")
            xr = [A.f32([128, D], "xr%d" % i) for i in range(4)]
            zt = [A.f32([128, D], "zt%d" % i) for i in range(6)]
            bstO = [A.f32([128, 12], "bstO%d" % i) for i in range(4)]
            mvO = [A.f32([128, 2], "mvO%d" % i) for i in range(4)]
            rstdO = [A.f32([128, 1], "rstdO%d" % i) for i in range(4)]
            P.dma("sp", lnbc[:].rearrange("p a b -> p (a b)"), pbc(fln_d[l].rearrange("a b -> (a b)"), 128), writes=[r_lnbc], sres=r_lnbc)
            wo = wneed(["wo0", "wo1", "wo2", "wo3"])
            for ti, (t0, tn) in enumerate(TTS):
                if ti < 4:
                    xa, rx = xr[ti % 4]
                    rsrc = [] if l == 0 else [r_xres[ti]]
                    P.dma("sp", xa[0:tn, :], xsrc[t0:t0 + tn, :], reads=rsrc, writes=[rx], sres=rx)
            pairs = [list(range(p, min(p + 2, 17))) for p in range(0, 17, 2)]

            def O_mm(pr_tiles):
                for ti in pr_tiles:
                    t0, tn = TTS[ti]
                    g = tok_group_of(t0)
                    xa, rx = xr[ti % 4]
                    za, rz = zt[ti % 6]
                    pr = (ti % 2) * 2
                    for j in range(4):
                        bq = pr + j // 2
                        o_ap = bank(bq)[0:tn, (j % 2) * 256:(j % 2) * 256 + 256]
                        for fc in range(KC):
                            mm(o_ap, mrg[:, fc, t0:t0 + tn], wo[j][0][:, fc, :], fc == 0, fc == KC - 1, [r_mrg[g], wo[j][1]], [r_ps[bq]])
                    stt(za[0:tn, :], xa[0:tn, :], ALPHA, ps[pr // 2][0:tn, :], ALU.mult, ALU.add, [rx, r_ps[pr], r_ps[pr + 1]], [rz])
                    if ti + 4 < 17:
                        t0n, tnn = TTS[ti + 4]
                        rsrc = [] if l == 0 else [r_xres[ti + 4]]
                        P.dma("sp", xa[0:tnn, :], xsrc[t0n:t0n + tnn, :], reads=rsrc, writes=[rx], sres=rx)

            def O_ln(pr_tiles):
                for ti in pr_tiles:
                    t0, tn = TTS[ti]
                    za, rz = zt[ti % 6]
                    bst, r_bst = bstO[ti % 4]
                    for hf in range(2):
                        bnstats(bst[0:tn, hf * 6:hf * 6 + 6], za[0:tn, hf * 512:(hf + 1) * 512], [rz], [r_bst])
                for ti in pr_tiles:
                    t0, tn = TTS[ti]
                    bst, r_bst = bstO[ti % 4]
                    mv, r_mv = mvO[ti % 4]
                    bnaggr(mv[0:tn, :], bst[0:tn, :], [r_bst], [r_mv])
                for ti in pr_tiles:
                    t0, tn = TTS[ti]
                    mv, r_mv = mvO[ti % 4]
                    rstd, r_rstd = rstdO[ti % 4]
                    act(rstd[0:tn, :], mv[0:tn, 1:2], AF.Sqrt, [r_mv, r_eps], [r_rstd], bias=eps_t[0:tn, :])
                for ti in pr_tiles:
                    t0, tn = TTS[ti]
                    rstd, r_rstd = rstdO[ti % 4]
                    recip(rstd[0:tn, :], rstd[0:tn, :], [r_rstd], [r_rstd])
                for ti in pr_tiles:
                    t0, tn = TTS[ti]
                    za, rz = zt[ti % 6]
                    mv, r_mv = mvO[ti % 4]
                    rstd, r_rstd = rstdO[ti % 4]
                    ts("dve", za[0:tn, :], za[0:tn, :], mv[0:tn, 0:1], ALU.subtract, [rz, r_mv, r_rstd], [rz], s2=rstd[0:tn, 0:1], op1=ALU.mult)
                for ti in pr_tiles:
                    t0, tn = TTS[ti]
                    za, rz = zt[ti % 6]
                    tt("dve", za[0:tn, :], za[0:tn, :], lnbc[0:tn, 0, :], ALU.mult, [rz, r_lnbc], [rz])
                for ti in pr_tiles:
                    t0, tn = TTS[ti]
                    za, rz = zt[ti % 6]
                    tt("dve", za[0:tn, :], za[0:tn, :], lnbc[0:tn, 1, :], ALU.add, [rz, r_lnbc], [rz])

            def O_out(pr_tiles):
                for ti in pr_tiles:
                    t0, tn = TTS[ti]
                    za, rz = zt[ti % 6]
                    if last or l == depth - 1:
                        P.dma("sp", y_all[t0:t0 + tn, :], za[0:tn, :], reads=[rz], sres=rz, final=True)
                    else:
                        P.dma("sp", xres[t0:t0 + tn, :], za[0:tn, :], reads=[rz], writes=[r_xres[ti]], sres=rz)
                        tile_to_xT(za, rz, ti)

            O_mm(pairs[0])
            for pi in range(len(pairs)):
                O_ln(pairs[pi])
                if pi + 1 < len(pairs):
                    O_mm(pairs[pi + 1])
                O_out(pairs[pi])
        P.finish()
        global _LAST_PROG
        _LAST_PROG = P
        with nc.allow_non_contiguous_dma(reason="small strided state columns"):
            P.emit()
    return nc


def _consts():
    ident = np.eye(128, dtype=np.float32)
    s = np.arange(128)
    mask = (s[:, None] <= s[None, :]).astype(np.float32)
    t = np.arange(64)
    mask_s = ((t[:, None] // 4 == t[None, :] // 4) & (t[:, None] <= t[None, :])).astype(np.float32)
    a0s = np.ones((4, 64), np.float32)
    a0s[:, ::4] = 0.0
    hmask = np.zeros((4, 4, 32), np.float32)
    for h in range(4):
        hmask[h, h, :] = 1.0
    colmask = np.zeros((128, NB, 64), np.float32)
    rowmask = np.zeros((64, NB), np.float32)
    for b in range(NB):
        colmask[:, b, 4 * b:4 * b + 4] = 1.0
        rowmask[4 * b:4 * b + 4, b] = 1.0
    onesrow = np.zeros((128, 128), np.float32)
    onesrow[0, :] = 1.0
    return dict(c_ident=ident, c_mask=mask, c_mask_s=mask_s, c_a0s=a0s, c_hmask=hmask.reshape(4, 128),
                c_colmask=colmask.reshape(128, NB * 64), c_rowmask=rowmask, c_onesrow=onesrow)


_NC_CACHE = {}


def kernel(x_prompt, x_sample, state_mlstm_c, state_mlstm_n, state_mlstm_m, state_lru_conv, state_lru_h,
           w_in, b_in, mlstm_norm_g, gmlp_ln_g, gmlp_ln_b, gmlp_ws, gmlp_bs, lru_conv_w, lru_conv_b,
           lru_wa, lru_ba, lru_wx, lru_bx, lru_lambda, w_proj_a, w_proj_b, w_proj_c, w_out, ln_g, ln_b, _depth=NL):
    f = lambda a: np.ascontiguousarray(np.asarray(a, dtype=np.float32))
    x_prompt, x_sample = f(x_prompt), f(x_sample)
    w_in, b_in = f(w_in), f(b_in)
    if _depth not in _NC_CACHE:
        _NC_CACHE[_depth] = build(_depth)
    nc = _NC_CACHE[_depth]
    bblocks = np.stack([b_in[:, OFF[n]:OFF[n] + D] for n in BLK], axis=1)
    bpm = bblocks.reshape(NL, 13, 8, 128).transpose(3, 0, 1, 2).reshape(128, NL * 13 * 8)
    bgate = np.stack([b_in[:, 5120:5124], b_in[:, 5124:5128]], axis=2).transpose(1, 0, 2).reshape(4, NL * 2)
    pm = lambda a: f(a).reshape(NL, 8, 128).transpose(2, 0, 1)
    gpm = pm(mlstm_norm_g).reshape(128, NL * 8)
    cw = f(lru_conv_w).reshape(NL, 4, 8, 128).transpose(3, 0, 2, 1).reshape(128, NL * 8 * 4)
    cvec = np.stack([pm(lru_conv_b), pm(lru_ba), pm(lru_bx), pm(lru_lambda)], axis=1).reshape(128, 4 * NL * 8)
    gln = np.stack([f(gmlp_ln_g), f(gmlp_ln_b)], axis=1)
    fln = np.stack([f(ln_g), f(ln_b)], axis=1)
    gws = f(gmlp_ws)
    gws_s = np.stack([np.stack([np.tile(gws[l, g, :4, :4].T, (16, 16)) for g in range(4)], axis=1) for l in range(NL)], 0)
    gws_s = gws_s.reshape(NL, 64, 4 * 64)
    gbs = f(gmlp_bs).reshape(NL, 4 * 128)
    gbs_s = np.stack([np.concatenate([np.tile(f(gmlp_bs)[l, g, :4], 16) for g in range(4)]) for l in range(NL)], 0)
    shared = dict(w_in=w_in, w_pa=f(w_proj_a), w_pb=f(w_proj_b), w_pc=f(w_proj_c), w_out=f(w_out), b_in=b_in,
                  bpm=np.ascontiguousarray(bpm), bgate=np.ascontiguousarray(bgate), gpm=np.ascontiguousarray(gpm),
                  cw=np.ascontiguousarray(cw), cvec=np.ascontiguousarray(cvec), lru_wa=f(lru_wa), lru_wx=f(lru_wx),
                  gln=np.ascontiguousarray(gln), fln=np.ascontiguousarray(fln), gws=gws,
                  gws_s=np.ascontiguousarray(gws_s), gbs=np.ascontiguousarray(gbs), gbs_s=np.ascontiguousarray(gbs_s))
    shared.update(_consts())
    sc, sn, sm = f(state_mlstm_c), f(state_mlstm_n), f(state_mlstm_m)
    sconv, sh = f(state_lru_conv), f(state_lru_h)
    in_maps = []
    for c in range(8):
        b0 = c * NB
        m = dict(shared)
        m["x_all"] = np.ascontiguousarray(np.concatenate([x_prompt[c], x_sample[b0:b0 + NB].reshape(NS, D)], axis=0))
        m["st_c"] = np.ascontiguousarray(sc[:, b0:b0 + NB])
        m["st_n"] = np.ascontiguousarray(sn[:, b0:b0 + NB])
        m["st_m"] = np.ascontiguousarray(sm[:, b0:b0 + NB].transpose(2, 0, 1).reshape(4, NL * NB))
        m["st_conv"] = np.ascontiguousarray(sconv[:, b0:b0 + NB].reshape(NL, 48, D))
        m["st_h"] = np.ascontiguousarray(sh[:, b0:b0 + NB])
        in_maps.append(m)
    res = run_bass_kernel_spmd(nc, in_maps, core_ids=list(range(8)))
    R = res.results
    g = lambda k, c: np.asarray(R[c][k], dtype=np.float32)
    y_p = np.stack([g("y_all", c)[:NTP] for c in range(8)], 0)
    y_s = np.concatenate([g("y_all", c)[NTP:].reshape(NB, 4, D) for c in range(8)], 0)
    c_p = np.stack([g("o_cp", c) for c in range(8)], 1)
    n_p = np.stack([g("o_np", c) for c in range(8)], 1)
    m_p = np.stack([g("o_mp", c).T for c in range(8)], 1)
    ch = [g("o_convh", c) for c in range(8)]
    conv_p = np.stack([x[:, 17:20] for x in ch], 1)
    h_p = np.stack([x[:, 0] for x in ch], 1)
    c_s = np.concatenate([g("o_cs", c) for c in range(8)], 1)
    n_s = np.concatenate([g("o_ns", c) for c in range(8)], 1)
    m_s = np.concatenate([g("o_ms", c).reshape(4, NL, NB).transpose(1, 2, 0) for c in range(8)], 1)
    conv_s = np.concatenate([x[:, 20:68].reshape(NL, NB, 3, D) for x in ch], 1)
    h_s = np.concatenate([x[:, 1:17] for x in ch], 1)
    v_s = np.concatenate([g("o_vs", c).reshape(NL, NB, 4, D) for c in range(8)], 1)
    outs = (y_p, y_s, c_p, n_p, m_p, conv_p, h_p, c_s, n_s, m_s, conv_s, h_s, v_s)
    return tuple(np.ascontiguousarray(o, dtype=np.float32) for o in outs)
```

```python
import numpy as np
from contextlib import ExitStack
import concourse.bass as bass
import concourse.mybir as mybir
from concourse.bass_utils import run_bass_kernel_spmd

F32 = mybir.dt.float32
BF16 = mybir.dt.bfloat16
AF = mybir.ActivationFunctionType
ALU = mybir.AluOpType
AX = mybir.AxisListType
ENG = ("pe", "act", "dve", "pool", "sp")

NL = 4
D = 1024
KC = 8
NTP = 2048
NS = 64
NT = NTP + NS
NB = 16
NH = 4
DK = 256
IN_W = 13320
TGS = [(0, 512), (512, 512), (1024, 512), (1536, 512), (2048, 64)]
TTS = [(i * 128, 128) for i in range(16)] + [(2048, 64)]
OFF = dict(q=0, k=1024, v=2048, o=3072, za=4096, gate=5120, ub=5128, vb=6152, zb=7176, xc=8200, zc=9224,
           ga=10248, gb=11272, gc=12296)
BLK = ["q", "k", "v", "o", "za", "ub", "vb", "zb", "xc", "zc", "ga", "gb", "gc"]
ALPHA = float((2 * NL) ** 0.25)
LN_EPS = 1e-5
NSLOT = 10
AHEAD = 5
ARENA_F32 = 13312


class Tok:
    __slots__ = ("eng", "sem", "val")

    def __init__(self, eng, sem, val):
        self.eng, self.sem, self.val = eng, sem, val


class Res:
    __slots__ = ("name", "lw", "rd", "const", "dsem")

    def __init__(self, name, const=False):
        self.name, self.lw, self.rd, self.const, self.dsem = name, None, {}, const, None


class Prog:
    def __init__(self, nc, stack):
        self.nc, self.stack = nc, stack
        self.ops = {e: [] for e in ENG}
        self.sem = {e: stack.enter_context(nc.semaphore("sem_" + e)) for e in ENG}
        self.cnt = {e: 0 for e in ENG}
        self.cur = {e: Tok(e, self.sem[e], None) for e in ENG}
        self.waited = {}
        self.nds = 0
        self.free_ds = []
        self.store_toks = {}
        self.marks = []

    def mark(self, name):
        self.marks.append((name, sum(1 for o in self.ops["pe"] if o[1] is not None)))

    def _dsem(self, res):
        if res.dsem is None:
            if self.free_ds:
                res.dsem = self.free_ds.pop()
            else:
                self.nds += 1
                res.dsem = [self.stack.enter_context(self.nc.semaphore("ds_%d" % self.nds)), 0]
        return res.dsem

    def _wait(self, eng, t, waits):
        key = (eng, id(t.sem))
        if self.waited.get(key, 0) >= t.val:
            return
        self.waited[key] = t.val
        waits.append((t.sem, t.val))

    def _deps(self, eng, reads, writes, inorder):
        deps = []
        for r in reads:
            if r.lw is not None:
                deps.append((r.lw, "raw"))
        for w in writes:
            if w.lw is not None:
                deps.append((w.lw, "waw"))
            for t in w.rd.values():
                deps.append((t, "war"))
        waits = []
        for t, kind in deps:
            if t.eng == eng and inorder and eng == "pe":
                continue
            if t.val is None:
                raise RuntimeError("dependency on unsignaled op (%s)" % t.eng)
            self._wait(eng, t, waits)
        return waits

    def _commit(self, tok, reads, writes):
        for r in reads:
            if not r.const:
                r.rd[id(tok.sem)] = tok
        for w in writes:
            w.lw = tok
            w.rd = {}

    def op(self, eng, fn, reads=(), writes=(), sig=None):
        if sig is None:
            sig = eng != "pe"
        waits = self._deps(eng, reads, writes, True)
        tok = self.cur[eng]
        self.ops[eng].append((waits, fn, (self.sem[eng], 1) if sig else None))
        self._commit(tok, reads, writes)
        if sig:
            self.cnt[eng] += 1
            tok.val = self.cnt[eng]
            self.cur[eng] = Tok(eng, self.sem[eng], None)

    def dma(self, q, out, in_, reads=(), writes=(), sres=None, final=False):
        waits = self._deps(q, reads, writes, False)
        ds = self._dsem(sres)
        ds[1] += 1
        tok = Tok(None, ds[0], 16 * ds[1])
        self.ops[q].append((waits, lambda e: e.dma_start(out=out, in_=in_), (ds[0], 16)))
        self._commit(tok, reads, writes)
        if final:
            self.store_toks[id(tok.sem)] = tok

    def barrier(self, dma_res=()):
        toks = [Tok(e, self.sem[e], self.cnt[e]) for e in ENG if self.cnt[e] > 0]
        for r in dma_res:
            if r.dsem is not None and r.dsem[1] > 0:
                toks.append(Tok(None, r.dsem[0], 16 * r.dsem[1]))
        for e in ENG:
            waits = []
            for t in toks:
                if t.eng == e:
                    continue
                self._wait(e, t, waits)
            if waits:
                self.ops[e].append((waits, None, None))

    def finish(self):
        waits = [(t.sem, t.val) for t in self.store_toks.values()]
        self.ops["sp"].append((waits, None, None))

    def emit(self):
        def mk(name):
            def body(e):
                for waits, fn, inc in self.ops[name]:
                    for sem, val in waits:
                        e.wait_ge(sem, val)
                    if fn is None:
                        continue
                    ins = fn(e)
                    if inc is not None:
                        ins.then_inc(inc[0], inc[1])
            return body

        with self.nc.Block() as block:
            block.tensor(mk("pe"))
            block.scalar(mk("act"))
            block.vector(mk("dve"))
            block.gpsimd(mk("pool"))
            block.sync(mk("sp"))


def bc_last(ap, n):
    pat = [list(x) for x in ap.ap]
    return bass.AP(ap.tensor, ap.offset, pat + [[0, n]])


def bc_mid(ap, n):
    pat = [list(x) for x in ap.ap]
    return bass.AP(ap.tensor, ap.offset, [pat[0], [0, n]] + pat[1:])


def pbc(ap_row, n):
    pat = [list(x) for x in ap_row.ap]
    return bass.AP(ap_row.tensor, ap_row.offset, [[0, n]] + pat[-1:])


def build(depth=NL):
    nc = bass.Bass("TRN2", target_bir_lowering=False)

    def din(name, shape):
        return nc.dram_tensor(name, list(shape), F32, kind="ExternalInput").ap()

    def dout(name, shape):
        return nc.dram_tensor(name, list(shape), F32, kind="ExternalOutput").ap()

    x_all = din("x_all", [NT, D])
    w_in = din("w_in", [NL, D, IN_W])
    w_p = {"a": din("w_pa", [NL, D, D]), "b": din("w_pb", [NL, D, D]), "c": din("w_pc", [NL, D, D])}
    w_out = din("w_out", [NL, D, D])
    b_in = din("b_in", [NL, IN_W])
    bpm_d = din("bpm", [128, NL * 13 * 8])
    bgate_d = din("bgate", [4, NL * 2])
    gpm_d = din("gpm", [128, NL * 8])
    cw_d = din("cw", [128, NL * 8 * 4])
    cvec_d = din("cvec", [128, 4 * NL * 8])
    wa_d = din("lru_wa", [NL, 8, 128, 128])
    wx_d = din("lru_wx", [NL, 8, 128, 128])
    gln_d = din("gln", [NL, 2, D])
    fln_d = din("fln", [NL, 2, D])
    gws_d = din("gws", [NL, 4, 128, 128])
    gws_s_d = din("gws_s", [NL, 64, 4 * 64])
    gbs_d = din("gbs", [NL, 4 * 128])
    gbs_s_d = din("gbs_s", [NL, 4 * 64])
    st_c = din("st_c", [NL, NB, NH, DK, DK])
    st_n = din("st_n", [NL, NB, NH, DK])
    st_m = din("st_m", [4, NL * NB])
    st_conv = din("st_conv", [NL, 48, D])
    st_h = din("st_h", [NL, NB, D])
    c_ident = din("c_ident", [128, 128])
    c_mask = din("c_mask", [128, 128])
    c_mask_s = din("c_mask_s", [64, 64])
    c_a0s = din("c_a0s", [4, 64])
    c_hmask = din("c_hmask", [4, 128])
    c_colmask = din("c_colmask", [128, NB * 64])
    c_rowmask = din("c_rowmask", [64, NB])
    c_onesrow = din("c_onesrow", [128, 128])

    y_all = dout("y_all", [NT, D])
    o_cp = dout("o_cp", [NL, NH, DK, DK])
    o_np = dout("o_np", [NL, NH, DK])
    o_mp = dout("o_mp", [4, NL])
    o_convh = dout("o_convh", [NL, 68, D])
    o_cs = dout("o_cs", [NL, NB, NH, DK, DK])
    o_ns = dout("o_ns", [NL, NB, NH, DK])
    o_ms = dout("o_ms", [4, NL * NB])
    o_vs = dout("o_vs", [NL, NS, D])
    xres = nc.dram_tensor("xres", [NT, D], F32, kind="Internal").ap()
    r_xres = [Res("xres%d" % i) for i in range(17)]

    with ExitStack() as st:
        P = Prog(nc, st)

        def sb(name, shape, dt=F32):
            return st.enter_context(nc.sbuf_tensor("s_" + name, list(shape), dt))

        def mm(out, lhsT, rhs, start, stop, reads, writes, sig=None):
            P.op("pe", lambda e: e.matmul(out, lhsT=lhsT, rhs=rhs, start=start, stop=stop), reads, writes,
                 sig=(stop if sig is None else sig))

        def tr(out, in_, ident, reads, writes, sig):
            P.op("pe", lambda e: e.transpose(out, in_, ident), reads, writes, sig=sig)

        def act(out, in_, func, reads, writes, bias=None, scale=None):
            kw = {}
            if bias is not None:
                kw["bias"] = bias
            if scale is not None:
                kw["scale"] = scale
            P.op("act", lambda e: e.activation(out=out, in_=in_, func=func, **kw), reads, writes)

        def tt(eng, out, in0, in1, op, reads, writes):
            P.op(eng, lambda e: e.tensor_tensor(out=out, in0=in0, in1=in1, op=op), reads, writes)

        def ts(eng, out, in0, s1, op0, reads, writes, s2=None, op1=None):
            if op1 is None:
                P.op(eng, lambda e: e.tensor_scalar(out=out, in0=in0, scalar1=s1, scalar2=None, op0=op0), reads, writes)
            else:
                P.op(eng, lambda e: e.tensor_scalar(out=out, in0=in0, scalar1=s1, scalar2=s2, op0=op0, op1=op1), reads, writes)

        def stt(out, in0, scalar, in1, op0, op1, reads, writes):
            P.op("dve", lambda e: e.scalar_tensor_tensor(out=out, in0=in0, scalar=scalar, in1=in1, op0=op0, op1=op1), reads, writes)

        def cp(eng, out, in_, reads, writes):
            if eng == "act":
                act(out, in_, AF.Identity, reads, writes)
            else:
                P.op(eng, lambda e: e.tensor_copy(out=out, in_=in_), reads, writes)

        def memset(eng, ap, val, writes):
            P.op(eng, lambda e: e.memset(ap, val), (), writes)

        def bnstats(out, in_, reads, writes):
            P.op("dve", lambda e: e.bn_stats(out=out, in_=in_), reads, writes)

        def bnaggr(out, in_, reads, writes):
            P.op("dve", lambda e: e.bn_aggr(out=out, in_=in_), reads, writes)

        def recip(out, in_, reads, writes):
            P.op("dve", lambda e: e.reciprocal(out=out, in_=in_), reads, writes)

        def scan(out, d0, d1, init, op0, op1, reads, writes):
            P.op("dve", lambda e: e.tensor_tensor_scan(out=out, data0=d0, data1=d1, initial=init, op0=op0, op1=op1), reads, writes)

        def treduce(out, in_, op, reads, writes):
            P.op("dve", lambda e: e.tensor_reduce(out=out, in_=in_, axis=AX.X, op=op), reads, writes)

        xT = sb("xT", [128, KC, NT], BF16)
        r_xT = [Res("xT%d" % i) for i in range(5)]
        ybr = sb("ybr", [128, KC, NT], BF16)
        r_ybr = [[Res("ybr%d_%d" % (c, g)) for g in range(5)] for c in range(KC)]
        mrg = sb("mrg", [128, KC, NT], BF16)
        r_mrg = [Res("mrg%d" % g) for g in range(5)]
        vn_v = mrg[:].rearrange("p k t -> p (k t)")
        wsl = [sb("wsl%d" % i, [128, KC, 256], BF16) for i in range(NSLOT)]
        r_wsl = [Res("wsl%d" % i) for i in range(NSLOT)]
        arena = sb("arena", [128, ARENA_F32], F32)
        ident_f = sb("ident_f", [128, 128], F32); r_idf = Res("idf", True)
        ident_b = sb("ident_b", [128, 128], BF16); r_idb = Res("idb", True)
        mask_f = sb("mask_f", [128, 128], F32); r_mask = Res("mask", True)
        mask_s = sb("mask_s", [64, 64], F32); r_mask_s = Res("mask_s", True)
        a0s = sb("a0s", [4, 64], F32); r_a0s = Res("a0s", True)
        ones4 = sb("ones4", [4, 512], F32); r_ones4 = Res("ones4", True)
        ones4x = sb("ones4x", [4, 128], F32); r_ones4x = Res("ones4x", True)
        hmask = sb("hmask", [4, 128], F32); r_hmask = Res("hmask", True)
        colmask = sb("colmask", [128, NB, 64], BF16); r_colmask = Res("colmask", True)
        rowmask = sb("rowmask", [64, NB], F32); r_rowmask = Res("rowmask", True)
        onesrow = sb("onesrow", [128, 128], BF16); r_onesrow = Res("onesrow", True)
        bpm = sb("bpm", [128, NL, 13, 8], F32); r_bpm = Res("bpm", True)
        kbias = sb("kbias", [128, NL, 8], F32); r_kbias = Res("kbias", True)
        bgate = sb("bgate", [4, NL, 2], F32); r_bgate = Res("bgate", True)
        nbf = sb("nbf", [4, NL], F32); r_nbf = Res("nbf", True)
        gpm = sb("gpm", [128, NL, 8], F32); r_gpm = Res("gpm", True)
        gpmh = sb("gpmh", [128, NL, 8], F32); r_gpmh = Res("gpmh", True)
        cw = sb("cw", [128, NL, 8, 4], F32); r_cw = Res("cw", True)
        cvec = sb("cvec", [128, 4, NL, 8], F32); r_cvec = Res("cvec", True)
        cl = sb("cl", [128, 2, NL, 8], F32); r_cl = Res("cl", True)
        wg = sb("wg", [128, NL, KC, 8], BF16); r_wg = Res("wg", True)
        m0s = sb("m0s", [4, NL, NB], F32); r_m0s = Res("m0s", True)
        wthr = sb("wthr", [128, 17, 8], F32); r_wthr = Res("wthr")
        decbc = sb("decbc", [128, 128], F32); r_decbc = Res("decbc")
        mcall = sb("mcall", [4, 33], F32); r_mcall = Res("mcall")
        mcs = sb("mcs", [4, 3, NB], F32); r_mcs = Res("mcs")
        dec4 = sb("dec4", [4, 32], F32); r_dec4 = Res("dec4")
        decbd = sb("decbd", [4, 4, 32], F32); r_decbd = Res("decbd")
        mpo = sb("mpo", [4, 1], F32); r_mpo = Res("mpo")
        bbl = sb("bbl", [4, 2], F32); r_bbl = Res("bbl")
        eps_t = sb("eps_t", [128, 1], F32); r_eps = Res("eps", True)

        ps = [st.enter_context(nc.psum_tensor("ps%d" % i, [128, 1024], F32)) for i in range(4)]
        psb = [p.bitcast(BF16) for p in ps]
        r_ps = [Res("psb%d" % i) for i in range(8)]

        def bank(b):
            return ps[b // 2][:, (b % 2) * 512:(b % 2) * 512 + 512]

        def bank_bf(b):
            return psb[b // 2][:, (b % 2) * 1024:(b % 2) * 1024 + 1024]

        rot = {"mm": 0, "aux": 0}

        def next_mm():
            rot["mm"] = (rot["mm"] + 1) % 4
            return rot["mm"]

        def next_aux():
            rot["aux"] = (rot["aux"] + 1) % 2
            return 4 + rot["aux"]

        class Arena:
            def __init__(self):
                self.off = 0
                self.res = []

            def reset(self):
                P.barrier(self.res)
                for r in self.res:
                    if r.dsem is not None:
                        P.free_ds.append(r.dsem)
                        r.dsem = None
                self.off = 0
                self.res = []

            def f32(self, shape, name):
                n = int(np.prod(shape[1:]))
                ap = arena[0:shape[0], self.off:self.off + n]
                self.off += n
                assert self.off <= ARENA_F32, "arena overflow %d" % self.off
                r = Res(name)
                self.res.append(r)
                if len(shape) == 3:
                    ap = ap.rearrange("p (a b) -> p a b", a=shape[1])
                return ap, r

            def bf(self, shape, name):
                n = int(np.prod(shape[1:]))
                n32 = (n + 1) // 2
                ap = arena[0:shape[0], self.off:self.off + n32].bitcast(BF16)
                self.off += n32
                assert self.off <= ARENA_F32, "arena overflow %d" % self.off
                r = Res(name)
                self.res.append(r)
                ap = ap[:, 0:n]
                if len(shape) == 3:
                    ap = ap.rearrange("p (a b) -> p a b", a=shape[1])
                return ap, r

        A = Arena()

        wlist = []

        def wsrc(ap2d):
            return ap2d.rearrange("(kc p) n -> p kc n", p=128)

        for l in range(depth):
            for j in range(4):
                wlist.append(("vb%d" % j, wsrc(w_in[l, :, OFF["vb"] + j * 256:OFF["vb"] + (j + 1) * 256])))
            for j in range(4):
                wlist.append(("ub%d" % j, wsrc(w_in[l, :, OFF["ub"] + j * 256:OFF["ub"] + (j + 1) * 256])))
                wlist.append(("zb%d" % j, wsrc(w_in[l, :, OFF["zb"] + j * 256:OFF["zb"] + (j + 1) * 256])))
            for j in range(4):
                wlist.append(("pb%d" % j, wsrc(w_p["b"][l, :, j * 256:(j + 1) * 256])))
                wlist.append(("gb%d" % j, wsrc(w_in[l, :, OFF["gb"] + j * 256:OFF["gb"] + (j + 1) * 256])))
            for h in range(4):
                for nm in ("q", "k", "v", "o", "za"):
                    wlist.append(("%s%d" % (nm, h), wsrc(w_in[l, :, OFF[nm] + h * 256:OFF[nm] + (h + 1) * 256])))
            for j in range(4):
                wlist.append(("pa%d" % j, wsrc(w_p["a"][l, :, j * 256:(j + 1) * 256])))
                wlist.append(("ga%d" % j, wsrc(w_in[l, :, OFF["ga"] + j * 256:OFF["ga"] + (j + 1) * 256])))
            for j in range(4):
                wlist.append(("xc%d" % j, wsrc(w_in[l, :, OFF["xc"] + j * 256:OFF["xc"] + (j + 1) * 256])))
                wlist.append(("zc%d" % j, wsrc(w_in[l, :, OFF["zc"] + j * 256:OFF["zc"] + (j + 1) * 256])))
            for j in range(4):
                wlist.append(("pc%d" % j, wsrc(w_p["c"][l, :, j * 256:(j + 1) * 256])))
                wlist.append(("gc%d" % j, wsrc(w_in[l, :, OFF["gc"] + j * 256:OFF["gc"] + (j + 1) * 256])))
            for j in range(4):
                wlist.append(("wo%d" % j, wsrc(w_out[l, :, j * 256:(j + 1) * 256])))
        wst = {"issued": 0, "next": 0}

        def wneed(names):
            i0 = wst["next"]
            outl = []
            for k, nm in enumerate(names):
                assert wlist[i0 + k][0] == nm, (wlist[i0 + k][0], nm)
            last = min(i0 + len(names) - 1 + AHEAD, len(wlist) - 1)
            while wst["issued"] <= last:
                j = wst["issued"]
                s = j % NSLOT
                P.dma("pool", wsl[s][:], wlist[j][1], writes=[r_wsl[s]], sres=r_wsl[s])
                wst["issued"] += 1
            for k in range(len(names)):
                s = (i0 + k) % NSLOT
                outl.append((wsl[s], r_wsl[s]))
            wst["next"] = i0 + len(names)
            return outl

        P.dma("sp", ident_f[:], c_ident, writes=[r_idf], sres=r_idf)
        P.dma("pool", ident_b[:], c_ident, writes=[r_idb], sres=r_idb)
        P.dma("sp", mask_f[:], c_mask, writes=[r_mask], sres=r_mask)
        P.dma("sp", mask_s[:], c_mask_s, writes=[r_mask_s], sres=r_mask_s)
        P.dma("sp", a0s[:], c_a0s, writes=[r_a0s], sres=r_a0s)
        P.dma("sp", hmask[:], c_hmask, writes=[r_hmask], sres=r_hmask)
        P.dma("pool", colmask[:].rearrange("p a b -> p (a b)"), c_colmask, writes=[r_colmask], sres=r_colmask)
        P.dma("sp", rowmask[:], c_rowmask, writes=[r_rowmask], sres=r_rowmask)
        P.dma("pool", onesrow[:], c_onesrow, writes=[r_onesrow], sres=r_onesrow)
        P.dma("sp", bpm[:].rearrange("p a b c -> p (a b c)"), bpm_d, writes=[r_bpm], sres=r_bpm)
        P.dma("sp", bgate[:].rearrange("p a b -> p (a b)"), bgate_d, writes=[r_bgate], sres=r_bgate)
        P.dma("sp", gpm[:].rearrange("p a b -> p (a b)"), gpm_d, writes=[r_gpm], sres=r_gpm)
        P.dma("sp", cw[:].rearrange("p a b c -> p (a b c)"), cw_d, writes=[r_cw], sres=r_cw)
        P.dma("sp", cvec[:].rearrange("p a b c -> p (a b c)"), cvec_d, writes=[r_cvec], sres=r_cvec)
        P.dma("sp", m0s[:].rearrange("p a b -> p (a b)"), st_m, writes=[r_m0s], sres=r_m0s)
        for l in range(depth):
            P.dma("pool", wg[:, l, :, :], w_in[l, :, OFF["gate"]:OFF["gate"] + 8].rearrange("(kc p) n -> p kc n", p=128),
                  writes=[r_wg], sres=r_wg)
        memset("dve", ones4[:], 1.0, [r_ones4])
        memset("dve", ones4x[:], 1.0, [r_ones4x])
        memset("dve", eps_t[:], LN_EPS, [r_eps])
        memset("dve", mcall[:], 0.0, [r_mcall])
        ts("dve", kbias[:], bpm[:, :, 1, :], 1.0 / 16.0, ALU.mult, [r_bpm], [r_kbias])
        ts("dve", nbf[:], bgate[:, :, 1], -1.0, ALU.mult, [r_bgate], [r_nbf])
        ts("dve", gpmh[:], gpm[:], 0.5, ALU.mult, [r_gpm], [r_gpmh])
        act(cl[:, 0], cvec[:, 3], AF.Exp, [r_cvec], [r_cl], scale=-1.0)
        act(cl[:, 0], cl[:, 0], AF.Ln, [r_cl], [r_cl], bias=1.0)
        ts("dve", cl[:, 1], cl[:, 0], -16.0, ALU.mult, [r_cl], [r_cl])
        ts("dve", cl[:, 0], cl[:, 0], -8.0, ALU.mult, [r_cl], [r_cl])

        def tok_group_of(t0):
            return min(t0 // 512, 4)

        def tile_to_xT(src, r_src, ti, gscale=None):
            t0, tn = TTS[ti]
            g = tok_group_of(t0)
            for half in range(2):
                b = next_aux()
                for j in range(4):
                    kc = half * 4 + j
                    tr(bank(b)[:, j * 128:j * 128 + tn], src[0:tn, kc * 128:(kc + 1) * 128], ident_f[0:tn, 0:tn],
                       [r_src, r_idf], [r_ps[b]], sig=(j == 3))
                act(xT[:, half * 4:(half + 1) * 4, t0:t0 + tn],
                    bank(b).rearrange("p (a b) -> p a b", a=4)[:, :, 0:tn], AF.Identity, [r_ps[b]], [r_xT[g]])

        A.reset()
        xin = [A.f32([128, D], "xin%d" % i) for i in range(2)]
        for ti, (t0, tn) in enumerate(TTS):
            xa, rx = xin[ti % 2]
            P.dma("sp", xa[0:tn, :], x_all[t0:t0 + tn, :], writes=[rx], sres=rx)
            tile_to_xT(xa, rx, ti)

        for l in range(depth):
            last = (l == NL - 1)
            xsrc = x_all if l == 0 else xres

            A.reset()
            _gt = [A.f32([4, 512], "gt%d" % i) for i in range(3)]
            gt = [x[0] for x in _gt]
            r_gt = [x[1] for x in _gt]
            for g, (t0, tn) in enumerate(TGS):
                bi, bf_ = next_mm(), next_mm()
                for kc in range(KC):
                    mm(bank(bi)[0:4, 0:tn], wg[:, l, kc, 0:4], xT[:, kc, t0:t0 + tn], kc == 0, kc == KC - 1,
                       [r_wg, r_xT[g]], [r_ps[bi]])
                for kc in range(KC):
                    mm(bank(bf_)[0:4, 0:tn], wg[:, l, kc, 4:8], xT[:, kc, t0:t0 + tn], kc == 0, kc == KC - 1,
                       [r_wg, r_xT[g]], [r_ps[bf_]])
                it_, sp_, bb_ = gt[0][:, 0:tn], gt[1][:, 0:tn], gt[2][:, 0:tn]
                act(it_, bank(bi)[0:4, 0:tn], AF.Identity, [r_ps[bi], r_bgate], [r_gt[0]], bias=bgate[:, l, 0:1])
                act(sp_, bank(bf_)[0:4, 0:tn], AF.Exp, [r_ps[bf_], r_nbf], [r_gt[1]], bias=nbf[:, l:l + 1], scale=-1.0)
                act(sp_, sp_, AF.Ln, [r_gt[1]], [r_gt[1]], bias=1.0)
                if g < 4:
                    init = 0.0 if g == 0 else bbl[:, 0:1]
                    scan(bb_, ones4[:, 0:tn], sp_, init, ALU.mult, ALU.subtract, [r_ones4, r_gt[1], r_bbl], [r_gt[2]])
                    if g < 3:
                        cp("dve", bbl[:, 0:1], gt[2][:, tn - 1:tn], [r_gt[2]], [r_bbl])
                else:
                    scan(bb_, a0s[:], sp_, 0.0, ALU.mult, ALU.subtract, [r_a0s, r_gt[1]], [r_gt[2]])
                tt("dve", it_, it_, bb_, ALU.subtract, [r_gt[0], r_gt[2]], [r_gt[0]])
                if g < 4:
                    treduce(mcall[:, 17 + 4 * g:21 + 4 * g], gt[0][:, :].rearrange("p (c t) -> p c t", c=4), ALU.max, [r_gt[0]], [r_mcall])
                    scan(mcall[:, 1 + 4 * g:5 + 4 * g], ones4[:, 0:4], mcall[:, 17 + 4 * g:21 + 4 * g], mcall[:, 4 * g:4 * g + 1],
                         ALU.mult, ALU.max, [r_mcall, r_ones4], [r_mcall])
                    tt("dve", dec4[:, 4 * g:4 * g + 4], mcall[:, 4 * g:4 * g + 4], mcall[:, 4 * g + 1:4 * g + 5], ALU.subtract,
                       [r_mcall], [r_dec4])
                    mc_b = bc_last(mcall[:, 1 + 4 * g:5 + 4 * g], 128)
                    v3 = lambda a: a.rearrange("p (c t) -> p c t", c=4)
                    if g == 3:
                        tt("dve", mpo[:], gt[2][:, 511:512], mcall[:, 16:17], ALU.add, [r_gt[2], r_mcall], [r_mpo])
                        P.dma("sp", o_mp[:, l:l + 1], mpo[:], reads=[r_mpo], sres=r_mpo, final=True)
                else:
                    treduce(mcs[:, 0, :], gt[0][:, 0:64].rearrange("p (b t) -> p b t", t=4), ALU.max, [r_gt[0]], [r_mcs])
                    tt("dve", mcs[:, 1, :], mcs[:, 0, :], m0s[:, l, :], ALU.max, [r_mcs, r_m0s], [r_mcs])
                    tt("dve", dec4[:, 16:32], m0s[:, l, :], mcs[:, 1, :], ALU.subtract, [r_m0s, r_mcs], [r_dec4])
                    tt("dve", mcs[:, 2, :], gt[2][:, 0:64].rearrange("p (b t) -> p b t", t=4)[:, :, 3], mcs[:, 1, :], ALU.add,
                       [r_gt[2], r_mcs], [r_mcs])
                    P.dma("sp", o_ms[:, l * NB:(l + 1) * NB], mcs[:, 2, :], reads=[r_mcs], sres=r_mcs, final=True)
                    mc_b = bc_last(mcs[:, 1, :], 4)
                    v3 = lambda a: a.rearrange("p (c t) -> p c t", t=4)
                rd = [r_gt[0], r_gt[2], r_mcall, r_mcs]
                tt("dve", v3(sp_), v3(it_), mc_b, ALU.subtract, rd, [r_gt[1]])
                act(sp_, sp_, AF.Exp, [r_gt[1]], [r_gt[1]])
                tt("dve", v3(bb_), v3(bb_), mc_b, ALU.add, rd, [r_gt[2]])
                act(bb_, bb_, AF.Exp, [r_gt[2]], [r_gt[2]], scale=-1.0)
                ntile = 4 if g < 4 else 1
                b = next_aux()
                for c in range(ntile):
                    cn = 128 if g < 4 else 64
                    tr(bank(b)[0:cn, c * 8:c * 8 + 4], gt[1][:, c * 128:c * 128 + cn], ident_f[0:4, 0:4], [r_gt[1], r_idf], [r_ps[b]], False)
                    tr(bank(b)[0:cn, c * 8 + 4:c * 8 + 8], gt[2][:, c * 128:c * 128 + cn], ident_f[0:4, 0:4], [r_gt[2], r_idf], [r_ps[b]],
                       c == ntile - 1)
                cn = 128 if g < 4 else 64
                cp("dve", wthr[0:cn, 4 * g:4 * g + ntile, :], bank(b)[0:cn, 0:8 * ntile].rearrange("p (c k) -> p c k", k=8),
                   [r_ps[b]], [r_wthr])
            act(dec4[:], dec4[:], AF.Exp, [r_dec4], [r_dec4])
            tt("dve", decbd[:], bc_mid(dec4[:], 4), hmask[:].rearrange("p (a b) -> p a b", a=4), ALU.mult, [r_dec4, r_hmask], [r_decbd])
            b = next_aux()
            mm(bank(b)[:, 0:128], ones4x[:], decbd[:].rearrange("p a b -> p (a b)"), True, True, [r_ones4x, r_decbd], [r_ps[b]])
            cp("dve", decbc[:], bank(b)[:, 0:128], [r_ps[b]], [r_decbc])

            A.reset()
            lnbc, r_lnbc = A.f32([128, 2, D], "lnbc")
            vbrow, r_vbrow = A.bf([128, D], "vbrow")
            wsf, r_wsf = A.f32([128, 4, 128], "wsf")
            wmT, r_wmT = A.bf([128, 4, 128], "wmT")
            wss, r_wss = A.f32([64, 4, 64], "wss")
            mts, r_mts = A.bf([64, 4, 64], "mts")
            bsrow, r_bsrow = A.bf([128, 4 * 128], "bsrow")
            bsrow_s, r_bsrow_s = A.bf([128, 4 * 64], "bsrow_s")
            v32 = [A.f32([128, D], "v32_%d" % i) for i in range(2)]
            bst, r_bst = A.f32([128, 12], "bst")
            mv, r_mv = A.f32([128, 2], "mv")
            rstd, r_rstd = A.f32([128, 1], "rstd")
            szb = [A.f32([128, 512], "szb%d" % i) for i in range(2)]
            uzb = [A.f32([128, 512], "uzb%d" % i) for i in range(2)]
            vns, r_vns = A.bf([64, D], "vns")
            P.dma("sp", lnbc[:].rearrange("p a b -> p (a b)"), pbc(gln_d[l].rearrange("a b -> (a b)"), 128), writes=[r_lnbc], sres=r_lnbc)
            memset("pool", vbrow[:], 0.0, [r_vbrow])
            P.dma("pool", vbrow[0:1, :], b_in[l:l + 1, OFF["vb"]:OFF["vb"] + D], writes=[r_vbrow], sres=r_vbrow)
            memset("pool", bsrow[:], 0.0, [r_bsrow])
            P.dma("pool", bsrow[0:1, :], gbs_d[l:l + 1, :], writes=[r_bsrow], sres=r_bsrow)
            memset("pool", bsrow_s[:], 0.0, [r_bsrow_s])
            P.dma("pool", bsrow_s[0:1, :], gbs_s_d[l:l + 1, :], writes=[r_bsrow_s], sres=r_bsrow_s)
            P.dma("sp", wsf[:], gws_d[l].rearrange("g t s -> t g s"), writes=[r_wsf], sres=r_wsf)
            P.dma("sp", wss[:].rearrange("p a b -> p (a b)"), gws_s_d[l], writes=[r_wss], sres=r_wss)
            b = next_aux()
            for g4 in range(4):
                tr(bank(b)[:, g4 * 128:(g4 + 1) * 128], wsf[:, g4, :], ident_f[:], [r_wsf, r_idf], [r_ps[b]], g4 == 3)
            tt("dve", wmT[:], bank(b).rearrange("p (a b) -> p a b", a=4), bc_mid(mask_f[:], 4), ALU.mult, [r_ps[b], r_mask], [r_wmT])
            tt("dve", mts[:], wss[:], bc_mid(mask_s[:], 4), ALU.mult, [r_wss, r_mask_s], [r_mts])
            wv = wneed(["vb0", "vb1", "vb2", "vb3"])
            r_vn = r_mrg
            for ti, (t0, tn) in enumerate(TTS):
                g = tok_group_of(t0)
                va, rv = v32[ti % 2]
                pr = (ti % 2) * 2
                for j in range(4):
                    bq = pr + j // 2
                    o_ap = bank(bq)[0:tn, (j % 2) * 256:(j % 2) * 256 + 256]
                    for kc in range(KC):
                        mm(o_ap, xT[:, kc, t0:t0 + tn], wv[j][0][:, kc, :], kc == 0, False, [r_xT[g], wv[j][1]], [r_ps[bq]], sig=False)
                    mm(o_ap, onesrow[:, 0:tn], vbrow[:, j * 256:(j + 1) * 256], False, True, [r_onesrow, r_vbrow], [r_ps[bq]])
                for hf in range(2):
                    bnstats(bst[0:tn, hf * 6:hf * 6 + 6], bank(pr + hf)[0:tn, :], [r_ps[pr + hf]], [r_bst])
                bnaggr(mv[0:tn, :], bst[0:tn, :], [r_bst], [r_mv])
                act(rstd[0:tn, :], mv[0:tn, 1:2], AF.Sqrt, [r_mv, r_eps], [r_rstd], bias=eps_t[0:tn, :])
                recip(rstd[0:tn, :], rstd[0:tn, :], [r_rstd], [r_rstd])
                ts("dve", va[0:tn, :], ps[pr // 2][0:tn, :], mv[0:tn, 0:1], ALU.subtract, [r_ps[pr], r_ps[pr + 1], r_mv, r_rstd], [rv],
                   s2=rstd[0:tn, 0:1], op1=ALU.mult)
                tt("pool", va[0:tn, :], va[0:tn, :], lnbc[0:tn, 0, :], ALU.mult, [rv, r_lnbc], [rv])
                if ti < 16:
                    tt("pool", vn_v[0:tn, ti * 1024:(ti + 1) * 1024], va[0:tn, :], lnbc[0:tn, 1, :], ALU.add, [rv, r_lnbc], [r_vn[g]])
                else:
                    tt("pool", va[0:tn, :], va[0:tn, :], lnbc[0:tn, 1, :], ALU.add, [rv, r_lnbc], [rv])
                    P.dma("sp", o_vs[l], va[0:tn, :], reads=[rv], sres=rv, final=True)
                    cp("pool", vns[:, :], va[0:tn, :], [rv], [r_vns])
            for j in range(4):
                wu, wz = wneed(["ub%d" % j, "zb%d" % j])
                for cc in range(2):
                    c = 2 * j + cc
                    g4 = c // 2
                    for g, (t0, tn) in enumerate(TGS):
                        bu, bz, bm = next_mm(), next_mm(), next_aux()
                        for kc in range(KC):
                            mm(bank(bz)[:, 0:tn], wz[0][:, kc, cc * 128:(cc + 1) * 128], xT[:, kc, t0:t0 + tn], kc == 0, kc == KC - 1,
                               [wz[1], r_xT[g]], [r_ps[bz]])
                        for kc in range(KC):
                            mm(bank(bu)[:, 0:tn], wu[0][:, kc, cc * 128:(cc + 1) * 128], xT[:, kc, t0:t0 + tn], kc == 0, kc == KC - 1,
                               [wu[1], r_xT[g]], [r_ps[bu]])
                        if g < 4:
                            for ci in range(4):
                                ti = 4 * g + ci
                                o_ap = bank(bm)[:, ci * 128:(ci + 1) * 128]
                                mm(o_ap, vn_v[:, ti * 1024 + c * 128:ti * 1024 + (c + 1) * 128], wmT[:, g4, :], True, False,
                                   [r_vn[g], r_wmT], [r_ps[bm]], sig=False)
                                mm(o_ap, onesrow[:], bsrow[:, g4 * 128:(g4 + 1) * 128], False, True, [r_onesrow, r_bsrow], [r_ps[bm]],
                                   sig=(ci == 3))
                        else:
                            o_ap = bank(bm)[:, 0:64]
                            mm(o_ap, vns[:, c * 128:(c + 1) * 128], mts[:, g4, :], True, False,
                               [r_vns, r_mts], [r_ps[bm]], sig=False)
                            mm(o_ap, onesrow[:], bsrow_s[:, g4 * 64:(g4 + 1) * 64], False, True, [r_onesrow, r_bsrow_s], [r_ps[bm]])
                        sz, rsz = szb[g % 2]
                        uz, ruz = uzb[g % 2]
                        act(sz[:, 0:tn], bank(bz)[:, 0:tn], AF.Silu, [r_ps[bz], r_bpm], [rsz], bias=bpm[:, l, 7, c:c + 1])
                        stt(uz[:, 0:tn], bank(bu)[:, 0:tn], bpm[:, l, 5, c:c + 1], sz[:, 0:tn], ALU.add, ALU.mult,
                            [r_ps[bu], r_bpm, rsz], [ruz])
                        tt("dve", ybr[:, c, t0:t0 + tn], bank(bm)[:, 0:tn], uz[:, 0:tn], ALU.mult, [r_ps[bm], ruz], [r_ybr[c][g]])

            def proj_merge(br, first):
                A.reset()
                sgb = [A.f32([128, 512], "sg%d" % i) for i in range(2)]
                tmpb = [A.bf([128, 512], "pm%d" % i) for i in range(2)]
                gblk = {"a": 10, "b": 11, "c": 12}[br]
                for j in range(4):
                    wp, wgt = wneed(["p%s%d" % (br, j), "g%s%d" % (br, j)])
                    for cc in range(2):
                        dm = 2 * j + cc
                        for g, (t0, tn) in enumerate(TGS):
                            bp_, bg = next_mm(), next_mm()
                            for kc in range(KC):
                                mm(bank(bg)[:, 0:tn], wgt[0][:, kc, cc * 128:(cc + 1) * 128], xT[:, kc, t0:t0 + tn], kc == 0, kc == KC - 1,
                                   [wgt[1], r_xT[g]], [r_ps[bg]])
                            for fc in range(KC):
                                mm(bank(bp_)[:, 0:tn], wp[0][:, fc, cc * 128:(cc + 1) * 128], ybr[:, fc, t0:t0 + tn], fc == 0, fc == KC - 1,
                                   [wp[1], r_ybr[fc][g]], [r_ps[bp_]])
                            sg, rsg = sgb[g % 2]
                            act(sg[:, 0:tn], bank(bg)[:, 0:tn], AF.Sigmoid, [r_ps[bg], r_bpm], [rsg], bias=bpm[:, l, gblk, dm:dm + 1])
                            if first:
                                tt("dve", mrg[:, dm, t0:t0 + tn], bank(bp_)[:, 0:tn], sg[:, 0:tn], ALU.mult, [r_ps[bp_], rsg], [r_mrg[g]])
                            else:
                                tm, rtm = tmpb[g % 2]
                                tt("dve", tm[:, 0:tn], bank(bp_)[:, 0:tn], sg[:, 0:tn], ALU.mult, [r_ps[bp_], rsg], [rtm])
                                tt("dve", mrg[:, dm, t0:t0 + tn], mrg[:, dm, t0:t0 + tn], tm[:, 0:tn], ALU.add, [r_mrg[g], rtm], [r_mrg[g]])

            proj_merge("b", True)

            A.reset()
            cext = [A.f32([128, 2, 257], "cext%d" % i) for i in range(2)]
            cbf_t = [A.bf([128, 2, 258], "cbf%d" % i) for i in range(4)]
            qTg = [A.bf([128, 2, 512], "qTg%d" % i) for i in range(2)]
            kTg = [A.bf([128, 2, 512], "kTg%d" % i) for i in range(2)]
            vext = [A.bf([128, 258], "vext%d" % i) for i in range(6)]
            so_t = [A.f32([128, 256], "so%d" % i) for i in range(6)]
            sz_t = [A.f32([128, 256], "sza%d" % i) for i in range(6)]
            wk_t = [A.bf([128, 256], "wk%d" % i) for i in range(4)]
            sp_t = [A.bf([128, 128], "sp%d" % i) for i in range(4)]
            ya_t = [A.bf([128, 256], "ya%d" % i) for i in range(4)]
            qd_t = [A.bf([128, 2, 128], "qd%d" % i) for i in range(4)]
            dmx_t = [A.f32([128, 2], "dmx%d" % i) for i in range(4)]
            bst_t = [A.f32([128, 6], "bstA%d" % i) for i in range(4)]
            mv_t = [A.f32([128, 2], "mvA%d" % i) for i in range(4)]
            rstd_t = [A.f32([128, 1], "rstdA%d" % i) for i in range(4)]
            brow, r_brow = A.bf([128, 768], "browA")
            cs_t = [A.f32([128, 2, 257], "cs%d" % i) for i in range(4)]
            cdbs = [A.bf([128, 2, 258], "cdbs%d" % i) for i in range(2)]
            qtm = [A.bf([128, 2, 64], "qtm%d" % i) for i in range(2)]
            vms = [A.bf([64, 258], "vms%d" % i) for i in range(2)]
            nall, r_nall = A.f32([128, NB, 2], "nall")
            nout, r_nout = A.f32([128, NB, 2], "nout")
            memset("pool", brow[:], 0.0, [r_brow])
            for i in range(6):
                memset("pool", vext[i][0][:, 256:258], 1.0, [vext[i][1]])
            nrot = {"i": 0}

            def next_num():
                nrot["i"] = (nrot["i"] + 1) % 4
                return nrot["i"]

            def chunks_of(g):
                t0, tn = TGS[g]
                if g < 4:
                    return [(ci, 4 * g + ci, ci * 128, 128, t0 + ci * 128, (4 * g + ci) % 6) for ci in range(4)]
                return [(0, 16, 0, 64, t0, 16 % 6)]

            for h in range(NH):
                wq, wk_, wv_, wo_, wz_ = wneed(["q%d" % h, "k%d" % h, "v%d" % h, "o%d" % h, "za%d" % h])
                for i3, nm in enumerate(("v", "o", "za")):
                    P.dma("pool", brow[0:1, i3 * 256:(i3 + 1) * 256], b_in[l:l + 1, OFF[nm] + h * 256:OFF[nm] + (h + 1) * 256],
                          writes=[r_brow], sres=r_brow)
                ce, rce = cext[h % 2]
                memset("dve", ce[:], 0.0, [rce])

                def emit_qk(g, h=h, wq=wq, wk_=wk_):
                    t0, tn = TGS[g]
                    qt, rqt = qTg[g % 2]
                    kt, rkt = kTg[g % 2]
                    for dc in range(2):
                        bq, bk = next_mm(), next_mm()
                        for kc in range(KC):
                            mm(bank(bq)[:, 0:tn], wq[0][:, kc, dc * 128:(dc + 1) * 128], xT[:, kc, t0:t0 + tn], kc == 0, kc == KC - 1,
                               [wq[1], r_xT[g]], [r_ps[bq]])
                        for kc in range(KC):
                            mm(bank(bk)[:, 0:tn], wk_[0][:, kc, dc * 128:(dc + 1) * 128], xT[:, kc, t0:t0 + tn], kc == 0, kc == KC - 1,
                               [wk_[1], r_xT[g]], [r_ps[bk]])
                        act(qt[:, dc, 0:tn], bank(bq)[:, 0:tn], AF.Identity, [r_ps[bq], r_bpm], [rqt], bias=bpm[:, l, 0, 2 * h + dc:2 * h + dc + 1])
                        act(kt[:, dc, 0:tn], bank(bk)[:, 0:tn], AF.Identity, [r_ps[bk], r_kbias], [rkt],
                            bias=kbias[:, l, 2 * h + dc:2 * h + dc + 1], scale=1.0 / 16.0)

                def L1(g, sel, wv_=wv_, wo_=wo_, wz_=wz_):
                    for (ci, ti, c0, cn, tk0, bi) in chunks_of(g):
                        if ci not in sel:
                            continue
                        ve, rve = vext[bi]
                        so, rso = so_t[bi]
                        sza, rsza = sz_t[bi]
                        bv, bo = next_mm(), next_mm()
                        for (o_ap, wt, i3, rb) in ((bank(bv)[0:cn, 0:256], wv_, 0, r_ps[bv]), (bank(bv)[0:cn, 256:512], wo_, 1, r_ps[bv]),
                                                   (bank(bo)[0:cn, 0:256], wz_, 2, r_ps[bo])):
                            for kc in range(KC):
                                mm(o_ap, xT[:, kc, tk0:tk0 + cn], wt[0][:, kc, :], kc == 0, False, [r_xT[g], wt[1]], [rb], sig=False)
                            mm(o_ap, onesrow[:, 0:cn], brow[:, i3 * 256:(i3 + 1) * 256], False, True, [r_onesrow, r_brow], [rb])
                        cp("act", ve[0:cn, 0:256], bank(bv)[0:cn, 0:256], [r_ps[bv]], [rve])
                        act(so[0:cn, :], bank(bv)[0:cn, 256:512], AF.Tanh, [r_ps[bv]], [rso], scale=0.5)
                        act(sza[0:cn, :], bank(bo)[0:cn, 0:256], AF.Tanh, [r_ps[bo]], [rsza], scale=0.5)
                        ts("dve", so[0:cn, :], so[0:cn, :], 0.5, ALU.mult, [rso], [rso], s2=0.5, op1=ALU.add)
                        stt(sza[0:cn, :], sza[0:cn, :], 1.0, bank(bo)[0:cn, 0:256], ALU.add, ALU.mult, [rsza, r_ps[bo]], [rsza])

                def L2(g, h=h):
                    qt, rqt = qTg[g % 2]
                    kt, rkt = kTg[g % 2]
                    msk, rmsk = (mask_f, r_mask) if g < 4 else (mask_s, r_mask_s)
                    for (ci, ti, c0, cn, tk0, bi) in chunks_of(g):
                        wkt, rwk = wk_t[ci]
                        spt, rsp = sp_t[ci]
                        bt = next_aux()
                        for dc in range(2):
                            tr(bank_bf(bt)[0:cn, dc * 128:(dc + 1) * 128], kt[:, dc, c0:c0 + cn], ident_b[:], [rkt, r_idb], [r_ps[bt]], dc == 1)
                        ts("dve", wkt[0:cn, :], bank_bf(bt)[0:cn, 0:256], wthr[0:cn, ti, h:h + 1], ALU.mult, [r_ps[bt], r_wthr], [rwk])
                        bs_ = next_aux()
                        for dc in range(2):
                            mm(bank(bs_)[0:cn, 0:cn], kt[:, dc, c0:c0 + cn], qt[:, dc, c0:c0 + cn], dc == 0, dc == 1, [rkt, rqt], [r_ps[bs_]])
                        stt(spt[0:cn, 0:cn], bank(bs_)[0:cn, 0:cn], wthr[0:cn, ti, h:h + 1], msk[0:cn, 0:cn], ALU.mult, ALU.mult,
                            [r_ps[bs_], r_wthr, rmsk], [rsp])
                        if g < 4 and ti > 0:
                            qd, rqd = qd_t[ci]
                            ts("dve", qd[:], qt[:, :, c0:c0 + cn], decbc[:, h * 32 + ti:h * 32 + ti + 1], ALU.mult, [rqt, r_decbc], [rqd])

                def L3(g, h=h, ce=ce, rce=rce):
                    qt, rqt = qTg[g % 2]
                    pre_num = {}
                    post = []

                    def emit_num(ch):
                        (ci, ti, c0, cn, tk0, bi) = ch
                        ve, rve = vext[bi]
                        spt, rsp = sp_t[ci]
                        bn_ = next_num()
                        first = (ti == 0)
                        mm(bank(bn_)[0:cn, 0:257], spt[0:cn, 0:cn], ve[0:cn, 0:257], True, first, [rsp, rve], [r_ps[bn_]], sig=first)
                        if not first:
                            qd, rqd = qd_t[ci]
                            cbf, r_cbf = cbf_t[(ti - 1) % 4]
                            for dc in range(2):
                                mm(bank(bn_)[0:cn, 0:257], qd[:, dc, :], cbf[:, dc, 0:257], False, dc == 1, [rqd, r_cbf], [r_ps[bn_]])
                        return bn_

                    if g < 4:
                        pre_num[0] = emit_num(chunks_of(g)[0])
                        for (ci, ti, c0, cn, tk0, bi) in chunks_of(g):
                            ve, rve = vext[bi]
                            wkt, rwk = wk_t[ci]
                            up = 6 if ci % 2 == 0 else 4
                            for dc in range(2):
                                mm(bank(up + dc)[:, 0:257], wkt[0:cn, dc * 128:(dc + 1) * 128], ve[0:cn, 0:257], True, True, [rwk, rve], [r_ps[up + dc]])
                            stt(ce[:], ce[:], decbc[:, h * 32 + ti:h * 32 + ti + 1], ps[up // 2][:, :].rearrange("p (a b) -> p a b", a=2)[:, :, 0:257],
                                ALU.mult, ALU.add, [rce, r_decbc, r_ps[up], r_ps[up + 1]], [rce])
                            if ti < 15:
                                cbf, r_cbf = cbf_t[ti % 4]
                                cp("act", cbf[:, :, 0:257], ce[:], [rce], [r_cbf])
                            else:
                                P.dma("sp", o_cp[l, h].rearrange("(dc p) e -> p dc e", p=128), ce[:, :, 0:256], reads=[rce], sres=rce, final=True)
                                P.dma("sp", o_np[l, h].rearrange("(dc p) -> p dc", p=128), ce[:, :, 256], reads=[rce], sres=rce, final=True)
                    for (ci, ti, c0, cn, tk0, bi) in chunks_of(g):
                        ve, rve = vext[bi]
                        so, rso = so_t[bi]
                        wkt, rwk = wk_t[ci]
                        spt, rsp = sp_t[ci]
                        dmx, r_dmx = dmx_t[ci]
                        if g < 4:
                            bn_ = pre_num[ci] if ci in pre_num else emit_num((ci, ti, c0, cn, tk0, bi))
                        else:
                            bn_ = next_num()
                            def load_cs(b_):
                                cs, rcs = cs_t[b_ % 4]
                                P.dma("act", cs[:, :, 0:256], st_c[l, b_, h].rearrange("(dc p) e -> p dc e", p=128), writes=[rcs], sres=rcs)
                            for dc in range(2):
                                P.dma("sp", nall[:, :, dc], st_n[l, :, h, dc * 128:(dc + 1) * 128].rearrange("b p -> p b"), writes=[r_nall], sres=r_nall)
                            load_cs(0)
                            load_cs(1)
                            mm(bank(bn_)[0:cn, 0:257], spt[0:cn, 0:cn], ve[0:cn, 0:257], True, False, [rsp, rve], [r_ps[bn_]], sig=False)
                            for b_ in range(NB):
                                if b_ + 2 < NB:
                                    load_cs(b_ + 2)
                                cs, rcs = cs_t[b_ % 4]
                                cb_, rcb = cdbs[b_ % 2]
                                qm, rqm = qtm[b_ % 2]
                                vm, rvm = vms[b_ % 2]
                                up = 6 if b_ % 2 == 0 else 4
                                cp("dve", cs[:, :, 256], nall[:, b_, :], [r_nall], [rcs])
                                cp("act", cb_[:, :, 0:257], cs[:], [rcs], [rcb])
                                stt(qm[:], qt[:, :, 0:64], decbc[:, h * 32 + 16 + b_:h * 32 + 17 + b_], bc_mid(colmask[:, b_, :], 2),
                                    ALU.mult, ALU.mult, [rqt, r_decbc, r_colmask], [rqm])
                                for dc in range(2):
                                    mm(bank(bn_)[0:cn, 0:257], qm[:, dc, :], cb_[:, dc, 0:257], False, (b_ == NB - 1 and dc == 1),
                                       [rqm, rcb], [r_ps[bn_]])
                                act(vm[:, 0:257], ve[0:64, 0:257], AF.Copy, [rve, r_rowmask], [rvm], scale=rowmask[:, b_:b_ + 1])
                                for dc in range(2):
                                    mm(bank(up + dc)[:, 0:257], wkt[0:64, dc * 128:(dc + 1) * 128], vm[:, 0:257], True, True, [rwk, rvm], [r_ps[up + dc]])
                                stt(cs[:], cs[:], decbc[:, h * 32 + 16 + b_:h * 32 + 17 + b_],
                                    ps[up // 2][:, :].rearrange("p (a b) -> p a b", a=2)[:, :, 0:257], ALU.mult, ALU.add,
                                    [rcs, r_decbc, r_ps[up], r_ps[up + 1]], [rcs])
                                P.dma("sp", o_cs[l, b_, h].rearrange("(dc p) e -> p dc e", p=128), cs[:, :, 0:256], reads=[rcs], sres=rcs, final=True)
                                cp("dve", nout[:, b_, :], cs[:, :, 256], [rcs], [r_nout])
                            for dc in range(2):
                                P.dma("sp", o_ns[l, :, h, dc * 128:(dc + 1) * 128].rearrange("b p -> p b"), nout[:, :, dc], reads=[r_nout], sres=r_nout, final=True)
                        post.append((ci, ti, cn, bi, bn_))
                    for (ci, ti, cn, bi, bn_) in post:
                        dmx, r_dmx = dmx_t[ci]
                        act(dmx[0:cn, 0:1], bank(bn_)[0:cn, 256:257], AF.Abs, [r_ps[bn_]], [r_dmx])
                    for (ci, ti, cn, bi, bn_) in post:
                        dmx, r_dmx = dmx_t[ci]
                        ts("dve", dmx[0:cn, 0:1], dmx[0:cn, 0:1], wthr[0:cn, ti, 4 + h:5 + h], ALU.max, [r_dmx, r_wthr], [r_dmx])
                    for (ci, ti, cn, bi, bn_) in post:
                        dmx, r_dmx = dmx_t[ci]
                        recip(dmx[0:cn, 1:2], dmx[0:cn, 0:1], [r_dmx], [r_dmx])
                    for (ci, ti, cn, bi, bn_) in post:
                        dmx, r_dmx = dmx_t[ci]
                        so, rso = so_t[bi]
                        stt(so[0:cn, :], bank(bn_)[0:cn, 0:256], dmx[0:cn, 1:2], so[0:cn, :], ALU.mult, ALU.mult, [r_ps[bn_], r_dmx, rso], [rso])

                def L4(g):
                    CH = chunks_of(g)
                    for (ci, ti, c0, cn, tk0, bi) in CH:
                        hs, rhs = so_t[bi]
                        bst, r_bst = bst_t[ci]
                        mv, r_mv = mv_t[ci]
                        bnstats(bst[0:cn, :], hs[0:cn, :], [rhs], [r_bst])
                    for (ci, ti, c0, cn, tk0, bi) in CH:
                        bst, r_bst = bst_t[ci]
                        mv, r_mv = mv_t[ci]
                        bnaggr(mv[0:cn, :], bst[0:cn, :], [r_bst], [r_mv])
                    for (ci, ti, c0, cn, tk0, bi) in CH:
                        mv, r_mv = mv_t[ci]
                        rstd, r_rstd = rstd_t[ci]
                        act(rstd[0:cn, :], mv[0:cn, 1:2], AF.Sqrt, [r_mv, r_eps], [r_rstd], bias=eps_t[0:cn, :])
                    for (ci, ti, c0, cn, tk0, bi) in CH:
                        hs, rhs = so_t[bi]
                        sza, rsza = sz_t[bi]
                        ya, rya = ya_t[ci]
                        mv, r_mv = mv_t[ci]
                        rstd, r_rstd = rstd_t[ci]
                        recip(rstd[0:cn, :], rstd[0:cn, :], [r_rstd], [r_rstd])
                    for (ci, ti, c0, cn, tk0, bi) in CH:
                        hs, rhs = so_t[bi]
                        sza, rsza = sz_t[bi]
                        mv, r_mv = mv_t[ci]
                        stt(hs[0:cn, :], hs[0:cn, :], mv[0:cn, 0:1], sza[0:cn, :], ALU.subtract, ALU.mult, [rhs, r_mv, rsza], [rhs])
                    for (ci, ti, c0, cn, tk0, bi) in CH:
                        hs, rhs = so_t[bi]
                        ya, rya = ya_t[ci]
                        rstd, r_rstd = rstd_t[ci]
                        act(ya[0:cn, :], hs[0:cn, :], AF.Copy, [rhs, r_rstd], [rya], scale=rstd[0:cn, 0:1])

                def L5(g, h=h):
                    for (ci, ti, c0, cn, tk0, bi) in chunks_of(g):
                        ya, rya = ya_t[ci]
                        bt2 = next_aux()
                        for dc in range(2):
                            tr(bank_bf(bt2)[:, dc * 128:dc * 128 + cn], ya[0:cn, dc * 128:(dc + 1) * 128], ident_b[0:cn, 0:cn], [rya, r_idb],
                               [r_ps[bt2]], dc == 1)
                        for dc in range(2):
                            fc = 2 * h + dc
                            act(ybr[:, fc, tk0:tk0 + cn], bank_bf(bt2)[:, dc * 128:dc * 128 + cn], AF.Copy, [r_ps[bt2], r_gpmh], [r_ybr[fc][g]],
                                scale=gpmh[:, l, fc:fc + 1])

                emit_qk(0)
                L1(0, (0, 1, 2, 3))
                for g in range(len(TGS)):
                    if g == 0 or g == 4:
                        P.mark("A_h%d_%s" % (h, "prompt" if g < 4 else "sample"))
                    L2(g)
                    L3(g)
                    if g + 1 < len(TGS):
                        emit_qk(g + 1)
                        L1(g + 1, (0, 1))
                    L4(g)
                    if g + 1 < len(TGS):
                        L1(g + 1, (2, 3))
                    L5(g)
            P.mark("A_end")
            proj_merge("a", False)

            A.reset()
            waT, r_wa = A.bf([128, 8, 128], "wa")
            wxT, r_wx = A.bf([128, 8, 128], "wx")
            stc, r_stc = A.f32([48, D], "stc")
            sth, r_sth = A.f32([16, D], "sth")
            stg_t, r_stg_t = A.f32([68, D], "stg_t")
            stg, r_stg = A.f32([128, KC, 68], "stg")
            hprev2 = [A.f32([128, 17], "hprev%d" % i) for i in range(2)]
            hist2 = [A.f32([128, 48], "hist%d" % i) for i in range(2)]
            ctail2 = [A.f32([128, 3], "ctail%d" % i) for i in range(2)]
            cbuf = [[A.f32([128, 515], "cb%d_%d" % (i, k)) for k in range(5)] for i in range(2)]
            xcb = [A.bf([128, 512], "xcb%d" % i) for i in range(2)]
            szc = [A.f32([128, 512], "szc%d" % i) for i in range(2)]
            P.dma("pool", waT[:], wa_d[l].rearrange("n i j -> i n j"), writes=[r_wa], sres=r_wa)
            P.dma("pool", wxT[:], wx_d[l].rearrange("n i j -> i n j"), writes=[r_wx], sres=r_wx)
            P.dma("sp", stc[:], st_conv[l], writes=[r_stc], sres=r_stc)
            P.dma("sp", sth[:], st_h[l], writes=[r_sth], sres=r_sth)
            for j in range(4):
                wxc, wzc = wneed(["xc%d" % j, "zc%d" % j])
                for cc in range(2):
                    c = 2 * j + cc
                    hprev, r_hprev = hprev2[cc]
                    hist, r_hist = hist2[cc]
                    ctail, r_ctail = ctail2[cc]
                    b = next_aux()
                    tr(bank(b)[:, 0:48], stc[:, c * 128:(c + 1) * 128], ident_f[0:48, 0:48], [r_stc, r_idf], [r_ps[b]], False)
                    tr(bank(b)[:, 48:64], sth[:, c * 128:(c + 1) * 128], ident_f[0:16, 0:16], [r_sth, r_idf], [r_ps[b]], True)
                    cp("dve", hprev[:, 1:17], bank(b)[:, 48:64], [r_ps[b]], [r_hprev])
                    cp("dve", hist[:], bank(b)[:, 0:48], [r_ps[b]], [r_hist])
                    memset("dve", ctail[:], 0.0, [r_ctail])
                for g, (t0, tn) in enumerate(TGS):
                    views = {}
                    for cc in range(2):
                        c = 2 * j + cc
                        hprev, r_hprev = hprev2[cc]
                        hist, r_hist = hist2[cc]
                        ctail, r_ctail = ctail2[cc]
                        (xp, rxp), (xc_, rxc), (ra, rra), (ib, rib), (t1, rt1) = cbuf[cc]
                        sz, rsz = szc[cc]
                        bx_, bz = next_mm(), next_mm()
                        for kc in range(KC):
                            mm(bank(bx_)[:, 0:tn], wxc[0][:, kc, cc * 128:(cc + 1) * 128], xT[:, kc, t0:t0 + tn], kc == 0, kc == KC - 1,
                               [wxc[1], r_xT[g]], [r_ps[bx_]])
                        for kc in range(KC):
                            mm(bank(bz)[:, 0:tn], wzc[0][:, kc, cc * 128:(cc + 1) * 128], xT[:, kc, t0:t0 + tn], kc == 0, kc == KC - 1,
                               [wzc[1], r_xT[g]], [r_ps[bz]])
                        if g < 4:
                            cp("dve", xp[:, 0:3], ctail[:], [r_ctail], [rxp])
                            act(xp[:, 3:3 + tn], bank(bx_)[:, 0:tn], AF.Identity, [r_ps[bx_], r_bpm], [rxp], bias=bpm[:, l, 8, c:c + 1])
                            if g < 3:
                                cp("dve", ctail[:], xp[:, tn:tn + 3], [rxp], [r_ctail])
                            else:
                                cp("dve", stg[:, c, 17:20], xp[:, tn:tn + 3], [rxp], [r_stg])
                            xp_ = xp
                            xpv = (lambda xp_: (lambda jj: xp_[:, jj:jj + 512]))(xp)
                            xcv = xc_[:, 0:tn]
                        else:
                            xp3 = xp[:, 0:112].rearrange("p (b k) -> p b k", k=7)
                            cp("dve", xp3[:, :, 0:3], hist[:].rearrange("p (b k) -> p b k", k=3), [r_hist], [rxp])
                            act(xp3[:, :, 3:7], bank(bx_)[:, 0:64].rearrange("p (b k) -> p b k", k=4), AF.Identity, [r_ps[bx_], r_bpm], [rxp],
                                bias=bpm[:, l, 8, c:c + 1])
                            cp("dve", stg[:, c, 20:68].rearrange("p (b k) -> p b k", k=3), xp3[:, :, 4:7], [rxp], [r_stg])
                            xpv = (lambda xp3: (lambda jj: xp3[:, :, jj:jj + 4]))(xp3)
                            xcv = xc_[:, 0:64].rearrange("p (b k) -> p b k", k=4)
                        act(sz[:, 0:tn], bank(bz)[:, 0:tn], AF.Silu, [r_ps[bz], r_bpm], [rsz], bias=bpm[:, l, 9, c:c + 1])
                        views[cc] = (xpv, xcv)
                    for cc in range(2):
                        c = 2 * j + cc
                        (xp, rxp), (xc_, rxc), (ra, rra), (ib, rib), (t1, rt1) = cbuf[cc]
                        xb, rxb = xcb[cc]
                        xpv, xcv = views[cc]
                        ts("dve", xcv, xpv(0), cw[:, l, c, 0:1], ALU.mult, [rxp, r_cw, r_cvec], [rxc], s2=cvec[:, 0, l, c:c + 1], op1=ALU.add)
                        for jj in range(1, 4):
                            stt(xcv, xpv(jj), cw[:, l, c, jj:jj + 1], xcv, ALU.mult, ALU.add, [rxp, r_cw, rxc], [rxc])
                        cp("dve", xb[:, 0:tn], xc_[:, 0:tn], [rxc], [rxb])
                    for cc in range(2):
                        c = 2 * j + cc
                        (xp, rxp), (xc_, rxc), (ra, rra), (ib, rib), (t1, rt1) = cbuf[cc]
                        xb, rxb = xcb[cc]
                        br_, bi_ = next_aux(), next_aux()
                        mm(bank(br_)[:, 0:tn], waT[:, c, :], xb[:, 0:tn], True, True, [r_wa, rxb], [r_ps[br_]])
                        mm(bank(bi_)[:, 0:tn], wxT[:, c, :], xb[:, 0:tn], True, True, [r_wx, rxb], [r_ps[bi_]])
                        act(ra[:, 0:tn], bank(br_)[:, 0:tn], AF.Sigmoid, [r_ps[br_], r_cvec], [rra], bias=cvec[:, 1, l, c:c + 1])
                        act(ib[:, 0:tn], bank(bi_)[:, 0:tn], AF.Sigmoid, [r_ps[bi_], r_cvec], [rib], bias=cvec[:, 2, l, c:c + 1])
                    for cc in range(2):
                        c = 2 * j + cc
                        (xp, rxp), (xc_, rxc), (ra, rra), (ib, rib), (t1, rt1) = cbuf[cc]
                        act(t1[:, 0:tn], ra[:, 0:tn], AF.Exp, [rra, r_cl], [rt1], scale=cl[:, 1, l, c:c + 1])
                        act(ra[:, 0:tn], ra[:, 0:tn], AF.Exp, [rra, r_cl], [rra], scale=cl[:, 0, l, c:c + 1])
                    for cc in range(2):
                        (xp, rxp), (xc_, rxc), (ra, rra), (ib, rib), (t1, rt1) = cbuf[cc]
                        act(t1[:, 0:tn], t1[:, 0:tn], AF.Sqrt, [rt1], [rt1], bias=1.0, scale=-1.0)
                    for cc in range(2):
                        c = 2 * j + cc
                        hprev, r_hprev = hprev2[cc]
                        (xp, rxp), (xc_, rxc), (ra, rra), (ib, rib), (t1, rt1) = cbuf[cc]
                        sz, rsz = szc[cc]
                        if g == 0:
                            memset("dve", t1[:, 0:1], 1.0, [rt1])
                        tt("dve", ib[:, 0:tn], ib[:, 0:tn], t1[:, 0:tn], ALU.mult, [rib, rt1], [rib])
                        tt("dve", ib[:, 0:tn], ib[:, 0:tn], xc_[:, 0:tn], ALU.mult, [rib, rxc], [rib])
                        if g < 4:
                            init = 0.0 if g == 0 else hprev[:, 0:1]
                            scan(t1[:, 0:tn], ra[:, 0:tn], ib[:, 0:tn], init, ALU.mult, ALU.add, [rra, rib, r_hprev], [rt1])
                            if g < 3:
                                cp("dve", hprev[:, 0:1], t1[:, tn - 1:tn], [rt1], [r_hprev])
                            else:
                                cp("dve", stg[:, c, 0:1], t1[:, tn - 1:tn], [rt1], [r_stg])
                        else:
                            ib3 = ib[:, 0:64].rearrange("p (b k) -> p b k", k=4)
                            ra3 = ra[:, 0:64].rearrange("p (b k) -> p b k", k=4)
                            tt("dve", hprev[:, 1:17], hprev[:, 1:17], ra3[:, :, 0], ALU.mult, [r_hprev, rra], [r_hprev])
                            tt("dve", ib3[:, :, 0], ib3[:, :, 0], hprev[:, 1:17], ALU.add, [rib, r_hprev], [rib])
                            memset("dve", ra3[:, :, 0], 0.0, [rra])
                            scan(t1[:, 0:64], ra[:, 0:64], ib[:, 0:64], 0.0, ALU.mult, ALU.add, [rra, rib], [rt1])
                            cp("dve", stg[:, c, 1:17], t1[:, 0:64].rearrange("p (b k) -> p b k", k=4)[:, :, 3], [rt1], [r_stg])
                        tt("dve", ybr[:, c, t0:t0 + tn], t1[:, 0:tn], sz[:, 0:tn], ALU.mult, [rt1, rsz], [r_ybr[c][g]])
            for half in range(2):
                b = next_aux()
                for jj in range(4):
                    c = half * 4 + jj
                    tr(bank(b)[0:68, jj * 128:(jj + 1) * 128], stg[:, c, :], ident_f[:], [r_stg, r_idf], [r_ps[b]], jj == 3)
                cp("dve", stg_t[:, half * 512:(half + 1) * 512], bank(b)[0:68, :], [r_ps[b]], [r_stg_t])
            P.dma("sp", o_convh[l], stg_t[:], reads=[r_stg_t], sres=r_stg_t, final=True)
            proj_merge("c", False)

            A.reset()
            lnbc, r_lnbc = A.f32([128, 2, D], "lnbcO")
            xr = [A.f32([128, D], "xr%d" % i) for i in range(4)]
            zt = [A.f32([128, D], "zt%d" % i) for i in range(6)]
            bstO = [A.f32([128, 12], "bstO%d" % i) for i in range(4)]
            mvO = [A.f32([128, 2], "mvO%d" % i) for i in range(4)]
            rstdO = [A.f32([128, 1], "rstdO%d" % i) for i in range(4)]
            P.dma("sp", lnbc[:].rearrange("p a b -> p (a b)"), pbc(fln_d[l].rearrange("a b -> (a b)"), 128), writes=[r_lnbc], sres=r_lnbc)
            wo = wneed(["wo0", "wo1", "wo2", "wo3"])
            for ti, (t0, tn) in enumerate(TTS):
                if ti < 4:
                    xa, rx = xr[ti % 4]
                    rsrc = [] if l == 0 else [r_xres[ti]]
                    P.dma("sp", xa[0:tn, :], xsrc[t0:t0 + tn, :], reads=rsrc, writes=[rx], sres=rx)
            pairs = [list(range(p, min(p + 2, 17))) for p in range(0, 17, 2)]

            def O_mm(pr_tiles):
                for ti in pr_tiles:
                    t0, tn = TTS[ti]
                    g = tok_group_of(t0)
                    xa, rx = xr[ti % 4]
                    za, rz = zt[ti % 6]
                    pr = (ti % 2) * 2
                    for j in range(4):
                        bq = pr + j // 2
                        o_ap = bank(bq)[0:tn, (j % 2) * 256:(j % 2) * 256 + 256]
                        for fc in range(KC):
                            mm(o_ap, mrg[:, fc, t0:t0 + tn], wo[j][0][:, fc, :], fc == 0, fc == KC - 1, [r_mrg[g], wo[j][1]], [r_ps[bq]])
                    stt(za[0:tn, :], xa[0:tn, :], ALPHA, ps[pr // 2][0:tn, :], ALU.mult, ALU.add, [rx, r_ps[pr], r_ps[pr + 1]], [rz])
                    if ti + 4 < 17:
                        t0n, tnn = TTS[ti + 4]
                        rsrc = [] if l == 0 else [r_xres[ti + 4]]
                        P.dma("sp", xa[0:tnn, :], xsrc[t0n:t0n + tnn, :], reads=rsrc, writes=[rx], sres=rx)

            def O_ln(pr_tiles):
                for ti in pr_tiles:
                    t0, tn = TTS[ti]
                    za, rz = zt[ti % 6]
                    bst, r_bst = bstO[ti % 4]
                    for hf in range(2):
                        bnstats(bst[0:tn, hf * 6:hf * 6 + 6], za[0:tn, hf * 512:(hf + 1) * 512], [rz], [r_bst])
                for ti in pr_tiles:
                    t0, tn = TTS[ti]
                    bst, r_bst = bstO[ti % 4]
                    mv, r_mv = mvO[ti % 4]
                    bnaggr(mv[0:tn, :], bst[0:tn, :], [r_bst], [r_mv])
                for ti in pr_tiles:
                    t0, tn = TTS[ti]
                    mv, r_mv = mvO[ti % 4]
                    rstd, r_rstd = rstdO[ti % 4]
                    act(rstd[0:tn, :], mv[0:tn, 1:2], AF.Sqrt, [r_mv, r_eps], [r_rstd], bias=eps_t[0:tn, :])
                for ti in pr_tiles:
                    t0, tn = TTS[ti]
                    rstd, r_rstd = rstdO[ti % 4]
                    recip(rstd[0:tn, :], rstd[0:tn, :], [r_rstd], [r_rstd])
                for ti in pr_tiles:
                    t0, tn = TTS[ti]
                    za, rz = zt[ti % 6]
                    mv, r_mv = mvO[ti % 4]
                    rstd, r_rstd = rstdO[ti % 4]
                    ts("dve", za[0:tn, :], za[0:tn, :], mv[0:tn, 0:1], ALU.subtract, [rz, r_mv, r_rstd], [rz], s2=rstd[0:tn, 0:1], op1=ALU.mult)
                for ti in pr_tiles:
                    t0, tn = TTS[ti]
                    za, rz = zt[ti % 6]
                    tt("dve", za[0:tn, :], za[0:tn, :], lnbc[0:tn, 0, :], ALU.mult, [rz, r_lnbc], [rz])
                for ti in pr_tiles:
                    t0, tn = TTS[ti]
                    za, rz = zt[ti % 6]
                    tt("dve", za[0:tn, :], za[0:tn, :], lnbc[0:tn, 1, :], ALU.add, [rz, r_lnbc], [rz])

            def O_out(pr_tiles):
                for ti in pr_tiles:
                    t0, tn = TTS[ti]
                    za, rz = zt[ti % 6]
                    if last or l == depth - 1:
                        P.dma("sp", y_all[t0:t0 + tn, :], za[0:tn, :], reads=[rz], sres=rz, final=True)
                    else:
                        P.dma("sp", xres[t0:t0 + tn, :], za[0:tn, :], reads=[rz], writes=[r_xres[ti]], sres=rz)
                        tile_to_xT(za, rz, ti)

            O_mm(pairs[0])
            for pi in range(len(pairs)):
                O_ln(pairs[pi])
                if pi + 1 < len(pairs):
                    O_mm(pairs[pi + 1])
                O_out(pairs[pi])
        P.finish()
        global _LAST_PROG
        _LAST_PROG = P
        with nc.allow_non_contiguous_dma(reason="small strided state columns"):
            P.emit()
    return nc


def _consts():
    ident = np.eye(128, dtype=np.float32)
    s = np.arange(128)
    mask = (s[:, None] <= s[None, :]).astype(np.float32)
    t = np.arange(64)
    mask_s = ((t[:, None] // 4 == t[None, :] // 4) & (t[:, None] <= t[None, :])).astype(np.float32)
    a0s = np.ones((4, 64), np.float32)
    a0s[:, ::4] = 0.0
    hmask = np.zeros((4, 4, 32), np.float32)
    for h in range(4):
        hmask[h, h, :] = 1.0
    colmask = np.zeros((128, NB, 64), np.float32)
    rowmask = np.zeros((64, NB), np.float32)
    for b in range(NB):
        colmask[:, b, 4 * b:4 * b + 4] = 1.0
        rowmask[4 * b:4 * b + 4, b] = 1.0
    onesrow = np.zeros((128, 128), np.float32)
    onesrow[0, :] = 1.0
    return dict(c_ident=ident, c_mask=mask, c_mask_s=mask_s, c_a0s=a0s, c_hmask=hmask.reshape(4, 128),
                c_colmask=colmask.reshape(128, NB * 64), c_rowmask=rowmask, c_onesrow=onesrow)


_NC_CACHE = {}


def kernel(x_prompt, x_sample, state_mlstm_c, state_mlstm_n, state_mlstm_m, state_lru_conv, state_lru_h,
           w_in, b_in, mlstm_norm_g, gmlp_ln_g, gmlp_ln_b, gmlp_ws, gmlp_bs, lru_conv_w, lru_conv_b,
           lru_wa, lru_ba, lru_wx, lru_bx, lru_lambda, w_proj_a, w_proj_b, w_proj_c, w_out, ln_g, ln_b, _depth=NL):
    f = lambda a: np.ascontiguousarray(np.asarray(a, dtype=np.float32))
    x_prompt, x_sample = f(x_prompt), f(x_sample)
    w_in, b_in = f(w_in), f(b_in)
    if _depth not in _NC_CACHE:
        _NC_CACHE[_depth] = build(_depth)
    nc = _NC_CACHE[_depth]
    bblocks = np.stack([b_in[:, OFF[n]:OFF[n] + D] for n in BLK], axis=1)
    bpm = bblocks.reshape(NL, 13, 8, 128).transpose(3, 0, 1, 2).reshape(128, NL * 13 * 8)
    bgate = np.stack([b_in[:, 5120:5124], b_in[:, 5124:5128]], axis=2).transpose(1, 0, 2).reshape(4, NL * 2)
    pm = lambda a: f(a).reshape(NL, 8, 128).transpose(2, 0, 1)
    gpm = pm(mlstm_norm_g).reshape(128, NL * 8)
    cw = f(lru_conv_w).reshape(NL, 4, 8, 128).transpose(3, 0, 2, 1).reshape(128, NL * 8 * 4)
    cvec = np.stack([pm(lru_conv_b), pm(lru_ba), pm(lru_bx), pm(lru_lambda)], axis=1).reshape(128, 4 * NL * 8)
    gln = np.stack([f(gmlp_ln_g), f(gmlp_ln_b)], axis=1)
    fln = np.stack([f(ln_g), f(ln_b)], axis=1)
    gws = f(gmlp_ws)
    gws_s = np.stack([np.stack([np.tile(gws[l, g, :4, :4].T, (16, 16)) for g in range(4)], axis=1) for l in range(NL)], 0)
    gws_s = gws_s.reshape(NL, 64, 4 * 64)
    gbs = f(gmlp_bs).reshape(NL, 4 * 128)
    gbs_s = np.stack([np.concatenate([np.tile(f(gmlp_bs)[l, g, :4], 16) for g in range(4)]) for l in range(NL)], 0)
    shared = dict(w_in=w_in, w_pa=f(w_proj_a), w_pb=f(w_proj_b), w_pc=f(w_proj_c), w_out=f(w_out), b_in=b_in,
                  bpm=np.ascontiguousarray(bpm), bgate=np.ascontiguousarray(bgate), gpm=np.ascontiguousarray(gpm),
                  cw=np.ascontiguousarray(cw), cvec=np.ascontiguousarray(cvec), lru_wa=f(lru_wa), lru_wx=f(lru_wx),
                  gln=np.ascontiguousarray(gln), fln=np.ascontiguousarray(fln), gws=gws,
                  gws_s=np.ascontiguousarray(gws_s), gbs=np.ascontiguousarray(gbs), gbs_s=np.ascontiguousarray(gbs_s))
    shared.update(_consts())
    sc, sn, sm = f(state_mlstm_c), f(state_mlstm_n), f(state_mlstm_m)
    sconv, sh = f(state_lru_conv), f(state_lru_h)
    in_maps = []
    for c in range(8):
        b0 = c * NB
        m = dict(shared)
        m["x_all"] = np.ascontiguousarray(np.concatenate([x_prompt[c], x_sample[b0:b0 + NB].reshape(NS, D)], axis=0))
        m["st_c"] = np.ascontiguousarray(sc[:, b0:b0 + NB])
        m["st_n"] = np.ascontiguousarray(sn[:, b0:b0 + NB])
        m["st_m"] = np.ascontiguousarray(sm[:, b0:b0 + NB].transpose(2, 0, 1).reshape(4, NL * NB))
        m["st_conv"] = np.ascontiguousarray(sconv[:, b0:b0 + NB].reshape(NL, 48, D))
        m["st_h"] = np.ascontiguousarray(sh[:, b0:b0 + NB])
        in_maps.append(m)
    res = run_bass_kernel_spmd(nc, in_maps, core_ids=list(range(8)))
    R = res.results
    g = lambda k, c: np.asarray(R[c][k], dtype=np.float32)
    y_p = np.stack([g("y_all", c)[:NTP] for c in range(8)], 0)
    y_s = np.concatenate([g("y_all", c)[NTP:].reshape(NB, 4, D) for c in range(8)], 0)
    c_p = np.stack([g("o_cp", c) for c in range(8)], 1)
    n_p = np.stack([g("o_np", c) for c in range(8)], 1)
    m_p = np.stack([g("o_mp", c).T for c in range(8)], 1)
    ch = [g("o_convh", c) for c in range(8)]
    conv_p = np.stack([x[:, 17:20] for x in ch], 1)
    h_p = np.stack([x[:, 0] for x in ch], 1)
    c_s = np.concatenate([g("o_cs", c) for c in range(8)], 1)
    n_s = np.concatenate([g("o_ns", c) for c in range(8)], 1)
    m_s = np.concatenate([g("o_ms", c).reshape(4, NL, NB).transpose(1, 2, 0) for c in range(8)], 1)
    conv_s = np.concatenate([x[:, 20:68].reshape(NL, NB, 3, D) for x in ch], 1)
    h_s = np.concatenate([x[:, 1:17] for x in ch], 1)
    v_s = np.concatenate([g("o_vs", c).reshape(NL, NB, 4, D) for c in range(8)], 1)
    outs = (y_p, y_s, c_p, n_p, m_p, conv_p, h_p, c_s, n_s, m_s, conv_s, h_s, v_s)
    return tuple(np.ascontiguousarray(o, dtype=np.float32) for o in outs)
```

```python
import numpy as np
from contextlib import ExitStack
import concourse.bass as bass
import concourse.mybir as mybir
from concourse.bass_utils import run_bass_kernel_spmd

F32 = mybir.dt.float32
BF16 = mybir.dt.bfloat16
AF = mybir.ActivationFunctionType
ALU = mybir.AluOpType
AX = mybir.AxisListType
ENG = ("pe", "act", "dve", "pool", "sp")

NL = 4
D = 1024
KC = 8
NTP = 2048
NS = 64
NT = NTP + NS
NB = 16
NH = 4
DK = 256
IN_W = 13320
TGS = [(0, 512), (512, 512), (1024, 512), (1536, 512), (2048, 64)]
TTS = [(i * 128, 128) for i in range(16)] + [(2048, 64)]
OFF = dict(q=0, k=1024, v=2048, o=3072, za=4096, gate=5120, ub=5128, vb=6152, zb=7176, xc=8200, zc=9224,
           ga=10248, gb=11272, gc=12296)
BLK = ["q", "k", "v", "o", "za", "ub", "vb", "zb", "xc", "zc", "ga", "gb", "gc"]
ALPHA = float((2 * NL) ** 0.25)
LN_EPS = 1e-5
NSLOT = 10
AHEAD = 5
ARENA_F32 = 13312


class Tok:
    __slots__ = ("eng", "sem", "val")

    def __init__(self, eng, sem, val):
        self.eng, self.sem, self.val = eng, sem, val


class Res:
    __slots__ = ("name", "lw", "rd", "const", "dsem")

    def __init__(self, name, const=False):
        self.name, self.lw, self.rd, self.const, self.dsem = name, None, {}, const, None


class Prog:
    def __init__(self, nc, stack):
        self.nc, self.stack = nc, stack
        self.ops = {e: [] for e in ENG}
        self.sem = {e: stack.enter_context(nc.semaphore("sem_" + e)) for e in ENG}
        self.cnt = {e: 0 for e in ENG}
        self.cur = {e: Tok(e, self.sem[e], None) for e in ENG}
        self.waited = {}
        self.nds = 0
        self.free_ds = {}
        self.store_toks = {}
        self.marks = []

    def mark(self, name):
        self.marks.append((name, sum(1 for o in self.ops["pe"] if o[1] is not None)))

    def _dsem(self, res, q):
        kind = "sw" if q == "pool" else "hw"
        if res.dsem is None:
            fl = self.free_ds.setdefault(kind, [])
            if fl:
                res.dsem = fl.pop()
            else:
                self.nds += 1
                res.dsem = [self.stack.enter_context(self.nc.semaphore("ds_%d" % self.nds)), 0, kind]
        assert res.dsem[2] == kind, "resource %s mixes software and hardware DMA queues" % res.name
        return res.dsem

    def _wait(self, eng, t, waits):
        key = (eng, id(t.sem))
        if self.waited.get(key, 0) >= t.val:
            return
        self.waited[key] = t.val
        waits.append((t.sem, t.val))

    def _deps(self, eng, reads, writes, inorder):
        deps = []
        for r in reads:
            if r.lw is not None:
                deps.append((r.lw, "raw"))
        for w in writes:
            if w.lw is not None:
                deps.append((w.lw, "waw"))
            for t in w.rd.values():
                deps.append((t, "war"))
        waits = []
        for t, kind in deps:
            if t.eng == eng and inorder and eng == "pe":
                continue
            if t.val is None:
                raise RuntimeError("dependency on unsignaled op (%s)" % t.eng)
            self._wait(eng, t, waits)
        return waits

    def _commit(self, tok, reads, writes):
        for r in reads:
            if not r.const:
                r.rd[id(tok.sem)] = tok
        for w in writes:
            w.lw = tok
            w.rd = {}

    def op(self, eng, fn, reads=(), writes=(), sig=None):
        if sig is None:
            sig = eng != "pe"
        waits = self._deps(eng, reads, writes, True)
        tok = self.cur[eng]
        self.ops[eng].append((waits, fn, (self.sem[eng], 1) if sig else None))
        self._commit(tok, reads, writes)
        if sig:
            self.cnt[eng] += 1
            tok.val = self.cnt[eng]
            self.cur[eng] = Tok(eng, self.sem[eng], None)

    def dma(self, q, out, in_, reads=(), writes=(), sres=None, final=False):
        waits = self._deps(q, reads, writes, False)
        ds = self._dsem(sres, q)
        ds[1] += 1
        tok = Tok(None, ds[0], 16 * ds[1])
        self.ops[q].append((waits, lambda e: e.dma_start(out=out, in_=in_), (ds[0], 16)))
        self._commit(tok, reads, writes)
        if final:
            self.store_toks[id(tok.sem)] = tok

    def barrier(self, dma_res=()):
        toks = [Tok(e, self.sem[e], self.cnt[e]) for e in ENG if self.cnt[e] > 0]
        for r in dma_res:
            if r.dsem is not None and r.dsem[1] > 0:
                toks.append(Tok(None, r.dsem[0], 16 * r.dsem[1]))
        for e in ENG:
            waits = []
            for t in toks:
                if t.eng == e and e == "pe":
                    continue
                self._wait(e, t, waits)
            if waits:
                self.ops[e].append((waits, None, None))

    def finish(self):
        waits = [(t.sem, t.val) for t in self.store_toks.values()]
        self.ops["sp"].append((waits, None, None))

    def emit(self):
        def mk(name):
            def body(e):
                for waits, fn, inc in self.ops[name]:
                    for sem, val in waits:
                        e.wait_ge(sem, val)
                    if fn is None:
                        continue
                    ins = fn(e)
                    if inc is not None:
                        ins.then_inc(inc[0], inc[1])
            return body

        with self.nc.Block() as block:
            block.tensor(mk("pe"))
            block.scalar(mk("act"))
            block.vector(mk("dve"))
            block.gpsimd(mk("pool"))
            block.sync(mk("sp"))


def bc_last(ap, n):
    pat = [list(x) for x in ap.ap]
    return bass.AP(ap.tensor, ap.offset, pat + [[0, n]])


def bc_mid(ap, n):
    pat = [list(x) for x in ap.ap]
    return bass.AP(ap.tensor, ap.offset, [pat[0], [0, n]] + pat[1:])


def pbc(ap_row, n):
    pat = [list(x) for x in ap_row.ap]
    return bass.AP(ap_row.tensor, ap_row.offset, [[0, n]] + pat[-1:])


def build(depth=NL):
    nc = bass.Bass("TRN2", target_bir_lowering=False)

    def din(name, shape):
        return nc.dram_tensor(name, list(shape), F32, kind="ExternalInput").ap()

    def dout(name, shape):
        return nc.dram_tensor(name, list(shape), F32, kind="ExternalOutput").ap()

    x_all = din("x_all", [NT, D])
    w_in = din("w_in", [NL, D, IN_W])
    w_p = {"a": din("w_pa", [NL, D, D]), "b": din("w_pb", [NL, D, D]), "c": din("w_pc", [NL, D, D])}
    w_out = din("w_out", [NL, D, D])
    b_in = din("b_in", [NL, IN_W])
    bpm_d = din("bpm", [128, NL * 13 * 8])
    bgate_d = din("bgate", [4, NL * 2])
    gpm_d = din("gpm", [128, NL * 8])
    cw_d = din("cw", [128, NL * 8 * 4])
    cvec_d = din("cvec", [128, 4 * NL * 8])
    wa_d = din("lru_wa", [NL, 8, 128, 128])
    wx_d = din("lru_wx", [NL, 8, 128, 128])
    gln_d = din("gln", [NL, 2, D])
    fln_d = din("fln", [NL, 2, D])
    gws_d = din("gws", [NL, 4, 128, 128])
    gws_s_d = din("gws_s", [NL, 64, 4 * 64])
    gbs_d = din("gbs", [NL, 4 * 128])
    gbs_s_d = din("gbs_s", [NL, 4 * 64])
    st_c = din("st_c", [NL, NB, NH, DK, DK])
    st_n = din("st_n", [NL, NB, NH, DK])
    st_m = din("st_m", [4, NL * NB])
    st_conv = din("st_conv", [NL, 48, D])
    st_h = din("st_h", [NL, NB, D])
    c_ident = din("c_ident", [128, 128])
    c_mask = din("c_mask", [128, 128])
    c_mask_s = din("c_mask_s", [64, 64])
    c_a0s = din("c_a0s", [4, 64])
    c_hmask = din("c_hmask", [4, 128])
    c_colmask = din("c_colmask", [128, NB * 64])
    c_rowmask = din("c_rowmask", [64, NB])
    c_onesrow = din("c_onesrow", [128, 128])

    y_all = dout("y_all", [NT, D])
    o_cp = dout("o_cp", [NL, NH, DK, DK])
    o_np = dout("o_np", [NL, NH, DK])
    o_mp = dout("o_mp", [4, NL])
    o_convh = dout("o_convh", [NL, 68, D])
    o_cs = dout("o_cs", [NL, NB, NH, DK, DK])
    o_ns = dout("o_ns", [NL, NB, NH, DK])
    o_ms = dout("o_ms", [4, NL * NB])
    o_vs = dout("o_vs", [NL, NS, D])
    xres = nc.dram_tensor("xres", [NT, D], F32, kind="Internal").ap()
    r_xres = [Res("xres%d" % i) for i in range(17)]

    with ExitStack() as st:
        P = Prog(nc, st)

        def sb(name, shape, dt=F32):
            return st.enter_context(nc.sbuf_tensor("s_" + name, list(shape), dt))

        def mm(out, lhsT, rhs, start, stop, reads, writes, sig=None):
            P.op("pe", lambda e: e.matmul(out, lhsT=lhsT, rhs=rhs, start=start, stop=stop), reads, writes,
                 sig=(stop if sig is None else sig))

        def tr(out, in_, ident, reads, writes, sig):
            P.op("pe", lambda e: e.transpose(out, in_, ident), reads, writes, sig=sig)

        def act(out, in_, func, reads, writes, bias=None, scale=None):
            kw = {}
            if bias is not None:
                kw["bias"] = bias
            if scale is not None:
                kw["scale"] = scale
            P.op("act", lambda e: e.activation(out=out, in_=in_, func=func, **kw), reads, writes)

        def tt(eng, out, in0, in1, op, reads, writes):
            P.op(eng, lambda e: e.tensor_tensor(out=out, in0=in0, in1=in1, op=op), reads, writes)

        def ts(eng, out, in0, s1, op0, reads, writes, s2=None, op1=None):
            if op1 is None:
                P.op(eng, lambda e: e.tensor_scalar(out=out, in0=in0, scalar1=s1, scalar2=None, op0=op0), reads, writes)
            else:
                P.op(eng, lambda e: e.tensor_scalar(out=out, in0=in0, scalar1=s1, scalar2=s2, op0=op0, op1=op1), reads, writes)

        def stt(out, in0, scalar, in1, op0, op1, reads, writes):
            P.op("dve", lambda e: e.scalar_tensor_tensor(out=out, in0=in0, scalar=scalar, in1=in1, op0=op0, op1=op1), reads, writes)

        def cp(eng, out, in_, reads, writes):
            if eng == "act":
                act(out, in_, AF.Identity, reads, writes)
            else:
                P.op(eng, lambda e: e.tensor_copy(out=out, in_=in_), reads, writes)

        def memset(eng, ap, val, writes):
            P.op(eng, lambda e: e.memset(ap, val), (), writes)

        def bnstats(out, in_, reads, writes):
            P.op("dve", lambda e: e.bn_stats(out=out, in_=in_), reads, writes)

        def bnaggr(out, in_, reads, writes):
            P.op("dve", lambda e: e.bn_aggr(out=out, in_=in_), reads, writes)

        def recip(out, in_, reads, writes):
            P.op("dve", lambda e: e.reciprocal(out=out, in_=in_), reads, writes)

        def scan(out, d0, d1, init, op0, op1, reads, writes):
            P.op("dve", lambda e: e.tensor_tensor_scan(out=out, data0=d0, data1=d1, initial=init, op0=op0, op1=op1), reads, writes)

        def treduce(out, in_, op, reads, writes):
            P.op("dve", lambda e: e.tensor_reduce(out=out, in_=in_, axis=AX.X, op=op), reads, writes)

        xT = sb("xT", [128, KC, NT], BF16)
        r_xT = [Res("xT%d" % i) for i in range(5)]
        ybr = sb("ybr", [128, KC, NT], BF16)
        r_ybr = [[Res("ybr%d_%d" % (c, g)) for g in range(5)] for c in range(KC)]
        mrg = sb("mrg", [128, KC, NT], BF16)
        r_mrg = [Res("mrg%d" % g) for g in range(5)]
        vn_v = mrg[:].rearrange("p k t -> p (k t)")
        wsl = [sb("wsl%d" % i, [128, KC, 256], BF16) for i in range(NSLOT)]
        r_wsl = [Res("wsl%d" % i) for i in range(NSLOT)]
        arena = sb("arena", [128, ARENA_F32], F32)
        ident_f = sb("ident_f", [128, 128], F32); r_idf = Res("idf", True)
        ident_b = sb("ident_b", [128, 128], BF16); r_idb = Res("idb", True)
        mask_f = sb("mask_f", [128, 128], F32); r_mask = Res("mask", True)
        mask_s = sb("mask_s", [64, 64], F32); r_mask_s = Res("mask_s", True)
        a0s = sb("a0s", [4, 64], F32); r_a0s = Res("a0s", True)
        ones4 = sb("ones4", [4, 512], F32); r_ones4 = Res("ones4", True)
        ones4x = sb("ones4x", [4, 128], F32); r_ones4x = Res("ones4x", True)
        hmask = sb("hmask", [4, 128], F32); r_hmask = Res("hmask", True)
        colmask = sb("colmask", [128, NB, 64], BF16); r_colmask = Res("colmask", True)
        rowmask = sb("rowmask", [64, NB], F32); r_rowmask = Res("rowmask", True)
        onesrow = sb("onesrow", [128, 128], BF16); r_onesrow = Res("onesrow", True)
        bpm = sb("bpm", [128, NL, 13, 8], F32); r_bpm = Res("bpm", True)
        kbias = sb("kbias", [128, NL, 8], F32); r_kbias = Res("kbias", True)
        bgate = sb("bgate", [4, NL, 2], F32); r_bgate = Res("bgate", True)
        nbf = sb("nbf", [4, NL], F32); r_nbf = Res("nbf", True)
        gpm = sb("gpm", [128, NL, 8], F32); r_gpm = Res("gpm", True)
        gpmh = sb("gpmh", [128, NL, 8], F32); r_gpmh = Res("gpmh", True)
        cw = sb("cw", [128, NL, 8, 4], F32); r_cw = Res("cw", True)
        cvec = sb("cvec", [128, 4, NL, 8], F32); r_cvec = Res("cvec", True)
        cl = sb("cl", [128, 2, NL, 8], F32); r_cl = Res("cl", True)
        wg = sb("wg", [128, NL, KC, 8], BF16); r_wg = Res("wg", True)
        m0s = sb("m0s", [4, NL, NB], F32); r_m0s = Res("m0s", True)
        wthr = sb("wthr", [128, 17, 8], F32); r_wthr = Res("wthr")
        decbc = sb("decbc", [128, 128], F32); r_decbc = Res("decbc")
        mcall = sb("mcall", [4, 33], F32); r_mcall = Res("mcall")
        mcs = sb("mcs", [4, 3, NB], F32); r_mcs = Res("mcs")
        dec4 = sb("dec4", [4, 32], F32); r_dec4 = Res("dec4")
        decbd = sb("decbd", [4, 4, 32], F32); r_decbd = Res("decbd")
        mpo = sb("mpo", [4, 1], F32); r_mpo = Res("mpo")
        bbl = sb("bbl", [4, 2], F32); r_bbl = Res("bbl")
        eps_t = sb("eps_t", [128, 1], F32); r_eps = Res("eps", True)

        ps = [st.enter_context(nc.psum_tensor("ps%d" % i, [128, 1024], F32)) for i in range(4)]
        psb = [p.bitcast(BF16) for p in ps]
        r_ps = [Res("psb%d" % i) for i in range(8)]

        def bank(b):
            return ps[b // 2][:, (b % 2) * 512:(b % 2) * 512 + 512]

        def bank_bf(b):
            return psb[b // 2][:, (b % 2) * 1024:(b % 2) * 1024 + 1024]

        rot = {"mm": 0, "aux": 0}

        def next_mm():
            rot["mm"] = (rot["mm"] + 1) % 4
            return rot["mm"]

        def next_aux():
            rot["aux"] = (rot["aux"] + 1) % 2
            return 4 + rot["aux"]

        class Arena:
            def __init__(self):
                self.off = 0
                self.res = []

            def reset(self):
                P.barrier(self.res)
                for r in self.res:
                    if r.dsem is not None:
                        P.free_ds.setdefault(r.dsem[2], []).append(r.dsem)
                        r.dsem = None
                self.off = 0
                self.res = []

            def f32(self, shape, name):
                n = int(np.prod(shape[1:]))
                ap = arena[0:shape[0], self.off:self.off + n]
                self.off += n
                assert self.off <= ARENA_F32, "arena overflow %d" % self.off
                r = Res(name)
                self.res.append(r)
                if len(shape) == 3:
                    ap = ap.rearrange("p (a b) -> p a b", a=shape[1])
                return ap, r

            def bf(self, shape, name):
                n = int(np.prod(shape[1:]))
                n32 = (n + 1) // 2
                ap = arena[0:shape[0], self.off:self.off + n32].bitcast(BF16)
                self.off += n32
                assert self.off <= ARENA_F32, "arena overflow %d" % self.off
                r = Res(name)
                self.res.append(r)
                ap = ap[:, 0:n]
                if len(shape) == 3:
                    ap = ap.rearrange("p (a b) -> p a b", a=shape[1])
                return ap, r

        A = Arena()

        wlist = []

        def wsrc(ap2d):
            return ap2d.rearrange("(kc p) n -> p kc n", p=128)

        for l in range(depth):
            for j in range(4):
                wlist.append(("vb%d" % j, wsrc(w_in[l, :, OFF["vb"] + j * 256:OFF["vb"] + (j + 1) * 256])))
            for j in range(4):
                wlist.append(("ub%d" % j, wsrc(w_in[l, :, OFF["ub"] + j * 256:OFF["ub"] + (j + 1) * 256])))
                wlist.append(("zb%d" % j, wsrc(w_in[l, :, OFF["zb"] + j * 256:OFF["zb"] + (j + 1) * 256])))
            for j in range(4):
                wlist.append(("pb%d" % j, wsrc(w_p["b"][l, :, j * 256:(j + 1) * 256])))
                wlist.append(("gb%d" % j, wsrc(w_in[l, :, OFF["gb"] + j * 256:OFF["gb"] + (j + 1) * 256])))
            for h in range(4):
                for nm in ("q", "k", "v", "o", "za"):
                    wlist.append(("%s%d" % (nm, h), wsrc(w_in[l, :, OFF[nm] + h * 256:OFF[nm] + (h + 1) * 256])))
            for j in range(4):
                wlist.append(("pa%d" % j, wsrc(w_p["a"][l, :, j * 256:(j + 1) * 256])))
                wlist.append(("ga%d" % j, wsrc(w_in[l, :, OFF["ga"] + j * 256:OFF["ga"] + (j + 1) * 256])))
            for j in range(4):
                wlist.append(("xc%d" % j, wsrc(w_in[l, :, OFF["xc"] + j * 256:OFF["xc"] + (j + 1) * 256])))
                wlist.append(("zc%d" % j, wsrc(w_in[l, :, OFF["zc"] + j * 256:OFF["zc"] + (j + 1) * 256])))
            for j in range(4):
                wlist.append(("pc%d" % j, wsrc(w_p["c"][l, :, j * 256:(j + 1) * 256])))
                wlist.append(("gc%d" % j, wsrc(w_in[l, :, OFF["gc"] + j * 256:OFF["gc"] + (j + 1) * 256])))
            for j in range(4):
                wlist.append(("wo%d" % j, wsrc(w_out[l, :, j * 256:(j + 1) * 256])))
        wst = {"issued": 0, "next": 0}

        def wneed(names):
            i0 = wst["next"]
            outl = []
            for k, nm in enumerate(names):
                assert wlist[i0 + k][0] == nm, (wlist[i0 + k][0], nm)
            last = min(i0 + len(names) - 1 + AHEAD, len(wlist) - 1)
            while wst["issued"] <= last:
                j = wst["issued"]
                s = j % NSLOT
                P.dma("pool", wsl[s][:], wlist[j][1], writes=[r_wsl[s]], sres=r_wsl[s])
                wst["issued"] += 1
            for k in range(len(names)):
                s = (i0 + k) % NSLOT
                outl.append((wsl[s], r_wsl[s]))
            wst["next"] = i0 + len(names)
            return outl

        P.dma("sp", ident_f[:], c_ident, writes=[r_idf], sres=r_idf)
        P.dma("pool", ident_b[:], c_ident, writes=[r_idb], sres=r_idb)
        P.dma("sp", mask_f[:], c_mask, writes=[r_mask], sres=r_mask)
        P.dma("sp", mask_s[:], c_mask_s, writes=[r_mask_s], sres=r_mask_s)
        P.dma("sp", a0s[:], c_a0s, writes=[r_a0s], sres=r_a0s)
        P.dma("sp", hmask[:], c_hmask, writes=[r_hmask], sres=r_hmask)
        P.dma("pool", colmask[:].rearrange("p a b -> p (a b)"), c_colmask, writes=[r_colmask], sres=r_colmask)
        P.dma("sp", rowmask[:], c_rowmask, writes=[r_rowmask], sres=r_rowmask)
        P.dma("pool", onesrow[:], c_onesrow, writes=[r_onesrow], sres=r_onesrow)
        P.dma("sp", bpm[:].rearrange("p a b c -> p (a b c)"), bpm_d, writes=[r_bpm], sres=r_bpm)
        P.dma("sp", bgate[:].rearrange("p a b -> p (a b)"), bgate_d, writes=[r_bgate], sres=r_bgate)
        P.dma("sp", gpm[:].rearrange("p a b -> p (a b)"), gpm_d, writes=[r_gpm], sres=r_gpm)
        P.dma("sp", cw[:].rearrange("p a b c -> p (a b c)"), cw_d, writes=[r_cw], sres=r_cw)
        P.dma("sp", cvec[:].rearrange("p a b c -> p (a b c)"), cvec_d, writes=[r_cvec], sres=r_cvec)
        P.dma("sp", m0s[:].rearrange("p a b -> p (a b)"), st_m, writes=[r_m0s], sres=r_m0s)
        for l in range(depth):
            P.dma("pool", wg[:, l, :, :], w_in[l, :, OFF["gate"]:OFF["gate"] + 8].rearrange("(kc p) n -> p kc n", p=128),
                  writes=[r_wg], sres=r_wg)
        memset("dve", ones4[:], 1.0, [r_ones4])
        memset("dve", ones4x[:], 1.0, [r_ones4x])
        memset("dve", eps_t[:], LN_EPS, [r_eps])
        memset("dve", mcall[:], 0.0, [r_mcall])
        ts("dve", kbias[:], bpm[:, :, 1, :], 1.0 / 16.0, ALU.mult, [r_bpm], [r_kbias])
        ts("dve", nbf[:], bgate[:, :, 1], -1.0, ALU.mult, [r_bgate], [r_nbf])
        ts("dve", gpmh[:], gpm[:], 0.5, ALU.mult, [r_gpm], [r_gpmh])
        act(cl[:, 0], cvec[:, 3], AF.Exp, [r_cvec], [r_cl], scale=-1.0)
        act(cl[:, 0], cl[:, 0], AF.Ln, [r_cl], [r_cl], bias=1.0)
        ts("dve", cl[:, 1], cl[:, 0], -16.0, ALU.mult, [r_cl], [r_cl])
        ts("dve", cl[:, 0], cl[:, 0], -8.0, ALU.mult, [r_cl], [r_cl])

        def tok_group_of(t0):
            return min(t0 // 512, 4)

        def tile_to_xT(src, r_src, ti, gscale=None):
            t0, tn = TTS[ti]
            g = tok_group_of(t0)
            for half in range(2):
                b = next_aux()
                for j in range(4):
                    kc = half * 4 + j
                    tr(bank(b)[:, j * 128:j * 128 + tn], src[0:tn, kc * 128:(kc + 1) * 128], ident_f[0:tn, 0:tn],
                       [r_src, r_idf], [r_ps[b]], sig=(j == 3))
                act(xT[:, half * 4:(half + 1) * 4, t0:t0 + tn],
                    bank(b).rearrange("p (a b) -> p a b", a=4)[:, :, 0:tn], AF.Identity, [r_ps[b]], [r_xT[g]])

        A.reset()
        xin = [A.f32([128, D], "xin%d" % i) for i in range(2)]
        for ti, (t0, tn) in enumerate(TTS):
            xa, rx = xin[ti % 2]
            P.dma("sp", xa[0:tn, :], x_all[t0:t0 + tn, :], writes=[rx], sres=rx)
            tile_to_xT(xa, rx, ti)

        for l in range(depth):
            last = (l == NL - 1)
            xsrc = x_all if l == 0 else xres

            A.reset()
            _gt = [A.f32([4, 512], "gt%d" % i) for i in range(3)]
            gt = [x[0] for x in _gt]
            r_gt = [x[1] for x in _gt]
            for g, (t0, tn) in enumerate(TGS):
                bi, bf_ = next_mm(), next_mm()
                for kc in range(KC):
                    mm(bank(bi)[0:4, 0:tn], wg[:, l, kc, 0:4], xT[:, kc, t0:t0 + tn], kc == 0, kc == KC - 1,
                       [r_wg, r_xT[g]], [r_ps[bi]])
                for kc in range(KC):
                    mm(bank(bf_)[0:4, 0:tn], wg[:, l, kc, 4:8], xT[:, kc, t0:t0 + tn], kc == 0, kc == KC - 1,
                       [r_wg, r_xT[g]], [r_ps[bf_]])
                it_, sp_, bb_ = gt[0][:, 0:tn], gt[1][:, 0:tn], gt[2][:, 0:tn]
                act(it_, bank(bi)[0:4, 0:tn], AF.Identity, [r_ps[bi], r_bgate], [r_gt[0]], bias=bgate[:, l, 0:1])
                act(sp_, bank(bf_)[0:4, 0:tn], AF.Exp, [r_ps[bf_], r_nbf], [r_gt[1]], bias=nbf[:, l:l + 1], scale=-1.0)
                act(sp_, sp_, AF.Ln, [r_gt[1]], [r_gt[1]], bias=1.0)
                if g < 4:
                    init = 0.0 if g == 0 else bbl[:, 0:1]
                    scan(bb_, ones4[:, 0:tn], sp_, init, ALU.mult, ALU.subtract, [r_ones4, r_gt[1], r_bbl], [r_gt[2]])
                    if g < 3:
                        cp("dve", bbl[:, 0:1], gt[2][:, tn - 1:tn], [r_gt[2]], [r_bbl])
                else:
                    scan(bb_, a0s[:], sp_, 0.0, ALU.mult, ALU.subtract, [r_a0s, r_gt[1]], [r_gt[2]])
                tt("dve", it_, it_, bb_, ALU.subtract, [r_gt[0], r_gt[2]], [r_gt[0]])
                if g < 4:
                    treduce(mcall[:, 17 + 4 * g:21 + 4 * g], gt[0][:, :].rearrange("p (c t) -> p c t", c=4), ALU.max, [r_gt[0]], [r_mcall])
                    scan(mcall[:, 1 + 4 * g:5 + 4 * g], ones4[:, 0:4], mcall[:, 17 + 4 * g:21 + 4 * g], mcall[:, 4 * g:4 * g + 1],
                         ALU.mult, ALU.max, [r_mcall, r_ones4], [r_mcall])
                    tt("dve", dec4[:, 4 * g:4 * g + 4], mcall[:, 4 * g:4 * g + 4], mcall[:, 4 * g + 1:4 * g + 5], ALU.subtract,
                       [r_mcall], [r_dec4])
                    mc_b = bc_last(mcall[:, 1 + 4 * g:5 + 4 * g], 128)
                    v3 = lambda a: a.rearrange("p (c t) -> p c t", c=4)
                    if g == 3:
                        tt("dve", mpo[:], gt[2][:, 511:512], mcall[:, 16:17], ALU.add, [r_gt[2], r_mcall], [r_mpo])
                        P.dma("sp", o_mp[:, l:l + 1], mpo[:], reads=[r_mpo], sres=r_mpo, final=True)
                else:
                    treduce(mcs[:, 0, :], gt[0][:, 0:64].rearrange("p (b t) -> p b t", t=4), ALU.max, [r_gt[0]], [r_mcs])
                    tt("dve", mcs[:, 1, :], mcs[:, 0, :], m0s[:, l, :], ALU.max, [r_mcs, r_m0s], [r_mcs])
                    tt("dve", dec4[:, 16:32], m0s[:, l, :], mcs[:, 1, :], ALU.subtract, [r_m0s, r_mcs], [r_dec4])
                    tt("dve", mcs[:, 2, :], gt[2][:, 0:64].rearrange("p (b t) -> p b t", t=4)[:, :, 3], mcs[:, 1, :], ALU.add,
                       [r_gt[2], r_mcs], [r_mcs])
                    P.dma("sp", o_ms[:, l * NB:(l + 1) * NB], mcs[:, 2, :], reads=[r_mcs], sres=r_mcs, final=True)
                    mc_b = bc_last(mcs[:, 1, :], 4)
                    v3 = lambda a: a.rearrange("p (c t) -> p c t", t=4)
                rd = [r_gt[0], r_gt[2], r_mcall, r_mcs]
                tt("dve", v3(sp_), v3(it_), mc_b, ALU.subtract, rd, [r_gt[1]])
                act(sp_, sp_, AF.Exp, [r_gt[1]], [r_gt[1]])
                tt("dve", v3(bb_), v3(bb_), mc_b, ALU.add, rd, [r_gt[2]])
                act(bb_, bb_, AF.Exp, [r_gt[2]], [r_gt[2]], scale=-1.0)
                ntile = 4 if g < 4 else 1
                b = next_aux()
                for c in range(ntile):
                    cn = 128 if g < 4 else 64
                    tr(bank(b)[0:cn, c * 8:c * 8 + 4], gt[1][:, c * 128:c * 128 + cn], ident_f[0:4, 0:4], [r_gt[1], r_idf], [r_ps[b]], False)
                    tr(bank(b)[0:cn, c * 8 + 4:c * 8 + 8], gt[2][:, c * 128:c * 128 + cn], ident_f[0:4, 0:4], [r_gt[2], r_idf], [r_ps[b]],
                       c == ntile - 1)
                cn = 128 if g < 4 else 64
                cp("dve", wthr[0:cn, 4 * g:4 * g + ntile, :], bank(b)[0:cn, 0:8 * ntile].rearrange("p (c k) -> p c k", k=8),
                   [r_ps[b]], [r_wthr])
            act(dec4[:], dec4[:], AF.Exp, [r_dec4], [r_dec4])
            tt("dve", decbd[:], bc_mid(dec4[:], 4), hmask[:].rearrange("p (a b) -> p a b", a=4), ALU.mult, [r_dec4, r_hmask], [r_decbd])
            b = next_aux()
            mm(bank(b)[:, 0:128], ones4x[:], decbd[:].rearrange("p a b -> p (a b)"), True, True, [r_ones4x, r_decbd], [r_ps[b]])
            cp("dve", decbc[:], bank(b)[:, 0:128], [r_ps[b]], [r_decbc])

            A.reset()
            lnbc, r_lnbc = A.f32([128, 2, D], "lnbc")
            vbrow, r_vbrow = A.bf([128, D], "vbrow")
            wsf, r_wsf = A.f32([128, 4, 128], "wsf")
            wmT, r_wmT = A.bf([128, 4, 128], "wmT")
            wss, r_wss = A.f32([64, 4, 64], "wss")
            mts, r_mts = A.bf([64, 4, 64], "mts")
            bsrow, r_bsrow = A.bf([128, 4 * 128], "bsrow")
            bsrow_s, r_bsrow_s = A.bf([128, 4 * 64], "bsrow_s")
            v32 = [A.f32([128, D], "v32_%d" % i) for i in range(2)]
            bst, r_bst = A.f32([128, 12], "bst")
            mv, r_mv = A.f32([128, 2], "mv")
            rstd, r_rstd = A.f32([128, 1], "rstd")
            szb = [A.f32([128, 512], "szb%d" % i) for i in range(2)]
            uzb = [A.f32([128, 512], "uzb%d" % i) for i in range(2)]
            vns, r_vns = A.bf([64, D], "vns")
            P.dma("sp", lnbc[:].rearrange("p a b -> p (a b)"), pbc(gln_d[l].rearrange("a b -> (a b)"), 128), writes=[r_lnbc], sres=r_lnbc)
            memset("pool", vbrow[:], 0.0, [r_vbrow])
            P.dma("pool", vbrow[0:1, :], b_in[l:l + 1, OFF["vb"]:OFF["vb"] + D], writes=[r_vbrow], sres=r_vbrow)
            memset("pool", bsrow[:], 0.0, [r_bsrow])
            P.dma("pool", bsrow[0:1, :], gbs_d[l:l + 1, :], writes=[r_bsrow], sres=r_bsrow)
            memset("pool", bsrow_s[:], 0.0, [r_bsrow_s])
            P.dma("pool", bsrow_s[0:1, :], gbs_s_d[l:l + 1, :], writes=[r_bsrow_s], sres=r_bsrow_s)
            P.dma("sp", wsf[:], gws_d[l].rearrange("g t s -> t g s"), writes=[r_wsf], sres=r_wsf)
            P.dma("sp", wss[:].rearrange("p a b -> p (a b)"), gws_s_d[l], writes=[r_wss], sres=r_wss)
            b = next_aux()
            for g4 in range(4):
                tr(bank(b)[:, g4 * 128:(g4 + 1) * 128], wsf[:, g4, :], ident_f[:], [r_wsf, r_idf], [r_ps[b]], g4 == 3)
            tt("dve", wmT[:], bank(b).rearrange("p (a b) -> p a b", a=4), bc_mid(mask_f[:], 4), ALU.mult, [r_ps[b], r_mask], [r_wmT])
            tt("dve", mts[:], wss[:], bc_mid(mask_s[:], 4), ALU.mult, [r_wss, r_mask_s], [r_mts])
            wv = wneed(["vb0", "vb1", "vb2", "vb3"])
            r_vn = r_mrg
            for ti, (t0, tn) in enumerate(TTS):
                g = tok_group_of(t0)
                va, rv = v32[ti % 2]
                pr = (ti % 2) * 2
                for j in range(4):
                    bq = pr + j // 2
                    o_ap = bank(bq)[0:tn, (j % 2) * 256:(j % 2) * 256 + 256]
                    for kc in range(KC):
                        mm(o_ap, xT[:, kc, t0:t0 + tn], wv[j][0][:, kc, :], kc == 0, False, [r_xT[g], wv[j][1]], [r_ps[bq]], sig=False)
                    mm(o_ap, onesrow[:, 0:tn], vbrow[:, j * 256:(j + 1) * 256], False, True, [r_onesrow, r_vbrow], [r_ps[bq]])
                for hf in range(2):
                    bnstats(bst[0:tn, hf * 6:hf * 6 + 6], bank(pr + hf)[0:tn, :], [r_ps[pr + hf]], [r_bst])
                bnaggr(mv[0:tn, :], bst[0:tn, :], [r_bst], [r_mv])
                act(rstd[0:tn, :], mv[0:tn, 1:2], AF.Sqrt, [r_mv, r_eps], [r_rstd], bias=eps_t[0:tn, :])
                recip(rstd[0:tn, :], rstd[0:tn, :], [r_rstd], [r_rstd])
                ts("dve", va[0:tn, :], ps[pr // 2][0:tn, :], mv[0:tn, 0:1], ALU.subtract, [r_ps[pr], r_ps[pr + 1], r_mv, r_rstd], [rv],
                   s2=rstd[0:tn, 0:1], op1=ALU.mult)
                tt("pool", va[0:tn, :], va[0:tn, :], lnbc[0:tn, 0, :], ALU.mult, [rv, r_lnbc], [rv])
                if ti < 16:
                    tt("pool", vn_v[0:tn, ti * 1024:(ti + 1) * 1024], va[0:tn, :], lnbc[0:tn, 1, :], ALU.add, [rv, r_lnbc], [r_vn[g]])
                else:
                    tt("pool", va[0:tn, :], va[0:tn, :], lnbc[0:tn, 1, :], ALU.add, [rv, r_lnbc], [rv])
                    P.dma("sp", o_vs[l], va[0:tn, :], reads=[rv], sres=rv, final=True)
                    cp("pool", vns[:, :], va[0:tn, :], [rv], [r_vns])
            for j in range(4):
                wu, wz = wneed(["ub%d" % j, "zb%d" % j])
                for cc in range(2):
                    c = 2 * j + cc
                    g4 = c // 2
                    for g, (t0, tn) in enumerate(TGS):
                        bu, bz, bm = next_mm(), next_mm(), next_aux()
                        for kc in range(KC):
                            mm(bank(bz)[:, 0:tn], wz[0][:, kc, cc * 128:(cc + 1) * 128], xT[:, kc, t0:t0 + tn], kc == 0, kc == KC - 1,
                               [wz[1], r_xT[g]], [r_ps[bz]])
                        for kc in range(KC):
                            mm(bank(bu)[:, 0:tn], wu[0][:, kc, cc * 128:(cc + 1) * 128], xT[:, kc, t0:t0 + tn], kc == 0, kc == KC - 1,
                               [wu[1], r_xT[g]], [r_ps[bu]])
                        if g < 4:
                            for ci in range(4):
                                ti = 4 * g + ci
                                o_ap = bank(bm)[:, ci * 128:(ci + 1) * 128]
                                mm(o_ap, vn_v[:, ti * 1024 + c * 128:ti * 1024 + (c + 1) * 128], wmT[:, g4, :], True, False,
                                   [r_vn[g], r_wmT], [r_ps[bm]], sig=False)
                                mm(o_ap, onesrow[:], bsrow[:, g4 * 128:(g4 + 1) * 128], False, True, [r_onesrow, r_bsrow], [r_ps[bm]],
                                   sig=(ci == 3))
                        else:
                            o_ap = bank(bm)[:, 0:64]
                            mm(o_ap, vns[:, c * 128:(c + 1) * 128], mts[:, g4, :], True, False,
                               [r_vns, r_mts], [r_ps[bm]], sig=False)
                            mm(o_ap, onesrow[:], bsrow_s[:, g4 * 64:(g4 + 1) * 64], False, True, [r_onesrow, r_bsrow_s], [r_ps[bm]])
                        sz, rsz = szb[g % 2]
                        uz, ruz = uzb[g % 2]
                        act(sz[:, 0:tn], bank(bz)[:, 0:tn], AF.Silu, [r_ps[bz], r_bpm], [rsz], bias=bpm[:, l, 7, c:c + 1])
                        stt(uz[:, 0:tn], bank(bu)[:, 0:tn], bpm[:, l, 5, c:c + 1], sz[:, 0:tn], ALU.add, ALU.mult,
                            [r_ps[bu], r_bpm, rsz], [ruz])
                        tt("dve", ybr[:, c, t0:t0 + tn], bank(bm)[:, 0:tn], uz[:, 0:tn], ALU.mult, [r_ps[bm], ruz], [r_ybr[c][g]])

            def proj_merge(br, first):
                A.reset()
                sgb = [A.f32([128, 512], "sg%d" % i) for i in range(2)]
                tmpb = [A.bf([128, 512], "pm%d" % i) for i in range(2)]
                gblk = {"a": 10, "b": 11, "c": 12}[br]
                for j in range(4):
                    wp, wgt = wneed(["p%s%d" % (br, j), "g%s%d" % (br, j)])
                    for cc in range(2):
                        dm = 2 * j + cc
                        for g, (t0, tn) in enumerate(TGS):
                            bp_, bg = next_mm(), next_mm()
                            for kc in range(KC):
                                mm(bank(bg)[:, 0:tn], wgt[0][:, kc, cc * 128:(cc + 1) * 128], xT[:, kc, t0:t0 + tn], kc == 0, kc == KC - 1,
                                   [wgt[1], r_xT[g]], [r_ps[bg]])
                            for fc in range(KC):
                                mm(bank(bp_)[:, 0:tn], wp[0][:, fc, cc * 128:(cc + 1) * 128], ybr[:, fc, t0:t0 + tn], fc == 0, fc == KC - 1,
                                   [wp[1], r_ybr[fc][g]], [r_ps[bp_]])
                            sg, rsg = sgb[g % 2]
                            act(sg[:, 0:tn], bank(bg)[:, 0:tn], AF.Sigmoid, [r_ps[bg], r_bpm], [rsg], bias=bpm[:, l, gblk, dm:dm + 1])
                            if first:
                                tt("dve", mrg[:, dm, t0:t0 + tn], bank(bp_)[:, 0:tn], sg[:, 0:tn], ALU.mult, [r_ps[bp_], rsg], [r_mrg[g]])
                            else:
                                tm, rtm = tmpb[g % 2]
                                tt("dve", tm[:, 0:tn], bank(bp_)[:, 0:tn], sg[:, 0:tn], ALU.mult, [r_ps[bp_], rsg], [rtm])
                                tt("dve", mrg[:, dm, t0:t0 + tn], mrg[:, dm, t0:t0 + tn], tm[:, 0:tn], ALU.add, [r_mrg[g], rtm], [r_mrg[g]])

            proj_merge("b", True)

            A.reset()
            cext = [A.f32([128, 2, 257], "cext%d" % i) for i in range(2)]
            cbf_t = [A.bf([128, 2, 258], "cbf%d" % i) for i in range(4)]
            qTg = [A.bf([128, 2, 512], "qTg%d" % i) for i in range(2)]
            kTg = [A.bf([128, 2, 512], "kTg%d" % i) for i in range(2)]
            vext = [A.bf([128, 258], "vext%d" % i) for i in range(6)]
            so_t = [A.f32([128, 256], "so%d" % i) for i in range(6)]
            sz_t = [A.f32([128, 256], "sza%d" % i) for i in range(6)]
            wk_t = [A.bf([128, 256], "wk%d" % i) for i in range(4)]
            sp_t = [A.bf([128, 128], "sp%d" % i) for i in range(4)]
            ya_t = [A.bf([128, 256], "ya%d" % i) for i in range(4)]
            qd_t = [A.bf([128, 2, 128], "qd%d" % i) for i in range(4)]
            dmx_t = [A.f32([128, 2], "dmx%d" % i) for i in range(4)]
            bst_t = [A.f32([128, 6], "bstA%d" % i) for i in range(4)]
            mv_t = [A.f32([128, 2], "mvA%d" % i) for i in range(4)]
            rstd_t = [A.f32([128, 1], "rstdA%d" % i) for i in range(4)]
            brow, r_brow = A.bf([128, 768], "browA")
            cs_t = [A.f32([128, 2, 257], "cs%d" % i) for i in range(4)]
            cdbs = [A.bf([128, 2, 258], "cdbs%d" % i) for i in range(2)]
            qtm = [A.bf([128, 2, 64], "qtm%d" % i) for i in range(2)]
            vms = [A.bf([64, 258], "vms%d" % i) for i in range(2)]
            nall, r_nall = A.f32([128, NB, 2], "nall")
            nout, r_nout = A.f32([128, NB, 2], "nout")
            memset("pool", brow[:], 0.0, [r_brow])
            for i in range(6):
                memset("pool", vext[i][0][:, 256:258], 1.0, [vext[i][1]])
            nrot = {"i": 0}

            def next_num():
                nrot["i"] = (nrot["i"] + 1) % 4
                return nrot["i"]

            def chunks_of(g):
                t0, tn = TGS[g]
                if g < 4:
                    return [(ci, 4 * g + ci, ci * 128, 128, t0 + ci * 128, (4 * g + ci) % 6) for ci in range(4)]
                return [(0, 16, 0, 64, t0, 16 % 6)]

            for h in range(NH):
                wq, wk_, wv_, wo_, wz_ = wneed(["q%d" % h, "k%d" % h, "v%d" % h, "o%d" % h, "za%d" % h])
                for i3, nm in enumerate(("v", "o", "za")):
                    P.dma("pool", brow[0:1, i3 * 256:(i3 + 1) * 256], b_in[l:l + 1, OFF[nm] + h * 256:OFF[nm] + (h + 1) * 256],
                          writes=[r_brow], sres=r_brow)
                ce, rce = cext[h % 2]
                memset("dve", ce[:], 0.0, [rce])

                def emit_qk(g, h=h, wq=wq, wk_=wk_):
                    t0, tn = TGS[g]
                    qt, rqt = qTg[g % 2]
                    kt, rkt = kTg[g % 2]
                    for dc in range(2):
                        bq, bk = next_mm(), next_mm()
                        for kc in range(KC):
                            mm(bank(bq)[:, 0:tn], wq[0][:, kc, dc * 128:(dc + 1) * 128], xT[:, kc, t0:t0 + tn], kc == 0, kc == KC - 1,
                               [wq[1], r_xT[g]], [r_ps[bq]])
                        for kc in range(KC):
                            mm(bank(bk)[:, 0:tn], wk_[0][:, kc, dc * 128:(dc + 1) * 128], xT[:, kc, t0:t0 + tn], kc == 0, kc == KC - 1,
                               [wk_[1], r_xT[g]], [r_ps[bk]])
                        act(qt[:, dc, 0:tn], bank(bq)[:, 0:tn], AF.Identity, [r_ps[bq], r_bpm], [rqt], bias=bpm[:, l, 0, 2 * h + dc:2 * h + dc + 1])
                        act(kt[:, dc, 0:tn], bank(bk)[:, 0:tn], AF.Identity, [r_ps[bk], r_kbias], [rkt],
                            bias=kbias[:, l, 2 * h + dc:2 * h + dc + 1], scale=1.0 / 16.0)

                def L1(g, sel, wv_=wv_, wo_=wo_, wz_=wz_):
                    for (ci, ti, c0, cn, tk0, bi) in chunks_of(g):
                        if ci not in sel:
                            continue
                        ve, rve = vext[bi]
                        so, rso = so_t[bi]
                        sza, rsza = sz_t[bi]
                        bv, bo = next_mm(), next_mm()
                        for (o_ap, wt, i3, rb) in ((bank(bv)[0:cn, 0:256], wv_, 0, r_ps[bv]), (bank(bv)[0:cn, 256:512], wo_, 1, r_ps[bv]),
                                                   (bank(bo)[0:cn, 0:256], wz_, 2, r_ps[bo])):
                            for kc in range(KC):
                                mm(o_ap, xT[:, kc, tk0:tk0 + cn], wt[0][:, kc, :], kc == 0, False, [r_xT[g], wt[1]], [rb], sig=False)
                            mm(o_ap, onesrow[:, 0:cn], brow[:, i3 * 256:(i3 + 1) * 256], False, True, [r_onesrow, r_brow], [rb])
                        cp("act", ve[0:cn, 0:256], bank(bv)[0:cn, 0:256], [r_ps[bv]], [rve])
                        act(so[0:cn, :], bank(bv)[0:cn, 256:512], AF.Tanh, [r_ps[bv]], [rso], scale=0.5)
                        act(sza[0:cn, :], bank(bo)[0:cn, 0:256], AF.Tanh, [r_ps[bo]], [rsza], scale=0.5)
                        ts("dve", so[0:cn, :], so[0:cn, :], 0.5, ALU.mult, [rso], [rso], s2=0.5, op1=ALU.add)
                        stt(sza[0:cn, :], sza[0:cn, :], 1.0, bank(bo)[0:cn, 0:256], ALU.add, ALU.mult, [rsza, r_ps[bo]], [rsza])

                def L2(g, h=h):
                    qt, rqt = qTg[g % 2]
                    kt, rkt = kTg[g % 2]
                    msk, rmsk = (mask_f, r_mask) if g < 4 else (mask_s, r_mask_s)
                    for (ci, ti, c0, cn, tk0, bi) in chunks_of(g):
                        wkt, rwk = wk_t[ci]
                        spt, rsp = sp_t[ci]
                        bt = next_aux()
                        for dc in range(2):
                            tr(bank_bf(bt)[0:cn, dc * 128:(dc + 1) * 128], kt[:, dc, c0:c0 + cn], ident_b[:], [rkt, r_idb], [r_ps[bt]], dc == 1)
                        ts("dve", wkt[0:cn, :], bank_bf(bt)[0:cn, 0:256], wthr[0:cn, ti, h:h + 1], ALU.mult, [r_ps[bt], r_wthr], [rwk])
                        bs_ = next_aux()
                        for dc in range(2):
                            mm(bank(bs_)[0:cn, 0:cn], kt[:, dc, c0:c0 + cn], qt[:, dc, c0:c0 + cn], dc == 0, dc == 1, [rkt, rqt], [r_ps[bs_]])
                        stt(spt[0:cn, 0:cn], bank(bs_)[0:cn, 0:cn], wthr[0:cn, ti, h:h + 1], msk[0:cn, 0:cn], ALU.mult, ALU.mult,
                            [r_ps[bs_], r_wthr, rmsk], [rsp])
                        if g < 4 and ti > 0:
                            qd, rqd = qd_t[ci]
                            ts("dve", qd[:], qt[:, :, c0:c0 + cn], decbc[:, h * 32 + ti:h * 32 + ti + 1], ALU.mult, [rqt, r_decbc], [rqd])

                def L3(g, h=h, ce=ce, rce=rce):
                    qt, rqt = qTg[g % 2]
                    pre_num = {}
                    post = []

                    def emit_num(ch):
                        (ci, ti, c0, cn, tk0, bi) = ch
                        ve, rve = vext[bi]
                        spt, rsp = sp_t[ci]
                        bn_ = next_num()
                        first = (ti == 0)
                        mm(bank(bn_)[0:cn, 0:257], spt[0:cn, 0:cn], ve[0:cn, 0:257], True, first, [rsp, rve], [r_ps[bn_]], sig=first)
                        if not first:
                            qd, rqd = qd_t[ci]
                            cbf, r_cbf = cbf_t[(ti - 1) % 4]
                            for dc in range(2):
                                mm(bank(bn_)[0:cn, 0:257], qd[:, dc, :], cbf[:, dc, 0:257], False, dc == 1, [rqd, r_cbf], [r_ps[bn_]])
                        return bn_

                    if g < 4:
                        pre_num[0] = emit_num(chunks_of(g)[0])
                        for (ci, ti, c0, cn, tk0, bi) in chunks_of(g):
                            ve, rve = vext[bi]
                            wkt, rwk = wk_t[ci]
                            up = 6 if ci % 2 == 0 else 4
                            for dc in range(2):
                                mm(bank(up + dc)[:, 0:257], wkt[0:cn, dc * 128:(dc + 1) * 128], ve[0:cn, 0:257], True, True, [rwk, rve], [r_ps[up + dc]])
                            stt(ce[:], ce[:], decbc[:, h * 32 + ti:h * 32 + ti + 1], ps[up // 2][:, :].rearrange("p (a b) -> p a b", a=2)[:, :, 0:257],
                                ALU.mult, ALU.add, [rce, r_decbc, r_ps[up], r_ps[up + 1]], [rce])
                            if ti < 15:
                                cbf, r_cbf = cbf_t[ti % 4]
                                cp("act", cbf[:, :, 0:257], ce[:], [rce], [r_cbf])
                            else:
                                P.dma("sp", o_cp[l, h].rearrange("(dc p) e -> p dc e", p=128), ce[:, :, 0:256], reads=[rce], sres=rce, final=True)
                                P.dma("sp", o_np[l, h].rearrange("(dc p) -> p dc", p=128), ce[:, :, 256], reads=[rce], sres=rce, final=True)
                    for (ci, ti, c0, cn, tk0, bi) in chunks_of(g):
                        ve, rve = vext[bi]
                        so, rso = so_t[bi]
                        wkt, rwk = wk_t[ci]
                        spt, rsp = sp_t[ci]
                        dmx, r_dmx = dmx_t[ci]
                        if g < 4:
                            bn_ = pre_num[ci] if ci in pre_num else emit_num((ci, ti, c0, cn, tk0, bi))
                        else:
                            bn_ = next_num()
                            def load_cs(b_):
                                cs, rcs = cs_t[b_ % 4]
                                P.dma("act", cs[:, :, 0:256], st_c[l, b_, h].rearrange("(dc p) e -> p dc e", p=128), writes=[rcs], sres=rcs)
                            for dc in range(2):
                                P.dma("sp", nall[:, :, dc], st_n[l, :, h, dc * 128:(dc + 1) * 128].rearrange("b p -> p b"), writes=[r_nall], sres=r_nall)
                            load_cs(0)
                            load_cs(1)
                            mm(bank(bn_)[0:cn, 0:257], spt[0:cn, 0:cn], ve[0:cn, 0:257], True, False, [rsp, rve], [r_ps[bn_]], sig=False)
                            for b_ in range(NB):
                                if b_ + 2 < NB:
                                    load_cs(b_ + 2)
                                cs, rcs = cs_t[b_ % 4]
                                cb_, rcb = cdbs[b_ % 2]
                                qm, rqm = qtm[b_ % 2]
                                vm, rvm = vms[b_ % 2]
                                up = 6 if b_ % 2 == 0 else 4
                                cp("dve", cs[:, :, 256], nall[:, b_, :], [r_nall], [rcs])
                                cp("act", cb_[:, :, 0:257], cs[:], [rcs], [rcb])
                                stt(qm[:], qt[:, :, 0:64], decbc[:, h * 32 + 16 + b_:h * 32 + 17 + b_], bc_mid(colmask[:, b_, :], 2),
                                    ALU.mult, ALU.mult, [rqt, r_decbc, r_colmask], [rqm])
                                for dc in range(2):
                                    mm(bank(bn_)[0:cn, 0:257], qm[:, dc, :], cb_[:, dc, 0:257], False, (b_ == NB - 1 and dc == 1),
                                       [rqm, rcb], [r_ps[bn_]])
                                act(vm[:, 0:257], ve[0:64, 0:257], AF.Copy, [rve, r_rowmask], [rvm], scale=rowmask[:, b_:b_ + 1])
                                for dc in range(2):
                                    mm(bank(up + dc)[:, 0:257], wkt[0:64, dc * 128:(dc + 1) * 128], vm[:, 0:257], True, True, [rwk, rvm], [r_ps[up + dc]])
                                stt(cs[:], cs[:], decbc[:, h * 32 + 16 + b_:h * 32 + 17 + b_],
                                    ps[up // 2][:, :].rearrange("p (a b) -> p a b", a=2)[:, :, 0:257], ALU.mult, ALU.add,
                                    [rcs, r_decbc, r_ps[up], r_ps[up + 1]], [rcs])
                                P.dma("sp", o_cs[l, b_, h].rearrange("(dc p) e -> p dc e", p=128), cs[:, :, 0:256], reads=[rcs], sres=rcs, final=True)
                                cp("dve", nout[:, b_, :], cs[:, :, 256], [rcs], [r_nout])
                            for dc in range(2):
                                P.dma("sp", o_ns[l, :, h, dc * 128:(dc + 1) * 128].rearrange("b p -> p b"), nout[:, :, dc], reads=[r_nout], sres=r_nout, final=True)
                        post.append((ci, ti, cn, bi, bn_))
                    for (ci, ti, cn, bi, bn_) in post:
                        dmx, r_dmx = dmx_t[ci]
                        act(dmx[0:cn, 0:1], bank(bn_)[0:cn, 256:257], AF.Abs, [r_ps[bn_]], [r_dmx])
                    for (ci, ti, cn, bi, bn_) in post:
                        dmx, r_dmx = dmx_t[ci]
                        ts("dve", dmx[0:cn, 0:1], dmx[0:cn, 0:1], wthr[0:cn, ti, 4 + h:5 + h], ALU.max, [r_dmx, r_wthr], [r_dmx])
                    for (ci, ti, cn, bi, bn_) in post:
                        dmx, r_dmx = dmx_t[ci]
                        recip(dmx[0:cn, 1:2], dmx[0:cn, 0:1], [r_dmx], [r_dmx])
                    for (ci, ti, cn, bi, bn_) in post:
                        dmx, r_dmx = dmx_t[ci]
                        so, rso = so_t[bi]
                        stt(so[0:cn, :], bank(bn_)[0:cn, 0:256], dmx[0:cn, 1:2], so[0:cn, :], ALU.mult, ALU.mult, [r_ps[bn_], r_dmx, rso], [rso])

                def L4(g):
                    CH = chunks_of(g)
                    for (ci, ti, c0, cn, tk0, bi) in CH:
                        hs, rhs = so_t[bi]
                        bst, r_bst = bst_t[ci]
                        mv, r_mv = mv_t[ci]
                        bnstats(bst[0:cn, :], hs[0:cn, :], [rhs], [r_bst])
                    for (ci, ti, c0, cn, tk0, bi) in CH:
                        bst, r_bst = bst_t[ci]
                        mv, r_mv = mv_t[ci]
                        bnaggr(mv[0:cn, :], bst[0:cn, :], [r_bst], [r_mv])
                    for (ci, ti, c0, cn, tk0, bi) in CH:
                        mv, r_mv = mv_t[ci]
                        rstd, r_rstd = rstd_t[ci]
                        act(rstd[0:cn, :], mv[0:cn, 1:2], AF.Sqrt, [r_mv, r_eps], [r_rstd], bias=eps_t[0:cn, :])
                    for (ci, ti, c0, cn, tk0, bi) in CH:
                        hs, rhs = so_t[bi]
                        sza, rsza = sz_t[bi]
                        ya, rya = ya_t[ci]
                        mv, r_mv = mv_t[ci]
                        rstd, r_rstd = rstd_t[ci]
                        recip(rstd[0:cn, :], rstd[0:cn, :], [r_rstd], [r_rstd])
                    for (ci, ti, c0, cn, tk0, bi) in CH:
                        hs, rhs = so_t[bi]
                        sza, rsza = sz_t[bi]
                        mv, r_mv = mv_t[ci]
                        stt(hs[0:cn, :], hs[0:cn, :], mv[0:cn, 0:1], sza[0:cn, :], ALU.subtract, ALU.mult, [rhs, r_mv, rsza], [rhs])
                    for (ci, ti, c0, cn, tk0, bi) in CH:
                        hs, rhs = so_t[bi]
                        ya, rya = ya_t[ci]
                        rstd, r_rstd = rstd_t[ci]
                        act(ya[0:cn, :], hs[0:cn, :], AF.Copy, [rhs, r_rstd], [rya], scale=rstd[0:cn, 0:1])

                def L5(g, h=h):
                    for (ci, ti, c0, cn, tk0, bi) in chunks_of(g):
                        ya, rya = ya_t[ci]
                        bt2 = next_aux()
                        for dc in range(2):
                            tr(bank_bf(bt2)[:, dc * 128:dc * 128 + cn], ya[0:cn, dc * 128:(dc + 1) * 128], ident_b[0:cn, 0:cn], [rya, r_idb],
                               [r_ps[bt2]], dc == 1)
                        for dc in range(2):
                            fc = 2 * h + dc
                            act(ybr[:, fc, tk0:tk0 + cn], bank_bf(bt2)[:, dc * 128:dc * 128 + cn], AF.Copy, [r_ps[bt2], r_gpmh], [r_ybr[fc][g]],
                                scale=gpmh[:, l, fc:fc + 1])

                emit_qk(0)
                L1(0, (0, 1, 2, 3))
                for g in range(len(TGS)):
                    if g == 0 or g == 4:
                        P.mark("A_h%d_%s" % (h, "prompt" if g < 4 else "sample"))
                    L2(g)
                    L3(g)
                    if g + 1 < len(TGS):
                        emit_qk(g + 1)
                        L1(g + 1, (0, 1))
                    L4(g)
                    if g + 1 < len(TGS):
                        L1(g + 1, (2, 3))
                    L5(g)
            P.mark("A_end")
            proj_merge("a", False)

            A.reset()
            waT, r_wa = A.bf([128, 8, 128], "wa")
            wxT, r_wx = A.bf([128, 8, 128], "wx")
            stc, r_stc = A.f32([48, D], "stc")
            sth, r_sth = A.f32([16, D], "sth")
            stg_t, r_stg_t = A.f32([68, D], "stg_t")
            stg, r_stg = A.f32([128, KC, 68], "stg")
            hprev2 = [A.f32([128, 17], "hprev%d" % i) for i in range(2)]
            hist2 = [A.f32([128, 48], "hist%d" % i) for i in range(2)]
            ctail2 = [A.f32([128, 3], "ctail%d" % i) for i in range(2)]
            cbuf = [[A.f32([128, 515], "cb%d_%d" % (i, k)) for k in range(5)] for i in range(2)]
            xcb = [A.bf([128, 512], "xcb%d" % i) for i in range(2)]
            szc = [A.f32([128, 512], "szc%d" % i) for i in range(2)]
            P.dma("pool", waT[:], wa_d[l].rearrange("n i j -> i n j"), writes=[r_wa], sres=r_wa)
            P.dma("pool", wxT[:], wx_d[l].rearrange("n i j -> i n j"), writes=[r_wx], sres=r_wx)
            P.dma("sp", stc[:], st_conv[l], writes=[r_stc], sres=r_stc)
            P.dma("sp", sth[:], st_h[l], writes=[r_sth], sres=r_sth)
            for j in range(4):
                wxc, wzc = wneed(["xc%d" % j, "zc%d" % j])
                for cc in range(2):
                    c = 2 * j + cc
                    hprev, r_hprev = hprev2[cc]
                    hist, r_hist = hist2[cc]
                    ctail, r_ctail = ctail2[cc]
                    b = next_aux()
                    tr(bank(b)[:, 0:48], stc[:, c * 128:(c + 1) * 128], ident_f[0:48, 0:48], [r_stc, r_idf], [r_ps[b]], False)
                    tr(bank(b)[:, 48:64], sth[:, c * 128:(c + 1) * 128], ident_f[0:16, 0:16], [r_sth, r_idf], [r_ps[b]], True)
                    cp("dve", hprev[:, 1:17], bank(b)[:, 48:64], [r_ps[b]], [r_hprev])
                    cp("dve", hist[:], bank(b)[:, 0:48], [r_ps[b]], [r_hist])
                    memset("dve", ctail[:], 0.0, [r_ctail])
                for g, (t0, tn) in enumerate(TGS):
                    views = {}
                    for cc in range(2):
                        c = 2 * j + cc
                        hprev, r_hprev = hprev2[cc]
                        hist, r_hist = hist2[cc]
                        ctail, r_ctail = ctail2[cc]
                        (xp, rxp), (xc_, rxc), (ra, rra), (ib, rib), (t1, rt1) = cbuf[cc]
                        sz, rsz = szc[cc]
                        bx_, bz = next_mm(), next_mm()
                        for kc in range(KC):
                            mm(bank(bx_)[:, 0:tn], wxc[0][:, kc, cc * 128:(cc + 1) * 128], xT[:, kc, t0:t0 + tn], kc == 0, kc == KC - 1,
                               [wxc[1], r_xT[g]], [r_ps[bx_]])
                        for kc in range(KC):
                            mm(bank(bz)[:, 0:tn], wzc[0][:, kc, cc * 128:(cc + 1) * 128], xT[:, kc, t0:t0 + tn], kc == 0, kc == KC - 1,
                               [wzc[1], r_xT[g]], [r_ps[bz]])
                        if g < 4:
                            cp("dve", xp[:, 0:3], ctail[:], [r_ctail], [rxp])
                            act(xp[:, 3:3 + tn], bank(bx_)[:, 0:tn], AF.Identity, [r_ps[bx_], r_bpm], [rxp], bias=bpm[:, l, 8, c:c + 1])
                            if g < 3:
                                cp("dve", ctail[:], xp[:, tn:tn + 3], [rxp], [r_ctail])
                            else:
                                cp("dve", stg[:, c, 17:20], xp[:, tn:tn + 3], [rxp], [r_stg])
                            xp_ = xp
                            xpv = (lambda xp_: (lambda jj: xp_[:, jj:jj + 512]))(xp)
                            xcv = xc_[:, 0:tn]
                        else:
                            xp3 = xp[:, 0:112].rearrange("p (b k) -> p b k", k=7)
                            cp("dve", xp3[:, :, 0:3], hist[:].rearrange("p (b k) -> p b k", k=3), [r_hist], [rxp])
                            act(xp3[:, :, 3:7], bank(bx_)[:, 0:64].rearrange("p (b k) -> p b k", k=4), AF.Identity, [r_ps[bx_], r_bpm], [rxp],
                                bias=bpm[:, l, 8, c:c + 1])
                            cp("dve", stg[:, c, 20:68].rearrange("p (b k) -> p b k", k=3), xp3[:, :, 4:7], [rxp], [r_stg])
                            xpv = (lambda xp3: (lambda jj: xp3[:, :, jj:jj + 4]))(xp3)
                            xcv = xc_[:, 0:64].rearrange("p (b k) -> p b k", k=4)
                        act(sz[:, 0:tn], bank(bz)[:, 0:tn], AF.Silu, [r_ps[bz], r_bpm], [rsz], bias=bpm[:, l, 9, c:c + 1])
                        views[cc] = (xpv, xcv)
                    for cc in range(2):
                        c = 2 * j + cc
                        (xp, rxp), (xc_, rxc), (ra, rra), (ib, rib), (t1, rt1) = cbuf[cc]
                        xb, rxb = xcb[cc]
                        xpv, xcv = views[cc]
                        ts("dve", xcv, xpv(0), cw[:, l, c, 0:1], ALU.mult, [rxp, r_cw, r_cvec], [rxc], s2=cvec[:, 0, l, c:c + 1], op1=ALU.add)
                        for jj in range(1, 4):
                            stt(xcv, xpv(jj), cw[:, l, c, jj:jj + 1], xcv, ALU.mult, ALU.add, [rxp, r_cw, rxc], [rxc])
                        cp("dve", xb[:, 0:tn], xc_[:, 0:tn], [rxc], [rxb])
                    for cc in range(2):
                        c = 2 * j + cc
                        (xp, rxp), (xc_, rxc), (ra, rra), (ib, rib), (t1, rt1) = cbuf[cc]
                        xb, rxb = xcb[cc]
                        br_, bi_ = next_aux(), next_aux()
                        mm(bank(br_)[:, 0:tn], waT[:, c, :], xb[:, 0:tn], True, True, [r_wa, rxb], [r_ps[br_]])
                        mm(bank(bi_)[:, 0:tn], wxT[:, c, :], xb[:, 0:tn], True, True, [r_wx, rxb], [r_ps[bi_]])
                        act(ra[:, 0:tn], bank(br_)[:, 0:tn], AF.Sigmoid, [r_ps[br_], r_cvec], [rra], bias=cvec[:, 1, l, c:c + 1])
                        act(ib[:, 0:tn], bank(bi_)[:, 0:tn], AF.Sigmoid, [r_ps[bi_], r_cvec], [rib], bias=cvec[:, 2, l, c:c + 1])
                    for cc in range(2):
                        c = 2 * j + cc
                        (xp, rxp), (xc_, rxc), (ra, rra), (ib, rib), (t1, rt1) = cbuf[cc]
                        act(t1[:, 0:tn], ra[:, 0:tn], AF.Exp, [rra, r_cl], [rt1], scale=cl[:, 1, l, c:c + 1])
                        act(ra[:, 0:tn], ra[:, 0:tn], AF.Exp, [rra, r_cl], [rra], scale=cl[:, 0, l, c:c + 1])
                    for cc in range(2):
                        (xp, rxp), (xc_, rxc), (ra, rra), (ib, rib), (t1, rt1) = cbuf[cc]
                        act(t1[:, 0:tn], t1[:, 0:tn], AF.Sqrt, [rt1], [rt1], bias=1.0, scale=-1.0)
                    for cc in range(2):
                        c = 2 * j + cc
                        hprev, r_hprev = hprev2[cc]
                        (xp, rxp), (xc_, rxc), (ra, rra), (ib, rib), (t1, rt1) = cbuf[cc]
                        sz, rsz = szc[cc]
                        if g == 0:
                            memset("dve", t1[:, 0:1], 1.0, [rt1])
                        tt("dve", ib[:, 0:tn], ib[:, 0:tn], t1[:, 0:tn], ALU.mult, [rib, rt1], [rib])
                        tt("dve", ib[:, 0:tn], ib[:, 0:tn], xc_[:, 0:tn], ALU.mult, [rib, rxc], [rib])
                        if g < 4:
                            init = 0.0 if g == 0 else hprev[:, 0:1]
                            scan(t1[:, 0:tn], ra[:, 0:tn], ib[:, 0:tn], init, ALU.mult, ALU.add, [rra, rib, r_hprev], [rt1])
                            if g < 3:
                                cp("dve", hprev[:, 0:1], t1[:, tn - 1:tn], [rt1], [r_hprev])
                            else:
                                cp("dve", stg[:, c, 0:1], t1[:, tn - 1:tn], [rt1], [r_stg])
                        else:
                            ib3 = ib[:, 0:64].rearrange("p (b k) -> p b k", k=4)
                            ra3 = ra[:, 0:64].rearrange("p (b k) -> p b k", k=4)
                            tt("dve", hprev[:, 1:17], hprev[:, 1:17], ra3[:, :, 0], ALU.mult, [r_hprev, rra], [r_hprev])
                            tt("dve", ib3[:, :, 0], ib3[:, :, 0], hprev[:, 1:17], ALU.add, [rib, r_hprev], [rib])
                            memset("dve", ra3[:, :, 0], 0.0, [rra])
                            scan(t1[:, 0:64], ra[:, 0:64], ib[:, 0:64], 0.0, ALU.mult, ALU.add, [rra, rib], [rt1])
                            cp("dve", stg[:, c, 1:17], t1[:, 0:64].rearrange("p (b k) -> p b k", k=4)[:, :, 3], [rt1], [r_stg])
                        tt("dve", ybr[:, c, t0:t0 + tn], t1[:, 0:tn], sz[:, 0:tn], ALU.mult, [rt1, rsz], [r_ybr[c][g]])
            for half in range(2):
                b = next_aux()
                for jj in range(4):
                    c = half * 4 + jj
                    tr(bank(b)[0:68, jj * 128:(jj + 1) * 128], stg[:, c, :], ident_f[:], [r_stg, r_idf], [r_ps[b]], jj == 3)
                cp("dve", stg_t[:, half * 512:(half + 1) * 512], bank(b)[0:68, :], [r_ps[b]], [r_stg_t])
            P.dma("sp", o_convh[l], stg_t[:], reads=[r_stg_t], sres=r_stg_t, final=True)
            proj_merge("c", False)

            A.reset()
            lnbc, r_lnbc = A.f32([128, 2, D], "lnbcO")
            xr = [A.f32([128, D], "xr%d" % i) for i in range(4)]
            zt = [A.f32([128, D], "zt%d" % i) for i in range(6)]
            bstO = [A.f32([128, 12], "bstO%d" % i) for i in range(4)]
            mvO = [A.f32([128, 2], "mvO%d" % i) for i in range(4)]
            rstdO = [A.f32([128, 1], "rstdO%d" % i) for i in range(4)]
            P.dma("sp", lnbc[:].rearrange("p a b -> p (a b)"), pbc(fln_d[l].rearrange("a b -> (a b)"), 128), writes=[r_lnbc], sres=r_lnbc)
            wo = wneed(["wo0", "wo1", "wo2", "wo3"])
            for ti, (t0, tn) in enumerate(TTS):
                if ti < 4:
                    xa, rx = xr[ti % 4]
                    rsrc = [] if l == 0 else [r_xres[ti]]
                    P.dma("sp", xa[0:tn, :], xsrc[t0:t0 + tn, :], reads=rsrc, writes=[rx], sres=rx)
            pairs = [list(range(p, min(p + 2, 17))) for p in range(0, 17, 2)]

            def O_mm(pr_tiles):
                for ti in pr_tiles:
                    t0, tn = TTS[ti]
                    g = tok_group_of(t0)
                    xa, rx = xr[ti % 4]
                    za, rz = zt[ti % 6]
                    pr = (ti % 2) * 2
                    for j in range(4):
                        bq = pr + j // 2
                        o_ap = bank(bq)[0:tn, (j % 2) * 256:(j % 2) * 256 + 256]
                        for fc in range(KC):
                            mm(o_ap, mrg[:, fc, t0:t0 + tn], wo[j][0][:, fc, :], fc == 0, fc == KC - 1, [r_mrg[g], wo[j][1]], [r_ps[bq]])
                    stt(za[0:tn, :], xa[0:tn, :], ALPHA, ps[pr // 2][0:tn, :], ALU.mult, ALU.add, [rx, r_ps[pr], r_ps[pr + 1]], [rz])
                    if ti + 4 < 17:
                        t0n, tnn = TTS[ti + 4]
                        rsrc = [] if l == 0 else [r_xres[ti + 4]]
                        P.dma("sp", xa[0:tnn, :], xsrc[t0n:t0n + tnn, :], reads=rsrc, writes=[rx], sres=rx)

            def O_ln(pr_tiles):
                for ti in pr_tiles:
                    t0, tn = TTS[ti]
                    za, rz = zt[ti % 6]
                    bst, r_bst = bstO[ti % 4]
                    for hf in range(2):
                        bnstats(bst[0:tn, hf * 6:hf * 6 + 6], za[0:tn, hf * 512:(hf + 1) * 512], [rz], [r_bst])
                for ti in pr_tiles:
                    t0, tn = TTS[ti]
                    bst, r_bst = bstO[ti % 4]
                    mv, r_mv = mvO[ti % 4]
                    bnaggr(mv[0:tn, :], bst[0:tn, :], [r_bst], [r_mv])
                for ti in pr_tiles:
                    t0, tn = TTS[ti]
                    mv, r_mv = mvO[ti % 4]
                    rstd, r_rstd = rstdO[ti % 4]
                    act(rstd[0:tn, :], mv[0:tn, 1:2], AF.Sqrt, [r_mv, r_eps], [r_rstd], bias=eps_t[0:tn, :])
                for ti in pr_tiles:
                    t0, tn = TTS[ti]
                    rstd, r_rstd = rstdO[ti % 4]
                    recip(rstd[0:tn, :], rstd[0:tn, :], [r_rstd], [r_rstd])
                for ti in pr_tiles:
                    t0, tn = TTS[ti]
                    za, rz = zt[ti % 6]
                    mv, r_mv = mvO[ti % 4]
                    rstd, r_rstd = rstdO[ti % 4]
                    ts("dve", za[0:tn, :], za[0:tn, :], mv[0:tn, 0:1], ALU.subtract, [rz, r_mv, r_rstd], [rz], s2=rstd[0:tn, 0:1], op1=ALU.mult)
                for ti in pr_tiles:
                    t0, tn = TTS[ti]
                    za, rz = zt[ti % 6]
                    tt("dve", za[0:tn, :], za[0:tn, :], lnbc[0:tn, 0, :], ALU.mult, [rz, r_lnbc], [rz])
                for ti in pr_tiles:
                    t0, tn = TTS[ti]
                    za, rz = zt[ti % 6]
                    tt("dve", za[0:tn, :], za[0:tn, :], lnbc[0:tn, 1, :], ALU.add, [rz, r_lnbc], [rz])

            def O_out(pr_tiles):
                for ti in pr_tiles:
                    t0, tn = TTS[ti]
                    za, rz = zt[ti % 6]
                    if last or l == depth - 1:
                        P.dma("sp", y_all[t0:t0 + tn, :], za[0:tn, :], reads=[rz], sres=rz, final=True)
                    else:
                        P.dma("sp", xres[t0:t0 + tn, :], za[0:tn, :], reads=[rz], writes=[r_xres[ti]], sres=rz)
                        tile_to_xT(za, rz, ti)

            O_mm(pairs[0])
            for pi in range(len(pairs)):
                O_ln(pairs[pi])
                if pi + 1 < len(pairs):
                    O_mm(pairs[pi + 1])
                O_out(pairs[pi])
        P.finish()
        global _LAST_PROG
        _LAST_PROG = P
        with nc.allow_non_contiguous_dma(reason="small strided state columns"):
            P.emit()
    return nc


def _consts():
    ident = np.eye(128, dtype=np.float32)
    s = np.arange(128)
    mask = (s[:, None] <= s[None, :]).astype(np.float32)
    t = np.arange(64)
    mask_s = ((t[:, None] // 4 == t[None, :] // 4) & (t[:, None] <= t[None, :])).astype(np.float32)
    a0s = np.ones((4, 64), np.float32)
    a0s[:, ::4] = 0.0
    hmask = np.zeros((4, 4, 32), np.float32)
    for h in range(4):
        hmask[h, h, :] = 1.0
    colmask = np.zeros((128, NB, 64), np.float32)
    rowmask = np.zeros((64, NB), np.float32)
    for b in range(NB):
        colmask[:, b, 4 * b:4 * b + 4] = 1.0
        rowmask[4 * b:4 * b + 4, b] = 1.0
    onesrow = np.zeros((128, 128), np.float32)
    onesrow[0, :] = 1.0
    return dict(c_ident=ident, c_mask=mask, c_mask_s=mask_s, c_a0s=a0s, c_hmask=hmask.reshape(4, 128),
                c_colmask=colmask.reshape(128, NB * 64), c_rowmask=rowmask, c_onesrow=onesrow)


_NC_CACHE = {}


def kernel(x_prompt, x_sample, state_mlstm_c, state_mlstm_n, state_mlstm_m, state_lru_conv, state_lru_h,
           w_in, b_in, mlstm_norm_g, gmlp_ln_g, gmlp_ln_b, gmlp_ws, gmlp_bs, lru_conv_w, lru_conv_b,
           lru_wa, lru_ba, lru_wx, lru_bx, lru_lambda, w_proj_a, w_proj_b, w_proj_c, w_out, ln_g, ln_b, _depth=NL):
    f = lambda a: np.ascontiguousarray(np.asarray(a, dtype=np.float32))
    x_prompt, x_sample = f(x_prompt), f(x_sample)
    w_in, b_in = f(w_in), f(b_in)
    if _depth not in _NC_CACHE:
        _NC_CACHE[_depth] = build(_depth)
    nc = _NC_CACHE[_depth]
    bblocks = np.stack([b_in[:, OFF[n]:OFF[n] + D] for n in BLK], axis=1)
    bpm = bblocks.reshape(NL, 13, 8, 128).transpose(3, 0, 1, 2).reshape(128, NL * 13 * 8)
    bgate = np.stack([b_in[:, 5120:5124], b_in[:, 5124:5128]], axis=2).transpose(1, 0, 2).reshape(4, NL * 2)
    pm = lambda a: f(a).reshape(NL, 8, 128).transpose(2, 0, 1)
    gpm = pm(mlstm_norm_g).reshape(128, NL * 8)
    cw = f(lru_conv_w).reshape(NL, 4, 8, 128).transpose(3, 0, 2, 1).reshape(128, NL * 8 * 4)
    cvec = np.stack([pm(lru_conv_b), pm(lru_ba), pm(lru_bx), pm(lru_lambda)], axis=1).reshape(128, 4 * NL * 8)
    gln = np.stack([f(gmlp_ln_g), f(gmlp_ln_b)], axis=1)
    fln = np.stack([f(ln_g), f(ln_b)], axis=1)
    gws = f(gmlp_ws)
    gws_s = np.stack([np.stack([np.tile(gws[l, g, :4, :4].T, (16, 16)) for g in range(4)], axis=1) for l in range(NL)], 0)
    gws_s = gws_s.reshape(NL, 64, 4 * 64)
    gbs = f(gmlp_bs).reshape(NL, 4 * 128)
    gbs_s = np.stack([np.concatenate([np.tile(f(gmlp_bs)[l, g, :4], 16) for g in range(4)]) for l in range(NL)], 0)
    shared = dict(w_in=w_in, w_pa=f(w_proj_a), w_pb=f(w_proj_b), w_pc=f(w_proj_c), w_out=f(w_out), b_in=b_in,
                  bpm=np.ascontiguousarray(bpm), bgate=np.ascontiguousarray(bgate), gpm=np.ascontiguousarray(gpm),
                  cw=np.ascontiguousarray(cw), cvec=np.ascontiguousarray(cvec), lru_wa=f(lru_wa), lru_wx=f(lru_wx),
                  gln=np.ascontiguousarray(gln), fln=np.ascontiguousarray(fln), gws=gws,
                  gws_s=np.ascontiguousarray(gws_s), gbs=np.ascontiguousarray(gbs), gbs_s=np.ascontiguousarray(gbs_s))
    shared.update(_consts())
    sc, sn, sm = f(state_mlstm_c), f(state_mlstm_n), f(state_mlstm_m)
    sconv, sh = f(state_lru_conv), f(state_lru_h)
    in_maps = []
    for c in range(8):
        b0 = c * NB
        m = dict(shared)
        m["x_all"] = np.ascontiguousarray(np.concatenate([x_prompt[c], x_sample[b0:b0 + NB].reshape(NS, D)], axis=0))
        m["st_c"] = np.ascontiguousarray(sc[:, b0:b0 + NB])
        m["st_n"] = np.ascontiguousarray(sn[:, b0:b0 + NB])
        m["st_m"] = np.ascontiguousarray(sm[:, b0:b0 + NB].transpose(2, 0, 1).reshape(4, NL * NB))
        m["st_conv"] = np.ascontiguousarray(sconv[:, b0:b0 + NB].reshape(NL, 48, D))
        m["st_h"] = np.ascontiguousarray(sh[:, b0:b0 + NB])
        in_maps.append(m)
    res = run_bass_kernel_spmd(nc, in_maps, core_ids=list(range(8)))
    R = res.results
    g = lambda k, c: np.asarray(R[c][k], dtype=np.float32)
    y_p = np.stack([g("y_all", c)[:NTP] for c in range(8)], 0)
    y_s = np.concatenate([g("y_all", c)[NTP:].reshape(NB, 4, D) for c in range(8)], 0)
    c_p = np.stack([g("o_cp", c) for c in range(8)], 1)
    n_p = np.stack([g("o_np", c) for c in range(8)], 1)
    m_p = np.stack([g("o_mp", c).T for c in range(8)], 1)
    ch = [g("o_convh", c) for c in range(8)]
    conv_p = np.stack([x[:, 17:20] for x in ch], 1)
    h_p = np.stack([x[:, 0] for x in ch], 1)
    c_s = np.concatenate([g("o_cs", c) for c in range(8)], 1)
    n_s = np.concatenate([g("o_ns", c) for c in range(8)], 1)
    m_s = np.concatenate([g("o_ms", c).reshape(4, NL, NB).transpose(1, 2, 0) for c in range(8)], 1)
    conv_s = np.concatenate([x[:, 20:68].reshape(NL, NB, 3, D) for x in ch], 1)
    h_s = np.concatenate([x[:, 1:17] for x in ch], 1)
    v_s = np.concatenate([g("o_vs", c).reshape(NL, NB, 4, D) for c in range(8)], 1)
    outs = (y_p, y_s, c_p, n_p, m_p, conv_p, h_p, c_s, n_s, m_s, conv_s, h_s, v_s)
    return tuple(np.ascontiguousarray(o, dtype=np.float32) for o in outs)
```
